# Optimizing a Trainium2 kernel written in Bass

```python
import math
import jax
import jax.numpy as jnp
from jax import lax
import numpy as np


D_MODEL = 2048
BATCH = 2
SEQ = 8192
DEPTH = 4

CTX_LEN = 256
GRID_W = 64
HEAD_DIM = 64
ROPE_BASE = 10000.0
EPS = 1e-6
NEG_INF = -1e30

A_HEADS = D_MODEL // 512
A_WIDTH = A_HEADS * 2 * HEAD_DIM
B_HEADS = D_MODEL // 256
B_KV_HEADS = B_HEADS // 4
B_WIDTH = B_HEADS * HEAD_DIM
B_KV_WIDTH = B_KV_HEADS * HEAD_DIM
WINDOW = 128
BLOCK = 128
POOL_WINDOWS = (2, 4, 8, 16)
C_WIDTH = D_MODEL // 4
C_GROUP = C_WIDTH // len(POOL_WINDOWS)
D_WIDTH = D_MODEL - A_WIDTH - B_WIDTH - C_WIDTH
HY_EMB = 33
HY_HIDDEN = 64
HY_FAST_DECAY = 0.3
HY_SLOW_DECAY = 1.5
HY_TARGET = 1e-2

MIX_WIDTH = A_WIDTH + B_WIDTH + C_WIDTH + D_WIDTH
FFN_HIDDEN = ((8 * D_MODEL // 3 + 255) // 256) * 256

OFF_QA = 0
OFF_KA = OFF_QA + A_WIDTH
OFF_VA = OFF_KA + A_WIDTH
OFF_QB = OFF_VA + A_WIDTH
OFF_KB = OFF_QB + B_WIDTH
OFF_VB = OFF_KB + B_KV_WIDTH
OFF_POOL = OFF_VB + B_KV_WIDTH
OFF_HY = OFF_POOL + C_WIDTH
IN_COLS = OFF_HY + 3 * D_WIDTH

kernel_name = 'hybrid_parallel_mixer_dit_trunk'


def rms_norm(x, g):
    xf = x.astype(jnp.float32)
    y = xf * lax.rsqrt(jnp.mean(xf * xf, axis=-1, keepdims=True) + EPS)
    return (y * g.astype(jnp.float32)).astype(x.dtype)


def axial_rope_tables(L):
    rows = L // GRID_W
    row = jnp.repeat(jnp.arange(rows, dtype=jnp.float32), GRID_W)
    col = jnp.tile(jnp.arange(GRID_W, dtype=jnp.float32), rows)
    half = HEAD_DIM // 2
    inv = ROPE_BASE ** (-jnp.arange(0, half, 2, dtype=jnp.float32) / half)
    ang = jnp.concatenate([row[:, None] * inv, col[:, None] * inv], axis=-1)
    return jnp.cos(ang), jnp.sin(ang)


def apply_axial_rope(x, cos, sin):
    shp = x.shape
    nf = HEAD_DIM // 4
    xr = x.reshape(shp[0], shp[1], -1, 2, 2, nf).astype(jnp.float32)
    x1, x2 = xr[..., 0, :], xr[..., 1, :]
    c = cos.reshape(shp[1], 1, 2, nf)
    s = sin.reshape(shp[1], 1, 2, nf)
    out = jnp.stack([x1 * c - x2 * s, x2 * c + x1 * s], axis=-2)
    return out.reshape(shp).astype(x.dtype)


def dwconv3(u, w, b):
    up = jnp.pad(u, ((0, 0), (1, 1), (0, 0)))
    return up[:, :-2] * w[0] + up[:, 1:-1] * w[1] + up[:, 2:] * w[2] + b


def band_blocks(t, nb):
    L = t.shape[1]
    tp = jnp.pad(t, [(0, 0), (BLOCK, BLOCK)] + [(0, 0)] * (t.ndim - 2))
    parts = [tp[:, i * BLOCK: i * BLOCK + L].reshape((t.shape[0], nb, BLOCK) + t.shape[2:]) for i in range(3)]
    return jnp.concatenate(parts, axis=2)


def diff_attention(qa, ka, va, qa_c, ka_c, va_c, lam, sub_g, sub_scale):
    B, L = qa.shape[:2]
    nb = L // BLOCK
    k_all = jnp.concatenate([ka_c, ka], axis=1)
    v_all = jnp.concatenate([va_c, va], axis=1)

    def attend(q, k, v):
        s = jnp.einsum('bqhid,bkhid->bhiqk', q, k).astype(jnp.float32) * (HEAD_DIM ** -0.5)
        p = jax.nn.softmax(s, axis=-1)
        a = p[:, :, 0] - lam * p[:, :, 1]
        o = jnp.einsum('bhqk,bkhe->bqhe', a.astype(v.dtype), v)
        return rms_norm(o, sub_g) * sub_scale

    q_blocks = jnp.moveaxis(qa.reshape((B, nb, BLOCK) + qa.shape[2:]), 1, 0)
    o = lax.map(lambda qb: attend(qb, k_all, v_all), q_blocks)
    o = jnp.moveaxis(o, 0, 1).reshape(B, L, A_WIDTH)
    o_c = None
    if qa_c is not None:
        o_c = attend(qa_c, ka_c, va_c).reshape(B, qa_c.shape[1], A_WIDTH)
    return o, o_c


def window_attention(qb, kb, vb, qb_c, kb_c, vb_c, sink):
    B, L = qb.shape[:2]
    C = kb_c.shape[1]
    nb = L // BLOCK
    G = B_HEADS // B_KV_HEADS
    scale = HEAD_DIM ** -0.5
    sink_f = sink.astype(jnp.float32).reshape(B_KV_HEADS, G)

    def with_sink(s):
        sb = jnp.broadcast_to(sink_f[:, :, None, None], s.shape[:-1] + (1,))
        return sb

    qblk = qb.reshape(B, nb, BLOCK, B_KV_HEADS, G, HEAD_DIM)
    kband = band_blocks(kb, nb)
    vband = band_blocks(vb, nb)
    qpos = jnp.arange(L).reshape(nb, BLOCK)
    kpos = jnp.arange(nb)[:, None] * BLOCK - BLOCK + jnp.arange(3 * BLOCK)[None, :]
    valid = (kpos[:, None, :] >= 0) & (kpos[:, None, :] < L) & (jnp.abs(kpos[:, None, :] - qpos[:, :, None]) <= WINDOW)
    s_loc = jnp.einsum('bnqkgd,bnjkd->bnkgqj', qblk, kband).astype(jnp.float32) * scale
    s_loc = jnp.where(valid[None, :, None, None], s_loc, NEG_INF)
    s_ctx = jnp.einsum('bnqkgd,bjkd->bnkgqj', qblk, kb_c).astype(jnp.float32) * scale
    p = jax.nn.softmax(jnp.concatenate([with_sink(s_loc), s_ctx, s_loc], axis=-1), axis=-1)
    o = (jnp.einsum('bnkgqj,bjkd->bnqkgd', p[..., 1:1 + C].astype(vb.dtype), vb_c)
         + jnp.einsum('bnkgqj,bnjkd->bnqkgd', p[..., 1 + C:].astype(vb.dtype), vband))
    o = o.reshape(B, L, B_WIDTH)
    o_c = None
    if qb_c is not None:
        qc = qb_c.reshape(B, C, B_KV_HEADS, G, HEAD_DIM)
        s_c = jnp.einsum('bqkgd,bjkd->bkgqj', qc, kb_c).astype(jnp.float32) * scale
        p_c = jax.nn.softmax(jnp.concatenate([with_sink(s_c), s_c], axis=-1), axis=-1)
        o_c = jnp.einsum('bkgqj,bjkd->bqkgd', p_c[..., 1:].astype(vb_c.dtype), vb_c).reshape(B, C, B_WIDTH)
    return o, o_c


def pool_mixer(u, w_lin, ls):
    B, L, _ = u.shape
    cs = jnp.concatenate([jnp.zeros_like(u[:, :1], dtype=jnp.float32),
                          jnp.cumsum(u.astype(jnp.float32), axis=1)], axis=1)
    t = jnp.arange(L)
    outs = []
    for g, w in enumerate(POOL_WINDOWS):
        lo = w // 2
        hi = w - 1 - lo
        start = jnp.clip(t - lo, 0, L)
        end = jnp.clip(t + hi + 1, 0, L)
        csg = cs[..., g * C_GROUP:(g + 1) * C_GROUP]
        mean = (csg[:, end] - csg[:, start]) / (end - start).astype(jnp.float32)[None, :, None]
        outs.append(mean.astype(u.dtype) - u[..., g * C_GROUP:(g + 1) * C_GROUP])
    d = jnp.stack(outs, axis=2)
    y = jnp.einsum('blgc,gcd->blgd', d, w_lin).reshape(B, L, C_WIDTH)
    return y * ls


def hyena_kernel(L, w1, b1, w2, b2, w3, freq):
    t01 = jnp.linspace(0.0, 1.0, L, dtype=jnp.float32)[:, None]
    bands = (HY_EMB - 1) // 2
    w_ang = 2.0 * math.pi * jnp.arange(L, dtype=jnp.float32)[:, None] / L
    f = jnp.linspace(1e-4, bands - 1, bands, dtype=jnp.float32)[None, :]
    z = jnp.concatenate([t01, jnp.cos(f * w_ang), -jnp.sin(f * w_ang)], axis=-1)
    h = jnp.sin(freq * (z @ w1 + b1))
    h = jnp.sin(freq * (h @ w2 + b2))
    h = (h @ w3).astype(jnp.float32).reshape(L, 2, D_WIDTH)
    deltas = jnp.linspace(math.log(HY_TARGET) / HY_FAST_DECAY, math.log(HY_TARGET) / HY_SLOW_DECAY, D_WIDTH, dtype=jnp.float32)
    h = h * jnp.exp(-t01 * jnp.abs(deltas)[None, :])[:, None, :]
    kern = jnp.concatenate([h[:, 0], jnp.zeros((1, D_WIDTH), jnp.float32), h[:0:-1, 1]], axis=0)
    return kern * lax.rsqrt(jnp.sum(kern * kern, axis=0, keepdims=True) + EPS)


def long_conv(u, kern):
    L = u.shape[1]
    uf = jnp.fft.rfft(u.astype(jnp.float32), n=2 * L, axis=1)
    kf = jnp.fft.rfft(kern, n=2 * L, axis=0)
    return jnp.fft.irfft(uf * kf[None], n=2 * L, axis=1)[:, :L].astype(u.dtype)


def hyena_mixer(u, conv_w, conv_b, kern, bias):
    u = dwconv3(u, conv_w, conv_b)
    v, x1, x2 = jnp.split(u, 3, axis=-1)
    z = v * x1
    z = long_conv(z, kern) + z * bias
    return x2 * z


def conv_ffn(h, w_up, conv_w, conv_b, w_down):
    u = dwconv3(h @ w_up, conv_w, conv_b)
    gate, up = jnp.split(u, 2, axis=-1)
    return (jax.nn.silu(gate) * up) @ w_down


def setup_inputs(seed: int = 0) -> dict:
    key = jax.random.key(seed)
    ks = jax.random.split(key, 29)

    def nrm(k, shape, scale):
        return jax.random.normal(k, shape, jnp.float32) * scale

    return {
        'x': nrm(ks[0], (BATCH, SEQ, D_MODEL), 1.0),
        'c': nrm(ks[1], (BATCH, D_MODEL), 1.0),
        'ctx': nrm(ks[2], (BATCH, CTX_LEN, D_MODEL), 1.0),
        'c_ctx': nrm(ks[3], (D_MODEL,), 1.0),
        'w_mod': nrm(ks[4], (DEPTH, D_MODEL, 6 * D_MODEL), 0.5 * D_MODEL ** -0.5),
        'b_mod': nrm(ks[5], (DEPTH, 6 * D_MODEL), 0.01),
        'norm1_g': 1.0 + nrm(ks[6], (DEPTH, D_MODEL), 0.05),
        'norm2_g': 1.0 + nrm(ks[7], (DEPTH, D_MODEL), 0.05),
        'w_in': nrm(ks[8], (DEPTH, D_MODEL, IN_COLS), D_MODEL ** -0.5),
        'w_out': nrm(ks[9], (DEPTH, MIX_WIDTH, D_MODEL), MIX_WIDTH ** -0.5),
        'qk_gain': 1.0 + nrm(ks[10], (DEPTH, 4, HEAD_DIM), 0.05),
        'diff_lam': nrm(ks[11], (DEPTH, 4, HEAD_DIM), 0.1),
        'diff_subln': 1.0 + nrm(ks[12], (DEPTH, 2 * HEAD_DIM), 0.05),
        'win_sink': nrm(ks[13], (DEPTH, B_HEADS), 1.0),
        'pool_w': nrm(ks[14], (DEPTH, len(POOL_WINDOWS), C_GROUP, C_GROUP), C_GROUP ** -0.5),
        'pool_scale': 1.0 + nrm(ks[15], (DEPTH, C_WIDTH), 0.1),
        'hy_conv_w': nrm(ks[16], (DEPTH, 3, 3 * D_WIDTH), 3 ** -0.5),
        'hy_conv_b': nrm(ks[17], (DEPTH, 3 * D_WIDTH), 0.01),
        'hy_w1': nrm(ks[18], (DEPTH, HY_EMB, HY_HIDDEN), HY_EMB ** -0.5),
        'hy_b1': nrm(ks[19], (DEPTH, HY_HIDDEN), 0.1),
        'hy_w2': nrm(ks[20], (DEPTH, HY_HIDDEN, HY_HIDDEN), HY_HIDDEN ** -0.5),
        'hy_b2': nrm(ks[21], (DEPTH, HY_HIDDEN), 0.1),
        'hy_w3': nrm(ks[22], (DEPTH, HY_HIDDEN, 2 * D_WIDTH), HY_HIDDEN ** -0.5),
        'hy_freq': 1.0 + nrm(ks[23], (DEPTH, HY_HIDDEN), 0.1),
        'hy_bias': nrm(ks[24], (DEPTH, D_WIDTH), 0.5),
        'ffn_w_in': nrm(ks[25], (DEPTH, D_MODEL, 2 * FFN_HIDDEN), D_MODEL ** -0.5),
        'ffn_conv_w': nrm(ks[26], (DEPTH, 3, 2 * FFN_HIDDEN), 3 ** -0.5),
        'ffn_conv_b': nrm(ks[27], (DEPTH, 2 * FFN_HIDDEN), 0.01),
        'ffn_w_out': nrm(ks[28], (DEPTH, FFN_HIDDEN, D_MODEL), FFN_HIDDEN ** -0.5),
    }


def reference(x, c, ctx, c_ctx, w_mod, b_mod, norm1_g, norm2_g, w_in, w_out, qk_gain, diff_lam, diff_subln,
              win_sink, pool_w, pool_scale, hy_conv_w, hy_conv_b, hy_w1, hy_b1, hy_w2, hy_b2, hy_w3, hy_freq,
              hy_bias, ffn_w_in, ffn_conv_w, ffn_conv_b, ffn_w_out):
    B, L, _ = x.shape
    C = ctx.shape[1]
    cos, sin = axial_rope_tables(L)
    for l in range(DEPTH):
        last = l == DEPTH - 1
        wl = w_in[l]
        mod = jax.nn.silu(c) @ w_mod[l] + b_mod[l]
        mod_c = jax.nn.silu(c_ctx) @ w_mod[l] + b_mod[l]
        sh1, sc1, g1, sh2, sc2, g2 = jnp.split(mod[:, None, :], 6, axis=-1)
        csh1, csc1, cg1, csh2, csc2, cg2 = jnp.split(mod_c, 6, axis=-1)

        h = rms_norm(x, norm1_g[l]) * (1.0 + sc1) + sh1
        hc = rms_norm(ctx, norm1_g[l]) * (1.0 + csc1) + csh1
        p = h @ wl
        if last:
            pkv_a = hc @ wl[:, OFF_KA:OFF_QB]
            pkv_b = hc @ wl[:, OFF_KB:OFF_POOL]
            ka_c_raw, va_c_raw = pkv_a[..., :A_WIDTH], pkv_a[..., A_WIDTH:]
            kb_c_raw, vb_c_raw = pkv_b[..., :B_KV_WIDTH], pkv_b[..., B_KV_WIDTH:]
            pc = None
        else:
            pc = hc @ wl
            ka_c_raw, va_c_raw = pc[..., OFF_KA:OFF_VA], pc[..., OFF_VA:OFF_QB]
            kb_c_raw, vb_c_raw = pc[..., OFF_KB:OFF_VB], pc[..., OFF_VB:OFF_POOL]

        qa = apply_axial_rope(rms_norm(p[..., OFF_QA:OFF_KA].reshape(B, L, A_HEADS, 2, HEAD_DIM), qk_gain[l, 0]), cos, sin)
        ka = apply_axial_rope(rms_norm(p[..., OFF_KA:OFF_VA].reshape(B, L, A_HEADS, 2, HEAD_DIM), qk_gain[l, 1]), cos, sin)
        va = p[..., OFF_VA:OFF_QB].reshape(B, L, A_HEADS, 2 * HEAD_DIM)
        ka_c = rms_norm(ka_c_raw.reshape(B, C, A_HEADS, 2, HEAD_DIM), qk_gain[l, 1])
        va_c = va_c_raw.reshape(B, C, A_HEADS, 2 * HEAD_DIM)
        qa_c = None if last else rms_norm(pc[..., OFF_QA:OFF_KA].reshape(B, C, A_HEADS, 2, HEAD_DIM), qk_gain[l, 0])
        lam_p = diff_lam[l].astype(jnp.float32)
        lambda_init = 0.8 - 0.6 * math.exp(-0.3 * l)
        lam = jnp.exp(jnp.sum(lam_p[0] * lam_p[1])) - jnp.exp(jnp.sum(lam_p[2] * lam_p[3])) + lambda_init
        o_a, o_a_c = diff_attention(qa, ka, va, qa_c, ka_c, va_c, lam, diff_subln[l], 1.0 - lambda_init)

        qb = apply_axial_rope(rms_norm(p[..., OFF_QB:OFF_KB].reshape(B, L, B_HEADS, HEAD_DIM), qk_gain[l, 2]), cos, sin)
        kb = apply_axial_rope(rms_norm(p[..., OFF_KB:OFF_VB].reshape(B, L, B_KV_HEADS, HEAD_DIM), qk_gain[l, 3]), cos, sin)
        vb = p[..., OFF_VB:OFF_POOL].reshape(B, L, B_KV_HEADS, HEAD_DIM)
        kb_c = rms_norm(kb_c_raw.reshape(B, C, B_KV_HEADS, HEAD_DIM), qk_gain[l, 3])
        vb_c = vb_c_raw.reshape(B, C, B_KV_HEADS, HEAD_DIM)
        qb_c = None if last else rms_norm(pc[..., OFF_QB:OFF_KB].reshape(B, C, B_HEADS, HEAD_DIM), qk_gain[l, 2])
        o_b, o_b_c = window_attention(qb, kb, vb, qb_c, kb_c, vb_c, win_sink[l])

        o_c = pool_mixer(p[..., OFF_POOL:OFF_HY], pool_w[l], pool_scale[l])

        kern_lat = hyena_kernel(L, hy_w1[l], hy_b1[l], hy_w2[l], hy_b2[l], hy_w3[l], hy_freq[l])
        o_d = hyena_mixer(p[..., OFF_HY:], hy_conv_w[l], hy_conv_b[l], kern_lat, hy_bias[l])

        x = x + g1 * (jnp.concatenate([o_a, o_b, o_c, o_d], axis=-1) @ w_out[l])
        h2 = rms_norm(x, norm2_g[l]) * (1.0 + sc2) + sh2
        x = x + g2 * conv_ffn(h2, ffn_w_in[l], ffn_conv_w[l], ffn_conv_b[l], ffn_w_out[l])

        if not last:
            o_c_c = pool_mixer(pc[..., OFF_POOL:OFF_HY], pool_w[l], pool_scale[l])
            kern_ctx = hyena_kernel(C, hy_w1[l], hy_b1[l], hy_w2[l], hy_b2[l], hy_w3[l], hy_freq[l])
            o_d_c = hyena_mixer(pc[..., OFF_HY:], hy_conv_w[l], hy_conv_b[l], kern_ctx, hy_bias[l])
            ctx = ctx + cg1 * (jnp.concatenate([o_a_c, o_b_c, o_c_c, o_d_c], axis=-1) @ w_out[l])
            hc2 = rms_norm(ctx, norm2_g[l]) * (1.0 + csc2) + csh2
            ctx = ctx + cg2 * conv_ffn(hc2, ffn_w_in[l], ffn_conv_w[l], ffn_conv_b[l], ffn_w_out[l])
    return x
```

```python
import math
import numpy as np
import time
from contextlib import ExitStack
import concourse.bass as bass
import concourse.mybir as mybir
from concourse.bass_utils import run_bass_kernel_spmd

F32 = mybir.dt.float32
BF16 = mybir.dt.bfloat16
AF = mybir.ActivationFunctionType
ALU = mybir.AluOpType
AX = mybir.AxisListType


class Buf:
    __slots__ = ("name", "t", "w", "r", "sem", "dcount")

    def __init__(self, name, t):
        self.name = name
        self.t = t
        self.w = {}
        self.r = {}
        self.sem = None
        self.dcount = 0

    def __getitem__(self, idx):
        return self.t[idx]


class Eng:
    def __init__(self, fw, name, eng, sem, kind):
        self.fw = fw
        self.name = name
        self.eng = eng
        self.sem = sem
        self.kind = kind
        self.count = 0
        self.seen = {}

    def _wait(self, deps):
        for key, (sem, val) in deps.items():
            if self.seen.get(key, 0) >= val:
                continue
            if sem is self.sem and self.kind == 'pe':
                continue
            self.eng.wait_ge(sem, val)
            self.seen[key] = val

    def _deps(self, outs, ins):
        deps = {}
        for b in ins:
            for k, (s, v) in b.w.items():
                if deps.get(k, (None, 0))[1] < v:
                    deps[k] = (s, v)
        for b in outs:
            for d in (b.w, b.r):
                for k, (s, v) in d.items():
                    if deps.get(k, (None, 0))[1] < v:
                        deps[k] = (s, v)
        return deps

    def op(self, inst_fn, outs, ins):
        self._wait(self._deps(outs, ins))
        inst = inst_fn(self.eng)
        self.count += 1
        inst.then_inc(self.sem, 1)
        key = id(self.sem)
        tok = (self.sem, self.count)
        for b in ins:
            b.r[key] = tok
        for b in outs:
            b.w = {key: tok}
            b.r = {}
        return tok

    def dma(self, out_buf, out_ap, in_buf, in_ap, **kw):
        self._wait(self._deps([out_buf], [in_buf]))
        if out_buf.sem is None:
            out_buf.sem = self.fw.new_sem("d_" + out_buf.name)
        inst = self.eng.dma_start(out=out_ap, in_=in_ap, **kw)
        out_buf.dcount += 16
        inst.then_inc(out_buf.sem, 16)
        key = id(out_buf.sem)
        tok = (out_buf.sem, out_buf.dcount)
        in_buf.r[key] = tok
        out_buf.w = {key: tok}
        out_buf.r = {}
        return tok

    def wait_buf(self, b):
        self._wait(dict(b.w))


class FW:
    def __init__(self, name="k"):
        self.nc = bass.Bass("TRN2", target_bir_lowering=False)
        self.es = ExitStack()
        self.nsem = 0
        self.block = None
        self.all_bufs = []
        self.scope = None

    def new_sem(self, name):
        self.nsem += 1
        return self.es.enter_context(self.nc.semaphore(name + "_%d" % self.nsem))

    def dram(self, name, shape, dtype, kind):
        t = self.nc.dram_tensor(name, list(shape), dtype, kind=kind)
        b = Buf(name, t.ap())
        self.all_bufs.append(b)
        return b

    def sbuf(self, name, shape, dtype):
        t = (self.scope or self.es).enter_context(self.nc.sbuf_tensor(name, list(shape), dtype))
        b = Buf(name, t)
        self.all_bufs.append(b)
        return b

    def psum(self, name, shape, dtype=F32):
        t = (self.scope or self.es).enter_context(self.nc.psum_tensor(name, list(shape), dtype))
        b = Buf(name, t)
        self.all_bufs.append(b)
        return b

    def engines(self):
        nc = self.nc
        self.pe = Eng(self, "pe", nc.tensor, self.new_sem("pe"), 'pe')
        self.act = Eng(self, "act", nc.scalar, self.new_sem("act"), 'act')
        self.dve = Eng(self, "dve", nc.vector, self.new_sem("dve"), 'dve')
        self.pool = Eng(self, "pool", nc.gpsimd, self.new_sem("pool"), 'pool')
        self.sp = Eng(self, "sp", nc.sync, self.new_sem("sp"), 'sp')
        return self.pe, self.act, self.dve, self.pool, self.sp

    def push_scope(self):
        assert self.scope is None
        self.scope = ExitStack()

    def pop_scope(self):
        fw_barrier(self)
        self.scope.close()
        self.scope = None

    def close(self):
        self.es.close()


def sub_bufs(parent, aps, prefix):
    return [Buf("%s%d" % (prefix, i), ap) for i, ap in enumerate(aps)]


def fw_barrier(fw, bufs=()):
    engs = [fw.pe, fw.act, fw.dve, fw.pool, fw.sp]
    toks = {}
    for e in engs:
        if e.count > 0:
            toks[id(e.sem)] = (e.sem, e.count)
    for b in fw.all_bufs:
        if b.sem is not None and b.dcount > 0:
            toks[id(b.sem)] = (b.sem, b.dcount)
    for e in engs:
        e._wait(dict(toks))


D = 2048
KC = D // 128
EPS = 1e-6


def load_consts(fw, sp):
    ones = fw.sbuf("ones", [128, 128], F32)
    fw.dve.op(lambda e: e.memset(ones[:], 1.0), [ones], [])
    return ones


def norm_mod_phase(fw, xT, hT, tiles, vecs, ones, nm):
    pe, act, dve, pool, sp = fw.pe, fw.act, fw.dve, fw.pool, fw.sp
    xts = [fw.sbuf("%s_xt%d" % (nm, i), [128, KC, 512], F32) for i in range(2)]
    sqs = [fw.sbuf("%s_sq%d" % (nm, i), [128, KC, 512], F32) for i in range(1)]
    rstd = [fw.sbuf("%s_rstd%d" % (nm, i), [128, 512], F32) for i in range(2)]
    ps = [fw.psum("%s_ps%d" % (nm, i), [128, 512], F32) for i in range(2)]
    xv = xT.t.rearrange("k p t -> p k t")
    for i, (t0, n, A, sh) in enumerate(tiles):
        xt, sq, rs, p = xts[i % 2], sqs[0], rstd[i % 2], ps[i % 2]
        sp.dma(xt, xt[:, :, 0:n], xT, xv[:, :, t0:t0 + n])
        act.op(lambda e: e.activation(sq[:, :, 0:n], xt[:, :, 0:n], AF.Square), [sq], [xt])
        for kc in range(KC):
            pe.op(lambda e: e.matmul(p[:, 0:n], ones[:], sq[:, kc, 0:n], start=(kc == 0), stop=(kc == KC - 1)), [p], [ones, sq])
        dve.op(lambda e: e.tensor_scalar(rs[:, 0:n], p[:, 0:n], 1.0 / D, EPS, ALU.mult, ALU.add), [rs], [p])
        act.op(lambda e: e.activation(rs[:, 0:n], rs[:, 0:n], AF.Sqrt), [rs], [rs])
        dve.op(lambda e: e.reciprocal(rs[:, 0:n], rs[:, 0:n]), [rs], [rs])
        for kc in range(KC):
            dve.op(lambda e: e.tensor_tensor(sq[:, kc, 0:n], xt[:, kc, 0:n], rs[:, 0:n], ALU.mult), [sq], [xt, rs])
        for kc in range(KC):
            act.op(lambda e: e.activation(hT[:, kc, t0:t0 + n], sq[:, kc, 0:n], AF.Identity,
                                          bias=sh[:, kc:kc + 1], scale=A[:, kc:kc + 1]), [hT], [sq, A, sh])


def linear_phase(fw, hT, KCn, w, ncoltiles, tiles, epilogue, nm, nbuf=3):
    pe, pool = fw.pe, fw.pool
    wts = [fw.sbuf("%s_w%d" % (nm, i), [128, KCn, 256], BF16) for i in range(nbuf)]
    ps = [fw.psum("%s_lp%d" % (nm, i), [128, 512], F32) for i in range(4)]
    cnt = 0
    for ct in range(ncoltiles):
        wt = wts[ct % nbuf]
        pool.dma(wt, wt[:], w, w.t[ct])
        for ti, (t0, n) in enumerate(tiles):
            for half in range(2):
                p = ps[cnt % 4]
                cnt += 1
                for kc in range(KCn):
                    pe.op(lambda e: e.matmul(p[:, 0:n], wt[:, kc, half * 128:(half + 1) * 128], hT[:, kc, t0:t0 + n],
                                             start=(kc == 0), stop=(kc == KCn - 1)), [p], [wt, hT])
                epilogue(ct * 2 + half, ti, t0, n, p)


def build_k1(T_lat=2048, T_ctx=64, NCOL=4352):
    fw = FW()
    T = T_lat + T_ctx
    xT = fw.dram("xT", [KC, 128, T], F32, "ExternalInput")
    vec = fw.dram("vec", [128, 5, KC], F32, "ExternalInput")
    w = fw.dram("w", [NCOL // 256, 128, KC, 256], F32, "ExternalInput")
    pT = fw.dram("pT", [NCOL // 128, 128, T], F32, "ExternalOutput")
    fw.engines()
    pe, act, dve, pool, sp = fw.pe, fw.act, fw.dve, fw.pool, fw.sp
    ones = load_consts(fw, sp)
    vt = fw.sbuf("vt", [128, 5, KC], F32)
    sp.dma(vt, vt[:], vec, vec[:])
    A_lat = fw.sbuf("A_lat", [128, KC], F32)
    A_ctx = fw.sbuf("A_ctx", [128, KC], F32)
    sh_lat = fw.sbuf("sh_lat", [128, KC], F32)
    sh_ctx = fw.sbuf("sh_ctx", [128, KC], F32)
    dve.op(lambda e: e.scalar_tensor_tensor(A_lat[:], vt[:, 1, :], 1.0, vt[:, 0, :], ALU.add, ALU.mult), [A_lat], [vt])
    dve.op(lambda e: e.scalar_tensor_tensor(A_ctx[:], vt[:, 3, :], 1.0, vt[:, 0, :], ALU.add, ALU.mult), [A_ctx], [vt])
    dve.op(lambda e: e.tensor_copy(sh_lat[:], vt[:, 2, :]), [sh_lat], [vt])
    dve.op(lambda e: e.tensor_copy(sh_ctx[:], vt[:, 4, :]), [sh_ctx], [vt])
    hT = fw.sbuf("hT", [128, KC, T], BF16)
    tiles = [(t0, 512, A_lat, sh_lat) for t0 in range(0, T_lat, 512)]
    if T_ctx:
        tiles.append((T_lat, T_ctx, A_ctx, sh_ctx))
    norm_mod_phase(fw, xT, hT, tiles, None, ones, "n1")
    ots = [fw.sbuf("ot%d" % i, [128, 512], F32) for i in range(4)]
    st = {"i": 0}

    def epi(ci, ti, t0, n, p):
        ot = ots[st["i"] % 4]
        if st["i"] % 2 == 0:
            dve.op(lambda e: e.tensor_copy(ot[:, 0:n], p[:, 0:n]), [ot], [p])
        else:
            act.op(lambda e: e.activation(ot[:, 0:n], p[:, 0:n], AF.Copy), [ot], [p])
        st["i"] += 1
        sp.dma(pT, pT.t[ci, :, t0:t0 + n], ot, ot[:, 0:n])

    linear_phase(fw, hT, KC, w, NCOL // 256, [(t[0], t[1]) for t in tiles], epi, "l1")
    sp.wait_buf(pT)
    fw.close()
    return fw.nc


def tile_w(w):
    K, N = w.shape
    return np.ascontiguousarray(w.reshape(K // 128, 128, N // 256, 256).transpose(2, 1, 0, 3))


def vec_pk(v):
    return np.ascontiguousarray(v.reshape(KC, 128).T)


D = 2048
KC = 16
NCOLS = 1536


def build_k0(NL=4):
    fw = FW()
    cT = fw.dram("cT", [128, KC, 3], F32, "ExternalInput")
    w = fw.dram("w", [NL, 3, 128, KC, 512], F32, "ExternalInput")
    bm = fw.dram("bm", [NL, NCOLS], F32, "ExternalInput")
    mod = fw.dram("mod", [NL, 3, NCOLS], F32, "ExternalOutput")
    fw.engines()
    pe, act, dve, pool, sp = fw.pe, fw.act, fw.dve, fw.pool, fw.sp
    ct = fw.sbuf("ct", [128, KC, 3], F32)
    sp.dma(ct, ct[:], cT, cT[:])
    st = fw.sbuf("st", [128, KC, 3], F32)
    act.op(lambda e: e.activation(st[:], ct[:], AF.Silu), [st], [ct])
    wts = [fw.sbuf("wt%d" % i, [128, KC, 512], F32) for i in range(2)]
    bts = [fw.sbuf("bt%d" % i, [3, 512], F32) for i in range(2)]
    ots = [fw.sbuf("ot%d" % i, [3, 512], F32) for i in range(2)]
    ps = [fw.psum("ps%d" % i, [128, 512], F32) for i in range(2)]
    i = 0
    for l in range(NL):
        for t in range(3):
            wt, bt, ot, p = wts[i % 2], bts[i % 2], ots[i % 2], ps[i % 2]
            i += 1
            (sp if i % 2 == 0 else act).dma(wt, wt[:], w, w.t[l, t])
            bsrc = bass.AP(tensor=bm.t.tensor, offset=l * NCOLS + t * 512, ap=[[0, 3], [1, 512]])
            sp.dma(bt, bt[:], bm, bsrc)
            for k in range(KC):
                pe.op(lambda e: e.matmul(p[0:3, :], st[:, k, :], wt[:, k, :], start=(k == 0), stop=(k == KC - 1)), [p], [st, wt])
            dve.op(lambda e: e.tensor_tensor(ot[:], p[0:3, :], bt[:], ALU.add), [ot], [p, bt])
            sp.dma(mod, mod.t[l, :, t * 512:(t + 1) * 512], ot, ot[:])
    sp.wait_buf(mod)
    fw.close()
    return fw.nc


def k0_inputs(core, c, c_ctx, w_mod, b_mod):
    NL = w_mod.shape[0]
    cs = np.stack([c[0], c[1], c_ctx], axis=1)
    cT = np.ascontiguousarray(cs.reshape(KC, 128, 3).transpose(1, 0, 2))
    cols = slice(core * NCOLS, (core + 1) * NCOLS)
    w = w_mod[:, :, cols].reshape(NL, KC, 128, 3, 512).transpose(0, 3, 2, 1, 4)
    return {"cT": cT, "w": np.ascontiguousarray(w), "bm": np.ascontiguousarray(b_mod[:, cols])}


EPS = 1e-6
CTX = 256


def qknorm_rope(fw, xT, xsT, CS, gcol, dst, col_map, tiles, blockones, nm, inv_n):
    pe, act, dve, pool, sp = fw.pe, fw.act, fw.dve, fw.pool, fw.sp
    gbuf, c0 = gcol
    if "qk_scr" not in fw.__dict__:
        fw.qk_scr = dict(
            xt=[fw.sbuf("qk_x%d" % i, [128, 512], F32) for i in range(2)],
            xs=[fw.sbuf("qk_xs%d" % i, [128, 512], F32) for i in range(2)],
            ct=[fw.sbuf("qk_c%d" % i, [128, 512], F32) for i in range(2)],
            st=[fw.sbuf("qk_s%d" % i, [128, 512], F32) for i in range(2)],
            sq=[fw.sbuf("qk_sq%d" % i, [128, 512], F32) for i in range(2)],
            rs=[fw.sbuf("qk_rs%d" % i, [128, 512], F32) for i in range(2)],
            ps=[fw.psum("qk_ps%d" % i, [128, 512], F32) for i in range(2)])
    xt, xs, ct, st, sq, rs, ps = (fw.qk_scr[k] for k in ("xt", "xs", "ct", "st", "sq", "rs", "ps"))
    for i, (x0, tb0, n) in enumerate(tiles):
        j = i % 2
        sp.dma(xt[j], xt[j][:, 0:n], xT, xT.t[:, x0:x0 + n])
        sp.dma(xs[j], xs[j][:, 0:n], xsT, xsT.t[:, x0:x0 + n])
        sp.dma(ct[j], ct[j][:, 0:n], CS, CS.t[0, :, tb0:tb0 + n])
        sp.dma(st[j], st[j][:, 0:n], CS, CS.t[1, :, tb0:tb0 + n])
        act.op(lambda e: e.activation(sq[j][:, 0:n], xt[j][:, 0:n], AF.Square), [sq[j]], [xt[j]])
        pe.op(lambda e: e.matmul(ps[j][:, 0:n], blockones[:], sq[j][:, 0:n], start=True, stop=True), [ps[j]], [blockones, sq[j]])
        dve.op(lambda e: e.tensor_scalar(rs[j][:, 0:n], ps[j][:, 0:n], inv_n, EPS, ALU.mult, ALU.add), [rs[j]], [ps[j]])
        act.op(lambda e: e.activation(rs[j][:, 0:n], rs[j][:, 0:n], AF.Sqrt), [rs[j]], [rs[j]])
        dve.op(lambda e: e.reciprocal(rs[j][:, 0:n], rs[j][:, 0:n]), [rs[j]], [rs[j]])
        dve.op(lambda e: e.scalar_tensor_tensor(ct[j][:, 0:n], xt[j][:, 0:n], gbuf[:, c0:c0 + 1], ct[j][:, 0:n], ALU.mult, ALU.mult),
               [ct[j]], [xt[j], gbuf])
        dve.op(lambda e: e.scalar_tensor_tensor(st[j][:, 0:n], xs[j][:, 0:n], gbuf[:, c0 + 1:c0 + 2], st[j][:, 0:n], ALU.mult, ALU.mult),
                [st[j]], [xs[j], gbuf])
        dve.op(lambda e: e.tensor_tensor(ct[j][:, 0:n], ct[j][:, 0:n], st[j][:, 0:n], ALU.add), [ct[j]], [st[j]])
        dve.op(lambda e: e.tensor_tensor(dst[:, x0:x0 + n], ct[j][:, 0:n], rs[j][:, 0:n], ALU.mult), [dst], [ct[j], rs[j]])


def build_k2a(LQ=8192):
    fw = FW()
    T = LQ + CTX
    NKC = T // 128
    qT = fw.dram("qT", [128, T], F32, "ExternalInput")
    qsT = fw.dram("qsT", [128, T], F32, "ExternalInput")
    kT = fw.dram("kT", [128, T], F32, "ExternalInput")
    ksT = fw.dram("ksT", [128, T], F32, "ExternalInput")
    v = fw.dram("v", [128, NKC, 128], F32, "ExternalInput")
    CS = fw.dram("CS", [2, 128, T], F32, "ExternalInput")
    sm = fw.dram("sm", [128, 8], F32, "ExternalInput")
    lamp = fw.dram("lamp", [1, 256], F32, "ExternalInput")
    oT = fw.dram("oT", [128, T], F32, "ExternalOutput")
    fw.engines()
    pe, act, dve, pool, sp = fw.pe, fw.act, fw.dve, fw.pool, fw.sp
    ones = fw.sbuf("ones", [128, 128], F32)
    dve.op(lambda e: e.memset(ones[:], 1.0), [ones], [])
    onesb = fw.sbuf("onesb", [128, 128], BF16)
    dve.op(lambda e: e.memset(onesb[:], 1.0), [onesb], [])
    blockones = fw.sbuf("blockones", [128, 128], F32)
    dve.op(lambda e: e.memset(blockones[:], 0.0), [blockones], [])
    dve.op(lambda e: e.memset(blockones[0:64, 0:64], 1.0), [blockones], [])
    dve.op(lambda e: e.memset(blockones[64:128, 64:128], 1.0), [blockones], [])
    smt = fw.sbuf("smt", [128, 8], F32)
    sp.dma(smt, smt[:], sm, sm[:])
    lt = fw.sbuf("lt", [1, 256], F32)
    sp.dma(lt, lt[:], lamp, lamp[:])
    lw = fw.sbuf("lw", [1, 128], F32)
    dve.op(lambda e: e.tensor_tensor(lw[:, 0:64], lt[:, 0:64], lt[:, 64:128], ALU.mult), [lw], [lt])
    dve.op(lambda e: e.tensor_tensor(lw[:, 64:128], lt[:, 128:192], lt[:, 192:256], ALU.mult), [lw], [lt])
    l2 = fw.sbuf("l2", [1, 4], F32)
    dve.op(lambda e: e.reduce_sum(l2[:, 0:1], lw[:, 0:64], AX.X), [l2], [lw])
    dve.op(lambda e: e.reduce_sum(l2[:, 1:2], lw[:, 64:128], AX.X), [l2], [lw])
    act.op(lambda e: e.activation(l2[:, 0:2], l2[:, 0:2], AF.Exp), [l2], [l2])
    dve.op(lambda e: e.tensor_tensor(l2[:, 2:3], l2[:, 1:2], l2[:, 0:1], ALU.subtract), [l2], [l2])
    dve.op(lambda e: e.tensor_tensor(l2[:, 2:3], l2[:, 2:3], smt[0:1, 5:6], ALU.subtract), [l2], [l2, smt])
    psl = fw.psum("psl", [128, 512], F32)
    pe.op(lambda e: e.matmul(psl[:, 0:1], ones[0:1, :], l2[0:1, 2:3], start=True, stop=True), [psl], [ones, l2])
    sc = fw.sbuf("sc", [128, 2], F32)
    dve.op(lambda e: e.tensor_copy(sc[:, 0:1], psl[:, 0:1]), [sc], [psl])
    dve.op(lambda e: e.tensor_tensor(sc[:, 1:2], smt[:, 4:5], smt[:, 6:7], ALU.mult), [sc], [smt])

    QT = fw.sbuf("QT", [128, T], BF16)
    KT = fw.sbuf("KT", [128, T], BF16)
    V = fw.sbuf("V", [128, NKC, 128], BF16)
    pool.dma(V, V[:], v, v[:])
    qtiles = [(t0, t0, 512) for t0 in range(0, LQ, 512)] + [(LQ, LQ, CTX)]
    ktiles = [(0, LQ, CTX)] + [(CTX + t0, t0, 512) for t0 in range(0, LQ, 512)]
    qknorm_rope(fw, qT, qsT, CS, (smt, 0), QT, None, qtiles, blockones, "qn", 1.0 / 64)
    qknorm_rope(fw, kT, ksT, CS, (smt, 2), KT, None, ktiles, blockones, "kn", 1.0 / 64)

    ps_s = [fw.psum("ps_s%d" % i, [128, 512], F32) for i in range(2)]
    ps_o = fw.psum("ps_o", [128, 512], F32)
    ps_z = fw.psum("ps_z", [128, 512], F32)
    pts = [fw.sbuf("pt%d" % i, [128, 512], BF16) for i in range(3)]
    om = [fw.sbuf("om%d" % i, [128, 512], F32) for i in range(2)]
    rz = fw.sbuf("rz", [128, 512], F32)
    osq = fw.sbuf("osq", [128, 512], F32)
    ors = fw.sbuf("ors", [128, 512], F32)
    ots = [fw.sbuf("ot%d" % i, [128, 512], F32) for i in range(2)]
    for qi, (q0, _, n) in enumerate(qtiles):
        nkc = NKC if q0 < LQ else CTX // 128
        for m in range(2):
            r0 = 64 * m
            seq = []
            for kc in range(nkc):
                seq.append(("s", kc))
                if kc >= 1:
                    seq.append(("av", kc - 1))
            seq.append(("av", nkc - 1))
            for kind, kc in seq:
                if kind == "s":
                    p = ps_s[kc % 2]
                    pe.op(lambda e: e.matmul(p[:, 0:n], KT[r0:r0 + 64, kc * 128:(kc + 1) * 128], QT[r0:r0 + 64, q0:q0 + n],
                                             start=True, stop=True), [p], [KT, QT])
                    pt = pts[kc % 3]
                    act.op(lambda e: e.activation(pt[:, 0:n], p[:, 0:n], AF.Exp, scale=0.125), [pt], [p])
                else:
                    pt = pts[kc % 3]
                    pe.op(lambda e: e.matmul(ps_o[:, 0:n], V[:, kc, :], pt[:, 0:n], start=(kc == 0), stop=(kc == nkc - 1)), [ps_o], [V, pt])
                    pe.op(lambda e: e.matmul(ps_z[:, 0:n], onesb[:], pt[:, 0:n], start=(kc == 0), stop=(kc == nkc - 1)), [ps_z], [onesb, pt])
            dve.op(lambda e: e.reciprocal(rz[:, 0:n], ps_z[:, 0:n]), [rz], [ps_z])
            dve.op(lambda e: e.tensor_tensor(om[m][:, 0:n], ps_o[:, 0:n], rz[:, 0:n], ALU.mult), [om[m]], [ps_o, rz])
        dve.op(lambda e: e.scalar_tensor_tensor(om[0][:, 0:n], om[1][:, 0:n], sc[:, 0:1], om[0][:, 0:n], ALU.mult, ALU.add), [om[0]], [om[1], sc])
        act.op(lambda e: e.activation(osq[:, 0:n], om[0][:, 0:n], AF.Square), [osq], [om[0]])
        pe.op(lambda e: e.matmul(psl[:, 0:n], ones[:], osq[:, 0:n], start=True, stop=True), [psl], [ones, osq])
        dve.op(lambda e: e.tensor_scalar(ors[:, 0:n], psl[:, 0:n], 1.0 / 128, EPS, ALU.mult, ALU.add), [ors], [psl])
        act.op(lambda e: e.activation(ors[:, 0:n], ors[:, 0:n], AF.Sqrt), [ors], [ors])
        dve.op(lambda e: e.reciprocal(ors[:, 0:n], ors[:, 0:n]), [ors], [ors])
        ot = ots[qi % 2]
        dve.op(lambda e: e.scalar_tensor_tensor(ot[:, 0:n], om[0][:, 0:n], sc[:, 1:2], ors[:, 0:n], ALU.mult, ALU.mult), [ot], [om[0], sc, ors])
        sp.dma(oT, oT.t[:, q0:q0 + n], ot, ot[:, 0:n])
    sp.wait_buf(oT)
    fw.close()
    return fw.nc


def swap_halves(xT):
    r = xT.reshape(-1, 2, 2, 16, xT.shape[-1])
    return np.ascontiguousarray(r[:, :, ::-1]).reshape(xT.shape)


def rope_tables(L, n_ctx, reps):
    GRID_W = 64
    rows = L // GRID_W
    row = np.repeat(np.arange(rows, dtype=np.float32), GRID_W)
    col = np.tile(np.arange(GRID_W, dtype=np.float32), rows)
    inv = (10000.0 ** (-np.arange(0, 32, 2, dtype=np.float32) / 32)).astype(np.float32)
    ang = np.concatenate([row[:, None] * inv, col[:, None] * inv], axis=-1)
    cos = np.cos(ang).astype(np.float32).reshape(L, 2, 16)
    sin = np.sin(ang).astype(np.float32).reshape(L, 2, 16)
    C = np.ones((2, 2, 16, L + n_ctx), np.float32)
    S = np.zeros((2, 2, 16, L + n_ctx), np.float32)
    C[:, 0, :, :L] = cos.transpose(1, 2, 0)
    C[:, 1, :, :L] = cos.transpose(1, 2, 0)
    S[:, 0, :, :L] = -sin.transpose(1, 2, 0)
    S[:, 1, :, :L] = sin.transpose(1, 2, 0)
    C = np.tile(C.reshape(64, -1), (reps, 1))
    S = np.tile(S.reshape(64, -1), (reps, 1))
    return np.ascontiguousarray(np.stack([C, S]))


def build_k2b(LQ=8192):
    fw = FW()
    T = LQ + CTX
    NKC = T // 128
    NB = LQ // 128
    qT = fw.dram("qT", [128, T], F32, "ExternalInput")
    qsT = fw.dram("qsT", [128, T], F32, "ExternalInput")
    kT = fw.dram("kT", [128, T], F32, "ExternalInput")
    ksT = fw.dram("ksT", [128, T], F32, "ExternalInput")
    v = fw.dram("v", [128, NKC, 64], F32, "ExternalInput")
    CS = fw.dram("CS", [2, 128, T], F32, "ExternalInput")
    sm = fw.dram("sm", [128, 8], F32, "ExternalInput")
    masks = fw.dram("masks", [128, 6, 512], F32, "ExternalInput")
    oT = fw.dram("oT", [2, 64, T], F32, "ExternalOutput")
    fw.engines()
    pe, act, dve, pool, sp = fw.pe, fw.act, fw.dve, fw.pool, fw.sp
    onesb = fw.sbuf("onesb", [128, 128], BF16)
    dve.op(lambda e: e.memset(onesb[:], 1.0), [onesb], [])
    blockones = fw.sbuf("blockones", [128, 128], F32)
    dve.op(lambda e: e.memset(blockones[:], 0.0), [blockones], [])
    dve.op(lambda e: e.memset(blockones[0:64, 0:64], 1.0), [blockones], [])
    dve.op(lambda e: e.memset(blockones[64:128, 64:128], 1.0), [blockones], [])
    smt = fw.sbuf("smt", [128, 8], F32)
    sp.dma(smt, smt[:], sm, sm[:])
    es = fw.sbuf("es", [128, 2], F32)
    act.op(lambda e: e.activation(es[:], smt[:, 4:6], AF.Exp), [es], [smt])
    mk = fw.sbuf("mk", [128, 6, 512], BF16)
    pool.dma(mk, mk[:], masks, masks[:])
    QT = fw.sbuf("QT", [128, T], BF16)
    KT = fw.sbuf("KT", [128, T], BF16)
    V = fw.sbuf("V", [128, NKC, 64], BF16)
    pool.dma(V, V[:], v, v[:])
    qtiles = [(t0, t0, 512) for t0 in range(0, LQ, 512)] + [(LQ, LQ, CTX)]
    ktiles = [(0, LQ, CTX)] + [(CTX + t0, t0, 512) for t0 in range(0, LQ, 512)]
    qknorm_rope(fw, qT, qsT, CS, (smt, 0), QT, None, qtiles, blockones, "qn", 1.0 / 64)
    qknorm_rope(fw, kT, ksT, CS, (smt, 2), KT, None, ktiles, blockones, "kn", 1.0 / 64)

    ps_s = [fw.psum("ps_s%d" % i, [128, 512], F32) for i in range(2)]
    ps_o = fw.psum("ps_o", [64, 512], F32)
    ps_z = fw.psum("ps_z", [64, 512], F32)
    pts = [fw.sbuf("pt%d" % i, [128, 512], BF16) for i in range(3)]
    rz = fw.sbuf("rz", [64, 512], F32)
    ots = [fw.sbuf("ot%d" % i, [64, 512], F32) for i in range(2)]
    cnt = 0
    for qi, (q0, _, n) in enumerate(qtiles):
        if q0 < LQ:
            n0 = q0 // 128
            chunks = [(0, None), (1, None)]
            for rel in range(-1, 5):
                j = n0 + rel
                if 0 <= j < NB:
                    chunks.append((CTX // 128 + j, rel + 1))
        else:
            chunks = [(0, None), (1, None)]
        for m in range(2):
            r0 = 64 * m
            seq = []
            for ci in range(len(chunks)):
                seq.append(("s", ci))
                if ci >= 1:
                    seq.append(("av", ci - 1))
            seq.append(("av", len(chunks) - 1))
            for kind, ci in seq:
                kc, mi = chunks[ci]
                if kind == "s":
                    p = ps_s[ci % 2]
                    pe.op(lambda e: e.matmul(p[:, 0:n], KT[r0:r0 + 64, kc * 128:(kc + 1) * 128], QT[r0:r0 + 64, q0:q0 + n],
                                             start=True, stop=True), [p], [KT, QT])
                    pt = pts[ci % 3]
                    act.op(lambda e: e.activation(pt[:, 0:n], p[:, 0:n], AF.Exp, scale=0.125), [pt], [p])
                    if mi is not None:
                        dve.op(lambda e: e.tensor_tensor(pt[:, 0:n], pt[:, 0:n], mk[:, mi, 0:n], ALU.mult), [pt], [pt, mk])
                else:
                    pt = pts[ci % 3]
                    last = ci == len(chunks) - 1
                    pe.op(lambda e: e.matmul(ps_o[:, 0:n], V[:, kc, :], pt[:, 0:n], start=(ci == 0), stop=last), [ps_o], [V, pt])
                    pe.op(lambda e: e.matmul(ps_z[:, 0:n], onesb[:, 0:64], pt[:, 0:n], start=(ci == 0), stop=last), [ps_z], [onesb, pt])
            dve.op(lambda e: e.tensor_scalar(rz[:, 0:n], ps_z[:, 0:n], es[0:64, m:m + 1], None, ALU.add), [rz], [ps_z, es])
            dve.op(lambda e: e.reciprocal(rz[:, 0:n], rz[:, 0:n]), [rz], [rz])
            ot = ots[cnt % 2]
            cnt += 1
            dve.op(lambda e: e.tensor_tensor(ot[:, 0:n], ps_o[:, 0:n], rz[:, 0:n], ALU.mult), [ot], [ps_o, rz])
            sp.dma(oT, oT.t[m, :, q0:q0 + n], ot, ot[:, 0:n])
    sp.wait_buf(oT)
    fw.close()
    return fw.nc


def band_masks():
    ki = np.arange(128)[:, None, None]
    rel = np.arange(-1, 5)[None, :, None]
    qq = np.arange(512)[None, None, :]
    return (np.abs(128 * rel + ki - qq) <= 128).astype(np.float32)


CTX = 256
POOL_WINDOWS = (2, 4, 8, 16)


def build_k2c(LQ=8192):
    fw = FW()
    T = LQ + CTX
    PADL = 8
    uT = fw.dram("uT", [128, T], F32, "ExternalInput")
    wsel = fw.dram("wsel", [128, 16], F32, "ExternalInput")
    invc = fw.dram("invc", [128, T], F32, "ExternalInput")
    wl = fw.dram("wl", [128, 128], F32, "ExternalInput")
    ls = fw.dram("ls", [128, 1], F32, "ExternalInput")
    yT = fw.dram("yT", [128, T], F32, "ExternalOutput")
    fw.engines()
    pe, act, dve, pool, sp = fw.pe, fw.act, fw.dve, fw.pool, fw.sp
    wst = fw.sbuf("wst", [128, 16], F32)
    sp.dma(wst, wst[:], wsel, wsel[:])
    lst = fw.sbuf("lst", [128, 1], F32)
    sp.dma(lst, lst[:], ls, ls[:])
    wlt = fw.sbuf("wlt", [128, 128], BF16)
    pool.dma(wlt, wlt[:], wl, wl[:])
    TT = 2048
    ut = fw.sbuf("ut", [128, TT + 16], F32)
    ic = fw.sbuf("ic", [128, TT], F32)
    acc = fw.sbuf("acc", [128, TT], F32)
    db = fw.sbuf("db", [128, TT], BF16)
    ps = [fw.psum("ps%d" % i, [128, 512], F32) for i in range(2)]
    ots = [fw.sbuf("ot%d" % i, [128, 512], F32) for i in range(2)]
    cnt = 0
    for (s0, Ls) in ((0, LQ), (LQ, CTX)):
        tt_ = min(TT, Ls)
        for t0 in range(0, Ls, tt_):
            lo = max(t0 - 8, 0)
            hi = min(t0 + tt_ + 8, Ls)
            if lo > t0 - 8:
                dve.op(lambda e: e.memset(ut[:, 0:8], 0.0), [ut], [])
            if hi < t0 + tt_ + 8:
                dve.op(lambda e: e.memset(ut[:, tt_ + 8:tt_ + 16], 0.0), [ut], [])
            sp.dma(ut, ut[:, lo - (t0 - 8):hi - (t0 - 8)], uT, uT.t[:, s0 + lo:s0 + hi])
            sp.dma(ic, ic[:, 0:tt_], invc, invc.t[:, s0 + t0:s0 + t0 + tt_])
            dve.op(lambda e: e.tensor_scalar(acc[:, 0:tt_], ut[:, 0:tt_], wst[:, 0:1], None, ALU.mult), [acc], [ut, wst])
            for k in range(1, 16):
                dve.op(lambda e: e.scalar_tensor_tensor(acc[:, 0:tt_], ut[:, k:k + tt_], wst[:, k:k + 1], acc[:, 0:tt_], ALU.mult, ALU.add), [acc], [ut, wst])
            dve.op(lambda e: e.tensor_tensor(acc[:, 0:tt_], acc[:, 0:tt_], ic[:, 0:tt_], ALU.mult), [acc], [ic])
            dve.op(lambda e: e.tensor_tensor(db[:, 0:tt_], acc[:, 0:tt_], ut[:, 8:8 + tt_], ALU.subtract), [db], [acc, ut])
            for c0 in range(0, tt_, 512):
                n = min(512, tt_ - c0)
                p = ps[cnt % 2]
                ot = ots[cnt % 2]
                cnt += 1
                pe.op(lambda e: e.matmul(p[:, 0:n], wlt[:], db[:, c0:c0 + n], start=True, stop=True), [p], [wlt, db])
                act.op(lambda e: e.activation(ot[:, 0:n], p[:, 0:n], AF.Copy, scale=lst[:, 0:1]), [ot], [p, lst])
                sp.dma(yT, yT.t[:, s0 + t0 + c0:s0 + t0 + c0 + n], ot, ot[:, 0:n])
    sp.wait_buf(yT)
    fw.close()
    return fw.nc


def pool_consts(g, LQ):
    w = POOL_WINDOWS[g]
    lo = w // 2
    hi = w - 1 - lo
    sel = np.zeros(16, np.float32)
    for s in range(-lo, hi + 1):
        sel[s + 8] = 1.0
    outs = []
    for Ls in (LQ, CTX):
        t = np.arange(Ls)
        start = np.clip(t - lo, 0, Ls)
        end = np.clip(t + hi + 1, 0, Ls)
        outs.append((1.0 / (end - start).astype(np.float32)).astype(np.float32))
    ic = np.concatenate(outs)
    return np.tile(sel[None], (128, 1)), np.ascontiguousarray(np.tile(ic[None], (128, 1)))


EPS = 1e-6
CTX = 256
HY_EMB = 33


def sin_act(fw, out_buf, out_ap, p, n, fb, scr):
    act, dve = fw.act, fw.dve
    s2, s4 = scr
    P = 64
    act.op(lambda e: e.activation(s2[0:P, 0:n], p[0:P, 0:n], AF.Sin, bias=fb[0:P, 1:2], scale=fb[0:P, 0:1]), [s2], [p, fb])
    act.op(lambda e: e.activation(s4[0:P, 0:n], p[0:P, 0:n], AF.Sin, bias=fb[0:P, 3:4], scale=fb[0:P, 2:3]), [s4], [p, fb])
    dve.op(lambda e: e.tensor_tensor(s4[0:P, 0:n], s4[0:P, 0:n], s4[0:P, 0:n], ALU.mult), [s4], [s4])
    dve.op(lambda e: e.tensor_scalar(s4[0:P, 0:n], s4[0:P, 0:n], -2.0, 1.0, ALU.mult, ALU.add), [s4], [s4])
    dve.op(lambda e: e.scalar_tensor_tensor(out_ap, s2[0:P, 0:n], 2.0, s4[0:P, 0:n], ALU.mult, ALU.mult), [out_buf], [s2, s4])


def filter_gen(fw, Lf, zf, t01, wts, KF, nm, scr):
    pe, act, dve, pool, sp = fw.pe, fw.act, fw.dve, fw.pool, fw.sp
    w1, w2, w3, fb1, fb2, ndelta = wts["w1"], wts["w2"], wts["w3"], wts["fb1"], wts["fb2"], wts["ndelta"]
    ps1, ps2, ps3, zt, h1, h2, s2, s4, tt, kt, ktb, sqt = scr
    nt = (Lf + 511) // 512
    part = fw.sbuf(nm + "_part", [128, 2 * nt], F32)
    dve.op(lambda e: e.memset(part[:], 0.0), [part], [])
    for di in range(2):
        for ti in range(nt):
            t0 = ti * 512
            n = min(512, Lf - t0)
            sp.dma(zt, zt[0:HY_EMB, 0:n], zf, zf.t[1 - di, :, t0:t0 + n])
            tsrc = bass.AP(tensor=t01.t.tensor, offset=(1 - di) * Lf + t0, ap=[[0, 128], [1, n]])
            sp.dma(tt, tt[:, 0:n], t01, tsrc)
            pe.op(lambda e: e.matmul(ps1[0:64, 0:n], w1[0:HY_EMB, :], zt[0:HY_EMB, 0:n], start=True, stop=True), [ps1], [w1, zt])
            sin_act(fw, h1, h1[0:64, 0:n], ps1, n, fb1, (s2, s4))
            pe.op(lambda e: e.matmul(ps2[0:64, 0:n], w2[0:64, :], h1[0:64, 0:n], start=True, stop=True), [ps2], [w2, h1])
            sin_act(fw, h2, h2[0:64, 0:n], ps2, n, fb2, (s2, s4))
            pe.op(lambda e: e.matmul(ps3[:, 0:n], w3[0:64, di, :], h2[0:64, 0:n], start=True, stop=True), [ps3], [w3, h2])
            act.op(lambda e: e.activation(tt[:, 0:n], tt[:, 0:n], AF.Exp, scale=ndelta[:, 0:1]), [tt], [tt, ndelta])
            dve.op(lambda e: e.tensor_tensor(kt[:, 0:n], ps3[:, 0:n], tt[:, 0:n], ALU.mult), [kt], [ps3, tt])
            if di == 1 and ti == 0:
                dve.op(lambda e: e.memset(kt[:, 0:1], 0.0), [kt], [])
            dve.op(lambda e: e.tensor_tensor(sqt[:, 0:n], kt[:, 0:n], kt[:, 0:n], ALU.mult), [sqt], [kt])
            dve.op(lambda e: e.reduce_sum(part[:, di * nt + ti:di * nt + ti + 1], sqt[:, 0:n], AX.X), [part], [sqt])
            act.op(lambda e: e.activation(ktb[:, 0:n], kt[:, 0:n], AF.Copy), [ktb], [kt])
            if di == 0:
                sp.dma(KF, KF.t[:, 1 + t0:1 + t0 + n], ktb, ktb[0:64, 0:n])
            elif ti == 0:
                sp.dma(KF, KF.t[:, Lf + 1:Lf + n], ktb, ktb[0:64, 1:n])
            else:
                sp.dma(KF, KF.t[:, Lf + t0:Lf + t0 + n], ktb, ktb[0:64, 0:n])
    scale = fw.sbuf(nm + "_scale", [128, 1], F32)
    dve.op(lambda e: e.reduce_sum(scale[:], part[:], AX.X), [scale], [part])
    dve.op(lambda e: e.tensor_scalar(scale[:], scale[:], EPS, None, ALU.add), [scale], [scale])
    act.op(lambda e: e.activation(scale[:], scale[:], AF.Sqrt), [scale], [scale])
    dve.op(lambda e: e.reciprocal(scale[:], scale[:]), [scale], [scale])
    return scale


def conv3(fw, dst, u, n, sm, part):
    dve = fw.dve
    c = 3 * part
    dve.op(lambda e: e.tensor_scalar(dst[:, 0:n], u[:, 1:n + 1], sm[:, c + 1:c + 2], sm[:, 9 + part:10 + part], ALU.mult, ALU.add), [dst], [u, sm])
    dve.op(lambda e: e.scalar_tensor_tensor(dst[:, 0:n], u[:, 0:n], sm[:, c:c + 1], dst[:, 0:n], ALU.mult, ALU.add), [dst], [u, sm])
    dve.op(lambda e: e.scalar_tensor_tensor(dst[:, 0:n], u[:, 2:n + 2], sm[:, c + 2:c + 3], dst[:, 0:n], ALU.mult, ALU.add), [dst], [u, sm])


def load_u_tile(fw, ut, uT, part, t0, n, Ls):
    sp, dve = fw.sp, fw.dve
    lo = max(t0 - 1, 0)
    hi = min(t0 + n + 1, Ls)
    if t0 == 0:
        dve.op(lambda e: e.memset(ut[:, 0:1], 0.0), [ut], [])
    if t0 + n == Ls:
        dve.op(lambda e: e.memset(ut[:, n + 1:n + 2], 0.0), [ut], [])
    sp.dma(ut, ut[:, lo - (t0 - 1):hi - (t0 - 1)], uT, uT.t[part, :, lo:hi])


def build_k2d(LQ=8192):
    fw = FW()
    NB = LQ // 128
    NBC = CTX // 128
    TT = min(1024, LQ)
    uT = fw.dram("uT", [3, 128, LQ], F32, "ExternalInput")
    ucT = fw.dram("ucT", [3, 128, CTX], F32, "ExternalInput")
    smd = fw.dram("sm", [128, 16], F32, "ExternalInput")
    w1d = fw.dram("w1", [HY_EMB, 64], F32, "ExternalInput")
    w2d = fw.dram("w2", [64, 64], F32, "ExternalInput")
    w3d = fw.dram("w3", [64, 2, 128], F32, "ExternalInput")
    fbd = fw.dram("fb", [64, 4], F32, "ExternalInput")
    zfL = fw.dram("zfL", [2, HY_EMB, LQ], F32, "ExternalInput")
    t01L = fw.dram("t01L", [2, LQ], F32, "ExternalInput")
    zfC = fw.dram("zfC", [2, HY_EMB, CTX], F32, "ExternalInput")
    t01C = fw.dram("t01C", [2, CTX], F32, "ExternalInput")
    oT = fw.dram("oT", [128, LQ], F32, "ExternalOutput")
    ocT = fw.dram("ocT", [128, CTX], F32, "ExternalOutput")
    KF = fw.dram("KF", [64, 2 * LQ], BF16, "Internal")
    KFC = fw.dram("KFC", [64, 2 * CTX], BF16, "Internal")
    fw.engines()
    pe, act, dve, pool, sp = fw.pe, fw.act, fw.dve, fw.pool, fw.sp

    identb = fw.sbuf("identb", [128, 128], BF16)
    identf = fw.sbuf("identf", [128, 128], F32)
    idd = fw.dram("ident", [128, 128], F32, "ExternalInput")
    sp.dma(identf, identf[:], idd, idd[:])
    dve.op(lambda e: e.tensor_copy(identb[:], identf[:]), [identb], [identf])
    antif = fw.sbuf("antif", [128, 128], F32)
    add = fw.dram("anti", [128, 128], F32, "ExternalInput")
    sp.dma(antif, antif[:], add, add[:])
    sm = fw.sbuf("smt", [128, 16], F32)
    sp.dma(sm, sm[:], smd, smd[:])
    w1 = fw.sbuf("w1s", [HY_EMB, 64], F32)
    sp.dma(w1, w1[:], w1d, w1d[:])
    w2 = fw.sbuf("w2s", [64, 64], F32)
    sp.dma(w2, w2[:], w2d, w2d[:])
    w3 = fw.sbuf("w3s", [64, 2, 128], F32)
    sp.dma(w3, w3[:], w3d, w3d[:])
    fb = fw.sbuf("fbs", [64, 4], F32)
    sp.dma(fb, fb[:], fbd, fbd[:])
    fb1 = fw.sbuf("fb1", [64, 4], F32)
    fb2 = fw.sbuf("fb2", [64, 4], F32)
    for dst, bc in ((fb1, 1), (fb2, 2)):
        dve.op(lambda e: e.tensor_scalar(dst[:, 0:1], fb[:, 0:1], 0.5, None, ALU.mult), [dst], [fb])
        dve.op(lambda e: e.tensor_scalar(dst[:, 2:3], fb[:, 0:1], 0.25, None, ALU.mult), [dst], [fb])
        dve.op(lambda e: e.tensor_tensor(dst[:, 1:2], dst[:, 0:1], fb[:, bc:bc + 1], ALU.mult), [dst], [dst, fb])
        dve.op(lambda e: e.tensor_tensor(dst[:, 3:4], dst[:, 2:3], fb[:, bc:bc + 1], ALU.mult), [dst], [dst, fb])
    ndelta = fw.sbuf("ndelta", [128, 1], F32)
    dve.op(lambda e: e.tensor_copy(ndelta[:], sm[:, 13:14]), [ndelta], [sm])
    wts = dict(w1=w1, w2=w2, w3=w3, fb1=fb1, fb2=fb2, ndelta=ndelta)
    scr = (fw.psum("fg_ps1", [128, 512], F32), fw.psum("fg_ps2", [128, 512], F32), fw.psum("fg_ps3", [128, 512], F32),
           fw.sbuf("fg_zt", [64, 512], F32), fw.sbuf("fg_h1", [64, 512], F32), fw.sbuf("fg_h2", [64, 512], F32),
           fw.sbuf("fg_s2", [64, 512], F32), fw.sbuf("fg_s4", [64, 512], F32), fw.sbuf("fg_tt", [128, 512], F32),
           fw.sbuf("fg_kt", [128, 512], F32), fw.sbuf("fg_ktb", [128, 512], BF16), fw.sbuf("fg_sq", [128, 512], F32))
    scaleL = filter_gen(fw, LQ, zfL, t01L, wts, KF, "fL", scr)
    scaleC = filter_gen(fw, CTX, zfC, t01C, wts, KFC, "fC", scr)

    Zt = fw.sbuf("Zt", [128, 64, NB, 2], BF16)
    Ztc = fw.sbuf("Ztc", [128, 64, NBC, 2], BF16)
    uts = [fw.sbuf("ut%d" % i, [128, TT + 2], F32) for i in range(3)]
    cv = [fw.sbuf("cv%d" % i, [128, TT], F32) for i in range(3)]
    zb = fw.sbuf("zb", [128, TT], BF16)
    pst = [fw.psum("pst%d" % i, [128, 512], BF16) for i in range(2)]

    def z_phase(src, Ls, Ztx, nblk):
        tt_ = min(TT, Ls)
        for t0 in range(0, Ls, tt_):
            for part in range(2):
                load_u_tile(fw, uts[part], src, part, t0, tt_, Ls)
                conv3(fw, cv[part], uts[part], tt_, sm, part)
            dve.op(lambda e: e.tensor_tensor(zb[:, 0:tt_], cv[0][:, 0:tt_], cv[1][:, 0:tt_], ALU.mult), [zb], [cv[0], cv[1]])
            for rb in range(tt_ // 128):
                r = t0 // 128 + rb
                p = pst[r % 2]
                pe.op(lambda e: e.transpose(p[:, 0:128], zb[:, rb * 128:(rb + 1) * 128], identb[:]), [p], [zb, identb])
                dst = Ztx[:, :, r, :].rearrange("p c b -> p b c")
                src_ap = p[:, 0:128].rearrange("p (b c) -> p b c", b=2)
                if r % 2 == 0:
                    dve.op(lambda e: e.tensor_copy(dst, src_ap), [Ztx], [p])
                else:
                    act.op(lambda e: e.activation(dst, src_ap, AF.Copy), [Ztx], [p])

    z_phase(uT, LQ, Zt, NB)
    z_phase(ucT, CTX, Ztc, NBC)

    W = (2 * NB - 1) * 128
    X0 = (NB - 1) * 128
    tbs = [fw.sbuf("tb%d" % i, [128, W], BF16) for i in range(2)]
    tbc = fw.sbuf("tbc", [128, 16, 3 * 128], BF16)
    Y = fw.sbuf("Y", [128, NB, 2, 64], F32)
    Yc = fw.sbuf("Yc", [128, NBC, 2, 64], F32)
    psy = [fw.psum("psy%d" % i, [128, 512], F32) for i in range(2)]
    kft = KF.t.tensor
    kfct = KFC.t.tensor
    for ch in range(64):
        tb = tbs[ch % 2]
        src = bass.AP(tensor=kft, offset=ch * 2 * LQ + 1, ap=[[1, 128], [1, W]])
        (sp if ch % 2 == 0 else act).dma(tb, tb[:], KF, src)
        p = psy[ch % 2]
        ds = [0] + [d for d in range(-(NB - 1), NB) if d != 0]
        for k, d in enumerate(ds):
            r0 = max(0, d)
            nb = NB - abs(d)
            pe.op(lambda e: e.matmul(p[:, r0 * 2:(r0 + nb) * 2], tb[:, X0 - 128 * d:X0 - 128 * d + 128],
                                     Zt[:, ch, r0 - d:r0 - d + nb, :].rearrange("p r b -> p (r b)"),
                                     start=(k == 0), stop=(k == len(ds) - 1), skip_group_check=True), [p], [tb, Zt])
        if ch % 2 == 0:
            dve.op(lambda e: e.tensor_copy(Y[:, :, :, ch], p[:, 0:NB * 2].rearrange('p (r b) -> p r b', b=2)), [Y], [p])
        else:
            act.op(lambda e: e.activation(Y[:, :, :, ch], p[:, 0:NB * 2].rearrange('p (r b) -> p r b', b=2), AF.Copy), [Y], [p])
    pc = psy[0]
    for g in range(4):
        src = bass.AP(tensor=kfct, offset=g * 16 * 2 * CTX + 1, ap=[[1, 128], [2 * CTX, 16], [1, 384]])
        sp.dma(tbc, tbc[:], KFC, src)
        for c16 in range(16):
            ch = g * 16 + c16
            o0 = ch * 4
            pe.op(lambda e: e.matmul(pc[:, o0:o0 + 4], tbc[:, c16, 128:256], Ztc[:, ch, :, :].rearrange("p r b -> p (r b)"),
                                     start=True, stop=False, skip_group_check=True), [pc], [tbc, Ztc])
            pe.op(lambda e: e.matmul(pc[:, o0 + 2:o0 + 4], tbc[:, c16, 0:128], Ztc[:, ch, 0, :],
                                     start=False, stop=False, skip_group_check=True), [pc], [tbc, Ztc])
            pe.op(lambda e: e.matmul(pc[:, o0:o0 + 2], tbc[:, c16, 256:384], Ztc[:, ch, 1, :],
                                     start=False, stop=True, skip_group_check=True), [pc], [tbc, Ztc])
    dve.op(lambda e: e.tensor_copy(Yc[:].rearrange("p r b c -> p c r b"), pc[:, 0:256].rearrange("p (c r b) -> p c r b", r=NBC, b=2)), [Yc], [pc])

    pso = [fw.psum("pso%d" % i, [128, 512], F32) for i in range(1)]
    ys = fw.sbuf("ys", [128, 512], F32)
    ots = [fw.sbuf("ot%d" % i, [128, 512], F32) for i in range(2)]

    def out_phase(src, Ls, Yx, scale, dstT):
        tt_ = min(TT, Ls)
        cnt = 0
        for t0 in range(0, Ls, tt_):
            for part in range(3):
                load_u_tile(fw, uts[part], src, part, t0, tt_, Ls)
                conv3(fw, cv[part], uts[part], tt_, sm, part)
            dve.op(lambda e: e.tensor_tensor(cv[0][:, 0:tt_], cv[0][:, 0:tt_], cv[1][:, 0:tt_], ALU.mult), [cv[0]], [cv[1]])
            for s0 in range(0, tt_, 512):
                ns = min(512, tt_ - s0)
                p = pso[0]
                for rb in range(ns // 128):
                    r = (t0 + s0) // 128 + rb
                    in_ap = Yx[:, r, :, :].rearrange("p b c -> p (b c)")
                    pe.op(lambda e: e.matmul(p[:, rb * 128:(rb + 1) * 128], in_ap, antif[:], start=True, stop=True), [p], [Yx, antif])
                act.op(lambda e: e.activation(ys[:, 0:ns], p[:, 0:ns], AF.Copy, scale=scale[:, 0:1]), [ys], [p, scale])
                dve.op(lambda e: e.scalar_tensor_tensor(ys[:, 0:ns], cv[0][:, s0:s0 + ns], sm[:, 12:13], ys[:, 0:ns], ALU.mult, ALU.add), [ys], [cv[0], sm])
                ot = ots[cnt % 2]
                cnt += 1
                dve.op(lambda e: e.tensor_tensor(ot[:, 0:ns], ys[:, 0:ns], cv[2][:, s0:s0 + ns], ALU.mult), [ot], [ys, cv[2]])
                sp.dma(dstT, dstT.t[:, t0 + s0:t0 + s0 + ns], ot, ot[:, 0:ns])

    out_phase(uT, LQ, Y, scaleL, oT)
    out_phase(ucT, CTX, Yc, scaleC, ocT)
    sp.wait_buf(oT)
    sp.wait_buf(ocT)
    fw.close()
    return fw.nc


def hy_feats(L):
    t01 = np.linspace(0.0, 1.0, L, dtype=np.float32)
    bands = (HY_EMB - 1) // 2
    w_ang = (2.0 * math.pi * np.arange(L, dtype=np.float32) / L).astype(np.float32)
    f = np.linspace(1e-4, bands - 1, bands, dtype=np.float32)
    ang = (f[None, :] * w_ang[:, None]).astype(np.float32)
    z = np.concatenate([t01[:, None], np.cos(ang), -np.sin(ang)], axis=-1).astype(np.float32)
    zf = np.stack([z.T, z[::-1].T])
    tt = np.stack([t01, t01[::-1]])
    return np.ascontiguousarray(zf), np.ascontiguousarray(tt)


def hy_ndelta(D_WIDTH=512):
    d = np.linspace(math.log(1e-2) / 0.3, math.log(1e-2) / 1.5, D_WIDTH, dtype=np.float32)
    return -np.abs(d)


def k2d_inputs(core, u_lat, u_ctx, conv_w, conv_b, w1, b1, w2, b2, w3, freq, bias, LQ):
    c0 = 64 * core
    DW = 512
    def pk(u, Ls):
        parts = []
        for part in range(3):
            cols = u[:, :, part * DW + c0: part * DW + c0 + 64]
            parts.append(cols.transpose(0, 2, 1).reshape(128, Ls))
        return np.ascontiguousarray(np.stack(parts))
    sm = np.zeros((128, 16), np.float32)
    for part in range(3):
        for tap in range(3):
            sm[:, part * 3 + tap] = np.tile(conv_w[tap, part * DW + c0: part * DW + c0 + 64], 2)
        sm[:, 9 + part] = np.tile(conv_b[part * DW + c0: part * DW + c0 + 64], 2)
    sm[:, 12] = np.tile(bias[c0:c0 + 64], 2)
    sm[:, 13] = np.tile(hy_ndelta()[c0:c0 + 64], 2)
    w3r = w3.reshape(64, 2, DW)[:, :, c0:c0 + 64]
    w3p = np.ascontiguousarray(np.concatenate([w3r, w3r], axis=2))
    fb = np.zeros((64, 4), np.float32)
    fb[:, 0] = freq; fb[:, 1] = b1; fb[:, 2] = b2
    zfL, t01L = hy_feats(LQ)
    zfC, t01C = hy_feats(CTX)
    return {"uT": pk(u_lat, LQ), "ucT": pk(u_ctx, CTX), "sm": sm, "w1": np.ascontiguousarray(w1), "w2": np.ascontiguousarray(w2),
            "w3": w3p, "fb": fb, "zfL": zfL, "t01L": t01L, "zfC": zfC, "t01C": t01C, "ident": np.eye(128, dtype=np.float32), "anti": np.ascontiguousarray(np.eye(128, dtype=np.float32)[::-1])}


D = 2048
KC = 16
EPS = 1e-6
FH = 5632
HC = FH // 128


def build_k3(TL=2048, TC=64):
    fw = FW()
    LW = TL + 2
    CW = TC + 2
    TW = LW + CW
    TO = TL + TC
    oT = fw.dram("oT", [KC, 128, TW], F32, "ExternalInput")
    xT = fw.dram("xT", [KC, 128, TW], F32, "ExternalInput")
    vec = fw.dram("vec", [128, 9, KC], F32, "ExternalInput")
    edge = fw.dram("edge", [128, 4], F32, "ExternalInput")
    w_out = fw.dram("w_out", [8, 128, KC, 256], F32, "ExternalInput")
    w_up = fw.dram("w_up", [2 * FH // 256, 128, KC, 256], F32, "ExternalInput")
    cwd = fw.dram("cw", [128, 4, 2 * HC], F32, "ExternalInput")
    w_dn = fw.dram("w_dn", [8, 128, HC, 256], F32, "ExternalInput")
    xo = fw.dram("xo", [KC, 128, TO], F32, "ExternalOutput")
    XN = fw.dram("XN", [KC, 128, TW], F32, "Internal")
    AT = fw.dram("AT", [HC, 128, TO], BF16, "Internal")
    fw.engines()
    pe, act, dve, pool, sp = fw.pe, fw.act, fw.dve, fw.pool, fw.sp
    ones = fw.sbuf("ones", [128, 128], F32)
    dve.op(lambda e: e.memset(ones[:], 1.0), [ones], [])
    vt = fw.sbuf("vt", [128, 9, KC], F32)
    sp.dma(vt, vt[:], vec, vec[:])
    eg = fw.sbuf("eg", [128, 4], F32)
    sp.dma(eg, eg[:], edge, edge[:])
    A_lat = fw.sbuf("A_lat", [128, KC], F32)
    A_ctx = fw.sbuf("A_ctx", [128, KC], F32)
    dve.op(lambda e: e.scalar_tensor_tensor(A_lat[:], vt[:, 3, :], 1.0, vt[:, 2, :], ALU.add, ALU.mult), [A_lat], [vt])
    dve.op(lambda e: e.scalar_tensor_tensor(A_ctx[:], vt[:, 5, :], 1.0, vt[:, 2, :], ALU.add, ALU.mult), [A_ctx], [vt])
    h2T = fw.sbuf("h2T", [128, KC, TW], BF16)

    fw.push_scope()
    wo = [fw.sbuf("wo%d" % i, [128, KC, 256], BF16) for i in range(8)]
    for i in range(8):
        pool.dma(wo[i], wo[i][:], w_out, w_out.t[i])
    ots = [fw.sbuf("a_ot%d" % i, [128, KC, 256], BF16) for i in range(2)]
    xts = [fw.sbuf("a_xt%d" % i, [128, KC, 256], F32) for i in range(2)]
    sq = fw.sbuf("a_sq", [128, KC, 256], F32)
    rs = fw.sbuf("a_rs", [128, 256], F32)
    psA = [fw.psum("a_ps%d" % i, [128, 512], F32) for i in range(4)]
    psn = fw.psum("a_psn", [128, 512], F32)
    tilesA = [(c0, 256, 0) for c0 in range(0, TL, 256)] + [(TL, 2, 0), (LW, CW, 1)]
    ov = oT.t.rearrange("k p t -> p k t")
    xv = xT.t.rearrange("k p t -> p k t")
    xnv = XN.t.rearrange("k p t -> p k t")
    cnt = 0
    for ti, (c0, n, kind) in enumerate(tilesA):
        ot, xt = ots[ti % 2], xts[ti % 2]
        g1c = 0 if kind == 0 else 1
        A2 = A_lat if kind == 0 else A_ctx
        shc = 4 if kind == 0 else 6
        pool.dma(ot, ot[:, :, 0:n], oT, ov[:, :, c0:c0 + n])
        sp.dma(xt, xt[:, :, 0:n], xT, xv[:, :, c0:c0 + n])
        for ci in range(KC):
            p = psA[cnt % 4]
            cnt += 1
            w = wo[ci // 2]
            h0 = (ci % 2) * 128
            for k in range(KC):
                pe.op(lambda e: e.matmul(p[:, 0:n], w[:, k, h0:h0 + 128], ot[:, k, 0:n], start=(k == 0), stop=(k == KC - 1)), [p], [w, ot])
            dve.op(lambda e: e.scalar_tensor_tensor(xt[:, ci, 0:n], p[:, 0:n], vt[:, g1c, ci:ci + 1], xt[:, ci, 0:n], ALU.mult, ALU.add), [xt], [p, vt])
        sp.dma(XN, xnv[:, :, c0:c0 + n], xt, xt[:, :, 0:n])
        act.op(lambda e: e.activation(sq[:, :, 0:n], xt[:, :, 0:n], AF.Square), [sq], [xt])
        for k in range(KC):
            pe.op(lambda e: e.matmul(psn[:, 0:n], ones[:], sq[:, k, 0:n], start=(k == 0), stop=(k == KC - 1)), [psn], [ones, sq])
        dve.op(lambda e: e.tensor_scalar(rs[:, 0:n], psn[:, 0:n], 1.0 / D, EPS, ALU.mult, ALU.add), [rs], [psn])
        act.op(lambda e: e.activation(rs[:, 0:n], rs[:, 0:n], AF.Sqrt), [rs], [rs])
        dve.op(lambda e: e.reciprocal(rs[:, 0:n], rs[:, 0:n]), [rs], [rs])
        for k in range(KC):
            dve.op(lambda e: e.tensor_tensor(sq[:, k, 0:n], xt[:, k, 0:n], rs[:, 0:n], ALU.mult), [sq], [xt, rs])
        for k in range(KC):
            act.op(lambda e: e.activation(h2T[:, k, c0:c0 + n], sq[:, k, 0:n], AF.Identity, bias=vt[:, shc, k:k + 1], scale=A2[:, k:k + 1]),
                   [h2T], [sq, A2, vt])
    fw.pop_scope()

    fw.push_scope()
    cw = fw.sbuf("cws", [128, 4, 2 * HC], F32)
    sp.dma(cw, cw[:], cwd, cwd[:])
    wgs = [fw.sbuf("b_wg%d" % i, [128, KC, 256], BF16) for i in range(2)]
    wus = [fw.sbuf("b_wu%d" % i, [128, KC, 256], BF16) for i in range(2)]
    psB = [fw.psum("b_ps%d" % i, [128, 512], F32) for i in range(4)]
    ug = [fw.sbuf("b_ug%d" % i, [128, 512], F32) for i in range(2)]
    uu = [fw.sbuf("b_uu%d" % i, [128, 512], F32) for i in range(2)]
    cg = [fw.sbuf("b_cg%d" % i, [128, 512], F32) for i in range(2)]
    cu = [fw.sbuf("b_cu%d" % i, [128, 512], F32) for i in range(2)]
    ab = [fw.sbuf("b_ab%d" % i, [128, 512], BF16) for i in range(2)]
    tilesB = []
    s = 0
    while s + 2 < LW:
        m = min(512, LW - s)
        tilesB.append((s, m, s, 0 if s == 0 else None, 1 if s + m == LW else None))
        s += m - 2
    tilesB.append((LW, CW, TL, 2, 3))

    def conv(dst, u, m, hc):
        dve.op(lambda e: e.tensor_scalar(dst[:, 0:m - 2], u[:, 1:m - 1], cw[:, 1, hc:hc + 1], cw[:, 3, hc:hc + 1], ALU.mult, ALU.add), [dst], [u, cw])
        dve.op(lambda e: e.scalar_tensor_tensor(dst[:, 0:m - 2], u[:, 0:m - 2], cw[:, 0, hc:hc + 1], dst[:, 0:m - 2], ALU.mult, ALU.add), [dst], [u, cw])
        dve.op(lambda e: e.scalar_tensor_tensor(dst[:, 0:m - 2], u[:, 2:m], cw[:, 2, hc:hc + 1], dst[:, 0:m - 2], ALU.mult, ALU.add), [dst], [u, cw])

    cnt = 0
    for j in range(FH // 256):
        wg, wu = wgs[j % 2], wus[j % 2]
        pool.dma(wg, wg[:], w_up, w_up.t[j])
        pool.dma(wu, wu[:], w_up, w_up.t[FH // 256 + j])
        for (s, m, o0, eL, eR) in tilesB:
            for half in range(2):
                hc = 2 * j + half
                h0 = half * 128
                i2 = cnt % 2
                cnt += 1
                pg, pu = psB[(2 * cnt) % 4], psB[(2 * cnt + 1) % 4]
                for k in range(KC):
                    pe.op(lambda e: e.matmul(pg[:, 0:m], wg[:, k, h0:h0 + 128], h2T[:, k, s:s + m], start=(k == 0), stop=(k == KC - 1)), [pg], [wg, h2T])
                for k in range(KC):
                    pe.op(lambda e: e.matmul(pu[:, 0:m], wu[:, k, h0:h0 + 128], h2T[:, k, s:s + m], start=(k == 0), stop=(k == KC - 1)), [pu], [wu, h2T])
                act.op(lambda e: e.activation(ug[i2][:, 0:m], pg[:, 0:m], AF.Copy), [ug[i2]], [pg])
                act.op(lambda e: e.activation(uu[i2][:, 0:m], pu[:, 0:m], AF.Copy), [uu[i2]], [pu])
                for ubuf in (ug[i2], uu[i2]):
                    if eL is not None:
                        dve.op(lambda e: e.tensor_scalar(ubuf[:, 0:1], ubuf[:, 0:1], eg[:, eL:eL + 1], None, ALU.mult), [ubuf], [ubuf, eg])
                    if eR is not None:
                        dve.op(lambda e: e.tensor_scalar(ubuf[:, m - 1:m], ubuf[:, m - 1:m], eg[:, eR:eR + 1], None, ALU.mult), [ubuf], [ubuf, eg])
                conv(cg[i2], ug[i2], m, hc)
                conv(cu[i2], uu[i2], m, HC + hc)
                act.op(lambda e: e.activation(cg[i2][:, 0:m - 2], cg[i2][:, 0:m - 2], AF.Silu), [cg[i2]], [cg[i2]])
                dve.op(lambda e: e.tensor_tensor(ab[i2][:, 0:m - 2], cg[i2][:, 0:m - 2], cu[i2][:, 0:m - 2], ALU.mult), [ab[i2]], [cg[i2], cu[i2]])
                sp.dma(AT, AT.t[hc, :, o0:o0 + m - 2], ab[i2], ab[i2][:, 0:m - 2])
    fw.pop_scope()

    fw.push_scope()
    HALF = TO // 3
    at = fw.sbuf("c_at", [128, HC, HALF], BF16)
    wds = [fw.sbuf("c_wd%d" % i, [128, HC, 256], BF16) for i in range(2)]
    psC = [fw.psum("c_ps%d" % i, [128, 512], F32) for i in range(4)]
    xns = [fw.sbuf("c_xn%d" % i, [128, 512], F32) for i in range(3)]
    outs = [fw.sbuf("c_o%d" % i, [128, 512], F32) for i in range(3)]
    atv = AT.t.rearrange("k p t -> p k t")
    cnt = 0
    wcnt = 0
    for hf in range(3):
        h0, h1 = hf * HALF, (hf + 1) * HALF
        sp.dma(at, at[:], AT, atv[:, :, h0:h1])
        subs = []
        o = h0
        while o < h1:
            lim = TL if o < TL else TO
            n = min(512, min(h1, lim) - o)
            subs.append((o, n))
            o += n
        for ct in range(8):
            wd = wds[wcnt % 2]
            wcnt += 1
            pool.dma(wd, wd[:], w_dn, w_dn.t[ct])
            for (o0, n) in subs:
                kind = 0 if o0 < TL else 1
                xcol = o0 + 1 if kind == 0 else o0 + 3
                for h2 in range(2):
                    ci = 2 * ct + h2
                    p = psC[cnt % 4]
                    xn = xns[cnt % 3]
                    ob = outs[cnt % 3]
                    cnt += 1
                    sp.dma(xn, xn[:, 0:n], XN, XN.t[ci, :, xcol:xcol + n])
                    for k in range(HC):
                        pe.op(lambda e: e.matmul(p[:, 0:n], wd[:, k, h2 * 128:h2 * 128 + 128], at[:, k, o0 - h0:o0 - h0 + n],
                                                 start=(k == 0), stop=(k == HC - 1)), [p], [wd, at])
                    dve.op(lambda e: e.scalar_tensor_tensor(ob[:, 0:n], p[:, 0:n], vt[:, 7 + kind, ci:ci + 1], xn[:, 0:n], ALU.mult, ALU.add), [ob], [p, vt, xn])
                    sp.dma(xo, xo.t[ci, :, o0:o0 + n], ob, ob[:, 0:n])
    fw.pop_scope()
    sp.wait_buf(xo)
    fw.close()
    return fw.nc


def tile_w(w):
    K, N = w.shape
    return np.ascontiguousarray(w.reshape(K // 128, 128, N // 256, 256).transpose(2, 1, 0, 3))


def vec_pk(v):
    return np.ascontiguousarray(v.reshape(-1, 128).T)


OFF_QA, OFF_KA, OFF_VA, OFF_QB, OFF_KB, OFF_VB, OFF_POOL, OFF_HY = 0, 512, 1024, 1536, 2048, 2176, 2304, 2816
SEQ = 8192
NCTX = 256
_CORES = list(range(8))
_PROGS = {}
_N = {"launches": 0}


def _prog(name, builder):
    if name not in _PROGS:
        _PROGS[name] = builder()
    return _PROGS[name]


def _run(name, builder, ins):
    nc = _prog(name, builder)
    res = run_bass_kernel_spmd(nc, ins, core_ids=_CORES)
    return res.results


def _seg(aT, lo, hi):
    F, S = aT.shape
    out = np.zeros((F, hi - lo + 2), np.float32)
    l2, h2 = max(lo - 1, 0), min(hi + 1, S)
    out[:, l2 - (lo - 1):h2 - (lo - 1)] = aT[:, l2:h2]
    return out


def kernel(x, c, ctx, c_ctx, w_mod, b_mod, norm1_g, norm2_g, w_in, w_out, qk_gain, diff_lam, diff_subln,
           win_sink, pool_w, pool_scale, hy_conv_w, hy_conv_b, hy_w1, hy_b1, hy_w2, hy_b2, hy_w3, hy_freq,
           hy_bias, ffn_w_in, ffn_conv_w, ffn_conv_b, ffn_w_out):
    f32 = lambda a: np.ascontiguousarray(np.asarray(a), dtype=np.float32)
    (x, c, ctx, c_ctx, w_mod, b_mod, norm1_g, norm2_g, w_in, w_out, qk_gain, diff_lam, diff_subln, win_sink, pool_w,
     pool_scale, hy_conv_w, hy_conv_b, hy_w1, hy_b1, hy_w2, hy_b2, hy_w3, hy_freq, hy_bias, ffn_w_in, ffn_conv_w,
     ffn_conv_b, ffn_w_out) = [f32(a) for a in (x, c, ctx, c_ctx, w_mod, b_mod, norm1_g, norm2_g, w_in, w_out, qk_gain,
                                                diff_lam, diff_subln, win_sink, pool_w, pool_scale, hy_conv_w, hy_conv_b,
                                                hy_w1, hy_b1, hy_w2, hy_b2, hy_w3, hy_freq, hy_bias, ffn_w_in, ffn_conv_w,
                                                ffn_conv_b, ffn_w_out)]
    B, L, Dm = x.shape
    NL = w_mod.shape[0]
    T = L + NCTX
    r = _run("k0", build_k0, [k0_inputs(k, c, c_ctx, w_mod, b_mod) for k in _CORES])
    mod = np.concatenate([r[k]["mod"] for k in _CORES], axis=2)
    XT_lat = [np.ascontiguousarray(x[b].T) for b in range(B)]
    XT_ctx = [np.ascontiguousarray(ctx[b].T) for b in range(B)]
    CSt = rope_tables(L, NCTX, 2)
    mk = band_masks()
    zfL, t01L = hy_feats(L)
    zfC, t01C = hy_feats(NCTX)
    ident = np.eye(128, dtype=np.float32)
    anti = np.ascontiguousarray(ident[::-1])
    ndel = hy_ndelta()
    for l in range(NL):
        m = mod[l].reshape(3, 6, Dm)
        wt = tile_w(w_in[l])
        ins = []
        for k in _CORES:
            b, q = divmod(k, 4)
            xT = np.concatenate([XT_lat[b][:, q * 2048:(q + 1) * 2048], XT_ctx[b][:, q * 64:(q + 1) * 64]], axis=1)
            vec = np.stack([vec_pk(norm1_g[l]), vec_pk(m[b, 1]), vec_pk(m[b, 0]), vec_pk(m[2, 1]), vec_pk(m[2, 0])], axis=1)
            ins.append({"xT": np.ascontiguousarray(xT.reshape(16, 128, 2112)), "vec": np.ascontiguousarray(vec), "w": wt})
        r = _run("k1", build_k1, ins)
        pts = [r[k]["pT"].reshape(4352, 2112) for k in _CORES]
        PT_lat = [np.concatenate([pts[b * 4 + q][:, :2048] for q in range(4)], axis=1) for b in range(B)]
        PT_ctx = [np.concatenate([pts[b * 4 + q][:, 2048:] for q in range(4)], axis=1) for b in range(B)]
        del pts, r
        OT_lat = [np.zeros((Dm, L), np.float32) for _ in range(B)]
        OT_ctx = [np.zeros((Dm, NCTX), np.float32) for _ in range(B)]
        lambda_init = 0.8 - 0.6 * math.exp(-0.3 * l)
        g0 = np.tile(qk_gain[l, 0], 2)
        g1 = np.tile(qk_gain[l, 1], 2)
        ins = []
        for k in _CORES:
            b, h = divmod(k, 4)
            rq = slice(OFF_QA + h * 128, OFF_QA + (h + 1) * 128)
            rk = slice(OFF_KA + h * 128, OFF_KA + (h + 1) * 128)
            rv = slice(OFF_VA + h * 128, OFF_VA + (h + 1) * 128)
            q_full = np.ascontiguousarray(np.concatenate([PT_lat[b][rq], PT_ctx[b][rq]], axis=1))
            k_full = np.ascontiguousarray(np.concatenate([PT_ctx[b][rk], PT_lat[b][rk]], axis=1))
            v_full = np.concatenate([PT_ctx[b][rv], PT_lat[b][rv]], axis=1).T
            sm = np.zeros((128, 8), np.float32)
            sm[:, 0] = g0
            sm[:, 1] = swap_halves(g0[:, None])[:, 0]
            sm[:, 2] = g1
            sm[:, 3] = swap_halves(g1[:, None])[:, 0]
            sm[:, 4] = diff_subln[l]
            sm[:, 5] = lambda_init
            sm[:, 6] = 1.0 - lambda_init
            ins.append({"qT": q_full, "qsT": swap_halves(q_full), "kT": k_full, "ksT": swap_halves(k_full),
                        "v": np.ascontiguousarray(v_full.reshape(T // 128, 128, 128).transpose(1, 0, 2)),
                        "CS": CSt, "sm": sm, "lamp": np.ascontiguousarray(diff_lam[l].reshape(1, 256))})
        r = _run("k2a", build_k2a, ins)
        for k in _CORES:
            b, h = divmod(k, 4)
            o = r[k]["oT"]
            OT_lat[b][h * 128:(h + 1) * 128] = o[:, :L]
            OT_ctx[b][h * 128:(h + 1) * 128] = o[:, L:]
        g2 = np.tile(qk_gain[l, 2], 2)
        g3 = np.tile(qk_gain[l, 3], 2)
        ins = []
        for k in _CORES:
            b = k // 4
            h0 = 2 * (k % 4)
            kvh = (k % 4) // 2
            rq = slice(OFF_QB + h0 * 64, OFF_QB + (h0 + 2) * 64)
            rk = slice(OFF_KB + kvh * 64, OFF_KB + (kvh + 1) * 64)
            rv = slice(OFF_VB + kvh * 64, OFF_VB + (kvh + 1) * 64)
            q_full = np.ascontiguousarray(np.concatenate([PT_lat[b][rq], PT_ctx[b][rq]], axis=1))
            k1 = np.concatenate([PT_ctx[b][rk], PT_lat[b][rk]], axis=1)
            k_full = np.ascontiguousarray(np.concatenate([k1, k1], axis=0))
            v_full = np.concatenate([PT_ctx[b][rv], PT_lat[b][rv]], axis=1).T
            sm = np.zeros((128, 8), np.float32)
            sm[:, 0] = g2
            sm[:, 1] = swap_halves(g2[:, None])[:, 0]
            sm[:, 2] = g3
            sm[:, 3] = swap_halves(g3[:, None])[:, 0]
            sm[:, 4] = win_sink[l, h0]
            sm[:, 5] = win_sink[l, h0 + 1]
            ins.append({"qT": q_full, "qsT": swap_halves(q_full), "kT": k_full, "ksT": swap_halves(k_full),
                        "v": np.ascontiguousarray(v_full.reshape(T // 128, 128, 64).transpose(1, 0, 2)),
                        "CS": CSt, "sm": sm, "masks": mk})
        r = _run("k2b", build_k2b, ins)
        for k in _CORES:
            b = k // 4
            h0 = 2 * (k % 4)
            o = r[k]["oT"].reshape(128, T)
            OT_lat[b][512 + h0 * 64:512 + (h0 + 2) * 64] = o[:, :L]
            OT_ctx[b][512 + h0 * 64:512 + (h0 + 2) * 64] = o[:, L:]
        ins = []
        for k in _CORES:
            b, g = divmod(k, 4)
            ru = slice(OFF_POOL + g * 128, OFF_POOL + (g + 1) * 128)
            sel, ic = pool_consts(g, L)
            ins.append({"uT": np.ascontiguousarray(np.concatenate([PT_lat[b][ru], PT_ctx[b][ru]], axis=1)), "wsel": sel, "invc": ic,
                        "wl": np.ascontiguousarray(pool_w[l, g]), "ls": np.ascontiguousarray(pool_scale[l, g * 128:(g + 1) * 128, None])})
        r = _run("k2c", build_k2c, ins)
        for k in _CORES:
            b, g = divmod(k, 4)
            o = r[k]["yT"]
            OT_lat[b][1024 + g * 128:1024 + (g + 1) * 128] = o[:, :L]
            OT_ctx[b][1024 + g * 128:1024 + (g + 1) * 128] = o[:, L:]
        ins = []
        for k in _CORES:
            c0 = 64 * k
            uL, uC = [], []
            smh = np.zeros((128, 16), np.float32)
            for part in range(3):
                rr = slice(OFF_HY + part * 512 + c0, OFF_HY + part * 512 + c0 + 64)
                uL.append(np.concatenate([PT_lat[0][rr], PT_lat[1][rr]], axis=0))
                uC.append(np.concatenate([PT_ctx[0][rr], PT_ctx[1][rr]], axis=0))
                for tap in range(3):
                    smh[:, part * 3 + tap] = np.tile(hy_conv_w[l, tap, part * 512 + c0: part * 512 + c0 + 64], 2)
                smh[:, 9 + part] = np.tile(hy_conv_b[l, part * 512 + c0: part * 512 + c0 + 64], 2)
            smh[:, 12] = np.tile(hy_bias[l, c0:c0 + 64], 2)
            smh[:, 13] = np.tile(ndel[c0:c0 + 64], 2)
            w3r = hy_w3[l].reshape(64, 2, 512)[:, :, c0:c0 + 64]
            fb = np.zeros((64, 4), np.float32)
            fb[:, 0] = hy_freq[l]
            fb[:, 1] = hy_b1[l]
            fb[:, 2] = hy_b2[l]
            ins.append({"uT": np.ascontiguousarray(np.stack(uL)), "ucT": np.ascontiguousarray(np.stack(uC)), "sm": smh,
                        "w1": np.ascontiguousarray(hy_w1[l]), "w2": np.ascontiguousarray(hy_w2[l]),
                        "w3": np.ascontiguousarray(np.concatenate([w3r, w3r], axis=2)), "fb": fb,
                        "zfL": zfL, "t01L": t01L, "zfC": zfC, "t01C": t01C, "ident": ident, "anti": anti})
        r = _run("k2d", build_k2d, ins)
        for k in _CORES:
            c0 = 64 * k
            o = r[k]["oT"]
            oc = r[k]["ocT"]
            for b in range(B):
                OT_lat[b][1536 + c0:1536 + c0 + 64] = o[b * 64:(b + 1) * 64]
                OT_ctx[b][1536 + c0:1536 + c0 + 64] = oc[b * 64:(b + 1) * 64]
        del PT_lat, PT_ctx
        wo_t = tile_w(w_out[l])
        wu_t = tile_w(ffn_w_in[l])
        wd_t = tile_w(ffn_w_out[l])
        cwp = np.ascontiguousarray(np.stack([vec_pk(ffn_conv_w[l, 0]), vec_pk(ffn_conv_w[l, 1]), vec_pk(ffn_conv_w[l, 2]),
                                             vec_pk(ffn_conv_b[l])], axis=1))
        ins = []
        for k in _CORES:
            b, q = divmod(k, 4)
            ocol = np.concatenate([_seg(OT_lat[b], q * 2048, (q + 1) * 2048), _seg(OT_ctx[b], q * 64, (q + 1) * 64)], axis=1)
            xcol = np.concatenate([_seg(XT_lat[b], q * 2048, (q + 1) * 2048), _seg(XT_ctx[b], q * 64, (q + 1) * 64)], axis=1)
            TW = ocol.shape[1]
            vec = np.stack([vec_pk(m[b, 2]), vec_pk(m[2, 2]), vec_pk(norm2_g[l]), vec_pk(m[b, 4]), vec_pk(m[b, 3]),
                            vec_pk(m[2, 4]), vec_pk(m[2, 3]), vec_pk(m[b, 5]), vec_pk(m[2, 5])], axis=1)
            eg = np.ones((128, 4), np.float32)
            if q == 0:
                eg[:, 0] = 0
                eg[:, 2] = 0
            if q == 3:
                eg[:, 1] = 0
                eg[:, 3] = 0
            ins.append({"oT": np.ascontiguousarray(ocol.reshape(16, 128, TW)), "xT": np.ascontiguousarray(xcol.reshape(16, 128, TW)),
                        "vec": np.ascontiguousarray(vec), "edge": eg, "w_out": wo_t, "w_up": wu_t, "cw": cwp, "w_dn": wd_t})
        r = _run("k3", build_k3, ins)
        for k in _CORES:
            b, q = divmod(k, 4)
            o = r[k]["xo"].reshape(Dm, 2112)
            XT_lat[b][:, q * 2048:(q + 1) * 2048] = o[:, :2048]
            XT_ctx[b][:, q * 64:(q + 1) * 64] = o[:, 2048:]
        del r, ins
    return np.ascontiguousarray(np.stack([XT_lat[b].T for b in range(B)])).astype(np.float32)
```

```python
import math
import numpy as np
import time
from contextlib import ExitStack
import concourse.bass as bass
import concourse.mybir as mybir
from concourse.bass_utils import run_bass_kernel_spmd

F32 = mybir.dt.float32
BF16 = mybir.dt.bfloat16
AF = mybir.ActivationFunctionType
ALU = mybir.AluOpType
AX = mybir.AxisListType


class Buf:
    __slots__ = ("name", "t", "w", "r", "sem", "dcount")

    def __init__(self, name, t):
        self.name = name
        self.t = t
        self.w = {}
        self.r = {}
        self.sem = None
        self.dcount = 0

    def __getitem__(self, idx):
        return self.t[idx]


class Eng:
    def __init__(self, fw, name, eng, sem, kind):
        self.fw = fw
        self.name = name
        self.eng = eng
        self.sem = sem
        self.kind = kind
        self.count = 0
        self.seen = {}

    def _wait(self, deps):
        for key, (sem, val) in deps.items():
            if self.seen.get(key, 0) >= val:
                continue
            if sem is self.sem and self.kind == 'pe':
                continue
            self.eng.wait_ge(sem, val)
            self.seen[key] = val

    def _deps(self, outs, ins):
        deps = {}
        for b in ins:
            for k, (s, v) in b.w.items():
                if deps.get(k, (None, 0))[1] < v:
                    deps[k] = (s, v)
        for b in outs:
            for d in (b.w, b.r):
                for k, (s, v) in d.items():
                    if deps.get(k, (None, 0))[1] < v:
                        deps[k] = (s, v)
        return deps

    def op(self, inst_fn, outs, ins):
        self._wait(self._deps(outs, ins))
        inst = inst_fn(self.eng)
        self.count += 1
        inst.then_inc(self.sem, 1)
        key = id(self.sem)
        tok = (self.sem, self.count)
        for b in ins:
            b.r[key] = tok
        for b in outs:
            b.w = {key: tok}
            b.r = {}
        return tok

    def dma(self, out_buf, out_ap, in_buf, in_ap, **kw):
        self._wait(self._deps([out_buf], [in_buf]))
        if out_buf.sem is None:
            out_buf.sem = self.fw.new_sem("d_" + out_buf.name)
        inst = self.eng.dma_start(out=out_ap, in_=in_ap, **kw)
        out_buf.dcount += 16
        inst.then_inc(out_buf.sem, 16)
        key = id(out_buf.sem)
        tok = (out_buf.sem, out_buf.dcount)
        in_buf.r[key] = tok
        out_buf.w = {key: tok}
        out_buf.r = {}
        return tok

    def wait_buf(self, b):
        self._wait(dict(b.w))


class FW:
    def __init__(self, name="k"):
        self.nc = bass.Bass("TRN2", target_bir_lowering=False)
        self.es = ExitStack()
        self.nsem = 0
        self.block = None
        self.all_bufs = []
        self.scope = None

    def new_sem(self, name):
        self.nsem += 1
        return self.es.enter_context(self.nc.semaphore(name + "_%d" % self.nsem))

    def dram(self, name, shape, dtype, kind):
        t = self.nc.dram_tensor(name, list(shape), dtype, kind=kind)
        b = Buf(name, t.ap())
        self.all_bufs.append(b)
        return b

    def sbuf(self, name, shape, dtype):
        t = (self.scope or self.es).enter_context(self.nc.sbuf_tensor(name, list(shape), dtype))
        b = Buf(name, t)
        self.all_bufs.append(b)
        return b

    def psum(self, name, shape, dtype=F32):
        t = (self.scope or self.es).enter_context(self.nc.psum_tensor(name, list(shape), dtype))
        b = Buf(name, t)
        self.all_bufs.append(b)
        return b

    def engines(self):
        nc = self.nc
        self.pe = Eng(self, "pe", nc.tensor, self.new_sem("pe"), 'pe')
        self.act = Eng(self, "act", nc.scalar, self.new_sem("act"), 'act')
        self.dve = Eng(self, "dve", nc.vector, self.new_sem("dve"), 'dve')
        self.pool = Eng(self, "pool", nc.gpsimd, self.new_sem("pool"), 'pool')
        self.sp = Eng(self, "sp", nc.sync, self.new_sem("sp"), 'sp')
        return self.pe, self.act, self.dve, self.pool, self.sp

    def push_scope(self):
        assert self.scope is None
        self.scope = ExitStack()

    def pop_scope(self):
        fw_barrier(self)
        self.scope.close()
        self.scope = None

    def close(self):
        self.es.close()


def sub_bufs(parent, aps, prefix):
    return [Buf("%s%d" % (prefix, i), ap) for i, ap in enumerate(aps)]


def fw_barrier(fw, bufs=()):
    engs = [fw.pe, fw.act, fw.dve, fw.pool, fw.sp]
    toks = {}
    for e in engs:
        if e.count > 0:
            toks[id(e.sem)] = (e.sem, e.count)
    for b in fw.all_bufs:
        if b.sem is not None and b.dcount > 0:
            toks[id(b.sem)] = (b.sem, b.dcount)
    for e in engs:
        e._wait(dict(toks))


D = 2048
KC = D // 128
EPS = 1e-6


def load_consts(fw, sp):
    ones = fw.sbuf("ones", [128, 128], F32)
    fw.dve.op(lambda e: e.memset(ones[:], 1.0), [ones], [])
    return ones


def norm_mod_phase(fw, xT, hT, tiles, vecs, ones, nm):
    pe, act, dve, pool, sp = fw.pe, fw.act, fw.dve, fw.pool, fw.sp
    xts = [fw.sbuf("%s_xt%d" % (nm, i), [128, KC, 512], F32) for i in range(2)]
    sqs = [fw.sbuf("%s_sq%d" % (nm, i), [128, KC, 512], F32) for i in range(1)]
    rstd = [fw.sbuf("%s_rstd%d" % (nm, i), [128, 512], F32) for i in range(2)]
    ps = [fw.psum("%s_ps%d" % (nm, i), [128, 512], F32) for i in range(2)]
    xv = xT.t.rearrange("k p t -> p k t")
    for i, (t0, n, A, sh) in enumerate(tiles):
        xt, sq, rs, p = xts[i % 2], sqs[0], rstd[i % 2], ps[i % 2]
        sp.dma(xt, xt[:, :, 0:n], xT, xv[:, :, t0:t0 + n])
        act.op(lambda e: e.activation(sq[:, :, 0:n], xt[:, :, 0:n], AF.Square), [sq], [xt])
        for kc in range(KC):
            pe.op(lambda e: e.matmul(p[:, 0:n], ones[:], sq[:, kc, 0:n], start=(kc == 0), stop=(kc == KC - 1)), [p], [ones, sq])
        dve.op(lambda e: e.tensor_scalar(rs[:, 0:n], p[:, 0:n], 1.0 / D, EPS, ALU.mult, ALU.add), [rs], [p])
        act.op(lambda e: e.activation(rs[:, 0:n], rs[:, 0:n], AF.Sqrt), [rs], [rs])
        dve.op(lambda e: e.reciprocal(rs[:, 0:n], rs[:, 0:n]), [rs], [rs])
        for kc in range(KC):
            dve.op(lambda e: e.tensor_tensor(sq[:, kc, 0:n], xt[:, kc, 0:n], rs[:, 0:n], ALU.mult), [sq], [xt, rs])
        for kc in range(KC):
            act.op(lambda e: e.activation(hT[:, kc, t0:t0 + n], sq[:, kc, 0:n], AF.Identity,
                                          bias=sh[:, kc:kc + 1], scale=A[:, kc:kc + 1]), [hT], [sq, A, sh])


def linear_phase(fw, hT, KCn, w, ncoltiles, tiles, epilogue, nm, nbuf=3):
    pe, pool = fw.pe, fw.pool
    wts = [fw.sbuf("%s_w%d" % (nm, i), [128, KCn, 256], BF16) for i in range(nbuf)]
    ps = [fw.psum("%s_lp%d" % (nm, i), [128, 512], F32) for i in range(4)]
    cnt = 0
    for ct in range(ncoltiles):
        wt = wts[ct % nbuf]
        pool.dma(wt, wt[:], w, w.t[ct])
        for ti, (t0, n) in enumerate(tiles):
            for half in range(2):
                p = ps[cnt % 4]
                cnt += 1
                for kc in range(KCn):
                    pe.op(lambda e: e.matmul(p[:, 0:n], wt[:, kc, half * 128:(half + 1) * 128], hT[:, kc, t0:t0 + n],
                                             start=(kc == 0), stop=(kc == KCn - 1)), [p], [wt, hT])
                epilogue(ct * 2 + half, ti, t0, n, p)


def build_k1(T_lat=2048, T_ctx=64, NCOL=4352):
    fw = FW()
    T = T_lat + T_ctx
    xT = fw.dram("xT", [KC, 128, T], F32, "ExternalInput")
    vec = fw.dram("vec", [128, 5, KC], F32, "ExternalInput")
    w = fw.dram("w", [NCOL // 256, 128, KC, 256], F32, "ExternalInput")
    pT = fw.dram("pT", [NCOL // 128, 128, T], F32, "ExternalOutput")
    fw.engines()
    pe, act, dve, pool, sp = fw.pe, fw.act, fw.dve, fw.pool, fw.sp
    ones = load_consts(fw, sp)
    vt = fw.sbuf("vt", [128, 5, KC], F32)
    sp.dma(vt, vt[:], vec, vec[:])
    A_lat = fw.sbuf("A_lat", [128, KC], F32)
    A_ctx = fw.sbuf("A_ctx", [128, KC], F32)
    sh_lat = fw.sbuf("sh_lat", [128, KC], F32)
    sh_ctx = fw.sbuf("sh_ctx", [128, KC], F32)
    dve.op(lambda e: e.scalar_tensor_tensor(A_lat[:], vt[:, 1, :], 1.0, vt[:, 0, :], ALU.add, ALU.mult), [A_lat], [vt])
    dve.op(lambda e: e.scalar_tensor_tensor(A_ctx[:], vt[:, 3, :], 1.0, vt[:, 0, :], ALU.add, ALU.mult), [A_ctx], [vt])
    dve.op(lambda e: e.tensor_copy(sh_lat[:], vt[:, 2, :]), [sh_lat], [vt])
    dve.op(lambda e: e.tensor_copy(sh_ctx[:], vt[:, 4, :]), [sh_ctx], [vt])
    hT = fw.sbuf("hT", [128, KC, T], BF16)
    tiles = [(t0, 512, A_lat, sh_lat) for t0 in range(0, T_lat, 512)]
    if T_ctx:
        tiles.append((T_lat, T_ctx, A_ctx, sh_ctx))
    norm_mod_phase(fw, xT, hT, tiles, None, ones, "n1")
    ots = [fw.sbuf("ot%d" % i, [128, 512], F32) for i in range(4)]
    st = {"i": 0}

    def epi(ci, ti, t0, n, p):
        ot = ots[st["i"] % 4]
        if st["i"] % 2 == 0:
            dve.op(lambda e: e.tensor_copy(ot[:, 0:n], p[:, 0:n]), [ot], [p])
        else:
            act.op(lambda e: e.activation(ot[:, 0:n], p[:, 0:n], AF.Copy), [ot], [p])
        st["i"] += 1
        sp.dma(pT, pT.t[ci, :, t0:t0 + n], ot, ot[:, 0:n])

    linear_phase(fw, hT, KC, w, NCOL // 256, [(t[0], t[1]) for t in tiles], epi, "l1")
    sp.wait_buf(pT)
    fw.close()
    return fw.nc


def tile_w(w):
    K, N = w.shape
    return np.ascontiguousarray(w.reshape(K // 128, 128, N // 256, 256).transpose(2, 1, 0, 3))


def vec_pk(v):
    return np.ascontiguousarray(v.reshape(KC, 128).T)


D = 2048
KC = 16
NCOLS = 1536


def build_k0(NL=4):
    fw = FW()
    cT = fw.dram("cT", [128, KC, 3], F32, "ExternalInput")
    w = fw.dram("w", [NL, 3, 128, KC, 512], F32, "ExternalInput")
    bm = fw.dram("bm", [NL, NCOLS], F32, "ExternalInput")
    mod = fw.dram("mod", [NL, 3, NCOLS], F32, "ExternalOutput")
    fw.engines()
    pe, act, dve, pool, sp = fw.pe, fw.act, fw.dve, fw.pool, fw.sp
    ct = fw.sbuf("ct", [128, KC, 3], F32)
    sp.dma(ct, ct[:], cT, cT[:])
    st = fw.sbuf("st", [128, KC, 3], F32)
    act.op(lambda e: e.activation(st[:], ct[:], AF.Silu), [st], [ct])
    wts = [fw.sbuf("wt%d" % i, [128, KC, 512], F32) for i in range(2)]
    bts = [fw.sbuf("bt%d" % i, [3, 512], F32) for i in range(2)]
    ots = [fw.sbuf("ot%d" % i, [3, 512], F32) for i in range(2)]
    ps = [fw.psum("ps%d" % i, [128, 512], F32) for i in range(2)]
    i = 0
    for l in range(NL):
        for t in range(3):
            wt, bt, ot, p = wts[i % 2], bts[i % 2], ots[i % 2], ps[i % 2]
            i += 1
            (sp if i % 2 == 0 else act).dma(wt, wt[:], w, w.t[l, t])
            bsrc = bass.AP(tensor=bm.t.tensor, offset=l * NCOLS + t * 512, ap=[[0, 3], [1, 512]])
            sp.dma(bt, bt[:], bm, bsrc)
            for k in range(KC):
                pe.op(lambda e: e.matmul(p[0:3, :], st[:, k, :], wt[:, k, :], start=(k == 0), stop=(k == KC - 1)), [p], [st, wt])
            dve.op(lambda e: e.tensor_tensor(ot[:], p[0:3, :], bt[:], ALU.add), [ot], [p, bt])
            sp.dma(mod, mod.t[l, :, t * 512:(t + 1) * 512], ot, ot[:])
    sp.wait_buf(mod)
    fw.close()
    return fw.nc


def k0_inputs(core, c, c_ctx, w_mod, b_mod):
    NL = w_mod.shape[0]
    cs = np.stack([c[0], c[1], c_ctx], axis=1)
    cT = np.ascontiguousarray(cs.reshape(KC, 128, 3).transpose(1, 0, 2))
    cols = slice(core * NCOLS, (core + 1) * NCOLS)
    w = w_mod[:, :, cols].reshape(NL, KC, 128, 3, 512).transpose(0, 3, 2, 1, 4)
    return {"cT": cT, "w": np.ascontiguousarray(w), "bm": np.ascontiguousarray(b_mod[:, cols])}


EPS = 1e-6
CTX = 256


def qknorm_rope(fw, xT, xsT, CS, gcol, dst, col_map, tiles, blockones, nm, inv_n):
    pe, act, dve, pool, sp = fw.pe, fw.act, fw.dve, fw.pool, fw.sp
    gbuf, c0 = gcol
    if "qk_scr" not in fw.__dict__:
        fw.qk_scr = dict(
            xt=[fw.sbuf("qk_x%d" % i, [128, 512], F32) for i in range(2)],
            xs=[fw.sbuf("qk_xs%d" % i, [128, 512], F32) for i in range(2)],
            ct=[fw.sbuf("qk_c%d" % i, [128, 512], F32) for i in range(2)],
            st=[fw.sbuf("qk_s%d" % i, [128, 512], F32) for i in range(2)],
            sq=[fw.sbuf("qk_sq%d" % i, [128, 512], F32) for i in range(2)],
            rs=[fw.sbuf("qk_rs%d" % i, [128, 512], F32) for i in range(2)],
            ps=[fw.psum("qk_ps%d" % i, [128, 512], F32) for i in range(2)])
    xt, xs, ct, st, sq, rs, ps = (fw.qk_scr[k] for k in ("xt", "xs", "ct", "st", "sq", "rs", "ps"))
    for i, (x0, tb0, n) in enumerate(tiles):
        j = i % 2
        sp.dma(xt[j], xt[j][:, 0:n], xT, xT.t[:, x0:x0 + n])
        sp.dma(xs[j], xs[j][:, 0:n], xsT, xsT.t[:, x0:x0 + n])
        sp.dma(ct[j], ct[j][:, 0:n], CS, CS.t[0, :, tb0:tb0 + n])
        sp.dma(st[j], st[j][:, 0:n], CS, CS.t[1, :, tb0:tb0 + n])
        act.op(lambda e: e.activation(sq[j][:, 0:n], xt[j][:, 0:n], AF.Square), [sq[j]], [xt[j]])
        pe.op(lambda e: e.matmul(ps[j][:, 0:n], blockones[:], sq[j][:, 0:n], start=True, stop=True), [ps[j]], [blockones, sq[j]])
        dve.op(lambda e: e.tensor_scalar(rs[j][:, 0:n], ps[j][:, 0:n], inv_n, EPS, ALU.mult, ALU.add), [rs[j]], [ps[j]])
        act.op(lambda e: e.activation(rs[j][:, 0:n], rs[j][:, 0:n], AF.Sqrt), [rs[j]], [rs[j]])
        dve.op(lambda e: e.reciprocal(rs[j][:, 0:n], rs[j][:, 0:n]), [rs[j]], [rs[j]])
        dve.op(lambda e: e.scalar_tensor_tensor(ct[j][:, 0:n], xt[j][:, 0:n], gbuf[:, c0:c0 + 1], ct[j][:, 0:n], ALU.mult, ALU.mult),
               [ct[j]], [xt[j], gbuf])
        dve.op(lambda e: e.scalar_tensor_tensor(st[j][:, 0:n], xs[j][:, 0:n], gbuf[:, c0 + 1:c0 + 2], st[j][:, 0:n], ALU.mult, ALU.mult),
                [st[j]], [xs[j], gbuf])
        dve.op(lambda e: e.tensor_tensor(ct[j][:, 0:n], ct[j][:, 0:n], st[j][:, 0:n], ALU.add), [ct[j]], [st[j]])
        dve.op(lambda e: e.tensor_tensor(dst[:, x0:x0 + n], ct[j][:, 0:n], rs[j][:, 0:n], ALU.mult), [dst], [ct[j], rs[j]])


def build_k2a(LQ=8192):
    fw = FW()
    T = LQ + CTX
    NKC = T // 128
    qT = fw.dram("qT", [128, T], F32, "ExternalInput")
    qsT = fw.dram("qsT", [128, T], F32, "ExternalInput")
    kT = fw.dram("kT", [128, T], F32, "ExternalInput")
    ksT = fw.dram("ksT", [128, T], F32, "ExternalInput")
    v = fw.dram("v", [128, NKC, 128], F32, "ExternalInput")
    CS = fw.dram("CS", [2, 128, T], F32, "ExternalInput")
    sm = fw.dram("sm", [128, 8], F32, "ExternalInput")
    lamp = fw.dram("lamp", [1, 256], F32, "ExternalInput")
    oT = fw.dram("oT", [128, T], F32, "ExternalOutput")
    fw.engines()
    pe, act, dve, pool, sp = fw.pe, fw.act, fw.dve, fw.pool, fw.sp
    ones = fw.sbuf("ones", [128, 128], F32)
    dve.op(lambda e: e.memset(ones[:], 1.0), [ones], [])
    onesb = fw.sbuf("onesb", [128, 128], BF16)
    dve.op(lambda e: e.memset(onesb[:], 1.0), [onesb], [])
    blockones = fw.sbuf("blockones", [128, 128], F32)
    dve.op(lambda e: e.memset(blockones[:], 0.0), [blockones], [])
    dve.op(lambda e: e.memset(blockones[0:64, 0:64], 1.0), [blockones], [])
    dve.op(lambda e: e.memset(blockones[64:128, 64:128], 1.0), [blockones], [])
    smt = fw.sbuf("smt", [128, 8], F32)
    sp.dma(smt, smt[:], sm, sm[:])
    lt = fw.sbuf("lt", [1, 256], F32)
    sp.dma(lt, lt[:], lamp, lamp[:])
    lw = fw.sbuf("lw", [1, 128], F32)
    dve.op(lambda e: e.tensor_tensor(lw[:, 0:64], lt[:, 0:64], lt[:, 64:128], ALU.mult), [lw], [lt])
    dve.op(lambda e: e.tensor_tensor(lw[:, 64:128], lt[:, 128:192], lt[:, 192:256], ALU.mult), [lw], [lt])
    l2 = fw.sbuf("l2", [1, 4], F32)
    dve.op(lambda e: e.reduce_sum(l2[:, 0:1], lw[:, 0:64], AX.X), [l2], [lw])
    dve.op(lambda e: e.reduce_sum(l2[:, 1:2], lw[:, 64:128], AX.X), [l2], [lw])
    act.op(lambda e: e.activation(l2[:, 0:2], l2[:, 0:2], AF.Exp), [l2], [l2])
    dve.op(lambda e: e.tensor_tensor(l2[:, 2:3], l2[:, 1:2], l2[:, 0:1], ALU.subtract), [l2], [l2])
    dve.op(lambda e: e.tensor_tensor(l2[:, 2:3], l2[:, 2:3], smt[0:1, 5:6], ALU.subtract), [l2], [l2, smt])
    sc = fw.sbuf("sc", [128, 2], F32)
    fw.push_scope()
    psl = fw.psum("psl", [128, 512], F32)
    pe.op(lambda e: e.matmul(psl[:, 0:1], ones[0:1, :], l2[0:1, 2:3], start=True, stop=True), [psl], [ones, l2])
    dve.op(lambda e: e.tensor_copy(sc[:, 0:1], psl[:, 0:1]), [sc], [psl])
    dve.op(lambda e: e.tensor_tensor(sc[:, 1:2], smt[:, 4:5], smt[:, 6:7], ALU.mult), [sc], [smt])
    fw.pop_scope()

    QT = fw.sbuf("QT", [128, T], BF16)
    KT = fw.sbuf("KT", [128, T], BF16)
    V = fw.sbuf("V", [128, NKC, 128], BF16)
    pool.dma(V, V[:], v, v[:])
    qtiles = [(t0, t0, 512) for t0 in range(0, LQ, 512)] + [(LQ, LQ, CTX)]
    ktiles = [(0, LQ, CTX)] + [(CTX + t0, t0, 512) for t0 in range(0, LQ, 512)]
    fw.push_scope()
    qknorm_rope(fw, qT, qsT, CS, (smt, 0), QT, None, qtiles, blockones, "qn", 1.0 / 64)
    qknorm_rope(fw, kT, ksT, CS, (smt, 2), KT, None, ktiles, blockones, "kn", 1.0 / 64)
    del fw.qk_scr
    fw.pop_scope()

    NSB = 2
    NPT = 4
    ps_s = [[fw.psum("ps_s%d_%d" % (m, i), [128, 512], F32) for i in range(NSB)] for m in range(2)]
    ps_o = [fw.psum("ps_o%d" % m, [128, 512], F32) for m in range(2)]
    ps_z = [fw.psum("ps_z%d" % m, [128, 512], F32) for m in range(2)]
    pts = [[fw.sbuf("pt%d_%d" % (m, i), [128, 512], BF16) for i in range(NPT)] for m in range(2)]
    om = [fw.sbuf("om%d" % i, [128, 512], F32) for i in range(2)]
    rz = [fw.sbuf("rz%d" % i, [128, 512], F32) for i in range(2)]
    osq = fw.sbuf("osq", [128, 512], F32)
    ors = fw.sbuf("ors", [128, 512], F32)
    ots = [fw.sbuf("ot%d" % i, [128, 512], F32) for i in range(2)]
    for qi, (q0, _, n) in enumerate(qtiles):
        nkc = NKC if q0 < LQ else CTX // 128
        seq = []
        for kc in range(nkc):
            seq.append(("s", kc))
            if kc >= 1:
                seq.append(("av", kc - 1))
        seq.append(("av", nkc - 1))
        for kind, kc in seq:
            if kind == "s":
                for m in range(2):
                    r0 = 64 * m
                    p = ps_s[m][kc % NSB]
                    pe.op(lambda e: e.matmul(p[:, 0:n], KT[r0:r0 + 64, kc * 128:(kc + 1) * 128], QT[r0:r0 + 64, q0:q0 + n],
                                             start=True, stop=True), [p], [KT, QT])
                for m in range(2):
                    p = ps_s[m][kc % NSB]
                    pt = pts[m][kc % NPT]
                    act.op(lambda e: e.activation(pt[:, 0:n], p[:, 0:n], AF.Exp, scale=0.125), [pt], [p])
            else:
                for m in range(2):
                    pt = pts[m][kc % NPT]
                    pe.op(lambda e: e.matmul(ps_o[m][:, 0:n], V[:, kc, :], pt[:, 0:n], start=(kc == 0), stop=(kc == nkc - 1)), [ps_o[m]], [V, pt])
                    pe.op(lambda e: e.matmul(ps_z[m][:, 0:n], onesb[:], pt[:, 0:n], start=(kc == 0), stop=(kc == nkc - 1)), [ps_z[m]], [onesb, pt])
        for m in range(2):
            dve.op(lambda e: e.reciprocal(rz[m][:, 0:n], ps_z[m][:, 0:n]), [rz[m]], [ps_z[m]])
            dve.op(lambda e: e.tensor_tensor(om[m][:, 0:n], ps_o[m][:, 0:n], rz[m][:, 0:n], ALU.mult), [om[m]], [ps_o[m], rz[m]])
        dve.op(lambda e: e.scalar_tensor_tensor(om[0][:, 0:n], om[1][:, 0:n], sc[:, 0:1], om[0][:, 0:n], ALU.mult, ALU.add), [om[0]], [om[1], sc])
        act.op(lambda e: e.activation(osq[:, 0:n], om[0][:, 0:n], AF.Square), [osq], [om[0]])
        pl = ps_s[0][(nkc) % NSB]
        pe.op(lambda e: e.matmul(pl[:, 0:n], ones[:], osq[:, 0:n], start=True, stop=True), [pl], [ones, osq])
        dve.op(lambda e: e.tensor_scalar(ors[:, 0:n], pl[:, 0:n], 1.0 / 128, EPS, ALU.mult, ALU.add), [ors], [pl])
        act.op(lambda e: e.activation(ors[:, 0:n], ors[:, 0:n], AF.Sqrt), [ors], [ors])
        dve.op(lambda e: e.reciprocal(ors[:, 0:n], ors[:, 0:n]), [ors], [ors])
        ot = ots[qi % 2]
        dve.op(lambda e: e.scalar_tensor_tensor(ot[:, 0:n], om[0][:, 0:n], sc[:, 1:2], ors[:, 0:n], ALU.mult, ALU.mult), [ot], [om[0], sc, ors])
        sp.dma(oT, oT.t[:, q0:q0 + n], ot, ot[:, 0:n])
    sp.wait_buf(oT)
    fw.close()
    return fw.nc


def swap_halves(xT):
    r = xT.reshape(-1, 2, 2, 16, xT.shape[-1])
    return np.ascontiguousarray(r[:, :, ::-1]).reshape(xT.shape)


def rope_tables(L, n_ctx, reps):
    GRID_W = 64
    rows = L // GRID_W
    row = np.repeat(np.arange(rows, dtype=np.float32), GRID_W)
    col = np.tile(np.arange(GRID_W, dtype=np.float32), rows)
    inv = (10000.0 ** (-np.arange(0, 32, 2, dtype=np.float32) / 32)).astype(np.float32)
    ang = np.concatenate([row[:, None] * inv, col[:, None] * inv], axis=-1)
    cos = np.cos(ang).astype(np.float32).reshape(L, 2, 16)
    sin = np.sin(ang).astype(np.float32).reshape(L, 2, 16)
    C = np.ones((2, 2, 16, L + n_ctx), np.float32)
    S = np.zeros((2, 2, 16, L + n_ctx), np.float32)
    C[:, 0, :, :L] = cos.transpose(1, 2, 0)
    C[:, 1, :, :L] = cos.transpose(1, 2, 0)
    S[:, 0, :, :L] = -sin.transpose(1, 2, 0)
    S[:, 1, :, :L] = sin.transpose(1, 2, 0)
    C = np.tile(C.reshape(64, -1), (reps, 1))
    S = np.tile(S.reshape(64, -1), (reps, 1))
    return np.ascontiguousarray(np.stack([C, S]))


def build_k2b(LQ=8192):
    fw = FW()
    T = LQ + CTX
    NKC = T // 128
    NB = LQ // 128
    qT = fw.dram("qT", [128, T], F32, "ExternalInput")
    qsT = fw.dram("qsT", [128, T], F32, "ExternalInput")
    kT = fw.dram("kT", [128, T], F32, "ExternalInput")
    ksT = fw.dram("ksT", [128, T], F32, "ExternalInput")
    v = fw.dram("v", [128, NKC, 64], F32, "ExternalInput")
    CS = fw.dram("CS", [2, 128, T], F32, "ExternalInput")
    sm = fw.dram("sm", [128, 8], F32, "ExternalInput")
    masks = fw.dram("masks", [128, 6, 512], F32, "ExternalInput")
    oT = fw.dram("oT", [2, 64, T], F32, "ExternalOutput")
    fw.engines()
    pe, act, dve, pool, sp = fw.pe, fw.act, fw.dve, fw.pool, fw.sp
    onesb = fw.sbuf("onesb", [128, 128], BF16)
    dve.op(lambda e: e.memset(onesb[:], 1.0), [onesb], [])
    blockones = fw.sbuf("blockones", [128, 128], F32)
    dve.op(lambda e: e.memset(blockones[:], 0.0), [blockones], [])
    dve.op(lambda e: e.memset(blockones[0:64, 0:64], 1.0), [blockones], [])
    dve.op(lambda e: e.memset(blockones[64:128, 64:128], 1.0), [blockones], [])
    smt = fw.sbuf("smt", [128, 8], F32)
    sp.dma(smt, smt[:], sm, sm[:])
    es = fw.sbuf("es", [128, 2], F32)
    act.op(lambda e: e.activation(es[:], smt[:, 4:6], AF.Exp), [es], [smt])
    mk = fw.sbuf("mk", [128, 6, 512], BF16)
    pool.dma(mk, mk[:], masks, masks[:])
    QT = fw.sbuf("QT", [128, T], BF16)
    KT = fw.sbuf("KT", [128, T], BF16)
    V = fw.sbuf("V", [128, NKC, 64], BF16)
    pool.dma(V, V[:], v, v[:])
    qtiles = [(t0, t0, 512) for t0 in range(0, LQ, 512)] + [(LQ, LQ, CTX)]
    ktiles = [(0, LQ, CTX)] + [(CTX + t0, t0, 512) for t0 in range(0, LQ, 512)]
    fw.push_scope()
    qknorm_rope(fw, qT, qsT, CS, (smt, 0), QT, None, qtiles, blockones, "qn", 1.0 / 64)
    qknorm_rope(fw, kT, ksT, CS, (smt, 2), KT, None, ktiles, blockones, "kn", 1.0 / 64)
    del fw.qk_scr
    fw.pop_scope()

    NSB = 2
    NPT = 4
    ps_s = [[fw.psum("ps_s%d_%d" % (m, i), [128, 512], F32) for i in range(NSB)] for m in range(2)]
    ps_o = [fw.psum("ps_o%d" % m, [64, 512], F32) for m in range(2)]
    ps_z = [fw.psum("ps_z%d" % m, [64, 512], F32) for m in range(2)]
    pts = [[fw.sbuf("pt%d_%d" % (m, i), [128, 512], BF16) for i in range(NPT)] for m in range(2)]
    rz = [fw.sbuf("rz%d" % m, [64, 512], F32) for m in range(2)]
    ots = [fw.sbuf("ot%d" % i, [64, 512], F32) for i in range(4)]
    cnt = 0
    for qi, (q0, _, n) in enumerate(qtiles):
        if q0 < LQ:
            n0 = q0 // 128
            chunks = [(0, None), (1, None)]
            for rel in range(-1, 5):
                j = n0 + rel
                if 0 <= j < NB:
                    chunks.append((CTX // 128 + j, rel + 1))
        else:
            chunks = [(0, None), (1, None)]
        nch = len(chunks)
        seq = []
        for ci in range(nch):
            seq.append(("s", ci))
            if ci >= 1:
                seq.append(("av", ci - 1))
        seq.append(("av", nch - 1))
        for kind, ci in seq:
            kc, mi = chunks[ci]
            if kind == "s":
                for m in range(2):
                    r0 = 64 * m
                    p = ps_s[m][ci % NSB]
                    pe.op(lambda e: e.matmul(p[:, 0:n], KT[r0:r0 + 64, kc * 128:(kc + 1) * 128], QT[r0:r0 + 64, q0:q0 + n],
                                             start=True, stop=True), [p], [KT, QT])
                for m in range(2):
                    p = ps_s[m][ci % NSB]
                    pt = pts[m][ci % NPT]
                    act.op(lambda e: e.activation(pt[:, 0:n], p[:, 0:n], AF.Exp, scale=0.125), [pt], [p])
                    if mi is not None:
                        (dve if m == 0 else pool).op(lambda e: e.tensor_tensor(pt[:, 0:n], pt[:, 0:n], mk[:, mi, 0:n], ALU.mult), [pt], [pt, mk])
            else:
                last = ci == nch - 1
                for m in range(2):
                    pt = pts[m][ci % NPT]
                    pe.op(lambda e: e.matmul(ps_o[m][:, 0:n], V[:, kc, :], pt[:, 0:n], start=(ci == 0), stop=last), [ps_o[m]], [V, pt])
                    pe.op(lambda e: e.matmul(ps_z[m][:, 0:n], onesb[:, 0:64], pt[:, 0:n], start=(ci == 0), stop=last), [ps_z[m]], [onesb, pt])
        for m in range(2):
            dve.op(lambda e: e.tensor_scalar(rz[m][:, 0:n], ps_z[m][:, 0:n], es[0:64, m:m + 1], None, ALU.add), [rz[m]], [ps_z[m], es])
            dve.op(lambda e: e.reciprocal(rz[m][:, 0:n], rz[m][:, 0:n]), [rz[m]], [rz[m]])
            ot = ots[cnt % 4]
            cnt += 1
            dve.op(lambda e: e.tensor_tensor(ot[:, 0:n], ps_o[m][:, 0:n], rz[m][:, 0:n], ALU.mult), [ot], [ps_o[m], rz[m]])
            sp.dma(oT, oT.t[m, :, q0:q0 + n], ot, ot[:, 0:n])
    sp.wait_buf(oT)
    fw.close()
    return fw.nc


def band_masks():
    ki = np.arange(128)[:, None, None]
    rel = np.arange(-1, 5)[None, :, None]
    qq = np.arange(512)[None, None, :]
    return (np.abs(128 * rel + ki - qq) <= 128).astype(np.float32)


CTX = 256
POOL_WINDOWS = (2, 4, 8, 16)


def build_k2c(LQ=8192):
    fw = FW()
    T = LQ + CTX
    PADL = 8
    uT = fw.dram("uT", [128, T], F32, "ExternalInput")
    wsel = fw.dram("wsel", [128, 16], F32, "ExternalInput")
    invc = fw.dram("invc", [128, T], F32, "ExternalInput")
    wl = fw.dram("wl", [128, 128], F32, "ExternalInput")
    ls = fw.dram("ls", [128, 1], F32, "ExternalInput")
    yT = fw.dram("yT", [128, T], F32, "ExternalOutput")
    fw.engines()
    pe, act, dve, pool, sp = fw.pe, fw.act, fw.dve, fw.pool, fw.sp
    wst = fw.sbuf("wst", [128, 16], F32)
    sp.dma(wst, wst[:], wsel, wsel[:])
    lst = fw.sbuf("lst", [128, 1], F32)
    sp.dma(lst, lst[:], ls, ls[:])
    wlt = fw.sbuf("wlt", [128, 128], BF16)
    pool.dma(wlt, wlt[:], wl, wl[:])
    TT = 2048
    ut = fw.sbuf("ut", [128, TT + 16], F32)
    ic = fw.sbuf("ic", [128, TT], F32)
    acc = fw.sbuf("acc", [128, TT], F32)
    db = fw.sbuf("db", [128, TT], BF16)
    ps = [fw.psum("ps%d" % i, [128, 512], F32) for i in range(2)]
    ots = [fw.sbuf("ot%d" % i, [128, 512], F32) for i in range(2)]
    cnt = 0
    for (s0, Ls) in ((0, LQ), (LQ, CTX)):
        tt_ = min(TT, Ls)
        for t0 in range(0, Ls, tt_):
            lo = max(t0 - 8, 0)
            hi = min(t0 + tt_ + 8, Ls)
            if lo > t0 - 8:
                dve.op(lambda e: e.memset(ut[:, 0:8], 0.0), [ut], [])
            if hi < t0 + tt_ + 8:
                dve.op(lambda e: e.memset(ut[:, tt_ + 8:tt_ + 16], 0.0), [ut], [])
            sp.dma(ut, ut[:, lo - (t0 - 8):hi - (t0 - 8)], uT, uT.t[:, s0 + lo:s0 + hi])
            sp.dma(ic, ic[:, 0:tt_], invc, invc.t[:, s0 + t0:s0 + t0 + tt_])
            dve.op(lambda e: e.tensor_scalar(acc[:, 0:tt_], ut[:, 0:tt_], wst[:, 0:1], None, ALU.mult), [acc], [ut, wst])
            for k in range(1, 16):
                dve.op(lambda e: e.scalar_tensor_tensor(acc[:, 0:tt_], ut[:, k:k + tt_], wst[:, k:k + 1], acc[:, 0:tt_], ALU.mult, ALU.add), [acc], [ut, wst])
            dve.op(lambda e: e.tensor_tensor(acc[:, 0:tt_], acc[:, 0:tt_], ic[:, 0:tt_], ALU.mult), [acc], [ic])
            dve.op(lambda e: e.tensor_tensor(db[:, 0:tt_], acc[:, 0:tt_], ut[:, 8:8 + tt_], ALU.subtract), [db], [acc, ut])
            for c0 in range(0, tt_, 512):
                n = min(512, tt_ - c0)
                p = ps[cnt % 2]
                ot = ots[cnt % 2]
                cnt += 1
                pe.op(lambda e: e.matmul(p[:, 0:n], wlt[:], db[:, c0:c0 + n], start=True, stop=True), [p], [wlt, db])
                act.op(lambda e: e.activation(ot[:, 0:n], p[:, 0:n], AF.Copy, scale=lst[:, 0:1]), [ot], [p, lst])
                sp.dma(yT, yT.t[:, s0 + t0 + c0:s0 + t0 + c0 + n], ot, ot[:, 0:n])
    sp.wait_buf(yT)
    fw.close()
    return fw.nc


def pool_consts(g, LQ):
    w = POOL_WINDOWS[g]
    lo = w // 2
    hi = w - 1 - lo
    sel = np.zeros(16, np.float32)
    for s in range(-lo, hi + 1):
        sel[s + 8] = 1.0
    outs = []
    for Ls in (LQ, CTX):
        t = np.arange(Ls)
        start = np.clip(t - lo, 0, Ls)
        end = np.clip(t + hi + 1, 0, Ls)
        outs.append((1.0 / (end - start).astype(np.float32)).astype(np.float32))
    ic = np.concatenate(outs)
    return np.tile(sel[None], (128, 1)), np.ascontiguousarray(np.tile(ic[None], (128, 1)))


EPS = 1e-6
CTX = 256
HY_EMB = 33


def sin_act(fw, out_buf, out_ap, p, n, fb, scr):
    act, dve = fw.act, fw.dve
    s2, s4 = scr
    P = 64
    act.op(lambda e: e.activation(s2[0:P, 0:n], p[0:P, 0:n], AF.Sin, bias=fb[0:P, 1:2], scale=fb[0:P, 0:1]), [s2], [p, fb])
    act.op(lambda e: e.activation(s4[0:P, 0:n], p[0:P, 0:n], AF.Sin, bias=fb[0:P, 3:4], scale=fb[0:P, 2:3]), [s4], [p, fb])
    dve.op(lambda e: e.tensor_tensor(s4[0:P, 0:n], s4[0:P, 0:n], s4[0:P, 0:n], ALU.mult), [s4], [s4])
    dve.op(lambda e: e.tensor_scalar(s4[0:P, 0:n], s4[0:P, 0:n], -2.0, 1.0, ALU.mult, ALU.add), [s4], [s4])
    dve.op(lambda e: e.scalar_tensor_tensor(out_ap, s2[0:P, 0:n], 2.0, s4[0:P, 0:n], ALU.mult, ALU.mult), [out_buf], [s2, s4])


def filter_gen(fw, Lf, zf, t01, wts, KF, nm, scr):
    pe, act, dve, pool, sp = fw.pe, fw.act, fw.dve, fw.pool, fw.sp
    w1, w2, w3, fb1, fb2, ndelta = wts["w1"], wts["w2"], wts["w3"], wts["fb1"], wts["fb2"], wts["ndelta"]
    ps1, ps2, ps3, zt, h1, h2, s2, s4, tt, kt, ktb, sqt = scr
    nt = (Lf + 511) // 512
    part = fw.sbuf(nm + "_part", [128, 2 * nt], F32)
    dve.op(lambda e: e.memset(part[:], 0.0), [part], [])
    for di in range(2):
        for ti in range(nt):
            t0 = ti * 512
            n = min(512, Lf - t0)
            sp.dma(zt, zt[0:HY_EMB, 0:n], zf, zf.t[1 - di, :, t0:t0 + n])
            tsrc = bass.AP(tensor=t01.t.tensor, offset=(1 - di) * Lf + t0, ap=[[0, 128], [1, n]])
            sp.dma(tt, tt[:, 0:n], t01, tsrc)
            pe.op(lambda e: e.matmul(ps1[0:64, 0:n], w1[0:HY_EMB, :], zt[0:HY_EMB, 0:n], start=True, stop=True), [ps1], [w1, zt])
            sin_act(fw, h1, h1[0:64, 0:n], ps1, n, fb1, (s2, s4))
            pe.op(lambda e: e.matmul(ps2[0:64, 0:n], w2[0:64, :], h1[0:64, 0:n], start=True, stop=True), [ps2], [w2, h1])
            sin_act(fw, h2, h2[0:64, 0:n], ps2, n, fb2, (s2, s4))
            pe.op(lambda e: e.matmul(ps3[:, 0:n], w3[0:64, di, :], h2[0:64, 0:n], start=True, stop=True), [ps3], [w3, h2])
            act.op(lambda e: e.activation(tt[:, 0:n], tt[:, 0:n], AF.Exp, scale=ndelta[:, 0:1]), [tt], [tt, ndelta])
            dve.op(lambda e: e.tensor_tensor(kt[:, 0:n], ps3[:, 0:n], tt[:, 0:n], ALU.mult), [kt], [ps3, tt])
            if di == 1 and ti == 0:
                dve.op(lambda e: e.memset(kt[:, 0:1], 0.0), [kt], [])
            dve.op(lambda e: e.tensor_tensor(sqt[:, 0:n], kt[:, 0:n], kt[:, 0:n], ALU.mult), [sqt], [kt])
            dve.op(lambda e: e.reduce_sum(part[:, di * nt + ti:di * nt + ti + 1], sqt[:, 0:n], AX.X), [part], [sqt])
            act.op(lambda e: e.activation(ktb[:, 0:n], kt[:, 0:n], AF.Copy), [ktb], [kt])
            if di == 0:
                sp.dma(KF, KF.t[:, 1 + t0:1 + t0 + n], ktb, ktb[0:64, 0:n])
            elif ti == 0:
                sp.dma(KF, KF.t[:, Lf + 1:Lf + n], ktb, ktb[0:64, 1:n])
            else:
                sp.dma(KF, KF.t[:, Lf + t0:Lf + t0 + n], ktb, ktb[0:64, 0:n])
    scale = fw.sbuf(nm + "_scale", [128, 1], F32)
    dve.op(lambda e: e.reduce_sum(scale[:], part[:], AX.X), [scale], [part])
    dve.op(lambda e: e.tensor_scalar(scale[:], scale[:], EPS, None, ALU.add), [scale], [scale])
    act.op(lambda e: e.activation(scale[:], scale[:], AF.Sqrt), [scale], [scale])
    dve.op(lambda e: e.reciprocal(scale[:], scale[:]), [scale], [scale])
    return scale


def conv3(fw, dst, u, n, sm, part):
    dve = fw.dve
    c = 3 * part
    dve.op(lambda e: e.tensor_scalar(dst[:, 0:n], u[:, 1:n + 1], sm[:, c + 1:c + 2], sm[:, 9 + part:10 + part], ALU.mult, ALU.add), [dst], [u, sm])
    dve.op(lambda e: e.scalar_tensor_tensor(dst[:, 0:n], u[:, 0:n], sm[:, c:c + 1], dst[:, 0:n], ALU.mult, ALU.add), [dst], [u, sm])
    dve.op(lambda e: e.scalar_tensor_tensor(dst[:, 0:n], u[:, 2:n + 2], sm[:, c + 2:c + 3], dst[:, 0:n], ALU.mult, ALU.add), [dst], [u, sm])


def load_u_tile(fw, ut, uT, part, t0, n, Ls):
    sp, dve = fw.sp, fw.dve
    lo = max(t0 - 1, 0)
    hi = min(t0 + n + 1, Ls)
    if t0 == 0:
        dve.op(lambda e: e.memset(ut[:, 0:1], 0.0), [ut], [])
    if t0 + n == Ls:
        dve.op(lambda e: e.memset(ut[:, n + 1:n + 2], 0.0), [ut], [])
    sp.dma(ut, ut[:, lo - (t0 - 1):hi - (t0 - 1)], uT, uT.t[part, :, lo:hi])


def build_k2d(LQ=8192):
    fw = FW()
    NB = LQ // 128
    NBC = CTX // 128
    TT = min(1024, LQ)
    uT = fw.dram("uT", [3, 128, LQ], F32, "ExternalInput")
    ucT = fw.dram("ucT", [3, 128, CTX], F32, "ExternalInput")
    smd = fw.dram("sm", [128, 16], F32, "ExternalInput")
    w1d = fw.dram("w1", [HY_EMB, 64], F32, "ExternalInput")
    w2d = fw.dram("w2", [64, 64], F32, "ExternalInput")
    w3d = fw.dram("w3", [64, 2, 128], F32, "ExternalInput")
    fbd = fw.dram("fb", [64, 4], F32, "ExternalInput")
    zfL = fw.dram("zfL", [2, HY_EMB, LQ], F32, "ExternalInput")
    t01L = fw.dram("t01L", [2, LQ], F32, "ExternalInput")
    zfC = fw.dram("zfC", [2, HY_EMB, CTX], F32, "ExternalInput")
    t01C = fw.dram("t01C", [2, CTX], F32, "ExternalInput")
    oT = fw.dram("oT", [128, LQ], F32, "ExternalOutput")
    ocT = fw.dram("ocT", [128, CTX], F32, "ExternalOutput")
    KF = fw.dram("KF", [64, 2 * LQ], BF16, "Internal")
    KFC = fw.dram("KFC", [64, 2 * CTX], BF16, "Internal")
    fw.engines()
    pe, act, dve, pool, sp = fw.pe, fw.act, fw.dve, fw.pool, fw.sp

    identb = fw.sbuf("identb", [128, 128], BF16)
    identf = fw.sbuf("identf", [128, 128], F32)
    idd = fw.dram("ident", [128, 128], F32, "ExternalInput")
    sp.dma(identf, identf[:], idd, idd[:])
    dve.op(lambda e: e.tensor_copy(identb[:], identf[:]), [identb], [identf])
    antif = fw.sbuf("antif", [128, 128], F32)
    add = fw.dram("anti", [128, 128], F32, "ExternalInput")
    sp.dma(antif, antif[:], add, add[:])
    sm = fw.sbuf("smt", [128, 16], F32)
    sp.dma(sm, sm[:], smd, smd[:])
    w1 = fw.sbuf("w1s", [HY_EMB, 64], F32)
    sp.dma(w1, w1[:], w1d, w1d[:])
    w2 = fw.sbuf("w2s", [64, 64], F32)
    sp.dma(w2, w2[:], w2d, w2d[:])
    w3 = fw.sbuf("w3s", [64, 2, 128], F32)
    sp.dma(w3, w3[:], w3d, w3d[:])
    fb = fw.sbuf("fbs", [64, 4], F32)
    sp.dma(fb, fb[:], fbd, fbd[:])
    fb1 = fw.sbuf("fb1", [64, 4], F32)
    fb2 = fw.sbuf("fb2", [64, 4], F32)
    for dst, bc in ((fb1, 1), (fb2, 2)):
        dve.op(lambda e: e.tensor_scalar(dst[:, 0:1], fb[:, 0:1], 0.5, None, ALU.mult), [dst], [fb])
        dve.op(lambda e: e.tensor_scalar(dst[:, 2:3], fb[:, 0:1], 0.25, None, ALU.mult), [dst], [fb])
        dve.op(lambda e: e.tensor_tensor(dst[:, 1:2], dst[:, 0:1], fb[:, bc:bc + 1], ALU.mult), [dst], [dst, fb])
        dve.op(lambda e: e.tensor_tensor(dst[:, 3:4], dst[:, 2:3], fb[:, bc:bc + 1], ALU.mult), [dst], [dst, fb])
    ndelta = fw.sbuf("ndelta", [128, 1], F32)
    dve.op(lambda e: e.tensor_copy(ndelta[:], sm[:, 13:14]), [ndelta], [sm])
    wts = dict(w1=w1, w2=w2, w3=w3, fb1=fb1, fb2=fb2, ndelta=ndelta)
    scr = (fw.psum("fg_ps1", [128, 512], F32), fw.psum("fg_ps2", [128, 512], F32), fw.psum("fg_ps3", [128, 512], F32),
           fw.sbuf("fg_zt", [64, 512], F32), fw.sbuf("fg_h1", [64, 512], F32), fw.sbuf("fg_h2", [64, 512], F32),
           fw.sbuf("fg_s2", [64, 512], F32), fw.sbuf("fg_s4", [64, 512], F32), fw.sbuf("fg_tt", [128, 512], F32),
           fw.sbuf("fg_kt", [128, 512], F32), fw.sbuf("fg_ktb", [128, 512], BF16), fw.sbuf("fg_sq", [128, 512], F32))
    scaleL = filter_gen(fw, LQ, zfL, t01L, wts, KF, "fL", scr)
    scaleC = filter_gen(fw, CTX, zfC, t01C, wts, KFC, "fC", scr)

    Zt = fw.sbuf("Zt", [128, 64, NB, 2], BF16)
    Ztc = fw.sbuf("Ztc", [128, 64, NBC, 2], BF16)
    uts = [fw.sbuf("ut%d" % i, [128, TT + 2], F32) for i in range(3)]
    cv = [fw.sbuf("cv%d" % i, [128, TT], F32) for i in range(3)]
    zb = fw.sbuf("zb", [128, TT], BF16)
    pst = [fw.psum("pst%d" % i, [128, 512], BF16) for i in range(2)]

    def z_phase(src, Ls, Ztx, nblk):
        tt_ = min(TT, Ls)
        for t0 in range(0, Ls, tt_):
            for part in range(2):
                load_u_tile(fw, uts[part], src, part, t0, tt_, Ls)
                conv3(fw, cv[part], uts[part], tt_, sm, part)
            dve.op(lambda e: e.tensor_tensor(zb[:, 0:tt_], cv[0][:, 0:tt_], cv[1][:, 0:tt_], ALU.mult), [zb], [cv[0], cv[1]])
            for rb in range(tt_ // 128):
                r = t0 // 128 + rb
                p = pst[r % 2]
                pe.op(lambda e: e.transpose(p[:, 0:128], zb[:, rb * 128:(rb + 1) * 128], identb[:]), [p], [zb, identb])
                dst = Ztx[:, :, r, :].rearrange("p c b -> p b c")
                src_ap = p[:, 0:128].rearrange("p (b c) -> p b c", b=2)
                if r % 2 == 0:
                    dve.op(lambda e: e.tensor_copy(dst, src_ap), [Ztx], [p])
                else:
                    act.op(lambda e: e.activation(dst, src_ap, AF.Copy), [Ztx], [p])

    z_phase(uT, LQ, Zt, NB)
    z_phase(ucT, CTX, Ztc, NBC)

    W = (2 * NB - 1) * 128
    X0 = (NB - 1) * 128
    tbs = [fw.sbuf("tb%d" % i, [128, W], BF16) for i in range(2)]
    tbc = fw.sbuf("tbc", [128, 16, 3 * 128], BF16)
    Y = fw.sbuf("Y", [128, NB, 2, 64], F32)
    Yc = fw.sbuf("Yc", [128, NBC, 2, 64], F32)
    psy = [fw.psum("psy%d" % i, [128, 512], F32) for i in range(2)]
    kft = KF.t.tensor
    kfct = KFC.t.tensor
    for ch in range(64):
        tb = tbs[ch % 2]
        src = bass.AP(tensor=kft, offset=ch * 2 * LQ + 1, ap=[[1, 128], [1, W]])
        (sp if ch % 2 == 0 else act).dma(tb, tb[:], KF, src)
        p = psy[ch % 2]
        ds = [0] + [d for d in range(-(NB - 1), NB) if d != 0]
        for k, d in enumerate(ds):
            r0 = max(0, d)
            nb = NB - abs(d)
            pe.op(lambda e: e.matmul(p[:, r0 * 2:(r0 + nb) * 2], tb[:, X0 - 128 * d:X0 - 128 * d + 128],
                                     Zt[:, ch, r0 - d:r0 - d + nb, :].rearrange("p r b -> p (r b)"),
                                     start=(k == 0), stop=(k == len(ds) - 1), skip_group_check=True), [p], [tb, Zt])
        if ch % 2 == 0:
            dve.op(lambda e: e.tensor_copy(Y[:, :, :, ch], p[:, 0:NB * 2].rearrange('p (r b) -> p r b', b=2)), [Y], [p])
        else:
            act.op(lambda e: e.activation(Y[:, :, :, ch], p[:, 0:NB * 2].rearrange('p (r b) -> p r b', b=2), AF.Copy), [Y], [p])
    pc = psy[0]
    for g in range(4):
        src = bass.AP(tensor=kfct, offset=g * 16 * 2 * CTX + 1, ap=[[1, 128], [2 * CTX, 16], [1, 384]])
        sp.dma(tbc, tbc[:], KFC, src)
        for c16 in range(16):
            ch = g * 16 + c16
            o0 = ch * 4
            pe.op(lambda e: e.matmul(pc[:, o0:o0 + 4], tbc[:, c16, 128:256], Ztc[:, ch, :, :].rearrange("p r b -> p (r b)"),
                                     start=True, stop=False, skip_group_check=True), [pc], [tbc, Ztc])
            pe.op(lambda e: e.matmul(pc[:, o0 + 2:o0 + 4], tbc[:, c16, 0:128], Ztc[:, ch, 0, :],
                                     start=False, stop=False, skip_group_check=True), [pc], [tbc, Ztc])
            pe.op(lambda e: e.matmul(pc[:, o0:o0 + 2], tbc[:, c16, 256:384], Ztc[:, ch, 1, :],
                                     start=False, stop=True, skip_group_check=True), [pc], [tbc, Ztc])
    dve.op(lambda e: e.tensor_copy(Yc[:].rearrange("p r b c -> p c r b"), pc[:, 0:256].rearrange("p (c r b) -> p c r b", r=NBC, b=2)), [Yc], [pc])

    pso = [fw.psum("pso%d" % i, [128, 512], F32) for i in range(1)]
    ys = fw.sbuf("ys", [128, 512], F32)
    ots = [fw.sbuf("ot%d" % i, [128, 512], F32) for i in range(2)]

    def out_phase(src, Ls, Yx, scale, dstT):
        tt_ = min(TT, Ls)
        cnt = 0
        for t0 in range(0, Ls, tt_):
            for part in range(3):
                load_u_tile(fw, uts[part], src, part, t0, tt_, Ls)
                conv3(fw, cv[part], uts[part], tt_, sm, part)
            dve.op(lambda e: e.tensor_tensor(cv[0][:, 0:tt_], cv[0][:, 0:tt_], cv[1][:, 0:tt_], ALU.mult), [cv[0]], [cv[1]])
            for s0 in range(0, tt_, 512):
                ns = min(512, tt_ - s0)
                p = pso[0]
                for rb in range(ns // 128):
                    r = (t0 + s0) // 128 + rb
                    in_ap = Yx[:, r, :, :].rearrange("p b c -> p (b c)")
                    pe.op(lambda e: e.matmul(p[:, rb * 128:(rb + 1) * 128], in_ap, antif[:], start=True, stop=True), [p], [Yx, antif])
                act.op(lambda e: e.activation(ys[:, 0:ns], p[:, 0:ns], AF.Copy, scale=scale[:, 0:1]), [ys], [p, scale])
                dve.op(lambda e: e.scalar_tensor_tensor(ys[:, 0:ns], cv[0][:, s0:s0 + ns], sm[:, 12:13], ys[:, 0:ns], ALU.mult, ALU.add), [ys], [cv[0], sm])
                ot = ots[cnt % 2]
                cnt += 1
                dve.op(lambda e: e.tensor_tensor(ot[:, 0:ns], ys[:, 0:ns], cv[2][:, s0:s0 + ns], ALU.mult), [ot], [ys, cv[2]])
                sp.dma(dstT, dstT.t[:, t0 + s0:t0 + s0 + ns], ot, ot[:, 0:ns])

    out_phase(uT, LQ, Y, scaleL, oT)
    out_phase(ucT, CTX, Yc, scaleC, ocT)
    sp.wait_buf(oT)
    sp.wait_buf(ocT)
    fw.close()
    return fw.nc


def hy_feats(L):
    t01 = np.linspace(0.0, 1.0, L, dtype=np.float32)
    bands = (HY_EMB - 1) // 2
    w_ang = (2.0 * math.pi * np.arange(L, dtype=np.float32) / L).astype(np.float32)
    f = np.linspace(1e-4, bands - 1, bands, dtype=np.float32)
    ang = (f[None, :] * w_ang[:, None]).astype(np.float32)
    z = np.concatenate([t01[:, None], np.cos(ang), -np.sin(ang)], axis=-1).astype(np.float32)
    zf = np.stack([z.T, z[::-1].T])
    tt = np.stack([t01, t01[::-1]])
    return np.ascontiguousarray(zf), np.ascontiguousarray(tt)


def hy_ndelta(D_WIDTH=512):
    d = np.linspace(math.log(1e-2) / 0.3, math.log(1e-2) / 1.5, D_WIDTH, dtype=np.float32)
    return -np.abs(d)


def k2d_inputs(core, u_lat, u_ctx, conv_w, conv_b, w1, b1, w2, b2, w3, freq, bias, LQ):
    c0 = 64 * core
    DW = 512
    def pk(u, Ls):
        parts = []
        for part in range(3):
            cols = u[:, :, part * DW + c0: part * DW + c0 + 64]
            parts.append(cols.transpose(0, 2, 1).reshape(128, Ls))
        return np.ascontiguousarray(np.stack(parts))
    sm = np.zeros((128, 16), np.float32)
    for part in range(3):
        for tap in range(3):
            sm[:, part * 3 + tap] = np.tile(conv_w[tap, part * DW + c0: part * DW + c0 + 64], 2)
        sm[:, 9 + part] = np.tile(conv_b[part * DW + c0: part * DW + c0 + 64], 2)
    sm[:, 12] = np.tile(bias[c0:c0 + 64], 2)
    sm[:, 13] = np.tile(hy_ndelta()[c0:c0 + 64], 2)
    w3r = w3.reshape(64, 2, DW)[:, :, c0:c0 + 64]
    w3p = np.ascontiguousarray(np.concatenate([w3r, w3r], axis=2))
    fb = np.zeros((64, 4), np.float32)
    fb[:, 0] = freq; fb[:, 1] = b1; fb[:, 2] = b2
    zfL, t01L = hy_feats(LQ)
    zfC, t01C = hy_feats(CTX)
    return {"uT": pk(u_lat, LQ), "ucT": pk(u_ctx, CTX), "sm": sm, "w1": np.ascontiguousarray(w1), "w2": np.ascontiguousarray(w2),
            "w3": w3p, "fb": fb, "zfL": zfL, "t01L": t01L, "zfC": zfC, "t01C": t01C, "ident": np.eye(128, dtype=np.float32), "anti": np.ascontiguousarray(np.eye(128, dtype=np.float32)[::-1])}


D = 2048
KC = 16
EPS = 1e-6
FH = 5632
HC = FH // 128


def build_k3(TL=2048, TC=64):
    fw = FW()
    LW = TL + 2
    CW = TC + 2
    TW = LW + CW
    TO = TL + TC
    oT = fw.dram("oT", [KC, 128, TW], F32, "ExternalInput")
    xT = fw.dram("xT", [KC, 128, TW], F32, "ExternalInput")
    vec = fw.dram("vec", [128, 9, KC], F32, "ExternalInput")
    edge = fw.dram("edge", [128, 4], F32, "ExternalInput")
    w_out = fw.dram("w_out", [8, 128, KC, 256], F32, "ExternalInput")
    w_up = fw.dram("w_up", [2 * FH // 256, 128, KC, 256], F32, "ExternalInput")
    cwd = fw.dram("cw", [128, 4, 2 * HC], F32, "ExternalInput")
    w_dn = fw.dram("w_dn", [8, 128, HC, 256], F32, "ExternalInput")
    xo = fw.dram("xo", [KC, 128, TO], F32, "ExternalOutput")
    XN = fw.dram("XN", [KC, 128, TW], F32, "Internal")
    AT = fw.dram("AT", [HC, 128, TO], BF16, "Internal")
    fw.engines()
    pe, act, dve, pool, sp = fw.pe, fw.act, fw.dve, fw.pool, fw.sp
    ones = fw.sbuf("ones", [128, 128], F32)
    dve.op(lambda e: e.memset(ones[:], 1.0), [ones], [])
    vt = fw.sbuf("vt", [128, 9, KC], F32)
    sp.dma(vt, vt[:], vec, vec[:])
    eg = fw.sbuf("eg", [128, 4], F32)
    sp.dma(eg, eg[:], edge, edge[:])
    A_lat = fw.sbuf("A_lat", [128, KC], F32)
    A_ctx = fw.sbuf("A_ctx", [128, KC], F32)
    dve.op(lambda e: e.scalar_tensor_tensor(A_lat[:], vt[:, 3, :], 1.0, vt[:, 2, :], ALU.add, ALU.mult), [A_lat], [vt])
    dve.op(lambda e: e.scalar_tensor_tensor(A_ctx[:], vt[:, 5, :], 1.0, vt[:, 2, :], ALU.add, ALU.mult), [A_ctx], [vt])
    h2T = fw.sbuf("h2T", [128, KC, TW], BF16)

    fw.push_scope()
    wo = [fw.sbuf("wo%d" % i, [128, KC, 256], BF16) for i in range(8)]
    for i in range(8):
        pool.dma(wo[i], wo[i][:], w_out, w_out.t[i])
    ots = [fw.sbuf("a_ot%d" % i, [128, KC, 256], BF16) for i in range(2)]
    xts = [fw.sbuf("a_xt%d" % i, [128, KC, 256], F32) for i in range(2)]
    sq = fw.sbuf("a_sq", [128, KC, 256], F32)
    rs = fw.sbuf("a_rs", [128, 256], F32)
    psA = [fw.psum("a_ps%d" % i, [128, 512], F32) for i in range(4)]
    psn = fw.psum("a_psn", [128, 512], F32)
    tilesA = [(c0, 256, 0) for c0 in range(0, TL, 256)] + [(TL, 2, 0), (LW, CW, 1)]
    ov = oT.t.rearrange("k p t -> p k t")
    xv = xT.t.rearrange("k p t -> p k t")
    xnv = XN.t.rearrange("k p t -> p k t")
    cnt = 0
    for ti, (c0, n, kind) in enumerate(tilesA):
        ot, xt = ots[ti % 2], xts[ti % 2]
        g1c = 0 if kind == 0 else 1
        A2 = A_lat if kind == 0 else A_ctx
        shc = 4 if kind == 0 else 6
        pool.dma(ot, ot[:, :, 0:n], oT, ov[:, :, c0:c0 + n])
        sp.dma(xt, xt[:, :, 0:n], xT, xv[:, :, c0:c0 + n])
        for ci in range(KC):
            p = psA[cnt % 4]
            cnt += 1
            w = wo[ci // 2]
            h0 = (ci % 2) * 128
            for k in range(KC):
                pe.op(lambda e: e.matmul(p[:, 0:n], w[:, k, h0:h0 + 128], ot[:, k, 0:n], start=(k == 0), stop=(k == KC - 1)), [p], [w, ot])
            dve.op(lambda e: e.scalar_tensor_tensor(xt[:, ci, 0:n], p[:, 0:n], vt[:, g1c, ci:ci + 1], xt[:, ci, 0:n], ALU.mult, ALU.add), [xt], [p, vt])
        sp.dma(XN, xnv[:, :, c0:c0 + n], xt, xt[:, :, 0:n])
        act.op(lambda e: e.activation(sq[:, :, 0:n], xt[:, :, 0:n], AF.Square), [sq], [xt])
        for k in range(KC):
            pe.op(lambda e: e.matmul(psn[:, 0:n], ones[:], sq[:, k, 0:n], start=(k == 0), stop=(k == KC - 1)), [psn], [ones, sq])
        dve.op(lambda e: e.tensor_scalar(rs[:, 0:n], psn[:, 0:n], 1.0 / D, EPS, ALU.mult, ALU.add), [rs], [psn])
        act.op(lambda e: e.activation(rs[:, 0:n], rs[:, 0:n], AF.Sqrt), [rs], [rs])
        dve.op(lambda e: e.reciprocal(rs[:, 0:n], rs[:, 0:n]), [rs], [rs])
        for k in range(KC):
            dve.op(lambda e: e.tensor_tensor(sq[:, k, 0:n], xt[:, k, 0:n], rs[:, 0:n], ALU.mult), [sq], [xt, rs])
        for k in range(KC):
            act.op(lambda e: e.activation(h2T[:, k, c0:c0 + n], sq[:, k, 0:n], AF.Identity, bias=vt[:, shc, k:k + 1], scale=A2[:, k:k + 1]),
                   [h2T], [sq, A2, vt])
    fw.pop_scope()

    fw.push_scope()
    cw = fw.sbuf("cws", [128, 4, 2 * HC], F32)
    sp.dma(cw, cw[:], cwd, cwd[:])
    wgs = [fw.sbuf("b_wg%d" % i, [128, KC, 256], BF16) for i in range(2)]
    wus = [fw.sbuf("b_wu%d" % i, [128, KC, 256], BF16) for i in range(2)]
    psB = [fw.psum("b_ps%d" % i, [128, 512], F32) for i in range(4)]
    ug = [fw.sbuf("b_ug%d" % i, [128, 512], F32) for i in range(2)]
    uu = [fw.sbuf("b_uu%d" % i, [128, 512], F32) for i in range(2)]
    cg = [fw.sbuf("b_cg%d" % i, [128, 512], F32) for i in range(2)]
    cu = [fw.sbuf("b_cu%d" % i, [128, 512], F32) for i in range(2)]
    ab = [fw.sbuf("b_ab%d" % i, [128, 512], BF16) for i in range(2)]
    tilesB = []
    s = 0
    while s + 2 < LW:
        m = min(512, LW - s)
        tilesB.append((s, m, s, 0 if s == 0 else None, 1 if s + m == LW else None))
        s += m - 2
    tilesB.append((LW, CW, TL, 2, 3))

    def conv(dst, u, m, hc):
        dve.op(lambda e: e.tensor_scalar(dst[:, 0:m - 2], u[:, 1:m - 1], cw[:, 1, hc:hc + 1], cw[:, 3, hc:hc + 1], ALU.mult, ALU.add), [dst], [u, cw])
        dve.op(lambda e: e.scalar_tensor_tensor(dst[:, 0:m - 2], u[:, 0:m - 2], cw[:, 0, hc:hc + 1], dst[:, 0:m - 2], ALU.mult, ALU.add), [dst], [u, cw])
        dve.op(lambda e: e.scalar_tensor_tensor(dst[:, 0:m - 2], u[:, 2:m], cw[:, 2, hc:hc + 1], dst[:, 0:m - 2], ALU.mult, ALU.add), [dst], [u, cw])

    cnt = 0
    for j in range(FH // 256):
        wg, wu = wgs[j % 2], wus[j % 2]
        pool.dma(wg, wg[:], w_up, w_up.t[j])
        pool.dma(wu, wu[:], w_up, w_up.t[FH // 256 + j])
        for (s, m, o0, eL, eR) in tilesB:
            for half in range(2):
                hc = 2 * j + half
                h0 = half * 128
                i2 = cnt % 2
                cnt += 1
                pg, pu = psB[(2 * cnt) % 4], psB[(2 * cnt + 1) % 4]
                for k in range(KC):
                    pe.op(lambda e: e.matmul(pg[:, 0:m], wg[:, k, h0:h0 + 128], h2T[:, k, s:s + m], start=(k == 0), stop=(k == KC - 1)), [pg], [wg, h2T])
                for k in range(KC):
                    pe.op(lambda e: e.matmul(pu[:, 0:m], wu[:, k, h0:h0 + 128], h2T[:, k, s:s + m], start=(k == 0), stop=(k == KC - 1)), [pu], [wu, h2T])
                act.op(lambda e: e.activation(ug[i2][:, 0:m], pg[:, 0:m], AF.Copy), [ug[i2]], [pg])
                act.op(lambda e: e.activation(uu[i2][:, 0:m], pu[:, 0:m], AF.Copy), [uu[i2]], [pu])
                for ubuf in (ug[i2], uu[i2]):
                    if eL is not None:
                        dve.op(lambda e: e.tensor_scalar(ubuf[:, 0:1], ubuf[:, 0:1], eg[:, eL:eL + 1], None, ALU.mult), [ubuf], [ubuf, eg])
                    if eR is not None:
                        dve.op(lambda e: e.tensor_scalar(ubuf[:, m - 1:m], ubuf[:, m - 1:m], eg[:, eR:eR + 1], None, ALU.mult), [ubuf], [ubuf, eg])
                conv(cg[i2], ug[i2], m, hc)
                conv(cu[i2], uu[i2], m, HC + hc)
                act.op(lambda e: e.activation(cg[i2][:, 0:m - 2], cg[i2][:, 0:m - 2], AF.Silu), [cg[i2]], [cg[i2]])
                dve.op(lambda e: e.tensor_tensor(ab[i2][:, 0:m - 2], cg[i2][:, 0:m - 2], cu[i2][:, 0:m - 2], ALU.mult), [ab[i2]], [cg[i2], cu[i2]])
                sp.dma(AT, AT.t[hc, :, o0:o0 + m - 2], ab[i2], ab[i2][:, 0:m - 2])
    fw.pop_scope()

    fw.push_scope()
    HALF = TO // 3
    at = fw.sbuf("c_at", [128, HC, HALF], BF16)
    wds = [fw.sbuf("c_wd%d" % i, [128, HC, 256], BF16) for i in range(2)]
    psC = [fw.psum("c_ps%d" % i, [128, 512], F32) for i in range(4)]
    xns = [fw.sbuf("c_xn%d" % i, [128, 512], F32) for i in range(3)]
    outs = [fw.sbuf("c_o%d" % i, [128, 512], F32) for i in range(3)]
    atv = AT.t.rearrange("k p t -> p k t")
    cnt = 0
    wcnt = 0
    for hf in range(3):
        h0, h1 = hf * HALF, (hf + 1) * HALF
        sp.dma(at, at[:], AT, atv[:, :, h0:h1])
        subs = []
        o = h0
        while o < h1:
            lim = TL if o < TL else TO
            n = min(512, min(h1, lim) - o)
            subs.append((o, n))
            o += n
        for ct in range(8):
            wd = wds[wcnt % 2]
            wcnt += 1
            pool.dma(wd, wd[:], w_dn, w_dn.t[ct])
            for (o0, n) in subs:
                kind = 0 if o0 < TL else 1
                xcol = o0 + 1 if kind == 0 else o0 + 3
                for h2 in range(2):
                    ci = 2 * ct + h2
                    p = psC[cnt % 4]
                    xn = xns[cnt % 3]
                    ob = outs[cnt % 3]
                    cnt += 1
                    sp.dma(xn, xn[:, 0:n], XN, XN.t[ci, :, xcol:xcol + n])
                    for k in range(HC):
                        pe.op(lambda e: e.matmul(p[:, 0:n], wd[:, k, h2 * 128:h2 * 128 + 128], at[:, k, o0 - h0:o0 - h0 + n],
                                                 start=(k == 0), stop=(k == HC - 1)), [p], [wd, at])
                    dve.op(lambda e: e.scalar_tensor_tensor(ob[:, 0:n], p[:, 0:n], vt[:, 7 + kind, ci:ci + 1], xn[:, 0:n], ALU.mult, ALU.add), [ob], [p, vt, xn])
                    sp.dma(xo, xo.t[ci, :, o0:o0 + n], ob, ob[:, 0:n])
    fw.pop_scope()
    sp.wait_buf(xo)
    fw.close()
    return fw.nc


def tile_w(w):
    K, N = w.shape
    return np.ascontiguousarray(w.reshape(K // 128, 128, N // 256, 256).transpose(2, 1, 0, 3))


def vec_pk(v):
    return np.ascontiguousarray(v.reshape(-1, 128).T)


OFF_QA, OFF_KA, OFF_VA, OFF_QB, OFF_KB, OFF_VB, OFF_POOL, OFF_HY = 0, 512, 1024, 1536, 2048, 2176, 2304, 2816
SEQ = 8192
NCTX = 256
_CORES = list(range(8))
_PROGS = {}
_N = {"launches": 0}


def _prog(name, builder):
    if name not in _PROGS:
        _PROGS[name] = builder()
    return _PROGS[name]


def _run(name, builder, ins):
    nc = _prog(name, builder)
    res = run_bass_kernel_spmd(nc, ins, core_ids=_CORES)
    return res.results


def _seg(aT, lo, hi):
    F, S = aT.shape
    out = np.zeros((F, hi - lo + 2), np.float32)
    l2, h2 = max(lo - 1, 0), min(hi + 1, S)
    out[:, l2 - (lo - 1):h2 - (lo - 1)] = aT[:, l2:h2]
    return out


def kernel(x, c, ctx, c_ctx, w_mod, b_mod, norm1_g, norm2_g, w_in, w_out, qk_gain, diff_lam, diff_subln,
           win_sink, pool_w, pool_scale, hy_conv_w, hy_conv_b, hy_w1, hy_b1, hy_w2, hy_b2, hy_w3, hy_freq,
           hy_bias, ffn_w_in, ffn_conv_w, ffn_conv_b, ffn_w_out):
    f32 = lambda a: np.ascontiguousarray(np.asarray(a), dtype=np.float32)
    (x, c, ctx, c_ctx, w_mod, b_mod, norm1_g, norm2_g, w_in, w_out, qk_gain, diff_lam, diff_subln, win_sink, pool_w,
     pool_scale, hy_conv_w, hy_conv_b, hy_w1, hy_b1, hy_w2, hy_b2, hy_w3, hy_freq, hy_bias, ffn_w_in, ffn_conv_w,
     ffn_conv_b, ffn_w_out) = [f32(a) for a in (x, c, ctx, c_ctx, w_mod, b_mod, norm1_g, norm2_g, w_in, w_out, qk_gain,
                                                diff_lam, diff_subln, win_sink, pool_w, pool_scale, hy_conv_w, hy_conv_b,
                                                hy_w1, hy_b1, hy_w2, hy_b2, hy_w3, hy_freq, hy_bias, ffn_w_in, ffn_conv_w,
                                                ffn_conv_b, ffn_w_out)]
    B, L, Dm = x.shape
    NL = w_mod.shape[0]
    T = L + NCTX
    r = _run("k0", build_k0, [k0_inputs(k, c, c_ctx, w_mod, b_mod) for k in _CORES])
    mod = np.concatenate([r[k]["mod"] for k in _CORES], axis=2)
    XT_lat = [np.ascontiguousarray(x[b].T) for b in range(B)]
    XT_ctx = [np.ascontiguousarray(ctx[b].T) for b in range(B)]
    CSt = rope_tables(L, NCTX, 2)
    mk = band_masks()
    zfL, t01L = hy_feats(L)
    zfC, t01C = hy_feats(NCTX)
    ident = np.eye(128, dtype=np.float32)
    anti = np.ascontiguousarray(ident[::-1])
    ndel = hy_ndelta()
    for l in range(NL):
        m = mod[l].reshape(3, 6, Dm)
        wt = tile_w(w_in[l])
        ins = []
        for k in _CORES:
            b, q = divmod(k, 4)
            xT = np.concatenate([XT_lat[b][:, q * 2048:(q + 1) * 2048], XT_ctx[b][:, q * 64:(q + 1) * 64]], axis=1)
            vec = np.stack([vec_pk(norm1_g[l]), vec_pk(m[b, 1]), vec_pk(m[b, 0]), vec_pk(m[2, 1]), vec_pk(m[2, 0])], axis=1)
            ins.append({"xT": np.ascontiguousarray(xT.reshape(16, 128, 2112)), "vec": np.ascontiguousarray(vec), "w": wt})
        r = _run("k1", build_k1, ins)
        pts = [r[k]["pT"].reshape(4352, 2112) for k in _CORES]
        PT_lat = [np.concatenate([pts[b * 4 + q][:, :2048] for q in range(4)], axis=1) for b in range(B)]
        PT_ctx = [np.concatenate([pts[b * 4 + q][:, 2048:] for q in range(4)], axis=1) for b in range(B)]
        del pts, r
        OT_lat = [np.zeros((Dm, L), np.float32) for _ in range(B)]
        OT_ctx = [np.zeros((Dm, NCTX), np.float32) for _ in range(B)]
        lambda_init = 0.8 - 0.6 * math.exp(-0.3 * l)
        g0 = np.tile(qk_gain[l, 0], 2)
        g1 = np.tile(qk_gain[l, 1], 2)
        ins = []
        for k in _CORES:
            b, h = divmod(k, 4)
            rq = slice(OFF_QA + h * 128, OFF_QA + (h + 1) * 128)
            rk = slice(OFF_KA + h * 128, OFF_KA + (h + 1) * 128)
            rv = slice(OFF_VA + h * 128, OFF_VA + (h + 1) * 128)
            q_full = np.ascontiguousarray(np.concatenate([PT_lat[b][rq], PT_ctx[b][rq]], axis=1))
            k_full = np.ascontiguousarray(np.concatenate([PT_ctx[b][rk], PT_lat[b][rk]], axis=1))
            v_full = np.concatenate([PT_ctx[b][rv], PT_lat[b][rv]], axis=1).T
            sm = np.zeros((128, 8), np.float32)
            sm[:, 0] = g0
            sm[:, 1] = swap_halves(g0[:, None])[:, 0]
            sm[:, 2] = g1
            sm[:, 3] = swap_halves(g1[:, None])[:, 0]
            sm[:, 4] = diff_subln[l]
            sm[:, 5] = lambda_init
            sm[:, 6] = 1.0 - lambda_init
            ins.append({"qT": q_full, "qsT": swap_halves(q_full), "kT": k_full, "ksT": swap_halves(k_full),
                        "v": np.ascontiguousarray(v_full.reshape(T // 128, 128, 128).transpose(1, 0, 2)),
                        "CS": CSt, "sm": sm, "lamp": np.ascontiguousarray(diff_lam[l].reshape(1, 256))})
        r = _run("k2a", build_k2a, ins)
        for k in _CORES:
            b, h = divmod(k, 4)
            o = r[k]["oT"]
            OT_lat[b][h * 128:(h + 1) * 128] = o[:, :L]
            OT_ctx[b][h * 128:(h + 1) * 128] = o[:, L:]
        g2 = np.tile(qk_gain[l, 2], 2)
        g3 = np.tile(qk_gain[l, 3], 2)
        ins = []
        for k in _CORES:
            b = k // 4
            h0 = 2 * (k % 4)
            kvh = (k % 4) // 2
            rq = slice(OFF_QB + h0 * 64, OFF_QB + (h0 + 2) * 64)
            rk = slice(OFF_KB + kvh * 64, OFF_KB + (kvh + 1) * 64)
            rv = slice(OFF_VB + kvh * 64, OFF_VB + (kvh + 1) * 64)
            q_full = np.ascontiguousarray(np.concatenate([PT_lat[b][rq], PT_ctx[b][rq]], axis=1))
            k1 = np.concatenate([PT_ctx[b][rk], PT_lat[b][rk]], axis=1)
            k_full = np.ascontiguousarray(np.concatenate([k1, k1], axis=0))
            v_full = np.concatenate([PT_ctx[b][rv], PT_lat[b][rv]], axis=1).T
            sm = np.zeros((128, 8), np.float32)
            sm[:, 0] = g2
            sm[:, 1] = swap_halves(g2[:, None])[:, 0]
            sm[:, 2] = g3
            sm[:, 3] = swap_halves(g3[:, None])[:, 0]
            sm[:, 4] = win_sink[l, h0]
            sm[:, 5] = win_sink[l, h0 + 1]
            ins.append({"qT": q_full, "qsT": swap_halves(q_full), "kT": k_full, "ksT": swap_halves(k_full),
                        "v": np.ascontiguousarray(v_full.reshape(T // 128, 128, 64).transpose(1, 0, 2)),
                        "CS": CSt, "sm": sm, "masks": mk})
        r = _run("k2b", build_k2b, ins)
        for k in _CORES:
            b = k // 4
            h0 = 2 * (k % 4)
            o = r[k]["oT"].reshape(128, T)
            OT_lat[b][512 + h0 * 64:512 + (h0 + 2) * 64] = o[:, :L]
            OT_ctx[b][512 + h0 * 64:512 + (h0 + 2) * 64] = o[:, L:]
        ins = []
        for k in _CORES:
            b, g = divmod(k, 4)
            ru = slice(OFF_POOL + g * 128, OFF_POOL + (g + 1) * 128)
            sel, ic = pool_consts(g, L)
            ins.append({"uT": np.ascontiguousarray(np.concatenate([PT_lat[b][ru], PT_ctx[b][ru]], axis=1)), "wsel": sel, "invc": ic,
                        "wl": np.ascontiguousarray(pool_w[l, g]), "ls": np.ascontiguousarray(pool_scale[l, g * 128:(g + 1) * 128, None])})
        r = _run("k2c", build_k2c, ins)
        for k in _CORES:
            b, g = divmod(k, 4)
            o = r[k]["yT"]
            OT_lat[b][1024 + g * 128:1024 + (g + 1) * 128] = o[:, :L]
            OT_ctx[b][1024 + g * 128:1024 + (g + 1) * 128] = o[:, L:]
        ins = []
        for k in _CORES:
            c0 = 64 * k
            uL, uC = [], []
            smh = np.zeros((128, 16), np.float32)
            for part in range(3):
                rr = slice(OFF_HY + part * 512 + c0, OFF_HY + part * 512 + c0 + 64)
                uL.append(np.concatenate([PT_lat[0][rr], PT_lat[1][rr]], axis=0))
                uC.append(np.concatenate([PT_ctx[0][rr], PT_ctx[1][rr]], axis=0))
                for tap in range(3):
                    smh[:, part * 3 + tap] = np.tile(hy_conv_w[l, tap, part * 512 + c0: part * 512 + c0 + 64], 2)
                smh[:, 9 + part] = np.tile(hy_conv_b[l, part * 512 + c0: part * 512 + c0 + 64], 2)
            smh[:, 12] = np.tile(hy_bias[l, c0:c0 + 64], 2)
            smh[:, 13] = np.tile(ndel[c0:c0 + 64], 2)
            w3r = hy_w3[l].reshape(64, 2, 512)[:, :, c0:c0 + 64]
            fb = np.zeros((64, 4), np.float32)
            fb[:, 0] = hy_freq[l]
            fb[:, 1] = hy_b1[l]
            fb[:, 2] = hy_b2[l]
            ins.append({"uT": np.ascontiguousarray(np.stack(uL)), "ucT": np.ascontiguousarray(np.stack(uC)), "sm": smh,
                        "w1": np.ascontiguousarray(hy_w1[l]), "w2": np.ascontiguousarray(hy_w2[l]),
                        "w3": np.ascontiguousarray(np.concatenate([w3r, w3r], axis=2)), "fb": fb,
                        "zfL": zfL, "t01L": t01L, "zfC": zfC, "t01C": t01C, "ident": ident, "anti": anti})
        r = _run("k2d", build_k2d, ins)
        for k in _CORES:
            c0 = 64 * k
            o = r[k]["oT"]
            oc = r[k]["ocT"]
            for b in range(B):
                OT_lat[b][1536 + c0:1536 + c0 + 64] = o[b * 64:(b + 1) * 64]
                OT_ctx[b][1536 + c0:1536 + c0 + 64] = oc[b * 64:(b + 1) * 64]
        del PT_lat, PT_ctx
        wo_t = tile_w(w_out[l])
        wu_t = tile_w(ffn_w_in[l])
        wd_t = tile_w(ffn_w_out[l])
        cwp = np.ascontiguousarray(np.stack([vec_pk(ffn_conv_w[l, 0]), vec_pk(ffn_conv_w[l, 1]), vec_pk(ffn_conv_w[l, 2]),
                                             vec_pk(ffn_conv_b[l])], axis=1))
        ins = []
        for k in _CORES:
            b, q = divmod(k, 4)
            ocol = np.concatenate([_seg(OT_lat[b], q * 2048, (q + 1) * 2048), _seg(OT_ctx[b], q * 64, (q + 1) * 64)], axis=1)
            xcol = np.concatenate([_seg(XT_lat[b], q * 2048, (q + 1) * 2048), _seg(XT_ctx[b], q * 64, (q + 1) * 64)], axis=1)
            TW = ocol.shape[1]
            vec = np.stack([vec_pk(m[b, 2]), vec_pk(m[2, 2]), vec_pk(norm2_g[l]), vec_pk(m[b, 4]), vec_pk(m[b, 3]),
                            vec_pk(m[2, 4]), vec_pk(m[2, 3]), vec_pk(m[b, 5]), vec_pk(m[2, 5])], axis=1)
            eg = np.ones((128, 4), np.float32)
            if q == 0:
                eg[:, 0] = 0
                eg[:, 2] = 0
            if q == 3:
                eg[:, 1] = 0
                eg[:, 3] = 0
            ins.append({"oT": np.ascontiguousarray(ocol.reshape(16, 128, TW)), "xT": np.ascontiguousarray(xcol.reshape(16, 128, TW)),
                        "vec": np.ascontiguousarray(vec), "edge": eg, "w_out": wo_t, "w_up": wu_t, "cw": cwp, "w_dn": wd_t})
        r = _run("k3", build_k3, ins)
        for k in _CORES:
            b, q = divmod(k, 4)
            o = r[k]["xo"].reshape(Dm, 2112)
            XT_lat[b][:, q * 2048:(q + 1) * 2048] = o[:, :2048]
            XT_ctx[b][:, q * 64:(q + 1) * 64] = o[:, 2048:]
        del r, ins
    return np.ascontiguousarray(np.stack([XT_lat[b].T for b in range(B)])).astype(np.float32)
```

```python
import math
import numpy as np
import time
from contextlib import ExitStack
import concourse.bass as bass
import concourse.mybir as mybir
from concourse.bass_utils import run_bass_kernel_spmd

F32 = mybir.dt.float32
BF16 = mybir.dt.bfloat16
AF = mybir.ActivationFunctionType
ALU = mybir.AluOpType
AX = mybir.AxisListType


class Buf:
    __slots__ = ("name", "t", "w", "r", "sem", "dcount")

    def __init__(self, name, t):
        self.name = name
        self.t = t
        self.w = {}
        self.r = {}
        self.sem = None
        self.dcount = 0

    def __getitem__(self, idx):
        return self.t[idx]


class Eng:
    def __init__(self, fw, name, eng, sem, kind):
        self.fw = fw
        self.name = name
        self.eng = eng
        self.sem = sem
        self.kind = kind
        self.count = 0
        self.seen = {}

    def _wait(self, deps):
        for key, (sem, val) in deps.items():
            if self.seen.get(key, 0) >= val:
                continue
            if sem is self.sem and self.kind == 'pe':
                continue
            self.eng.wait_ge(sem, val)
            self.seen[key] = val

    def _deps(self, outs, ins):
        deps = {}
        for b in ins:
            for k, (s, v) in b.w.items():
                if deps.get(k, (None, 0))[1] < v:
                    deps[k] = (s, v)
        for b in outs:
            for d in (b.w, b.r):
                for k, (s, v) in d.items():
                    if deps.get(k, (None, 0))[1] < v:
                        deps[k] = (s, v)
        return deps

    def op(self, inst_fn, outs, ins):
        self._wait(self._deps(outs, ins))
        inst = inst_fn(self.eng)
        self.count += 1
        inst.then_inc(self.sem, 1)
        key = id(self.sem)
        tok = (self.sem, self.count)
        for b in ins:
            b.r[key] = tok
        for b in outs:
            b.w = {key: tok}
            b.r = {}
        return tok

    def dma(self, out_buf, out_ap, in_buf, in_ap, **kw):
        self._wait(self._deps([out_buf], [in_buf]))
        if out_buf.sem is None:
            out_buf.sem = self.fw.new_sem("d_" + out_buf.name)
        inst = self.eng.dma_start(out=out_ap, in_=in_ap, **kw)
        out_buf.dcount += 16
        inst.then_inc(out_buf.sem, 16)
        key = id(out_buf.sem)
        tok = (out_buf.sem, out_buf.dcount)
        in_buf.r[key] = tok
        out_buf.w = {key: tok}
        out_buf.r = {}
        return tok

    def wait_buf(self, b):
        self._wait(dict(b.w))


class FW:
    def __init__(self, name="k"):
        self.nc = bass.Bass("TRN2", target_bir_lowering=False)
        self.es = ExitStack()
        self.nsem = 0
        self.block = None
        self.all_bufs = []
        self.scopes = []
        self.pfx = ""

    def new_sem(self, name):
        self.nsem += 1
        return self.es.enter_context(self.nc.semaphore(name + "_%d" % self.nsem))

    def dram(self, name, shape, dtype, kind):
        name = self.pfx + name
        t = self.nc.dram_tensor(name, list(shape), dtype, kind=kind)
        b = Buf(name, t.ap())
        self.all_bufs.append(b)
        return b

    def sbuf(self, name, shape, dtype):
        name = self.pfx + name
        t = (self.scopes[-1] if self.scopes else self.es).enter_context(self.nc.sbuf_tensor(name, list(shape), dtype))
        b = Buf(name, t)
        self.all_bufs.append(b)
        return b

    def psum(self, name, shape, dtype=F32):
        name = self.pfx + name
        t = (self.scopes[-1] if self.scopes else self.es).enter_context(self.nc.psum_tensor(name, list(shape), dtype))
        b = Buf(name, t)
        self.all_bufs.append(b)
        return b

    def engines(self):
        nc = self.nc
        self.pe = Eng(self, "pe", nc.tensor, self.new_sem("pe"), 'pe')
        self.act = Eng(self, "act", nc.scalar, self.new_sem("act"), 'act')
        self.dve = Eng(self, "dve", nc.vector, self.new_sem("dve"), 'dve')
        self.pool = Eng(self, "pool", nc.gpsimd, self.new_sem("pool"), 'pool')
        self.sp = Eng(self, "sp", nc.sync, self.new_sem("sp"), 'sp')
        return self.pe, self.act, self.dve, self.pool, self.sp

    def push_scope(self):
        self.scopes.append(ExitStack())

    def pop_scope(self):
        fw_barrier(self)
        self.scopes.pop().close()

    def close(self):
        self.es.close()


def sub_bufs(parent, aps, prefix):
    return [Buf("%s%d" % (prefix, i), ap) for i, ap in enumerate(aps)]


def fw_barrier(fw, bufs=()):
    engs = [fw.pe, fw.act, fw.dve, fw.pool, fw.sp]
    toks = {}
    for e in engs:
        if e.count > 0:
            toks[id(e.sem)] = (e.sem, e.count)
    for b in fw.all_bufs:
        if b.sem is not None and b.dcount > 0:
            toks[id(b.sem)] = (b.sem, b.dcount)
    for e in engs:
        e._wait(dict(toks))


D = 2048
KC = D // 128
EPS = 1e-6


def load_consts(fw, sp):
    ones = fw.sbuf("ones", [128, 128], F32)
    fw.dve.op(lambda e: e.memset(ones[:], 1.0), [ones], [])
    return ones


def norm_mod_phase(fw, xT, hT, tiles, vecs, ones, nm):
    pe, act, dve, pool, sp = fw.pe, fw.act, fw.dve, fw.pool, fw.sp
    xts = [fw.sbuf("%s_xt%d" % (nm, i), [128, KC, 512], F32) for i in range(2)]
    sqs = [fw.sbuf("%s_sq%d" % (nm, i), [128, KC, 512], F32) for i in range(1)]
    rstd = [fw.sbuf("%s_rstd%d" % (nm, i), [128, 512], F32) for i in range(2)]
    ps = [fw.psum("%s_ps%d" % (nm, i), [128, 512], F32) for i in range(2)]
    xv = xT.t.rearrange("k p t -> p k t")
    for i, (t0, n, A, sh) in enumerate(tiles):
        xt, sq, rs, p = xts[i % 2], sqs[0], rstd[i % 2], ps[i % 2]
        sp.dma(xt, xt[:, :, 0:n], xT, xv[:, :, t0:t0 + n])
        act.op(lambda e: e.activation(sq[:, :, 0:n], xt[:, :, 0:n], AF.Square), [sq], [xt])
        for kc in range(KC):
            pe.op(lambda e: e.matmul(p[:, 0:n], ones[:], sq[:, kc, 0:n], start=(kc == 0), stop=(kc == KC - 1)), [p], [ones, sq])
        dve.op(lambda e: e.tensor_scalar(rs[:, 0:n], p[:, 0:n], 1.0 / D, EPS, ALU.mult, ALU.add), [rs], [p])
        act.op(lambda e: e.activation(rs[:, 0:n], rs[:, 0:n], AF.Sqrt), [rs], [rs])
        dve.op(lambda e: e.reciprocal(rs[:, 0:n], rs[:, 0:n]), [rs], [rs])
        for kc in range(KC):
            dve.op(lambda e: e.tensor_tensor(sq[:, kc, 0:n], xt[:, kc, 0:n], rs[:, 0:n], ALU.mult), [sq], [xt, rs])
        for kc in range(KC):
            act.op(lambda e: e.activation(hT[:, kc, t0:t0 + n], sq[:, kc, 0:n], AF.Identity,
                                          bias=sh[:, kc:kc + 1], scale=A[:, kc:kc + 1]), [hT], [sq, A, sh])


def linear_phase(fw, hT, KCn, w, ncoltiles, tiles, epilogue, nm, nbuf=3):
    pe, pool = fw.pe, fw.pool
    wts = [fw.sbuf("%s_w%d" % (nm, i), [128, KCn, 256], BF16) for i in range(nbuf)]
    ps = [fw.psum("%s_lp%d" % (nm, i), [128, 512], F32) for i in range(4)]
    cnt = 0
    for ct in range(ncoltiles):
        wt = wts[ct % nbuf]
        pool.dma(wt, wt[:], w, w.t[ct])
        for ti, (t0, n) in enumerate(tiles):
            for half in range(2):
                p = ps[cnt % 4]
                cnt += 1
                for kc in range(KCn):
                    pe.op(lambda e: e.matmul(p[:, 0:n], wt[:, kc, half * 128:(half + 1) * 128], hT[:, kc, t0:t0 + n],
                                             start=(kc == 0), stop=(kc == KCn - 1)), [p], [wt, hT])
                epilogue(ct * 2 + half, ti, t0, n, p)


def build_k1(T_lat=2048, T_ctx=64, NCOL=4352, fw=None, xT=None):
    own = fw is None
    if own:
        fw = FW()
    T = T_lat + T_ctx
    if xT is None:
        xT = fw.dram("xT", [KC, 128, T], F32, "ExternalInput")
    vec = fw.dram("vec", [128, 5, KC], F32, "ExternalInput")
    w = fw.dram("w", [NCOL // 256, 128, KC, 256], F32, "ExternalInput")
    pT = fw.dram("pT", [NCOL // 128, 128, T], F32, "ExternalOutput")
    if own:
        fw.engines()
    pe, act, dve, pool, sp = fw.pe, fw.act, fw.dve, fw.pool, fw.sp
    ones = load_consts(fw, sp)
    vt = fw.sbuf("vt", [128, 5, KC], F32)
    sp.dma(vt, vt[:], vec, vec[:])
    A_lat = fw.sbuf("A_lat", [128, KC], F32)
    A_ctx = fw.sbuf("A_ctx", [128, KC], F32)
    sh_lat = fw.sbuf("sh_lat", [128, KC], F32)
    sh_ctx = fw.sbuf("sh_ctx", [128, KC], F32)
    dve.op(lambda e: e.scalar_tensor_tensor(A_lat[:], vt[:, 1, :], 1.0, vt[:, 0, :], ALU.add, ALU.mult), [A_lat], [vt])
    dve.op(lambda e: e.scalar_tensor_tensor(A_ctx[:], vt[:, 3, :], 1.0, vt[:, 0, :], ALU.add, ALU.mult), [A_ctx], [vt])
    dve.op(lambda e: e.tensor_copy(sh_lat[:], vt[:, 2, :]), [sh_lat], [vt])
    dve.op(lambda e: e.tensor_copy(sh_ctx[:], vt[:, 4, :]), [sh_ctx], [vt])
    hT = fw.sbuf("hT", [128, KC, T], BF16)
    tiles = [(t0, 512, A_lat, sh_lat) for t0 in range(0, T_lat, 512)]
    if T_ctx:
        tiles.append((T_lat, T_ctx, A_ctx, sh_ctx))
    norm_mod_phase(fw, xT, hT, tiles, None, ones, "n1")
    ots = [fw.sbuf("ot%d" % i, [128, 512], F32) for i in range(4)]
    st = {"i": 0}

    def epi(ci, ti, t0, n, p):
        ot = ots[st["i"] % 4]
        if st["i"] % 2 == 0:
            dve.op(lambda e: e.tensor_copy(ot[:, 0:n], p[:, 0:n]), [ot], [p])
        else:
            act.op(lambda e: e.activation(ot[:, 0:n], p[:, 0:n], AF.Copy), [ot], [p])
        st["i"] += 1
        sp.dma(pT, pT.t[ci, :, t0:t0 + n], ot, ot[:, 0:n])

    linear_phase(fw, hT, KC, w, NCOL // 256, [(t[0], t[1]) for t in tiles], epi, "l1")
    sp.wait_buf(pT)
    if own:
        fw.close()
    return fw.nc


def tile_w(w):
    K, N = w.shape
    return np.ascontiguousarray(w.reshape(K // 128, 128, N // 256, 256).transpose(2, 1, 0, 3))


def vec_pk(v):
    return np.ascontiguousarray(v.reshape(KC, 128).T)


D = 2048
KC = 16
NCOLS = 1536


def build_k0(NL=4):
    fw = FW()
    cT = fw.dram("cT", [128, KC, 3], F32, "ExternalInput")
    w = fw.dram("w", [NL, 3, 128, KC, 512], F32, "ExternalInput")
    bm = fw.dram("bm", [NL, NCOLS], F32, "ExternalInput")
    mod = fw.dram("mod", [NL, 3, NCOLS], F32, "ExternalOutput")
    fw.engines()
    pe, act, dve, pool, sp = fw.pe, fw.act, fw.dve, fw.pool, fw.sp
    ct = fw.sbuf("ct", [128, KC, 3], F32)
    sp.dma(ct, ct[:], cT, cT[:])
    st = fw.sbuf("st", [128, KC, 3], F32)
    act.op(lambda e: e.activation(st[:], ct[:], AF.Silu), [st], [ct])
    wts = [fw.sbuf("wt%d" % i, [128, KC, 512], F32) for i in range(2)]
    bts = [fw.sbuf("bt%d" % i, [3, 512], F32) for i in range(2)]
    ots = [fw.sbuf("ot%d" % i, [3, 512], F32) for i in range(2)]
    ps = [fw.psum("ps%d" % i, [128, 512], F32) for i in range(2)]
    i = 0
    for l in range(NL):
        for t in range(3):
            wt, bt, ot, p = wts[i % 2], bts[i % 2], ots[i % 2], ps[i % 2]
            i += 1
            (sp if i % 2 == 0 else act).dma(wt, wt[:], w, w.t[l, t])
            bsrc = bass.AP(tensor=bm.t.tensor, offset=l * NCOLS + t * 512, ap=[[0, 3], [1, 512]])
            sp.dma(bt, bt[:], bm, bsrc)
            for k in range(KC):
                pe.op(lambda e: e.matmul(p[0:3, :], st[:, k, :], wt[:, k, :], start=(k == 0), stop=(k == KC - 1)), [p], [st, wt])
            dve.op(lambda e: e.tensor_tensor(ot[:], p[0:3, :], bt[:], ALU.add), [ot], [p, bt])
            sp.dma(mod, mod.t[l, :, t * 512:(t + 1) * 512], ot, ot[:])
    sp.wait_buf(mod)
    fw.close()
    return fw.nc


def k0_inputs(core, c, c_ctx, w_mod, b_mod):
    NL = w_mod.shape[0]
    cs = np.stack([c[0], c[1], c_ctx], axis=1)
    cT = np.ascontiguousarray(cs.reshape(KC, 128, 3).transpose(1, 0, 2))
    cols = slice(core * NCOLS, (core + 1) * NCOLS)
    w = w_mod[:, :, cols].reshape(NL, KC, 128, 3, 512).transpose(0, 3, 2, 1, 4)
    return {"cT": cT, "w": np.ascontiguousarray(w), "bm": np.ascontiguousarray(b_mod[:, cols])}


EPS = 1e-6
CTX = 256


def qknorm_rope(fw, xT, xsT, CS, gcol, dst, col_map, tiles, blockones, nm, inv_n):
    pe, act, dve, pool, sp = fw.pe, fw.act, fw.dve, fw.pool, fw.sp
    gbuf, c0 = gcol
    if "qk_scr" not in fw.__dict__:
        fw.qk_scr = dict(
            xt=[fw.sbuf("qk_x%d" % i, [128, 512], F32) for i in range(2)],
            xs=[fw.sbuf("qk_xs%d" % i, [128, 512], F32) for i in range(2)],
            ct=[fw.sbuf("qk_c%d" % i, [128, 512], F32) for i in range(2)],
            st=[fw.sbuf("qk_s%d" % i, [128, 512], F32) for i in range(2)],
            sq=[fw.sbuf("qk_sq%d" % i, [128, 512], F32) for i in range(2)],
            rs=[fw.sbuf("qk_rs%d" % i, [128, 512], F32) for i in range(2)],
            ps=[fw.psum("qk_ps%d" % i, [128, 512], F32) for i in range(2)])
    xt, xs, ct, st, sq, rs, ps = (fw.qk_scr[k] for k in ("xt", "xs", "ct", "st", "sq", "rs", "ps"))
    for i, (x0, tb0, n) in enumerate(tiles):
        j = i % 2
        sp.dma(xt[j], xt[j][:, 0:n], xT, xT.t[:, x0:x0 + n])
        sp.dma(xs[j], xs[j][:, 0:n], xsT, xsT.t[:, x0:x0 + n])
        sp.dma(ct[j], ct[j][:, 0:n], CS, CS.t[0, :, tb0:tb0 + n])
        sp.dma(st[j], st[j][:, 0:n], CS, CS.t[1, :, tb0:tb0 + n])
        act.op(lambda e: e.activation(sq[j][:, 0:n], xt[j][:, 0:n], AF.Square), [sq[j]], [xt[j]])
        pe.op(lambda e: e.matmul(ps[j][:, 0:n], blockones[:], sq[j][:, 0:n], start=True, stop=True), [ps[j]], [blockones, sq[j]])
        dve.op(lambda e: e.tensor_scalar(rs[j][:, 0:n], ps[j][:, 0:n], inv_n, EPS, ALU.mult, ALU.add), [rs[j]], [ps[j]])
        act.op(lambda e: e.activation(rs[j][:, 0:n], rs[j][:, 0:n], AF.Sqrt), [rs[j]], [rs[j]])
        dve.op(lambda e: e.reciprocal(rs[j][:, 0:n], rs[j][:, 0:n]), [rs[j]], [rs[j]])
        dve.op(lambda e: e.scalar_tensor_tensor(ct[j][:, 0:n], xt[j][:, 0:n], gbuf[:, c0:c0 + 1], ct[j][:, 0:n], ALU.mult, ALU.mult),
               [ct[j]], [xt[j], gbuf])
        dve.op(lambda e: e.scalar_tensor_tensor(st[j][:, 0:n], xs[j][:, 0:n], gbuf[:, c0 + 1:c0 + 2], st[j][:, 0:n], ALU.mult, ALU.mult),
                [st[j]], [xs[j], gbuf])
        dve.op(lambda e: e.tensor_tensor(ct[j][:, 0:n], ct[j][:, 0:n], st[j][:, 0:n], ALU.add), [ct[j]], [st[j]])
        dve.op(lambda e: e.tensor_tensor(dst[:, x0:x0 + n], ct[j][:, 0:n], rs[j][:, 0:n], ALU.mult), [dst], [ct[j], rs[j]])


def build_k2a(LQ=8192, fw=None):
    own = fw is None
    if own:
        fw = FW()
    T = LQ + CTX
    NKC = T // 128
    qT = fw.dram("qT", [128, T], F32, "ExternalInput")
    qsT = fw.dram("qsT", [128, T], F32, "ExternalInput")
    kT = fw.dram("kT", [128, T], F32, "ExternalInput")
    ksT = fw.dram("ksT", [128, T], F32, "ExternalInput")
    v = fw.dram("v", [128, NKC, 128], F32, "ExternalInput")
    CS = fw.dram("CS", [2, 128, T], F32, "ExternalInput")
    sm = fw.dram("sm", [128, 8], F32, "ExternalInput")
    lamp = fw.dram("lamp", [1, 256], F32, "ExternalInput")
    oT = fw.dram("oT", [128, T], F32, "ExternalOutput")
    if own:
        fw.engines()
    pe, act, dve, pool, sp = fw.pe, fw.act, fw.dve, fw.pool, fw.sp
    ones = fw.sbuf("ones", [128, 128], F32)
    dve.op(lambda e: e.memset(ones[:], 1.0), [ones], [])
    onesb = fw.sbuf("onesb", [128, 128], BF16)
    dve.op(lambda e: e.memset(onesb[:], 1.0), [onesb], [])
    blockones = fw.sbuf("blockones", [128, 128], F32)
    dve.op(lambda e: e.memset(blockones[:], 0.0), [blockones], [])
    dve.op(lambda e: e.memset(blockones[0:64, 0:64], 1.0), [blockones], [])
    dve.op(lambda e: e.memset(blockones[64:128, 64:128], 1.0), [blockones], [])
    smt = fw.sbuf("smt", [128, 8], F32)
    sp.dma(smt, smt[:], sm, sm[:])
    lt = fw.sbuf("lt", [1, 256], F32)
    sp.dma(lt, lt[:], lamp, lamp[:])
    lw = fw.sbuf("lw", [1, 128], F32)
    dve.op(lambda e: e.tensor_tensor(lw[:, 0:64], lt[:, 0:64], lt[:, 64:128], ALU.mult), [lw], [lt])
    dve.op(lambda e: e.tensor_tensor(lw[:, 64:128], lt[:, 128:192], lt[:, 192:256], ALU.mult), [lw], [lt])
    l2 = fw.sbuf("l2", [1, 4], F32)
    dve.op(lambda e: e.reduce_sum(l2[:, 0:1], lw[:, 0:64], AX.X), [l2], [lw])
    dve.op(lambda e: e.reduce_sum(l2[:, 1:2], lw[:, 64:128], AX.X), [l2], [lw])
    act.op(lambda e: e.activation(l2[:, 0:2], l2[:, 0:2], AF.Exp), [l2], [l2])
    dve.op(lambda e: e.tensor_tensor(l2[:, 2:3], l2[:, 1:2], l2[:, 0:1], ALU.subtract), [l2], [l2])
    dve.op(lambda e: e.tensor_tensor(l2[:, 2:3], l2[:, 2:3], smt[0:1, 5:6], ALU.subtract), [l2], [l2, smt])
    sc = fw.sbuf("sc", [128, 2], F32)
    fw.push_scope()
    psl = fw.psum("psl", [128, 512], F32)
    pe.op(lambda e: e.matmul(psl[:, 0:1], ones[0:1, :], l2[0:1, 2:3], start=True, stop=True), [psl], [ones, l2])
    dve.op(lambda e: e.tensor_copy(sc[:, 0:1], psl[:, 0:1]), [sc], [psl])
    dve.op(lambda e: e.tensor_tensor(sc[:, 1:2], smt[:, 4:5], smt[:, 6:7], ALU.mult), [sc], [smt])
    fw.pop_scope()

    QT = fw.sbuf("QT", [128, T], BF16)
    KT = fw.sbuf("KT", [128, T], BF16)
    V = fw.sbuf("V", [128, NKC, 128], BF16)
    pool.dma(V, V[:], v, v[:])
    qtiles = [(t0, t0, 512) for t0 in range(0, LQ, 512)] + [(LQ, LQ, CTX)]
    ktiles = [(0, LQ, CTX)] + [(CTX + t0, t0, 512) for t0 in range(0, LQ, 512)]
    fw.push_scope()
    qknorm_rope(fw, qT, qsT, CS, (smt, 0), QT, None, qtiles, blockones, "qn", 1.0 / 64)
    qknorm_rope(fw, kT, ksT, CS, (smt, 2), KT, None, ktiles, blockones, "kn", 1.0 / 64)
    del fw.qk_scr
    fw.pop_scope()

    NSB = 2
    NPT = 4
    ps_s = [[fw.psum("ps_s%d_%d" % (m, i), [128, 512], F32) for i in range(NSB)] for m in range(2)]
    ps_o = [fw.psum("ps_o%d" % m, [128, 512], F32) for m in range(2)]
    ps_z = [fw.psum("ps_z%d" % m, [128, 512], F32) for m in range(2)]
    pts = [[fw.sbuf("pt%d_%d" % (m, i), [128, 512], BF16) for i in range(NPT)] for m in range(2)]
    om = [fw.sbuf("om%d" % i, [128, 512], F32) for i in range(2)]
    rz = [fw.sbuf("rz%d" % i, [128, 512], F32) for i in range(2)]
    osq = fw.sbuf("osq", [128, 512], F32)
    ors = fw.sbuf("ors", [128, 512], F32)
    ots = [fw.sbuf("ot%d" % i, [128, 512], F32) for i in range(2)]
    for qi, (q0, _, n) in enumerate(qtiles):
        nkc = NKC if q0 < LQ else CTX // 128
        seq = []
        for kc in range(nkc):
            seq.append(("s", kc))
            if kc >= 1:
                seq.append(("av", kc - 1))
        seq.append(("av", nkc - 1))
        for kind, kc in seq:
            if kind == "s":
                for m in range(2):
                    r0 = 64 * m
                    p = ps_s[m][kc % NSB]
                    pe.op(lambda e: e.matmul(p[:, 0:n], KT[r0:r0 + 64, kc * 128:(kc + 1) * 128], QT[r0:r0 + 64, q0:q0 + n],
                                             start=True, stop=True), [p], [KT, QT])
                for m in range(2):
                    p = ps_s[m][kc % NSB]
                    pt = pts[m][kc % NPT]
                    act.op(lambda e: e.activation(pt[:, 0:n], p[:, 0:n], AF.Exp, scale=0.125), [pt], [p])
            else:
                for m in range(2):
                    pt = pts[m][kc % NPT]
                    pe.op(lambda e: e.matmul(ps_o[m][:, 0:n], V[:, kc, :], pt[:, 0:n], start=(kc == 0), stop=(kc == nkc - 1)), [ps_o[m]], [V, pt])
                    pe.op(lambda e: e.matmul(ps_z[m][:, 0:n], onesb[:], pt[:, 0:n], start=(kc == 0), stop=(kc == nkc - 1)), [ps_z[m]], [onesb, pt])
        for m in range(2):
            dve.op(lambda e: e.reciprocal(rz[m][:, 0:n], ps_z[m][:, 0:n]), [rz[m]], [ps_z[m]])
            dve.op(lambda e: e.tensor_tensor(om[m][:, 0:n], ps_o[m][:, 0:n], rz[m][:, 0:n], ALU.mult), [om[m]], [ps_o[m], rz[m]])
        dve.op(lambda e: e.scalar_tensor_tensor(om[0][:, 0:n], om[1][:, 0:n], sc[:, 0:1], om[0][:, 0:n], ALU.mult, ALU.add), [om[0]], [om[1], sc])
        act.op(lambda e: e.activation(osq[:, 0:n], om[0][:, 0:n], AF.Square), [osq], [om[0]])
        pl = ps_s[0][(nkc) % NSB]
        pe.op(lambda e: e.matmul(pl[:, 0:n], ones[:], osq[:, 0:n], start=True, stop=True), [pl], [ones, osq])
        dve.op(lambda e: e.tensor_scalar(ors[:, 0:n], pl[:, 0:n], 1.0 / 128, EPS, ALU.mult, ALU.add), [ors], [pl])
        act.op(lambda e: e.activation(ors[:, 0:n], ors[:, 0:n], AF.Sqrt), [ors], [ors])
        dve.op(lambda e: e.reciprocal(ors[:, 0:n], ors[:, 0:n]), [ors], [ors])
        ot = ots[qi % 2]
        dve.op(lambda e: e.scalar_tensor_tensor(ot[:, 0:n], om[0][:, 0:n], sc[:, 1:2], ors[:, 0:n], ALU.mult, ALU.mult), [ot], [om[0], sc, ors])
        sp.dma(oT, oT.t[:, q0:q0 + n], ot, ot[:, 0:n])
    sp.wait_buf(oT)
    if own:
        fw.close()
    return fw.nc


def swap_halves(xT):
    r = xT.reshape(-1, 2, 2, 16, xT.shape[-1])
    return np.ascontiguousarray(r[:, :, ::-1]).reshape(xT.shape)


def rope_tables(L, n_ctx, reps):
    GRID_W = 64
    rows = L // GRID_W
    row = np.repeat(np.arange(rows, dtype=np.float32), GRID_W)
    col = np.tile(np.arange(GRID_W, dtype=np.float32), rows)
    inv = (10000.0 ** (-np.arange(0, 32, 2, dtype=np.float32) / 32)).astype(np.float32)
    ang = np.concatenate([row[:, None] * inv, col[:, None] * inv], axis=-1)
    cos = np.cos(ang).astype(np.float32).reshape(L, 2, 16)
    sin = np.sin(ang).astype(np.float32).reshape(L, 2, 16)
    C = np.ones((2, 2, 16, L + n_ctx), np.float32)
    S = np.zeros((2, 2, 16, L + n_ctx), np.float32)
    C[:, 0, :, :L] = cos.transpose(1, 2, 0)
    C[:, 1, :, :L] = cos.transpose(1, 2, 0)
    S[:, 0, :, :L] = -sin.transpose(1, 2, 0)
    S[:, 1, :, :L] = sin.transpose(1, 2, 0)
    C = np.tile(C.reshape(64, -1), (reps, 1))
    S = np.tile(S.reshape(64, -1), (reps, 1))
    return np.ascontiguousarray(np.stack([C, S]))


def build_k2b(LQ=8192, fw=None):
    own = fw is None
    if own:
        fw = FW()
    T = LQ + CTX
    NKC = T // 128
    NB = LQ // 128
    qT = fw.dram("qT", [128, T], F32, "ExternalInput")
    qsT = fw.dram("qsT", [128, T], F32, "ExternalInput")
    kT = fw.dram("kT", [128, T], F32, "ExternalInput")
    ksT = fw.dram("ksT", [128, T], F32, "ExternalInput")
    v = fw.dram("v", [128, NKC, 64], F32, "ExternalInput")
    CS = fw.dram("CS", [2, 128, T], F32, "ExternalInput")
    sm = fw.dram("sm", [128, 8], F32, "ExternalInput")
    masks = fw.dram("masks", [128, 6, 512], F32, "ExternalInput")
    oT = fw.dram("oT", [2, 64, T], F32, "ExternalOutput")
    if own:
        fw.engines()
    pe, act, dve, pool, sp = fw.pe, fw.act, fw.dve, fw.pool, fw.sp
    onesb = fw.sbuf("onesb", [128, 128], BF16)
    dve.op(lambda e: e.memset(onesb[:], 1.0), [onesb], [])
    blockones = fw.sbuf("blockones", [128, 128], F32)
    dve.op(lambda e: e.memset(blockones[:], 0.0), [blockones], [])
    dve.op(lambda e: e.memset(blockones[0:64, 0:64], 1.0), [blockones], [])
    dve.op(lambda e: e.memset(blockones[64:128, 64:128], 1.0), [blockones], [])
    smt = fw.sbuf("smt", [128, 8], F32)
    sp.dma(smt, smt[:], sm, sm[:])
    es = fw.sbuf("es", [128, 2], F32)
    act.op(lambda e: e.activation(es[:], smt[:, 4:6], AF.Exp), [es], [smt])
    mk = fw.sbuf("mk", [128, 6, 512], BF16)
    pool.dma(mk, mk[:], masks, masks[:])
    QT = fw.sbuf("QT", [128, T], BF16)
    KT = fw.sbuf("KT", [128, T], BF16)
    V = fw.sbuf("V", [128, NKC, 64], BF16)
    pool.dma(V, V[:], v, v[:])
    qtiles = [(t0, t0, 512) for t0 in range(0, LQ, 512)] + [(LQ, LQ, CTX)]
    ktiles = [(0, LQ, CTX)] + [(CTX + t0, t0, 512) for t0 in range(0, LQ, 512)]
    fw.push_scope()
    qknorm_rope(fw, qT, qsT, CS, (smt, 0), QT, None, qtiles, blockones, "qn", 1.0 / 64)
    qknorm_rope(fw, kT, ksT, CS, (smt, 2), KT, None, ktiles, blockones, "kn", 1.0 / 64)
    del fw.qk_scr
    fw.pop_scope()

    NSB = 2
    NPT = 4
    ps_s = [[fw.psum("ps_s%d_%d" % (m, i), [128, 512], F32) for i in range(NSB)] for m in range(2)]
    ps_o = [fw.psum("ps_o%d" % m, [64, 512], F32) for m in range(2)]
    ps_z = [fw.psum("ps_z%d" % m, [64, 512], F32) for m in range(2)]
    pts = [[fw.sbuf("pt%d_%d" % (m, i), [128, 512], BF16) for i in range(NPT)] for m in range(2)]
    rz = [fw.sbuf("rz%d" % m, [64, 512], F32) for m in range(2)]
    ots = [fw.sbuf("ot%d" % i, [64, 512], F32) for i in range(4)]
    cnt = 0
    for qi, (q0, _, n) in enumerate(qtiles):
        if q0 < LQ:
            n0 = q0 // 128
            chunks = [(0, None), (1, None)]
            for rel in range(-1, 5):
                j = n0 + rel
                if 0 <= j < NB:
                    chunks.append((CTX // 128 + j, rel + 1))
        else:
            chunks = [(0, None), (1, None)]
        nch = len(chunks)
        seq = []
        for ci in range(nch):
            seq.append(("s", ci))
            if ci >= 1:
                seq.append(("av", ci - 1))
        seq.append(("av", nch - 1))
        for kind, ci in seq:
            kc, mi = chunks[ci]
            if kind == "s":
                for m in range(2):
                    r0 = 64 * m
                    p = ps_s[m][ci % NSB]
                    pe.op(lambda e: e.matmul(p[:, 0:n], KT[r0:r0 + 64, kc * 128:(kc + 1) * 128], QT[r0:r0 + 64, q0:q0 + n],
                                             start=True, stop=True), [p], [KT, QT])
                for m in range(2):
                    p = ps_s[m][ci % NSB]
                    pt = pts[m][ci % NPT]
                    act.op(lambda e: e.activation(pt[:, 0:n], p[:, 0:n], AF.Exp, scale=0.125), [pt], [p])
                    if mi is not None:
                        (dve if m == 0 else pool).op(lambda e: e.tensor_tensor(pt[:, 0:n], pt[:, 0:n], mk[:, mi, 0:n], ALU.mult), [pt], [pt, mk])
            else:
                last = ci == nch - 1
                for m in range(2):
                    pt = pts[m][ci % NPT]
                    pe.op(lambda e: e.matmul(ps_o[m][:, 0:n], V[:, kc, :], pt[:, 0:n], start=(ci == 0), stop=last), [ps_o[m]], [V, pt])
                    pe.op(lambda e: e.matmul(ps_z[m][:, 0:n], onesb[:, 0:64], pt[:, 0:n], start=(ci == 0), stop=last), [ps_z[m]], [onesb, pt])
        for m in range(2):
            dve.op(lambda e: e.tensor_scalar(rz[m][:, 0:n], ps_z[m][:, 0:n], es[0:64, m:m + 1], None, ALU.add), [rz[m]], [ps_z[m], es])
            dve.op(lambda e: e.reciprocal(rz[m][:, 0:n], rz[m][:, 0:n]), [rz[m]], [rz[m]])
            ot = ots[cnt % 4]
            cnt += 1
            dve.op(lambda e: e.tensor_tensor(ot[:, 0:n], ps_o[m][:, 0:n], rz[m][:, 0:n], ALU.mult), [ot], [ps_o[m], rz[m]])
            sp.dma(oT, oT.t[m, :, q0:q0 + n], ot, ot[:, 0:n])
    sp.wait_buf(oT)
    if own:
        fw.close()
    return fw.nc


def band_masks():
    ki = np.arange(128)[:, None, None]
    rel = np.arange(-1, 5)[None, :, None]
    qq = np.arange(512)[None, None, :]
    return (np.abs(128 * rel + ki - qq) <= 128).astype(np.float32)


CTX = 256
POOL_WINDOWS = (2, 4, 8, 16)


def build_k2c(LQ=8192, fw=None):
    own = fw is None
    if own:
        fw = FW()
    T = LQ + CTX
    PADL = 8
    uT = fw.dram("uT", [128, T], F32, "ExternalInput")
    wsel = fw.dram("wsel", [128, 16], F32, "ExternalInput")
    invc = fw.dram("invc", [128, T], F32, "ExternalInput")
    wl = fw.dram("wl", [128, 128], F32, "ExternalInput")
    ls = fw.dram("ls", [128, 1], F32, "ExternalInput")
    yT = fw.dram("yT", [128, T], F32, "ExternalOutput")
    if own:
        fw.engines()
    pe, act, dve, pool, sp = fw.pe, fw.act, fw.dve, fw.pool, fw.sp
    wst = fw.sbuf("wst", [128, 16], F32)
    sp.dma(wst, wst[:], wsel, wsel[:])
    lst = fw.sbuf("lst", [128, 1], F32)
    sp.dma(lst, lst[:], ls, ls[:])
    wlt = fw.sbuf("wlt", [128, 128], BF16)
    pool.dma(wlt, wlt[:], wl, wl[:])
    TT = 2048
    ut = fw.sbuf("ut", [128, TT + 16], F32)
    ic = fw.sbuf("ic", [128, TT], F32)
    acc = fw.sbuf("acc", [128, TT], F32)
    db = fw.sbuf("db", [128, TT], BF16)
    ps = [fw.psum("ps%d" % i, [128, 512], F32) for i in range(2)]
    ots = [fw.sbuf("ot%d" % i, [128, 512], F32) for i in range(2)]
    cnt = 0
    for (s0, Ls) in ((0, LQ), (LQ, CTX)):
        tt_ = min(TT, Ls)
        for t0 in range(0, Ls, tt_):
            lo = max(t0 - 8, 0)
            hi = min(t0 + tt_ + 8, Ls)
            if lo > t0 - 8:
                dve.op(lambda e: e.memset(ut[:, 0:8], 0.0), [ut], [])
            if hi < t0 + tt_ + 8:
                dve.op(lambda e: e.memset(ut[:, tt_ + 8:tt_ + 16], 0.0), [ut], [])
            sp.dma(ut, ut[:, lo - (t0 - 8):hi - (t0 - 8)], uT, uT.t[:, s0 + lo:s0 + hi])
            sp.dma(ic, ic[:, 0:tt_], invc, invc.t[:, s0 + t0:s0 + t0 + tt_])
            dve.op(lambda e: e.tensor_scalar(acc[:, 0:tt_], ut[:, 0:tt_], wst[:, 0:1], None, ALU.mult), [acc], [ut, wst])
            for k in range(1, 16):
                dve.op(lambda e: e.scalar_tensor_tensor(acc[:, 0:tt_], ut[:, k:k + tt_], wst[:, k:k + 1], acc[:, 0:tt_], ALU.mult, ALU.add), [acc], [ut, wst])
            dve.op(lambda e: e.tensor_tensor(acc[:, 0:tt_], acc[:, 0:tt_], ic[:, 0:tt_], ALU.mult), [acc], [ic])
            dve.op(lambda e: e.tensor_tensor(db[:, 0:tt_], acc[:, 0:tt_], ut[:, 8:8 + tt_], ALU.subtract), [db], [acc, ut])
            for c0 in range(0, tt_, 512):
                n = min(512, tt_ - c0)
                p = ps[cnt % 2]
                ot = ots[cnt % 2]
                cnt += 1
                pe.op(lambda e: e.matmul(p[:, 0:n], wlt[:], db[:, c0:c0 + n], start=True, stop=True), [p], [wlt, db])
                act.op(lambda e: e.activation(ot[:, 0:n], p[:, 0:n], AF.Copy, scale=lst[:, 0:1]), [ot], [p, lst])
                sp.dma(yT, yT.t[:, s0 + t0 + c0:s0 + t0 + c0 + n], ot, ot[:, 0:n])
    sp.wait_buf(yT)
    if own:
        fw.close()
    return fw.nc


def pool_consts(g, LQ):
    w = POOL_WINDOWS[g]
    lo = w // 2
    hi = w - 1 - lo
    sel = np.zeros(16, np.float32)
    for s in range(-lo, hi + 1):
        sel[s + 8] = 1.0
    outs = []
    for Ls in (LQ, CTX):
        t = np.arange(Ls)
        start = np.clip(t - lo, 0, Ls)
        end = np.clip(t + hi + 1, 0, Ls)
        outs.append((1.0 / (end - start).astype(np.float32)).astype(np.float32))
    ic = np.concatenate(outs)
    return np.tile(sel[None], (128, 1)), np.ascontiguousarray(np.tile(ic[None], (128, 1)))


EPS = 1e-6
CTX = 256
HY_EMB = 33


def sin_act(fw, out_buf, out_ap, p, n, fb, scr):
    act, dve = fw.act, fw.dve
    s2, s4 = scr
    P = 64
    act.op(lambda e: e.activation(s2[0:P, 0:n], p[0:P, 0:n], AF.Sin, bias=fb[0:P, 1:2], scale=fb[0:P, 0:1]), [s2], [p, fb])
    act.op(lambda e: e.activation(s4[0:P, 0:n], p[0:P, 0:n], AF.Sin, bias=fb[0:P, 3:4], scale=fb[0:P, 2:3]), [s4], [p, fb])
    dve.op(lambda e: e.tensor_tensor(s4[0:P, 0:n], s4[0:P, 0:n], s4[0:P, 0:n], ALU.mult), [s4], [s4])
    dve.op(lambda e: e.tensor_scalar(s4[0:P, 0:n], s4[0:P, 0:n], -2.0, 1.0, ALU.mult, ALU.add), [s4], [s4])
    dve.op(lambda e: e.scalar_tensor_tensor(out_ap, s2[0:P, 0:n], 2.0, s4[0:P, 0:n], ALU.mult, ALU.mult), [out_buf], [s2, s4])


def filter_gen(fw, Lf, zf, t01, wts, KF, nm, scr):
    pe, act, dve, pool, sp = fw.pe, fw.act, fw.dve, fw.pool, fw.sp
    w1, w2, w3, fb1, fb2, ndelta = wts["w1"], wts["w2"], wts["w3"], wts["fb1"], wts["fb2"], wts["ndelta"]
    ps1, ps2, ps3, zt, h1, h2, s2, s4, tt, kt, ktb, sqt = scr
    nt = (Lf + 511) // 512
    part = fw.sbuf(nm + "_part", [128, 2 * nt], F32)
    dve.op(lambda e: e.memset(part[:], 0.0), [part], [])
    for di in range(2):
        for ti in range(nt):
            t0 = ti * 512
            n = min(512, Lf - t0)
            sp.dma(zt, zt[0:HY_EMB, 0:n], zf, zf.t[1 - di, :, t0:t0 + n])
            tsrc = bass.AP(tensor=t01.t.tensor, offset=(1 - di) * Lf + t0, ap=[[0, 128], [1, n]])
            sp.dma(tt, tt[:, 0:n], t01, tsrc)
            pe.op(lambda e: e.matmul(ps1[0:64, 0:n], w1[0:HY_EMB, :], zt[0:HY_EMB, 0:n], start=True, stop=True), [ps1], [w1, zt])
            sin_act(fw, h1, h1[0:64, 0:n], ps1, n, fb1, (s2, s4))
            pe.op(lambda e: e.matmul(ps2[0:64, 0:n], w2[0:64, :], h1[0:64, 0:n], start=True, stop=True), [ps2], [w2, h1])
            sin_act(fw, h2, h2[0:64, 0:n], ps2, n, fb2, (s2, s4))
            pe.op(lambda e: e.matmul(ps3[:, 0:n], w3[0:64, di, :], h2[0:64, 0:n], start=True, stop=True), [ps3], [w3, h2])
            act.op(lambda e: e.activation(tt[:, 0:n], tt[:, 0:n], AF.Exp, scale=ndelta[:, 0:1]), [tt], [tt, ndelta])
            dve.op(lambda e: e.tensor_tensor(kt[:, 0:n], ps3[:, 0:n], tt[:, 0:n], ALU.mult), [kt], [ps3, tt])
            if di == 1 and ti == 0:
                dve.op(lambda e: e.memset(kt[:, 0:1], 0.0), [kt], [])
            dve.op(lambda e: e.tensor_tensor(sqt[:, 0:n], kt[:, 0:n], kt[:, 0:n], ALU.mult), [sqt], [kt])
            dve.op(lambda e: e.reduce_sum(part[:, di * nt + ti:di * nt + ti + 1], sqt[:, 0:n], AX.X), [part], [sqt])
            act.op(lambda e: e.activation(ktb[:, 0:n], kt[:, 0:n], AF.Copy), [ktb], [kt])
            if di == 0:
                sp.dma(KF, KF.t[:, 1 + t0:1 + t0 + n], ktb, ktb[0:64, 0:n])
            elif ti == 0:
                sp.dma(KF, KF.t[:, Lf + 1:Lf + n], ktb, ktb[0:64, 1:n])
            else:
                sp.dma(KF, KF.t[:, Lf + t0:Lf + t0 + n], ktb, ktb[0:64, 0:n])
    scale = fw.sbuf(nm + "_scale", [128, 1], F32)
    dve.op(lambda e: e.reduce_sum(scale[:], part[:], AX.X), [scale], [part])
    dve.op(lambda e: e.tensor_scalar(scale[:], scale[:], EPS, None, ALU.add), [scale], [scale])
    act.op(lambda e: e.activation(scale[:], scale[:], AF.Sqrt), [scale], [scale])
    dve.op(lambda e: e.reciprocal(scale[:], scale[:]), [scale], [scale])
    return scale


def conv3(fw, dst, u, n, sm, part):
    dve = fw.dve
    c = 3 * part
    dve.op(lambda e: e.tensor_scalar(dst[:, 0:n], u[:, 1:n + 1], sm[:, c + 1:c + 2], sm[:, 9 + part:10 + part], ALU.mult, ALU.add), [dst], [u, sm])
    dve.op(lambda e: e.scalar_tensor_tensor(dst[:, 0:n], u[:, 0:n], sm[:, c:c + 1], dst[:, 0:n], ALU.mult, ALU.add), [dst], [u, sm])
    dve.op(lambda e: e.scalar_tensor_tensor(dst[:, 0:n], u[:, 2:n + 2], sm[:, c + 2:c + 3], dst[:, 0:n], ALU.mult, ALU.add), [dst], [u, sm])


def load_u_tile(fw, ut, uT, part, t0, n, Ls):
    sp, dve = fw.sp, fw.dve
    lo = max(t0 - 1, 0)
    hi = min(t0 + n + 1, Ls)
    if t0 == 0:
        dve.op(lambda e: e.memset(ut[:, 0:1], 0.0), [ut], [])
    if t0 + n == Ls:
        dve.op(lambda e: e.memset(ut[:, n + 1:n + 2], 0.0), [ut], [])
    sp.dma(ut, ut[:, lo - (t0 - 1):hi - (t0 - 1)], uT, uT.t[part, :, lo:hi])


def build_k2d(LQ=8192, fw=None):
    own = fw is None
    if own:
        fw = FW()
    NB = LQ // 128
    NBC = CTX // 128
    TT = min(1024, LQ)
    uT = fw.dram("uT", [3, 128, LQ], F32, "ExternalInput")
    ucT = fw.dram("ucT", [3, 128, CTX], F32, "ExternalInput")
    smd = fw.dram("sm", [128, 16], F32, "ExternalInput")
    w1d = fw.dram("w1", [HY_EMB, 64], F32, "ExternalInput")
    w2d = fw.dram("w2", [64, 64], F32, "ExternalInput")
    w3d = fw.dram("w3", [64, 2, 128], F32, "ExternalInput")
    fbd = fw.dram("fb", [64, 4], F32, "ExternalInput")
    zfL = fw.dram("zfL", [2, HY_EMB, LQ], F32, "ExternalInput")
    t01L = fw.dram("t01L", [2, LQ], F32, "ExternalInput")
    zfC = fw.dram("zfC", [2, HY_EMB, CTX], F32, "ExternalInput")
    t01C = fw.dram("t01C", [2, CTX], F32, "ExternalInput")
    oT = fw.dram("oT", [128, LQ], F32, "ExternalOutput")
    ocT = fw.dram("ocT", [128, CTX], F32, "ExternalOutput")
    KF = fw.dram("KF", [64, 2 * LQ], BF16, "Internal")
    KFC = fw.dram("KFC", [64, 2 * CTX], BF16, "Internal")
    if own:
        fw.engines()
    pe, act, dve, pool, sp = fw.pe, fw.act, fw.dve, fw.pool, fw.sp

    identb = fw.sbuf("identb", [128, 128], BF16)
    identf = fw.sbuf("identf", [128, 128], F32)
    idd = fw.dram("ident", [128, 128], F32, "ExternalInput")
    sp.dma(identf, identf[:], idd, idd[:])
    dve.op(lambda e: e.tensor_copy(identb[:], identf[:]), [identb], [identf])
    antif = fw.sbuf("antif", [128, 128], F32)
    add = fw.dram("anti", [128, 128], F32, "ExternalInput")
    sp.dma(antif, antif[:], add, add[:])
    sm = fw.sbuf("smt", [128, 16], F32)
    sp.dma(sm, sm[:], smd, smd[:])
    w1 = fw.sbuf("w1s", [HY_EMB, 64], F32)
    sp.dma(w1, w1[:], w1d, w1d[:])
    w2 = fw.sbuf("w2s", [64, 64], F32)
    sp.dma(w2, w2[:], w2d, w2d[:])
    w3 = fw.sbuf("w3s", [64, 2, 128], F32)
    sp.dma(w3, w3[:], w3d, w3d[:])
    fb = fw.sbuf("fbs", [64, 4], F32)
    sp.dma(fb, fb[:], fbd, fbd[:])
    fb1 = fw.sbuf("fb1", [64, 4], F32)
    fb2 = fw.sbuf("fb2", [64, 4], F32)
    for dst, bc in ((fb1, 1), (fb2, 2)):
        dve.op(lambda e: e.tensor_scalar(dst[:, 0:1], fb[:, 0:1], 0.5, None, ALU.mult), [dst], [fb])
        dve.op(lambda e: e.tensor_scalar(dst[:, 2:3], fb[:, 0:1], 0.25, None, ALU.mult), [dst], [fb])
        dve.op(lambda e: e.tensor_tensor(dst[:, 1:2], dst[:, 0:1], fb[:, bc:bc + 1], ALU.mult), [dst], [dst, fb])
        dve.op(lambda e: e.tensor_tensor(dst[:, 3:4], dst[:, 2:3], fb[:, bc:bc + 1], ALU.mult), [dst], [dst, fb])
    ndelta = fw.sbuf("ndelta", [128, 1], F32)
    dve.op(lambda e: e.tensor_copy(ndelta[:], sm[:, 13:14]), [ndelta], [sm])
    wts = dict(w1=w1, w2=w2, w3=w3, fb1=fb1, fb2=fb2, ndelta=ndelta)
    scr = (fw.psum("fg_ps1", [128, 512], F32), fw.psum("fg_ps2", [128, 512], F32), fw.psum("fg_ps3", [128, 512], F32),
           fw.sbuf("fg_zt", [64, 512], F32), fw.sbuf("fg_h1", [64, 512], F32), fw.sbuf("fg_h2", [64, 512], F32),
           fw.sbuf("fg_s2", [64, 512], F32), fw.sbuf("fg_s4", [64, 512], F32), fw.sbuf("fg_tt", [128, 512], F32),
           fw.sbuf("fg_kt", [128, 512], F32), fw.sbuf("fg_ktb", [128, 512], BF16), fw.sbuf("fg_sq", [128, 512], F32))
    scaleL = filter_gen(fw, LQ, zfL, t01L, wts, KF, "fL", scr)
    scaleC = filter_gen(fw, CTX, zfC, t01C, wts, KFC, "fC", scr)

    Zt = fw.sbuf("Zt", [128, 64, NB, 2], BF16)
    Ztc = fw.sbuf("Ztc", [128, 64, NBC, 2], BF16)
    uts = [fw.sbuf("ut%d" % i, [128, TT + 2], F32) for i in range(3)]
    cv = [fw.sbuf("cv%d" % i, [128, TT], F32) for i in range(3)]
    zb = fw.sbuf("zb", [128, TT], BF16)
    pst = [fw.psum("pst%d" % i, [128, 512], BF16) for i in range(2)]

    def z_phase(src, Ls, Ztx, nblk):
        tt_ = min(TT, Ls)
        for t0 in range(0, Ls, tt_):
            for part in range(2):
                load_u_tile(fw, uts[part], src, part, t0, tt_, Ls)
                conv3(fw, cv[part], uts[part], tt_, sm, part)
            dve.op(lambda e: e.tensor_tensor(zb[:, 0:tt_], cv[0][:, 0:tt_], cv[1][:, 0:tt_], ALU.mult), [zb], [cv[0], cv[1]])
            for rb in range(tt_ // 128):
                r = t0 // 128 + rb
                p = pst[r % 2]
                pe.op(lambda e: e.transpose(p[:, 0:128], zb[:, rb * 128:(rb + 1) * 128], identb[:]), [p], [zb, identb])
                dst = Ztx[:, :, r, :].rearrange("p c b -> p b c")
                src_ap = p[:, 0:128].rearrange("p (b c) -> p b c", b=2)
                if r % 2 == 0:
                    dve.op(lambda e: e.tensor_copy(dst, src_ap), [Ztx], [p])
                else:
                    act.op(lambda e: e.activation(dst, src_ap, AF.Copy), [Ztx], [p])

    z_phase(uT, LQ, Zt, NB)
    z_phase(ucT, CTX, Ztc, NBC)

    W = (2 * NB - 1) * 128
    X0 = (NB - 1) * 128
    tbs = [fw.sbuf("tb%d" % i, [128, W], BF16) for i in range(2)]
    tbc = fw.sbuf("tbc", [128, 16, 3 * 128], BF16)
    Y = fw.sbuf("Y", [128, NB, 2, 64], F32)
    Yc = fw.sbuf("Yc", [128, NBC, 2, 64], F32)
    psy = [fw.psum("psy%d" % i, [128, 512], F32) for i in range(2)]
    kft = KF.t.tensor
    kfct = KFC.t.tensor
    for ch in range(64):
        tb = tbs[ch % 2]
        src = bass.AP(tensor=kft, offset=ch * 2 * LQ + 1, ap=[[1, 128], [1, W]])
        (sp if ch % 2 == 0 else act).dma(tb, tb[:], KF, src)
        p = psy[ch % 2]
        ds = [0] + [d for d in range(-(NB - 1), NB) if d != 0]
        for k, d in enumerate(ds):
            r0 = max(0, d)
            nb = NB - abs(d)
            pe.op(lambda e: e.matmul(p[:, r0 * 2:(r0 + nb) * 2], tb[:, X0 - 128 * d:X0 - 128 * d + 128],
                                     Zt[:, ch, r0 - d:r0 - d + nb, :].rearrange("p r b -> p (r b)"),
                                     start=(k == 0), stop=(k == len(ds) - 1), skip_group_check=True), [p], [tb, Zt])
        if ch % 2 == 0:
            dve.op(lambda e: e.tensor_copy(Y[:, :, :, ch], p[:, 0:NB * 2].rearrange('p (r b) -> p r b', b=2)), [Y], [p])
        else:
            act.op(lambda e: e.activation(Y[:, :, :, ch], p[:, 0:NB * 2].rearrange('p (r b) -> p r b', b=2), AF.Copy), [Y], [p])
    pc = psy[0]
    for g in range(4):
        src = bass.AP(tensor=kfct, offset=g * 16 * 2 * CTX + 1, ap=[[1, 128], [2 * CTX, 16], [1, 384]])
        sp.dma(tbc, tbc[:], KFC, src)
        for c16 in range(16):
            ch = g * 16 + c16
            o0 = ch * 4
            pe.op(lambda e: e.matmul(pc[:, o0:o0 + 4], tbc[:, c16, 128:256], Ztc[:, ch, :, :].rearrange("p r b -> p (r b)"),
                                     start=True, stop=False, skip_group_check=True), [pc], [tbc, Ztc])
            pe.op(lambda e: e.matmul(pc[:, o0 + 2:o0 + 4], tbc[:, c16, 0:128], Ztc[:, ch, 0, :],
                                     start=False, stop=False, skip_group_check=True), [pc], [tbc, Ztc])
            pe.op(lambda e: e.matmul(pc[:, o0:o0 + 2], tbc[:, c16, 256:384], Ztc[:, ch, 1, :],
                                     start=False, stop=True, skip_group_check=True), [pc], [tbc, Ztc])
    dve.op(lambda e: e.tensor_copy(Yc[:].rearrange("p r b c -> p c r b"), pc[:, 0:256].rearrange("p (c r b) -> p c r b", r=NBC, b=2)), [Yc], [pc])

    pso = [fw.psum("pso%d" % i, [128, 512], F32) for i in range(1)]
    ys = fw.sbuf("ys", [128, 512], F32)
    ots = [fw.sbuf("ot%d" % i, [128, 512], F32) for i in range(2)]

    def out_phase(src, Ls, Yx, scale, dstT):
        tt_ = min(TT, Ls)
        cnt = 0
        for t0 in range(0, Ls, tt_):
            for part in range(3):
                load_u_tile(fw, uts[part], src, part, t0, tt_, Ls)
                conv3(fw, cv[part], uts[part], tt_, sm, part)
            dve.op(lambda e: e.tensor_tensor(cv[0][:, 0:tt_], cv[0][:, 0:tt_], cv[1][:, 0:tt_], ALU.mult), [cv[0]], [cv[1]])
            for s0 in range(0, tt_, 512):
                ns = min(512, tt_ - s0)
                p = pso[0]
                for rb in range(ns // 128):
                    r = (t0 + s0) // 128 + rb
                    in_ap = Yx[:, r, :, :].rearrange("p b c -> p (b c)")
                    pe.op(lambda e: e.matmul(p[:, rb * 128:(rb + 1) * 128], in_ap, antif[:], start=True, stop=True), [p], [Yx, antif])
                act.op(lambda e: e.activation(ys[:, 0:ns], p[:, 0:ns], AF.Copy, scale=scale[:, 0:1]), [ys], [p, scale])
                dve.op(lambda e: e.scalar_tensor_tensor(ys[:, 0:ns], cv[0][:, s0:s0 + ns], sm[:, 12:13], ys[:, 0:ns], ALU.mult, ALU.add), [ys], [cv[0], sm])
                ot = ots[cnt % 2]
                cnt += 1
                dve.op(lambda e: e.tensor_tensor(ot[:, 0:ns], ys[:, 0:ns], cv[2][:, s0:s0 + ns], ALU.mult), [ot], [ys, cv[2]])
                sp.dma(dstT, dstT.t[:, t0 + s0:t0 + s0 + ns], ot, ot[:, 0:ns])

    out_phase(uT, LQ, Y, scaleL, oT)
    out_phase(ucT, CTX, Yc, scaleC, ocT)
    sp.wait_buf(oT)
    sp.wait_buf(ocT)
    if own:
        fw.close()
    return fw.nc


def hy_feats(L):
    t01 = np.linspace(0.0, 1.0, L, dtype=np.float32)
    bands = (HY_EMB - 1) // 2
    w_ang = (2.0 * math.pi * np.arange(L, dtype=np.float32) / L).astype(np.float32)
    f = np.linspace(1e-4, bands - 1, bands, dtype=np.float32)
    ang = (f[None, :] * w_ang[:, None]).astype(np.float32)
    z = np.concatenate([t01[:, None], np.cos(ang), -np.sin(ang)], axis=-1).astype(np.float32)
    zf = np.stack([z.T, z[::-1].T])
    tt = np.stack([t01, t01[::-1]])
    return np.ascontiguousarray(zf), np.ascontiguousarray(tt)


def hy_ndelta(D_WIDTH=512):
    d = np.linspace(math.log(1e-2) / 0.3, math.log(1e-2) / 1.5, D_WIDTH, dtype=np.float32)
    return -np.abs(d)


def k2d_inputs(core, u_lat, u_ctx, conv_w, conv_b, w1, b1, w2, b2, w3, freq, bias, LQ):
    c0 = 64 * core
    DW = 512
    def pk(u, Ls):
        parts = []
        for part in range(3):
            cols = u[:, :, part * DW + c0: part * DW + c0 + 64]
            parts.append(cols.transpose(0, 2, 1).reshape(128, Ls))
        return np.ascontiguousarray(np.stack(parts))
    sm = np.zeros((128, 16), np.float32)
    for part in range(3):
        for tap in range(3):
            sm[:, part * 3 + tap] = np.tile(conv_w[tap, part * DW + c0: part * DW + c0 + 64], 2)
        sm[:, 9 + part] = np.tile(conv_b[part * DW + c0: part * DW + c0 + 64], 2)
    sm[:, 12] = np.tile(bias[c0:c0 + 64], 2)
    sm[:, 13] = np.tile(hy_ndelta()[c0:c0 + 64], 2)
    w3r = w3.reshape(64, 2, DW)[:, :, c0:c0 + 64]
    w3p = np.ascontiguousarray(np.concatenate([w3r, w3r], axis=2))
    fb = np.zeros((64, 4), np.float32)
    fb[:, 0] = freq; fb[:, 1] = b1; fb[:, 2] = b2
    zfL, t01L = hy_feats(LQ)
    zfC, t01C = hy_feats(CTX)
    return {"uT": pk(u_lat, LQ), "ucT": pk(u_ctx, CTX), "sm": sm, "w1": np.ascontiguousarray(w1), "w2": np.ascontiguousarray(w2),
            "w3": w3p, "fb": fb, "zfL": zfL, "t01L": t01L, "zfC": zfC, "t01C": t01C, "ident": np.eye(128, dtype=np.float32), "anti": np.ascontiguousarray(np.eye(128, dtype=np.float32)[::-1])}


def build_mix(LQ=8192):
    fw = FW()
    fw.engines()
    for pfx, body in (("a_", build_k2a), ("b_", build_k2b), ("c_", build_k2c), ("d_", build_k2d)):
        fw.pfx = pfx
        fw.push_scope()
        body(LQ, fw=fw)
        fw.pop_scope()
    fw.pfx = ""
    fw.close()
    return fw.nc


D = 2048
KC = 16
EPS = 1e-6
FH = 5632
HC = FH // 128


def build_k3(TL=2048, TC=64, with_k1=False):
    fw = FW()
    LW = TL + 2
    CW = TC + 2
    TW = LW + CW
    TO = TL + TC
    oT = fw.dram("oT", [KC, 128, TW], F32, "ExternalInput")
    xT = fw.dram("xT", [KC, 128, TW], F32, "ExternalInput")
    vec = fw.dram("vec", [128, 9, KC], F32, "ExternalInput")
    edge = fw.dram("edge", [128, 4], F32, "ExternalInput")
    w_out = fw.dram("w_out", [8, 128, KC, 256], F32, "ExternalInput")
    w_up = fw.dram("w_up", [2 * FH // 256, 128, KC, 256], F32, "ExternalInput")
    cwd = fw.dram("cw", [128, 4, 2 * HC], F32, "ExternalInput")
    w_dn = fw.dram("w_dn", [8, 128, HC, 256], F32, "ExternalInput")
    xo = fw.dram("xo", [KC, 128, TO], F32, "ExternalOutput")
    XN = fw.dram("XN", [KC, 128, TW], F32, "Internal")
    AT = fw.dram("AT", [HC, 128, TO], BF16, "Internal")
    fw.engines()
    pe, act, dve, pool, sp = fw.pe, fw.act, fw.dve, fw.pool, fw.sp
    ones = fw.sbuf("ones", [128, 128], F32)
    dve.op(lambda e: e.memset(ones[:], 1.0), [ones], [])
    vt = fw.sbuf("vt", [128, 9, KC], F32)
    sp.dma(vt, vt[:], vec, vec[:])
    eg = fw.sbuf("eg", [128, 4], F32)
    sp.dma(eg, eg[:], edge, edge[:])
    A_lat = fw.sbuf("A_lat", [128, KC], F32)
    A_ctx = fw.sbuf("A_ctx", [128, KC], F32)
    dve.op(lambda e: e.scalar_tensor_tensor(A_lat[:], vt[:, 3, :], 1.0, vt[:, 2, :], ALU.add, ALU.mult), [A_lat], [vt])
    dve.op(lambda e: e.scalar_tensor_tensor(A_ctx[:], vt[:, 5, :], 1.0, vt[:, 2, :], ALU.add, ALU.mult), [A_ctx], [vt])
    fw.push_scope()
    h2T = fw.sbuf("h2T", [128, KC, TW], BF16)

    fw.push_scope()
    wo = [fw.sbuf("wo%d" % i, [128, KC, 256], BF16) for i in range(8)]
    for i in range(8):
        pool.dma(wo[i], wo[i][:], w_out, w_out.t[i])
    ots = [fw.sbuf("a_ot%d" % i, [128, KC, 256], BF16) for i in range(2)]
    xts = [fw.sbuf("a_xt%d" % i, [128, KC, 256], F32) for i in range(2)]
    sq = fw.sbuf("a_sq", [128, KC, 256], F32)
    rs = fw.sbuf("a_rs", [128, 256], F32)
    psA = [fw.psum("a_ps%d" % i, [128, 512], F32) for i in range(4)]
    psn = fw.psum("a_psn", [128, 512], F32)
    tilesA = [(c0, 256, 0) for c0 in range(0, TL, 256)] + [(TL, 2, 0), (LW, CW, 1)]
    ov = oT.t.rearrange("k p t -> p k t")
    xv = xT.t.rearrange("k p t -> p k t")
    xnv = XN.t.rearrange("k p t -> p k t")
    cnt = 0
    for ti, (c0, n, kind) in enumerate(tilesA):
        ot, xt = ots[ti % 2], xts[ti % 2]
        g1c = 0 if kind == 0 else 1
        A2 = A_lat if kind == 0 else A_ctx
        shc = 4 if kind == 0 else 6
        pool.dma(ot, ot[:, :, 0:n], oT, ov[:, :, c0:c0 + n])
        sp.dma(xt, xt[:, :, 0:n], xT, xv[:, :, c0:c0 + n])
        for ci in range(KC):
            p = psA[cnt % 4]
            cnt += 1
            w = wo[ci // 2]
            h0 = (ci % 2) * 128
            for k in range(KC):
                pe.op(lambda e: e.matmul(p[:, 0:n], w[:, k, h0:h0 + 128], ot[:, k, 0:n], start=(k == 0), stop=(k == KC - 1)), [p], [w, ot])
            dve.op(lambda e: e.scalar_tensor_tensor(xt[:, ci, 0:n], p[:, 0:n], vt[:, g1c, ci:ci + 1], xt[:, ci, 0:n], ALU.mult, ALU.add), [xt], [p, vt])
        sp.dma(XN, xnv[:, :, c0:c0 + n], xt, xt[:, :, 0:n])
        act.op(lambda e: e.activation(sq[:, :, 0:n], xt[:, :, 0:n], AF.Square), [sq], [xt])
        for k in range(KC):
            pe.op(lambda e: e.matmul(psn[:, 0:n], ones[:], sq[:, k, 0:n], start=(k == 0), stop=(k == KC - 1)), [psn], [ones, sq])
        dve.op(lambda e: e.tensor_scalar(rs[:, 0:n], psn[:, 0:n], 1.0 / D, EPS, ALU.mult, ALU.add), [rs], [psn])
        act.op(lambda e: e.activation(rs[:, 0:n], rs[:, 0:n], AF.Sqrt), [rs], [rs])
        dve.op(lambda e: e.reciprocal(rs[:, 0:n], rs[:, 0:n]), [rs], [rs])
        for k in range(KC):
            dve.op(lambda e: e.tensor_tensor(sq[:, k, 0:n], xt[:, k, 0:n], rs[:, 0:n], ALU.mult), [sq], [xt, rs])
        for k in range(KC):
            act.op(lambda e: e.activation(h2T[:, k, c0:c0 + n], sq[:, k, 0:n], AF.Identity, bias=vt[:, shc, k:k + 1], scale=A2[:, k:k + 1]),
                   [h2T], [sq, A2, vt])
    fw.pop_scope()

    fw.push_scope()
    cw = fw.sbuf("cws", [128, 4, 2 * HC], F32)
    sp.dma(cw, cw[:], cwd, cwd[:])
    wgs = [fw.sbuf("b_wg%d" % i, [128, KC, 256], BF16) for i in range(2)]
    wus = [fw.sbuf("b_wu%d" % i, [128, KC, 256], BF16) for i in range(2)]
    psB = [fw.psum("b_ps%d" % i, [128, 512], F32) for i in range(4)]
    ug = [fw.sbuf("b_ug%d" % i, [128, 512], F32) for i in range(2)]
    uu = [fw.sbuf("b_uu%d" % i, [128, 512], F32) for i in range(2)]
    cg = [fw.sbuf("b_cg%d" % i, [128, 512], F32) for i in range(2)]
    cu = [fw.sbuf("b_cu%d" % i, [128, 512], F32) for i in range(2)]
    ab = [fw.sbuf("b_ab%d" % i, [128, 512], BF16) for i in range(2)]
    tilesB = []
    s = 0
    while s + 2 < LW:
        m = min(512, LW - s)
        tilesB.append((s, m, s, 0 if s == 0 else None, 1 if s + m == LW else None))
        s += m - 2
    tilesB.append((LW, CW, TL, 2, 3))

    def conv(dst, u, m, hc):
        dve.op(lambda e: e.tensor_scalar(dst[:, 0:m - 2], u[:, 1:m - 1], cw[:, 1, hc:hc + 1], cw[:, 3, hc:hc + 1], ALU.mult, ALU.add), [dst], [u, cw])
        dve.op(lambda e: e.scalar_tensor_tensor(dst[:, 0:m - 2], u[:, 0:m - 2], cw[:, 0, hc:hc + 1], dst[:, 0:m - 2], ALU.mult, ALU.add), [dst], [u, cw])
        dve.op(lambda e: e.scalar_tensor_tensor(dst[:, 0:m - 2], u[:, 2:m], cw[:, 2, hc:hc + 1], dst[:, 0:m - 2], ALU.mult, ALU.add), [dst], [u, cw])

    cnt = 0
    for j in range(FH // 256):
        wg, wu = wgs[j % 2], wus[j % 2]
        pool.dma(wg, wg[:], w_up, w_up.t[j])
        pool.dma(wu, wu[:], w_up, w_up.t[FH // 256 + j])
        for (s, m, o0, eL, eR) in tilesB:
            for half in range(2):
                hc = 2 * j + half
                h0 = half * 128
                i2 = cnt % 2
                cnt += 1
                pg, pu = psB[(2 * cnt) % 4], psB[(2 * cnt + 1) % 4]
                for k in range(KC):
                    pe.op(lambda e: e.matmul(pg[:, 0:m], wg[:, k, h0:h0 + 128], h2T[:, k, s:s + m], start=(k == 0), stop=(k == KC - 1)), [pg], [wg, h2T])
                for k in range(KC):
                    pe.op(lambda e: e.matmul(pu[:, 0:m], wu[:, k, h0:h0 + 128], h2T[:, k, s:s + m], start=(k == 0), stop=(k == KC - 1)), [pu], [wu, h2T])
                act.op(lambda e: e.activation(ug[i2][:, 0:m], pg[:, 0:m], AF.Copy), [ug[i2]], [pg])
                act.op(lambda e: e.activation(uu[i2][:, 0:m], pu[:, 0:m], AF.Copy), [uu[i2]], [pu])
                for ubuf in (ug[i2], uu[i2]):
                    if eL is not None:
                        dve.op(lambda e: e.tensor_scalar(ubuf[:, 0:1], ubuf[:, 0:1], eg[:, eL:eL + 1], None, ALU.mult), [ubuf], [ubuf, eg])
                    if eR is not None:
                        dve.op(lambda e: e.tensor_scalar(ubuf[:, m - 1:m], ubuf[:, m - 1:m], eg[:, eR:eR + 1], None, ALU.mult), [ubuf], [ubuf, eg])
                conv(cg[i2], ug[i2], m, hc)
                conv(cu[i2], uu[i2], m, HC + hc)
                act.op(lambda e: e.activation(cg[i2][:, 0:m - 2], cg[i2][:, 0:m - 2], AF.Silu), [cg[i2]], [cg[i2]])
                dve.op(lambda e: e.tensor_tensor(ab[i2][:, 0:m - 2], cg[i2][:, 0:m - 2], cu[i2][:, 0:m - 2], ALU.mult), [ab[i2]], [cg[i2], cu[i2]])
                sp.dma(AT, AT.t[hc, :, o0:o0 + m - 2], ab[i2], ab[i2][:, 0:m - 2])
    fw.pop_scope()
    fw.pop_scope()

    fw.push_scope()
    HALF = TO // 3
    at = fw.sbuf("c_at", [128, HC, HALF], BF16)
    wds = [fw.sbuf("c_wd%d" % i, [128, HC, 256], BF16) for i in range(2)]
    psC = [fw.psum("c_ps%d" % i, [128, 512], F32) for i in range(4)]
    xns = [fw.sbuf("c_xn%d" % i, [128, 512], F32) for i in range(3)]
    outs = [fw.sbuf("c_o%d" % i, [128, 512], F32) for i in range(3)]
    atv = AT.t.rearrange("k p t -> p k t")
    cnt = 0
    wcnt = 0
    for hf in range(3):
        h0, h1 = hf * HALF, (hf + 1) * HALF
        sp.dma(at, at[:], AT, atv[:, :, h0:h1])
        subs = []
        o = h0
        while o < h1:
            lim = TL if o < TL else TO
            n = min(512, min(h1, lim) - o)
            subs.append((o, n))
            o += n
        for ct in range(8):
            wd = wds[wcnt % 2]
            wcnt += 1
            pool.dma(wd, wd[:], w_dn, w_dn.t[ct])
            for (o0, n) in subs:
                kind = 0 if o0 < TL else 1
                xcol = o0 + 1 if kind == 0 else o0 + 3
                for h2 in range(2):
                    ci = 2 * ct + h2
                    p = psC[cnt % 4]
                    xn = xns[cnt % 3]
                    ob = outs[cnt % 3]
                    cnt += 1
                    sp.dma(xn, xn[:, 0:n], XN, XN.t[ci, :, xcol:xcol + n])
                    for k in range(HC):
                        pe.op(lambda e: e.matmul(p[:, 0:n], wd[:, k, h2 * 128:h2 * 128 + 128], at[:, k, o0 - h0:o0 - h0 + n],
                                                 start=(k == 0), stop=(k == HC - 1)), [p], [wd, at])
                    dve.op(lambda e: e.scalar_tensor_tensor(ob[:, 0:n], p[:, 0:n], vt[:, 7 + kind, ci:ci + 1], xn[:, 0:n], ALU.mult, ALU.add), [ob], [p, vt, xn])
                    sp.dma(xo, xo.t[ci, :, o0:o0 + n], ob, ob[:, 0:n])
    fw.pop_scope()
    sp.wait_buf(xo)
    if with_k1:
        fw.pfx = "k1_"
        fw.push_scope()
        build_k1(T_lat=TL, T_ctx=TC, fw=fw, xT=xo)
        fw.pop_scope()
        fw.pfx = ""
    fw.close()
    return fw.nc


def tile_w(w):
    K, N = w.shape
    return np.ascontiguousarray(w.reshape(K // 128, 128, N // 256, 256).transpose(2, 1, 0, 3))


def vec_pk(v):
    return np.ascontiguousarray(v.reshape(-1, 128).T)


OFF_QA, OFF_KA, OFF_VA, OFF_QB, OFF_KB, OFF_VB, OFF_POOL, OFF_HY = 0, 512, 1024, 1536, 2048, 2176, 2304, 2816
SEQ = 8192
NCTX = 256
_CORES = list(range(8))
_PROGS = {}
_N = {"launches": 0}


def _prog(name, builder):
    if name not in _PROGS:
        _PROGS[name] = builder()
    return _PROGS[name]


def _run(name, builder, ins):
    nc = _prog(name, builder)
    res = run_bass_kernel_spmd(nc, ins, core_ids=_CORES)
    return res.results


def _build_k3k1():
    return build_k3(with_k1=True)


def _k1_x(XT_lat, XT_ctx, k):
    b, q = divmod(k, 4)
    xT = np.concatenate([XT_lat[b][:, q * 2048:(q + 1) * 2048], XT_ctx[b][:, q * 64:(q + 1) * 64]], axis=1)
    return np.ascontiguousarray(xT.reshape(16, 128, 2112))


def _k1_in(l, k, mod, norm1_g, wt):
    b = k // 4
    m = mod[l].reshape(3, 6, -1)
    vec = np.stack([vec_pk(norm1_g[l]), vec_pk(m[b, 1]), vec_pk(m[b, 0]), vec_pk(m[2, 1]), vec_pk(m[2, 0])], axis=1)
    return {"vec": np.ascontiguousarray(vec), "w": wt}


def _seg(aT, lo, hi):
    F, S = aT.shape
    out = np.zeros((F, hi - lo + 2), np.float32)
    l2, h2 = max(lo - 1, 0), min(hi + 1, S)
    out[:, l2 - (lo - 1):h2 - (lo - 1)] = aT[:, l2:h2]
    return out


def kernel(x, c, ctx, c_ctx, w_mod, b_mod, norm1_g, norm2_g, w_in, w_out, qk_gain, diff_lam, diff_subln,
           win_sink, pool_w, pool_scale, hy_conv_w, hy_conv_b, hy_w1, hy_b1, hy_w2, hy_b2, hy_w3, hy_freq,
           hy_bias, ffn_w_in, ffn_conv_w, ffn_conv_b, ffn_w_out):
    f32 = lambda a: np.ascontiguousarray(np.asarray(a), dtype=np.float32)
    (x, c, ctx, c_ctx, w_mod, b_mod, norm1_g, norm2_g, w_in, w_out, qk_gain, diff_lam, diff_subln, win_sink, pool_w,
     pool_scale, hy_conv_w, hy_conv_b, hy_w1, hy_b1, hy_w2, hy_b2, hy_w3, hy_freq, hy_bias, ffn_w_in, ffn_conv_w,
     ffn_conv_b, ffn_w_out) = [f32(a) for a in (x, c, ctx, c_ctx, w_mod, b_mod, norm1_g, norm2_g, w_in, w_out, qk_gain,
                                                diff_lam, diff_subln, win_sink, pool_w, pool_scale, hy_conv_w, hy_conv_b,
                                                hy_w1, hy_b1, hy_w2, hy_b2, hy_w3, hy_freq, hy_bias, ffn_w_in, ffn_conv_w,
                                                ffn_conv_b, ffn_w_out)]
    B, L, Dm = x.shape
    NL = w_mod.shape[0]
    T = L + NCTX
    r = _run("k0", build_k0, [k0_inputs(k, c, c_ctx, w_mod, b_mod) for k in _CORES])
    mod = np.concatenate([r[k]["mod"] for k in _CORES], axis=2)
    XT_lat = [np.ascontiguousarray(x[b].T) for b in range(B)]
    XT_ctx = [np.ascontiguousarray(ctx[b].T) for b in range(B)]
    CSt = rope_tables(L, NCTX, 2)
    mk = band_masks()
    zfL, t01L = hy_feats(L)
    zfC, t01C = hy_feats(NCTX)
    ident = np.eye(128, dtype=np.float32)
    anti = np.ascontiguousarray(ident[::-1])
    ndel = hy_ndelta()
    for l in range(NL):
        m = mod[l].reshape(3, 6, Dm)
        wt_in = tile_w(w_in[l]) if l == 0 else None
        if l == 0:
            r = _run("k1", build_k1, [dict(xT=_k1_x(XT_lat, XT_ctx, k), **_k1_in(l, k, mod, norm1_g, wt_in)) for k in _CORES])
            pts = [r[k]["pT"].reshape(4352, 2112) for k in _CORES]
            del r
        PT_lat = [np.concatenate([pts[b * 4 + q][:, :2048] for q in range(4)], axis=1) for b in range(B)]
        PT_ctx = [np.concatenate([pts[b * 4 + q][:, 2048:] for q in range(4)], axis=1) for b in range(B)]
        del pts
        OT_lat = [np.zeros((Dm, L), np.float32) for _ in range(B)]
        OT_ctx = [np.zeros((Dm, NCTX), np.float32) for _ in range(B)]
        lambda_init = 0.8 - 0.6 * math.exp(-0.3 * l)
        g0 = np.tile(qk_gain[l, 0], 2)
        g1 = np.tile(qk_gain[l, 1], 2)
        ins_a = []
        for k in _CORES:
            b, h = divmod(k, 4)
            rq = slice(OFF_QA + h * 128, OFF_QA + (h + 1) * 128)
            rk = slice(OFF_KA + h * 128, OFF_KA + (h + 1) * 128)
            rv = slice(OFF_VA + h * 128, OFF_VA + (h + 1) * 128)
            q_full = np.ascontiguousarray(np.concatenate([PT_lat[b][rq], PT_ctx[b][rq]], axis=1))
            k_full = np.ascontiguousarray(np.concatenate([PT_ctx[b][rk], PT_lat[b][rk]], axis=1))
            v_full = np.concatenate([PT_ctx[b][rv], PT_lat[b][rv]], axis=1).T
            sm = np.zeros((128, 8), np.float32)
            sm[:, 0] = g0
            sm[:, 1] = swap_halves(g0[:, None])[:, 0]
            sm[:, 2] = g1
            sm[:, 3] = swap_halves(g1[:, None])[:, 0]
            sm[:, 4] = diff_subln[l]
            sm[:, 5] = lambda_init
            sm[:, 6] = 1.0 - lambda_init
            ins_a.append({"qT": q_full, "qsT": swap_halves(q_full), "kT": k_full, "ksT": swap_halves(k_full),
                        "v": np.ascontiguousarray(v_full.reshape(T // 128, 128, 128).transpose(1, 0, 2)),
                        "CS": CSt, "sm": sm, "lamp": np.ascontiguousarray(diff_lam[l].reshape(1, 256))})
        g2 = np.tile(qk_gain[l, 2], 2)
        g3 = np.tile(qk_gain[l, 3], 2)
        ins_b = []
        for k in _CORES:
            b = k // 4
            h0 = 2 * (k % 4)
            kvh = (k % 4) // 2
            rq = slice(OFF_QB + h0 * 64, OFF_QB + (h0 + 2) * 64)
            rk = slice(OFF_KB + kvh * 64, OFF_KB + (kvh + 1) * 64)
            rv = slice(OFF_VB + kvh * 64, OFF_VB + (kvh + 1) * 64)
            q_full = np.ascontiguousarray(np.concatenate([PT_lat[b][rq], PT_ctx[b][rq]], axis=1))
            k1 = np.concatenate([PT_ctx[b][rk], PT_lat[b][rk]], axis=1)
            k_full = np.ascontiguousarray(np.concatenate([k1, k1], axis=0))
            v_full = np.concatenate([PT_ctx[b][rv], PT_lat[b][rv]], axis=1).T
            sm = np.zeros((128, 8), np.float32)
            sm[:, 0] = g2
            sm[:, 1] = swap_halves(g2[:, None])[:, 0]
            sm[:, 2] = g3
            sm[:, 3] = swap_halves(g3[:, None])[:, 0]
            sm[:, 4] = win_sink[l, h0]
            sm[:, 5] = win_sink[l, h0 + 1]
            ins_b.append({"qT": q_full, "qsT": swap_halves(q_full), "kT": k_full, "ksT": swap_halves(k_full),
                        "v": np.ascontiguousarray(v_full.reshape(T // 128, 128, 64).transpose(1, 0, 2)),
                        "CS": CSt, "sm": sm, "masks": mk})
        ins_c = []
        for k in _CORES:
            b, g = divmod(k, 4)
            ru = slice(OFF_POOL + g * 128, OFF_POOL + (g + 1) * 128)
            sel, ic = pool_consts(g, L)
            ins_c.append({"uT": np.ascontiguousarray(np.concatenate([PT_lat[b][ru], PT_ctx[b][ru]], axis=1)), "wsel": sel, "invc": ic,
                        "wl": np.ascontiguousarray(pool_w[l, g]), "ls": np.ascontiguousarray(pool_scale[l, g * 128:(g + 1) * 128, None])})
        ins_d = []
        for k in _CORES:
            c0 = 64 * k
            uL, uC = [], []
            smh = np.zeros((128, 16), np.float32)
            for part in range(3):
                rr = slice(OFF_HY + part * 512 + c0, OFF_HY + part * 512 + c0 + 64)
                uL.append(np.concatenate([PT_lat[0][rr], PT_lat[1][rr]], axis=0))
                uC.append(np.concatenate([PT_ctx[0][rr], PT_ctx[1][rr]], axis=0))
                for tap in range(3):
                    smh[:, part * 3 + tap] = np.tile(hy_conv_w[l, tap, part * 512 + c0: part * 512 + c0 + 64], 2)
                smh[:, 9 + part] = np.tile(hy_conv_b[l, part * 512 + c0: part * 512 + c0 + 64], 2)
            smh[:, 12] = np.tile(hy_bias[l, c0:c0 + 64], 2)
            smh[:, 13] = np.tile(ndel[c0:c0 + 64], 2)
            w3r = hy_w3[l].reshape(64, 2, 512)[:, :, c0:c0 + 64]
            fb = np.zeros((64, 4), np.float32)
            fb[:, 0] = hy_freq[l]
            fb[:, 1] = hy_b1[l]
            fb[:, 2] = hy_b2[l]
            ins_d.append({"uT": np.ascontiguousarray(np.stack(uL)), "ucT": np.ascontiguousarray(np.stack(uC)), "sm": smh,
                        "w1": np.ascontiguousarray(hy_w1[l]), "w2": np.ascontiguousarray(hy_w2[l]),
                        "w3": np.ascontiguousarray(np.concatenate([w3r, w3r], axis=2)), "fb": fb,
                        "zfL": zfL, "t01L": t01L, "zfC": zfC, "t01C": t01C, "ident": ident, "anti": anti})
        ins = []
        for k in _CORES:
            dct = {}
            for pfx, lst in (("a_", ins_a), ("b_", ins_b), ("c_", ins_c), ("d_", ins_d)):
                for kk, vv in lst[k].items():
                    dct[pfx + kk] = vv
            ins.append(dct)
        del ins_a, ins_b, ins_c, ins_d
        r = _run("mix", build_mix, ins)
        for k in _CORES:
            b, h = divmod(k, 4)
            o = r[k]["a_oT"]
            OT_lat[b][h * 128:(h + 1) * 128] = o[:, :L]
            OT_ctx[b][h * 128:(h + 1) * 128] = o[:, L:]
            h0 = 2 * (k % 4)
            o = r[k]["b_oT"].reshape(128, T)
            OT_lat[b][512 + h0 * 64:512 + (h0 + 2) * 64] = o[:, :L]
            OT_ctx[b][512 + h0 * 64:512 + (h0 + 2) * 64] = o[:, L:]
            g = k % 4
            o = r[k]["c_yT"]
            OT_lat[b][1024 + g * 128:1024 + (g + 1) * 128] = o[:, :L]
            OT_ctx[b][1024 + g * 128:1024 + (g + 1) * 128] = o[:, L:]
            c0 = 64 * k
            o = r[k]["d_oT"]
            oc = r[k]["d_ocT"]
            for bb in range(B):
                OT_lat[bb][1536 + c0:1536 + c0 + 64] = o[bb * 64:(bb + 1) * 64]
                OT_ctx[bb][1536 + c0:1536 + c0 + 64] = oc[bb * 64:(bb + 1) * 64]
        del r, ins
        del PT_lat, PT_ctx
        wo_t = tile_w(w_out[l])
        wt_next = tile_w(w_in[l + 1]) if l + 1 < NL else None
        wu_t = tile_w(ffn_w_in[l])
        wd_t = tile_w(ffn_w_out[l])
        cwp = np.ascontiguousarray(np.stack([vec_pk(ffn_conv_w[l, 0]), vec_pk(ffn_conv_w[l, 1]), vec_pk(ffn_conv_w[l, 2]),
                                             vec_pk(ffn_conv_b[l])], axis=1))
        ins = []
        for k in _CORES:
            b, q = divmod(k, 4)
            ocol = np.concatenate([_seg(OT_lat[b], q * 2048, (q + 1) * 2048), _seg(OT_ctx[b], q * 64, (q + 1) * 64)], axis=1)
            xcol = np.concatenate([_seg(XT_lat[b], q * 2048, (q + 1) * 2048), _seg(XT_ctx[b], q * 64, (q + 1) * 64)], axis=1)
            TW = ocol.shape[1]
            vec = np.stack([vec_pk(m[b, 2]), vec_pk(m[2, 2]), vec_pk(norm2_g[l]), vec_pk(m[b, 4]), vec_pk(m[b, 3]),
                            vec_pk(m[2, 4]), vec_pk(m[2, 3]), vec_pk(m[b, 5]), vec_pk(m[2, 5])], axis=1)
            eg = np.ones((128, 4), np.float32)
            if q == 0:
                eg[:, 0] = 0
                eg[:, 2] = 0
            if q == 3:
                eg[:, 1] = 0
                eg[:, 3] = 0
            dct = {"oT": np.ascontiguousarray(ocol.reshape(16, 128, TW)), "xT": np.ascontiguousarray(xcol.reshape(16, 128, TW)),
                   "vec": np.ascontiguousarray(vec), "edge": eg, "w_out": wo_t, "w_up": wu_t, "cw": cwp, "w_dn": wd_t}
            if l + 1 < NL:
                for kk, vv in _k1_in(l + 1, k, mod, norm1_g, wt_next).items():
                    dct["k1_" + kk] = vv
            ins.append(dct)
        if l + 1 < NL:
            r = _run("k3k1", _build_k3k1, ins)
            pts = [r[k]["k1_pT"].reshape(4352, 2112) for k in _CORES]
        else:
            r = _run("k3", build_k3, ins)
        for k in _CORES:
            b, q = divmod(k, 4)
            o = r[k]["xo"].reshape(Dm, 2112)
            XT_lat[b][:, q * 2048:(q + 1) * 2048] = o[:, :2048]
            XT_ctx[b][:, q * 64:(q + 1) * 64] = o[:, 2048:]
        del r, ins
    return np.ascontiguousarray(np.stack([XT_lat[b].T for b in range(B)])).astype(np.float32)
```

```python
import math
import numpy as np
import time
from contextlib import ExitStack
import concourse.bass as bass
import concourse.mybir as mybir
from concourse.bass_utils import run_bass_kernel_spmd

F32 = mybir.dt.float32
BF16 = mybir.dt.bfloat16
AF = mybir.ActivationFunctionType
ALU = mybir.AluOpType
AX = mybir.AxisListType


class Buf:
    __slots__ = ("name", "t", "w", "r", "sem", "dcount")

    def __init__(self, name, t):
        self.name = name
        self.t = t
        self.w = {}
        self.r = {}
        self.sem = None
        self.dcount = 0

    def __getitem__(self, idx):
        return self.t[idx]


class Eng:
    def __init__(self, fw, name, eng, sem, kind):
        self.fw = fw
        self.name = name
        self.eng = eng
        self.sem = sem
        self.kind = kind
        self.count = 0
        self.seen = {}

    def _wait(self, deps):
        for key, (sem, val) in deps.items():
            if self.seen.get(key, 0) >= val:
                continue
            if sem is self.sem and self.kind == 'pe':
                continue
            self.eng.wait_ge(sem, val)
            self.seen[key] = val

    def _deps(self, outs, ins):
        deps = {}
        for b in ins:
            for k, (s, v) in b.w.items():
                if deps.get(k, (None, 0))[1] < v:
                    deps[k] = (s, v)
        for b in outs:
            for d in (b.w, b.r):
                for k, (s, v) in d.items():
                    if deps.get(k, (None, 0))[1] < v:
                        deps[k] = (s, v)
        return deps

    def op(self, inst_fn, outs, ins):
        self._wait(self._deps(outs, ins))
        inst = inst_fn(self.eng)
        self.count += 1
        inst.then_inc(self.sem, 1)
        key = id(self.sem)
        tok = (self.sem, self.count)
        for b in ins:
            b.r[key] = tok
        for b in outs:
            b.w = {key: tok}
            b.r = {}
        return tok

    def dma(self, out_buf, out_ap, in_buf, in_ap, **kw):
        self._wait(self._deps([out_buf], [in_buf]))
        if out_buf.sem is None:
            out_buf.sem = self.fw.new_sem("d_" + out_buf.name)
        inst = self.eng.dma_start(out=out_ap, in_=in_ap, **kw)
        out_buf.dcount += 16
        inst.then_inc(out_buf.sem, 16)
        key = id(out_buf.sem)
        tok = (out_buf.sem, out_buf.dcount)
        in_buf.r[key] = tok
        out_buf.w = {key: tok}
        out_buf.r = {}
        return tok

    def wait_buf(self, b):
        self._wait(dict(b.w))


class FW:
    def __init__(self, name="k"):
        self.nc = bass.Bass("TRN2", target_bir_lowering=False)
        self.es = ExitStack()
        self.nsem = 0
        self.block = None
        self.all_bufs = []
        self.scopes = []
        self.pfx = ""

    def new_sem(self, name):
        self.nsem += 1
        return self.es.enter_context(self.nc.semaphore(name + "_%d" % self.nsem))

    def dram(self, name, shape, dtype, kind):
        name = self.pfx + name
        t = self.nc.dram_tensor(name, list(shape), dtype, kind=kind)
        b = Buf(name, t.ap())
        self.all_bufs.append(b)
        return b

    def sbuf(self, name, shape, dtype):
        name = self.pfx + name
        t = (self.scopes[-1] if self.scopes else self.es).enter_context(self.nc.sbuf_tensor(name, list(shape), dtype))
        b = Buf(name, t)
        self.all_bufs.append(b)
        return b

    def psum(self, name, shape, dtype=F32):
        name = self.pfx + name
        t = (self.scopes[-1] if self.scopes else self.es).enter_context(self.nc.psum_tensor(name, list(shape), dtype))
        b = Buf(name, t)
        self.all_bufs.append(b)
        return b

    def engines(self):
        nc = self.nc
        self.pe = Eng(self, "pe", nc.tensor, self.new_sem("pe"), 'pe')
        self.act = Eng(self, "act", nc.scalar, self.new_sem("act"), 'act')
        self.dve = Eng(self, "dve", nc.vector, self.new_sem("dve"), 'dve')
        self.pool = Eng(self, "pool", nc.gpsimd, self.new_sem("pool"), 'pool')
        self.sp = Eng(self, "sp", nc.sync, self.new_sem("sp"), 'sp')
        return self.pe, self.act, self.dve, self.pool, self.sp

    def push_scope(self):
        self.scopes.append(ExitStack())

    def pop_scope(self):
        fw_barrier(self)
        self.scopes.pop().close()

    def close(self):
        self.es.close()


def sub_bufs(parent, aps, prefix):
    return [Buf("%s%d" % (prefix, i), ap) for i, ap in enumerate(aps)]


def fw_barrier(fw, bufs=()):
    engs = [fw.pe, fw.act, fw.dve, fw.pool, fw.sp]
    toks = {}
    for e in engs:
        if e.count > 0:
            toks[id(e.sem)] = (e.sem, e.count)
    for b in fw.all_bufs:
        if b.sem is not None and b.dcount > 0:
            toks[id(b.sem)] = (b.sem, b.dcount)
    for e in engs:
        e._wait(dict(toks))


D = 2048
KC = D // 128
EPS = 1e-6


def load_consts(fw, sp):
    ones = fw.sbuf("ones", [128, 128], F32)
    fw.dve.op(lambda e: e.memset(ones[:], 1.0), [ones], [])
    return ones


def norm_mod_phase(fw, xT, hT, tiles, vecs, ones, nm):
    pe, act, dve, pool, sp = fw.pe, fw.act, fw.dve, fw.pool, fw.sp
    xts = [fw.sbuf("%s_xt%d" % (nm, i), [128, KC, 512], F32) for i in range(2)]
    sqs = [fw.sbuf("%s_sq%d" % (nm, i), [128, KC, 512], F32) for i in range(1)]
    rstd = [fw.sbuf("%s_rstd%d" % (nm, i), [128, 512], F32) for i in range(2)]
    ps = [fw.psum("%s_ps%d" % (nm, i), [128, 512], F32) for i in range(2)]
    xv = xT.t.rearrange("k p t -> p k t")
    for i, (t0, n, A, sh) in enumerate(tiles):
        xt, sq, rs, p = xts[i % 2], sqs[0], rstd[i % 2], ps[i % 2]
        sp.dma(xt, xt[:, :, 0:n], xT, xv[:, :, t0:t0 + n])
        act.op(lambda e: e.activation(sq[:, :, 0:n], xt[:, :, 0:n], AF.Square), [sq], [xt])
        for kc in range(KC):
            pe.op(lambda e: e.matmul(p[:, 0:n], ones[:], sq[:, kc, 0:n], start=(kc == 0), stop=(kc == KC - 1)), [p], [ones, sq])
        dve.op(lambda e: e.tensor_scalar(rs[:, 0:n], p[:, 0:n], 1.0 / D, EPS, ALU.mult, ALU.add), [rs], [p])
        act.op(lambda e: e.activation(rs[:, 0:n], rs[:, 0:n], AF.Sqrt), [rs], [rs])
        dve.op(lambda e: e.reciprocal(rs[:, 0:n], rs[:, 0:n]), [rs], [rs])
        for kc in range(KC):
            dve.op(lambda e: e.tensor_tensor(sq[:, kc, 0:n], xt[:, kc, 0:n], rs[:, 0:n], ALU.mult), [sq], [xt, rs])
        for kc in range(KC):
            act.op(lambda e: e.activation(hT[:, kc, t0:t0 + n], sq[:, kc, 0:n], AF.Identity,
                                          bias=sh[:, kc:kc + 1], scale=A[:, kc:kc + 1]), [hT], [sq, A, sh])


def linear_phase(fw, hT, KCn, w, ncoltiles, tiles, epilogue, nm, nbuf=3):
    pe, pool = fw.pe, fw.pool
    wts = [fw.sbuf("%s_w%d" % (nm, i), [128, KCn, 256], BF16) for i in range(nbuf)]
    ps = [fw.psum("%s_lp%d" % (nm, i), [128, 512], F32) for i in range(4)]
    cnt = 0
    for ct in range(ncoltiles):
        wt = wts[ct % nbuf]
        pool.dma(wt, wt[:], w, w.t[ct])
        for ti, (t0, n) in enumerate(tiles):
            for half in range(2):
                p = ps[cnt % 4]
                cnt += 1
                for kc in range(KCn):
                    pe.op(lambda e: e.matmul(p[:, 0:n], wt[:, kc, half * 128:(half + 1) * 128], hT[:, kc, t0:t0 + n],
                                             start=(kc == 0), stop=(kc == KCn - 1)), [p], [wt, hT])
                epilogue(ct * 2 + half, ti, t0, n, p)


def build_k1(T_lat=2048, T_ctx=64, NCOL=4352, fw=None, xT=None):
    own = fw is None
    if own:
        fw = FW()
    T = T_lat + T_ctx
    if xT is None:
        xT = fw.dram("xT", [KC, 128, T], F32, "ExternalInput")
    vec = fw.dram("vec", [128, 5, KC], F32, "ExternalInput")
    w = fw.dram("w", [NCOL // 256, 128, KC, 256], F32, "ExternalInput")
    pT = fw.dram("pT", [NCOL // 128, 128, T], F32, "ExternalOutput")
    if own:
        fw.engines()
    pe, act, dve, pool, sp = fw.pe, fw.act, fw.dve, fw.pool, fw.sp
    ones = load_consts(fw, sp)
    vt = fw.sbuf("vt", [128, 5, KC], F32)
    sp.dma(vt, vt[:], vec, vec[:])
    A_lat = fw.sbuf("A_lat", [128, KC], F32)
    A_ctx = fw.sbuf("A_ctx", [128, KC], F32)
    sh_lat = fw.sbuf("sh_lat", [128, KC], F32)
    sh_ctx = fw.sbuf("sh_ctx", [128, KC], F32)
    dve.op(lambda e: e.scalar_tensor_tensor(A_lat[:], vt[:, 1, :], 1.0, vt[:, 0, :], ALU.add, ALU.mult), [A_lat], [vt])
    dve.op(lambda e: e.scalar_tensor_tensor(A_ctx[:], vt[:, 3, :], 1.0, vt[:, 0, :], ALU.add, ALU.mult), [A_ctx], [vt])
    dve.op(lambda e: e.tensor_copy(sh_lat[:], vt[:, 2, :]), [sh_lat], [vt])
    dve.op(lambda e: e.tensor_copy(sh_ctx[:], vt[:, 4, :]), [sh_ctx], [vt])
    hT = fw.sbuf("hT", [128, KC, T], BF16)
    tiles = [(t0, 512, A_lat, sh_lat) for t0 in range(0, T_lat, 512)]
    if T_ctx:
        tiles.append((T_lat, T_ctx, A_ctx, sh_ctx))
    norm_mod_phase(fw, xT, hT, tiles, None, ones, "n1")
    ots = [fw.sbuf("ot%d" % i, [128, 512], F32) for i in range(4)]
    st = {"i": 0}

    def epi(ci, ti, t0, n, p):
        ot = ots[st["i"] % 4]
        if st["i"] % 2 == 0:
            dve.op(lambda e: e.tensor_copy(ot[:, 0:n], p[:, 0:n]), [ot], [p])
        else:
            act.op(lambda e: e.activation(ot[:, 0:n], p[:, 0:n], AF.Copy), [ot], [p])
        st["i"] += 1
        sp.dma(pT, pT.t[ci, :, t0:t0 + n], ot, ot[:, 0:n])

    linear_phase(fw, hT, KC, w, NCOL // 256, [(t[0], t[1]) for t in tiles], epi, "l1")
    sp.wait_buf(pT)
    if own:
        fw.close()
    return fw.nc


def tile_w(w):
    K, N = w.shape
    return np.ascontiguousarray(w.reshape(K // 128, 128, N // 256, 256).transpose(2, 1, 0, 3))


def vec_pk(v):
    return np.ascontiguousarray(v.reshape(KC, 128).T)


D = 2048
KC = 16
NCOLS = 1536


def build_k0(NL=4):
    fw = FW()
    cT = fw.dram("cT", [128, KC, 3], F32, "ExternalInput")
    w = fw.dram("w", [NL, 3, 128, KC, 512], F32, "ExternalInput")
    bm = fw.dram("bm", [NL, NCOLS], F32, "ExternalInput")
    mod = fw.dram("mod", [NL, 3, NCOLS], F32, "ExternalOutput")
    fw.engines()
    pe, act, dve, pool, sp = fw.pe, fw.act, fw.dve, fw.pool, fw.sp
    ct = fw.sbuf("ct", [128, KC, 3], F32)
    sp.dma(ct, ct[:], cT, cT[:])
    st = fw.sbuf("st", [128, KC, 3], F32)
    act.op(lambda e: e.activation(st[:], ct[:], AF.Silu), [st], [ct])
    wts = [fw.sbuf("wt%d" % i, [128, KC, 512], F32) for i in range(2)]
    bts = [fw.sbuf("bt%d" % i, [3, 512], F32) for i in range(2)]
    ots = [fw.sbuf("ot%d" % i, [3, 512], F32) for i in range(2)]
    ps = [fw.psum("ps%d" % i, [128, 512], F32) for i in range(2)]
    i = 0
    for l in range(NL):
        for t in range(3):
            wt, bt, ot, p = wts[i % 2], bts[i % 2], ots[i % 2], ps[i % 2]
            i += 1
            (sp if i % 2 == 0 else act).dma(wt, wt[:], w, w.t[l, t])
            bsrc = bass.AP(tensor=bm.t.tensor, offset=l * NCOLS + t * 512, ap=[[0, 3], [1, 512]])
            sp.dma(bt, bt[:], bm, bsrc)
            for k in range(KC):
                pe.op(lambda e: e.matmul(p[0:3, :], st[:, k, :], wt[:, k, :], start=(k == 0), stop=(k == KC - 1)), [p], [st, wt])
            dve.op(lambda e: e.tensor_tensor(ot[:], p[0:3, :], bt[:], ALU.add), [ot], [p, bt])
            sp.dma(mod, mod.t[l, :, t * 512:(t + 1) * 512], ot, ot[:])
    sp.wait_buf(mod)
    fw.close()
    return fw.nc


def k0_inputs(core, c, c_ctx, w_mod, b_mod):
    NL = w_mod.shape[0]
    cs = np.stack([c[0], c[1], c_ctx], axis=1)
    cT = np.ascontiguousarray(cs.reshape(KC, 128, 3).transpose(1, 0, 2))
    cols = slice(core * NCOLS, (core + 1) * NCOLS)
    w = w_mod[:, :, cols].reshape(NL, KC, 128, 3, 512).transpose(0, 3, 2, 1, 4)
    return {"cT": cT, "w": np.ascontiguousarray(w), "bm": np.ascontiguousarray(b_mod[:, cols])}


EPS = 1e-6
CTX = 256


def qknorm_rope(fw, xT, xsT, CS, gcol, dst, col_map, tiles, blockones, nm, inv_n):
    pe, act, dve, pool, sp = fw.pe, fw.act, fw.dve, fw.pool, fw.sp
    gbuf, c0 = gcol
    if "qk_scr" not in fw.__dict__:
        fw.qk_scr = dict(
            xt=[fw.sbuf("qk_x%d" % i, [128, 512], F32) for i in range(2)],
            xs=[fw.sbuf("qk_xs%d" % i, [128, 512], F32) for i in range(2)],
            ct=[fw.sbuf("qk_c%d" % i, [128, 512], F32) for i in range(2)],
            st=[fw.sbuf("qk_s%d" % i, [128, 512], F32) for i in range(2)],
            sq=[fw.sbuf("qk_sq%d" % i, [128, 512], F32) for i in range(2)],
            rs=[fw.sbuf("qk_rs%d" % i, [128, 512], F32) for i in range(2)],
            ps=[fw.psum("qk_ps%d" % i, [128, 512], F32) for i in range(2)])
    xt, xs, ct, st, sq, rs, ps = (fw.qk_scr[k] for k in ("xt", "xs", "ct", "st", "sq", "rs", "ps"))
    for i, (x0, tb0, n) in enumerate(tiles):
        j = i % 2
        sp.dma(xt[j], xt[j][:, 0:n], xT, xT.t[:, x0:x0 + n])
        sp.dma(xs[j], xs[j][:, 0:n], xsT, xsT.t[:, x0:x0 + n])
        sp.dma(ct[j], ct[j][:, 0:n], CS, CS.t[0, :, tb0:tb0 + n])
        sp.dma(st[j], st[j][:, 0:n], CS, CS.t[1, :, tb0:tb0 + n])
        act.op(lambda e: e.activation(sq[j][:, 0:n], xt[j][:, 0:n], AF.Square), [sq[j]], [xt[j]])
        pe.op(lambda e: e.matmul(ps[j][:, 0:n], blockones[:], sq[j][:, 0:n], start=True, stop=True), [ps[j]], [blockones, sq[j]])
        dve.op(lambda e: e.tensor_scalar(rs[j][:, 0:n], ps[j][:, 0:n], inv_n, EPS, ALU.mult, ALU.add), [rs[j]], [ps[j]])
        act.op(lambda e: e.activation(rs[j][:, 0:n], rs[j][:, 0:n], AF.Sqrt), [rs[j]], [rs[j]])
        dve.op(lambda e: e.reciprocal(rs[j][:, 0:n], rs[j][:, 0:n]), [rs[j]], [rs[j]])
        dve.op(lambda e: e.scalar_tensor_tensor(ct[j][:, 0:n], xt[j][:, 0:n], gbuf[:, c0:c0 + 1], ct[j][:, 0:n], ALU.mult, ALU.mult),
               [ct[j]], [xt[j], gbuf])
        dve.op(lambda e: e.scalar_tensor_tensor(st[j][:, 0:n], xs[j][:, 0:n], gbuf[:, c0 + 1:c0 + 2], st[j][:, 0:n], ALU.mult, ALU.mult),
                [st[j]], [xs[j], gbuf])
        dve.op(lambda e: e.tensor_tensor(ct[j][:, 0:n], ct[j][:, 0:n], st[j][:, 0:n], ALU.add), [ct[j]], [st[j]])
        dve.op(lambda e: e.tensor_tensor(dst[:, x0:x0 + n], ct[j][:, 0:n], rs[j][:, 0:n], ALU.mult), [dst], [ct[j], rs[j]])


def build_k2a(LQ=8192, fw=None):
    own = fw is None
    if own:
        fw = FW()
    T = LQ + CTX
    NKC = T // 128
    qT = fw.dram("qT", [128, T], F32, "ExternalInput")
    qsT = fw.dram("qsT", [128, T], F32, "ExternalInput")
    kT = fw.dram("kT", [128, T], F32, "ExternalInput")
    ksT = fw.dram("ksT", [128, T], F32, "ExternalInput")
    v = fw.dram("v", [128, NKC, 128], F32, "ExternalInput")
    CS = fw.dram("CS", [2, 128, T], F32, "ExternalInput")
    sm = fw.dram("sm", [128, 8], F32, "ExternalInput")
    lamp = fw.dram("lamp", [1, 256], F32, "ExternalInput")
    oT = fw.dram("oT", [128, T], F32, "ExternalOutput")
    if own:
        fw.engines()
    pe, act, dve, pool, sp = fw.pe, fw.act, fw.dve, fw.pool, fw.sp
    ones = fw.sbuf("ones", [128, 128], F32)
    dve.op(lambda e: e.memset(ones[:], 1.0), [ones], [])
    onesb = fw.sbuf("onesb", [128, 128], BF16)
    dve.op(lambda e: e.memset(onesb[:], 1.0), [onesb], [])
    blockones = fw.sbuf("blockones", [128, 128], F32)
    dve.op(lambda e: e.memset(blockones[:], 0.0), [blockones], [])
    dve.op(lambda e: e.memset(blockones[0:64, 0:64], 1.0), [blockones], [])
    dve.op(lambda e: e.memset(blockones[64:128, 64:128], 1.0), [blockones], [])
    smt = fw.sbuf("smt", [128, 8], F32)
    sp.dma(smt, smt[:], sm, sm[:])
    lt = fw.sbuf("lt", [1, 256], F32)
    sp.dma(lt, lt[:], lamp, lamp[:])
    lw = fw.sbuf("lw", [1, 128], F32)
    dve.op(lambda e: e.tensor_tensor(lw[:, 0:64], lt[:, 0:64], lt[:, 64:128], ALU.mult), [lw], [lt])
    dve.op(lambda e: e.tensor_tensor(lw[:, 64:128], lt[:, 128:192], lt[:, 192:256], ALU.mult), [lw], [lt])
    l2 = fw.sbuf("l2", [1, 4], F32)
    dve.op(lambda e: e.reduce_sum(l2[:, 0:1], lw[:, 0:64], AX.X), [l2], [lw])
    dve.op(lambda e: e.reduce_sum(l2[:, 1:2], lw[:, 64:128], AX.X), [l2], [lw])
    act.op(lambda e: e.activation(l2[:, 0:2], l2[:, 0:2], AF.Exp), [l2], [l2])
    dve.op(lambda e: e.tensor_tensor(l2[:, 2:3], l2[:, 1:2], l2[:, 0:1], ALU.subtract), [l2], [l2])
    dve.op(lambda e: e.tensor_tensor(l2[:, 2:3], l2[:, 2:3], smt[0:1, 5:6], ALU.subtract), [l2], [l2, smt])
    sc = fw.sbuf("sc", [128, 2], F32)
    fw.push_scope()
    psl = fw.psum("psl", [128, 512], F32)
    pe.op(lambda e: e.matmul(psl[:, 0:1], ones[0:1, :], l2[0:1, 2:3], start=True, stop=True), [psl], [ones, l2])
    dve.op(lambda e: e.tensor_copy(sc[:, 0:1], psl[:, 0:1]), [sc], [psl])
    dve.op(lambda e: e.tensor_tensor(sc[:, 1:2], smt[:, 4:5], smt[:, 6:7], ALU.mult), [sc], [smt])
    fw.pop_scope()

    QT = fw.sbuf("QT", [128, T], BF16)
    KT = fw.sbuf("KT", [128, T], BF16)
    V = fw.sbuf("V", [128, NKC, 128], BF16)
    pool.dma(V, V[:], v, v[:])
    qtiles = [(t0, t0, 512) for t0 in range(0, LQ, 512)] + [(LQ, LQ, CTX)]
    ktiles = [(0, LQ, CTX)] + [(CTX + t0, t0, 512) for t0 in range(0, LQ, 512)]
    fw.push_scope()
    qknorm_rope(fw, qT, qsT, CS, (smt, 0), QT, None, qtiles, blockones, "qn", 1.0 / 64)
    qknorm_rope(fw, kT, ksT, CS, (smt, 2), KT, None, ktiles, blockones, "kn", 1.0 / 64)
    del fw.qk_scr
    fw.pop_scope()

    NSB = 2
    NPT = 4
    ps_s = [[fw.psum("ps_s%d_%d" % (m, i), [128, 512], F32) for i in range(NSB)] for m in range(2)]
    ps_o = [fw.psum("ps_o%d" % m, [128, 512], F32) for m in range(2)]
    ps_z = [fw.psum("ps_z%d" % m, [128, 512], F32) for m in range(2)]
    pts = [[fw.sbuf("pt%d_%d" % (m, i), [128, 512], BF16) for i in range(NPT)] for m in range(2)]
    om = [fw.sbuf("om%d" % i, [128, 512], F32) for i in range(2)]
    rz = [fw.sbuf("rz%d" % i, [128, 512], F32) for i in range(2)]
    osq = fw.sbuf("osq", [128, 512], F32)
    ors = fw.sbuf("ors", [128, 512], F32)
    ots = [fw.sbuf("ot%d" % i, [128, 512], F32) for i in range(2)]
    for qi, (q0, _, n) in enumerate(qtiles):
        nkc = NKC if q0 < LQ else CTX // 128
        seq = []
        for kc in range(nkc):
            seq.append(("s", kc))
            if kc >= 1:
                seq.append(("av", kc - 1))
        seq.append(("av", nkc - 1))
        for kind, kc in seq:
            if kind == "s":
                for m in range(2):
                    r0 = 64 * m
                    p = ps_s[m][kc % NSB]
                    pe.op(lambda e: e.matmul(p[:, 0:n], KT[r0:r0 + 64, kc * 128:(kc + 1) * 128], QT[r0:r0 + 64, q0:q0 + n],
                                             start=True, stop=True), [p], [KT, QT])
                for m in range(2):
                    p = ps_s[m][kc % NSB]
                    pt = pts[m][kc % NPT]
                    act.op(lambda e: e.activation(pt[:, 0:n], p[:, 0:n], AF.Exp, scale=0.125), [pt], [p])
            else:
                for m in range(2):
                    pt = pts[m][kc % NPT]
                    pe.op(lambda e: e.matmul(ps_o[m][:, 0:n], V[:, kc, :], pt[:, 0:n], start=(kc == 0), stop=(kc == nkc - 1)), [ps_o[m]], [V, pt])
                    pe.op(lambda e: e.matmul(ps_z[m][:, 0:n], onesb[:], pt[:, 0:n], start=(kc == 0), stop=(kc == nkc - 1)), [ps_z[m]], [onesb, pt])
        for m in range(2):
            dve.op(lambda e: e.reciprocal(rz[m][:, 0:n], ps_z[m][:, 0:n]), [rz[m]], [ps_z[m]])
            dve.op(lambda e: e.tensor_tensor(om[m][:, 0:n], ps_o[m][:, 0:n], rz[m][:, 0:n], ALU.mult), [om[m]], [ps_o[m], rz[m]])
        dve.op(lambda e: e.scalar_tensor_tensor(om[0][:, 0:n], om[1][:, 0:n], sc[:, 0:1], om[0][:, 0:n], ALU.mult, ALU.add), [om[0]], [om[1], sc])
        act.op(lambda e: e.activation(osq[:, 0:n], om[0][:, 0:n], AF.Square), [osq], [om[0]])
        pl = ps_s[0][(nkc) % NSB]
        pe.op(lambda e: e.matmul(pl[:, 0:n], ones[:], osq[:, 0:n], start=True, stop=True), [pl], [ones, osq])
        dve.op(lambda e: e.tensor_scalar(ors[:, 0:n], pl[:, 0:n], 1.0 / 128, EPS, ALU.mult, ALU.add), [ors], [pl])
        act.op(lambda e: e.activation(ors[:, 0:n], ors[:, 0:n], AF.Sqrt), [ors], [ors])
        dve.op(lambda e: e.reciprocal(ors[:, 0:n], ors[:, 0:n]), [ors], [ors])
        ot = ots[qi % 2]
        dve.op(lambda e: e.scalar_tensor_tensor(ot[:, 0:n], om[0][:, 0:n], sc[:, 1:2], ors[:, 0:n], ALU.mult, ALU.mult), [ot], [om[0], sc, ors])
        sp.dma(oT, oT.t[:, q0:q0 + n], ot, ot[:, 0:n])
    sp.wait_buf(oT)
    if own:
        fw.close()
    return fw.nc


def swap_halves(xT):
    r = xT.reshape(-1, 2, 2, 16, xT.shape[-1])
    return np.ascontiguousarray(r[:, :, ::-1]).reshape(xT.shape)


def rope_tables(L, n_ctx, reps):
    GRID_W = 64
    rows = L // GRID_W
    row = np.repeat(np.arange(rows, dtype=np.float32), GRID_W)
    col = np.tile(np.arange(GRID_W, dtype=np.float32), rows)
    inv = (10000.0 ** (-np.arange(0, 32, 2, dtype=np.float32) / 32)).astype(np.float32)
    ang = np.concatenate([row[:, None] * inv, col[:, None] * inv], axis=-1)
    cos = np.cos(ang).astype(np.float32).reshape(L, 2, 16)
    sin = np.sin(ang).astype(np.float32).reshape(L, 2, 16)
    C = np.ones((2, 2, 16, L + n_ctx), np.float32)
    S = np.zeros((2, 2, 16, L + n_ctx), np.float32)
    C[:, 0, :, :L] = cos.transpose(1, 2, 0)
    C[:, 1, :, :L] = cos.transpose(1, 2, 0)
    S[:, 0, :, :L] = -sin.transpose(1, 2, 0)
    S[:, 1, :, :L] = sin.transpose(1, 2, 0)
    C = np.tile(C.reshape(64, -1), (reps, 1))
    S = np.tile(S.reshape(64, -1), (reps, 1))
    return np.ascontiguousarray(np.stack([C, S]))


def build_k2b(LQ=8192, fw=None, defer=False):
    own = fw is None
    if own:
        fw = FW()
    T = LQ + CTX
    NKC = T // 128
    NB = LQ // 128
    qT = fw.dram("qT", [128, T], F32, "ExternalInput")
    qsT = fw.dram("qsT", [128, T], F32, "ExternalInput")
    kT = fw.dram("kT", [128, T], F32, "ExternalInput")
    ksT = fw.dram("ksT", [128, T], F32, "ExternalInput")
    v = fw.dram("v", [128, NKC, 64], F32, "ExternalInput")
    CS = fw.dram("CS", [2, 128, T], F32, "ExternalInput")
    sm = fw.dram("sm", [128, 8], F32, "ExternalInput")
    masks = fw.dram("masks", [128, 6, 512], F32, "ExternalInput")
    oT = fw.dram("oT", [2, 64, T], F32, "ExternalOutput")
    if own:
        fw.engines()
    pe, act, dve, pool, sp = fw.pe, fw.act, fw.dve, fw.pool, fw.sp
    onesb = fw.sbuf("onesb", [128, 128], BF16)
    dve.op(lambda e: e.memset(onesb[:], 1.0), [onesb], [])
    blockones = fw.sbuf("blockones", [128, 128], F32)
    dve.op(lambda e: e.memset(blockones[:], 0.0), [blockones], [])
    dve.op(lambda e: e.memset(blockones[0:64, 0:64], 1.0), [blockones], [])
    dve.op(lambda e: e.memset(blockones[64:128, 64:128], 1.0), [blockones], [])
    smt = fw.sbuf("smt", [128, 8], F32)
    sp.dma(smt, smt[:], sm, sm[:])
    es = fw.sbuf("es", [128, 2], F32)
    act.op(lambda e: e.activation(es[:], smt[:, 4:6], AF.Exp), [es], [smt])
    mk = fw.sbuf("mk", [128, 6, 512], BF16)
    pool.dma(mk, mk[:], masks, masks[:])
    QT = fw.sbuf("QT", [128, T], BF16)
    KT = fw.sbuf("KT", [128, T], BF16)
    V = fw.sbuf("V", [128, NKC, 64], BF16)
    pool.dma(V, V[:], v, v[:])
    qtiles = [(t0, t0, 512) for t0 in range(0, LQ, 512)] + [(LQ, LQ, CTX)]
    ktiles = [(0, LQ, CTX)] + [(CTX + t0, t0, 512) for t0 in range(0, LQ, 512)]
    fw.push_scope()
    qknorm_rope(fw, qT, qsT, CS, (smt, 0), QT, None, qtiles, blockones, "qn", 1.0 / 64)
    qknorm_rope(fw, kT, ksT, CS, (smt, 2), KT, None, ktiles, blockones, "kn", 1.0 / 64)
    del fw.qk_scr
    fw.pop_scope()

    NSB = 1 if defer else 2
    NPT = 4
    ps_s = [[fw.psum("ps_s%d_%d" % (m, i), [128, 512], F32) for i in range(NSB)] for m in range(2)]
    ps_o = [fw.psum("ps_o%d" % m, [64, 512], F32) for m in range(2)]
    ps_z = [fw.psum("ps_z%d" % m, [64, 512], F32) for m in range(2)]
    pts = [[fw.sbuf("pt%d_%d" % (m, i), [128, 512], BF16) for i in range(NPT)] for m in range(2)]
    rz = [fw.sbuf("rz%d" % m, [64, 512], F32) for m in range(2)]
    ots = [fw.sbuf("ot%d" % i, [64, 512], F32) for i in range(4)]
    def steps():
        cntl = [0]
        for qi, (q0, _, n) in enumerate(qtiles):
            if q0 < LQ:
                n0 = q0 // 128
                chunks = [(0, None), (1, None)]
                for rel in range(-1, 5):
                    j = n0 + rel
                    if 0 <= j < NB:
                        chunks.append((CTX // 128 + j, rel + 1))
            else:
                chunks = [(0, None), (1, None)]
            nch = len(chunks)
            seq = []
            for ci in range(nch):
                seq.append(("s", ci))
                if ci >= 1:
                    seq.append(("av", ci - 1))
            seq.append(("av", nch - 1))
            for kind, ci in seq:
                kc, mi = chunks[ci]
                if kind == "s":
                    for m in range(2):
                        r0 = 64 * m
                        p = ps_s[m][ci % NSB]
                        pe.op(lambda e: e.matmul(p[:, 0:n], KT[r0:r0 + 64, kc * 128:(kc + 1) * 128], QT[r0:r0 + 64, q0:q0 + n],
                                                 start=True, stop=True), [p], [KT, QT])
                    for m in range(2):
                        p = ps_s[m][ci % NSB]
                        pt = pts[m][ci % NPT]
                        act.op(lambda e: e.activation(pt[:, 0:n], p[:, 0:n], AF.Exp, scale=0.125), [pt], [p])
                        if mi is not None:
                            (dve if m == 0 else pool).op(lambda e: e.tensor_tensor(pt[:, 0:n], pt[:, 0:n], mk[:, mi, 0:n], ALU.mult), [pt], [pt, mk])
                else:
                    last = ci == nch - 1
                    for m in range(2):
                        pt = pts[m][ci % NPT]
                        pe.op(lambda e: e.matmul(ps_o[m][:, 0:n], V[:, kc, :], pt[:, 0:n], start=(ci == 0), stop=last), [ps_o[m]], [V, pt])
                        pe.op(lambda e: e.matmul(ps_z[m][:, 0:n], onesb[:, 0:64], pt[:, 0:n], start=(ci == 0), stop=last), [ps_z[m]], [onesb, pt])
            for m in range(2):
                dve.op(lambda e: e.tensor_scalar(rz[m][:, 0:n], ps_z[m][:, 0:n], es[0:64, m:m + 1], None, ALU.add), [rz[m]], [ps_z[m], es])
                dve.op(lambda e: e.reciprocal(rz[m][:, 0:n], rz[m][:, 0:n]), [rz[m]], [rz[m]])
                ot = ots[cntl[0] % 4]
                cntl[0] += 1
                dve.op(lambda e: e.tensor_tensor(ot[:, 0:n], ps_o[m][:, 0:n], rz[m][:, 0:n], ALU.mult), [ot], [ps_o[m], rz[m]])
                (pool if defer else sp).dma(oT, oT.t[m, :, q0:q0 + n], ot, ot[:, 0:n])
            yield qi
        sp.wait_buf(oT)

    if defer:
        return steps()
    for _ in steps():
        pass
    if own:
        fw.close()
    return fw.nc


def band_masks():
    ki = np.arange(128)[:, None, None]
    rel = np.arange(-1, 5)[None, :, None]
    qq = np.arange(512)[None, None, :]
    return (np.abs(128 * rel + ki - qq) <= 128).astype(np.float32)


CTX = 256
POOL_WINDOWS = (2, 4, 8, 16)


def build_k2c(LQ=8192, fw=None):
    own = fw is None
    if own:
        fw = FW()
    T = LQ + CTX
    PADL = 8
    uT = fw.dram("uT", [128, T], F32, "ExternalInput")
    wsel = fw.dram("wsel", [128, 16], F32, "ExternalInput")
    invc = fw.dram("invc", [128, T], F32, "ExternalInput")
    wl = fw.dram("wl", [128, 128], F32, "ExternalInput")
    ls = fw.dram("ls", [128, 1], F32, "ExternalInput")
    yT = fw.dram("yT", [128, T], F32, "ExternalOutput")
    if own:
        fw.engines()
    pe, act, dve, pool, sp = fw.pe, fw.act, fw.dve, fw.pool, fw.sp
    wst = fw.sbuf("wst", [128, 16], F32)
    sp.dma(wst, wst[:], wsel, wsel[:])
    lst = fw.sbuf("lst", [128, 1], F32)
    sp.dma(lst, lst[:], ls, ls[:])
    wlt = fw.sbuf("wlt", [128, 128], BF16)
    pool.dma(wlt, wlt[:], wl, wl[:])
    TT = 2048
    ut = fw.sbuf("ut", [128, TT + 16], F32)
    ic = fw.sbuf("ic", [128, TT], F32)
    acc = fw.sbuf("acc", [128, TT], F32)
    db = fw.sbuf("db", [128, TT], BF16)
    ps = [fw.psum("ps%d" % i, [128, 512], F32) for i in range(2)]
    ots = [fw.sbuf("ot%d" % i, [128, 512], F32) for i in range(2)]
    cnt = 0
    for (s0, Ls) in ((0, LQ), (LQ, CTX)):
        tt_ = min(TT, Ls)
        for t0 in range(0, Ls, tt_):
            lo = max(t0 - 8, 0)
            hi = min(t0 + tt_ + 8, Ls)
            if lo > t0 - 8:
                dve.op(lambda e: e.memset(ut[:, 0:8], 0.0), [ut], [])
            if hi < t0 + tt_ + 8:
                dve.op(lambda e: e.memset(ut[:, tt_ + 8:tt_ + 16], 0.0), [ut], [])
            sp.dma(ut, ut[:, lo - (t0 - 8):hi - (t0 - 8)], uT, uT.t[:, s0 + lo:s0 + hi])
            sp.dma(ic, ic[:, 0:tt_], invc, invc.t[:, s0 + t0:s0 + t0 + tt_])
            dve.op(lambda e: e.tensor_scalar(acc[:, 0:tt_], ut[:, 0:tt_], wst[:, 0:1], None, ALU.mult), [acc], [ut, wst])
            for k in range(1, 16):
                dve.op(lambda e: e.scalar_tensor_tensor(acc[:, 0:tt_], ut[:, k:k + tt_], wst[:, k:k + 1], acc[:, 0:tt_], ALU.mult, ALU.add), [acc], [ut, wst])
            dve.op(lambda e: e.tensor_tensor(acc[:, 0:tt_], acc[:, 0:tt_], ic[:, 0:tt_], ALU.mult), [acc], [ic])
            dve.op(lambda e: e.tensor_tensor(db[:, 0:tt_], acc[:, 0:tt_], ut[:, 8:8 + tt_], ALU.subtract), [db], [acc, ut])
            for c0 in range(0, tt_, 512):
                n = min(512, tt_ - c0)
                p = ps[cnt % 2]
                ot = ots[cnt % 2]
                cnt += 1
                pe.op(lambda e: e.matmul(p[:, 0:n], wlt[:], db[:, c0:c0 + n], start=True, stop=True), [p], [wlt, db])
                act.op(lambda e: e.activation(ot[:, 0:n], p[:, 0:n], AF.Copy, scale=lst[:, 0:1]), [ot], [p, lst])
                sp.dma(yT, yT.t[:, s0 + t0 + c0:s0 + t0 + c0 + n], ot, ot[:, 0:n])
    sp.wait_buf(yT)
    if own:
        fw.close()
    return fw.nc


def pool_consts(g, LQ):
    w = POOL_WINDOWS[g]
    lo = w // 2
    hi = w - 1 - lo
    sel = np.zeros(16, np.float32)
    for s in range(-lo, hi + 1):
        sel[s + 8] = 1.0
    outs = []
    for Ls in (LQ, CTX):
        t = np.arange(Ls)
        start = np.clip(t - lo, 0, Ls)
        end = np.clip(t + hi + 1, 0, Ls)
        outs.append((1.0 / (end - start).astype(np.float32)).astype(np.float32))
    ic = np.concatenate(outs)
    return np.tile(sel[None], (128, 1)), np.ascontiguousarray(np.tile(ic[None], (128, 1)))


EPS = 1e-6
CTX = 256
HY_EMB = 33


def sin_act(fw, out_buf, out_ap, p, n, fb, scr):
    act, dve = fw.act, fw.dve
    s2, s4 = scr
    P = 64
    act.op(lambda e: e.activation(s2[0:P, 0:n], p[0:P, 0:n], AF.Sin, bias=fb[0:P, 1:2], scale=fb[0:P, 0:1]), [s2], [p, fb])
    act.op(lambda e: e.activation(s4[0:P, 0:n], p[0:P, 0:n], AF.Sin, bias=fb[0:P, 3:4], scale=fb[0:P, 2:3]), [s4], [p, fb])
    dve.op(lambda e: e.tensor_tensor(s4[0:P, 0:n], s4[0:P, 0:n], s4[0:P, 0:n], ALU.mult), [s4], [s4])
    dve.op(lambda e: e.tensor_scalar(s4[0:P, 0:n], s4[0:P, 0:n], -2.0, 1.0, ALU.mult, ALU.add), [s4], [s4])
    dve.op(lambda e: e.scalar_tensor_tensor(out_ap, s2[0:P, 0:n], 2.0, s4[0:P, 0:n], ALU.mult, ALU.mult), [out_buf], [s2, s4])


def filter_gen(fw, Lf, zf, t01, wts, KF, nm, scr, scale):
    pe, act, dve, pool, sp = fw.pe, fw.act, fw.dve, fw.pool, fw.sp
    w1, w2, w3, fb1, fb2, ndelta = wts["w1"], wts["w2"], wts["w3"], wts["fb1"], wts["fb2"], wts["ndelta"]
    ps1, ps2, ps3, zt, h1, h2, s2, s4, tt, kt, ktb, sqt = scr
    nt = (Lf + 511) // 512
    part = fw.sbuf(nm + "_part", [128, 2 * nt], F32)
    dve.op(lambda e: e.memset(part[:], 0.0), [part], [])
    for di in range(2):
        for ti in range(nt):
            t0 = ti * 512
            n = min(512, Lf - t0)
            sp.dma(zt, zt[0:HY_EMB, 0:n], zf, zf.t[1 - di, :, t0:t0 + n])
            tsrc = bass.AP(tensor=t01.t.tensor, offset=(1 - di) * Lf + t0, ap=[[0, 128], [1, n]])
            sp.dma(tt, tt[:, 0:n], t01, tsrc)
            pe.op(lambda e: e.matmul(ps1[0:64, 0:n], w1[0:HY_EMB, :], zt[0:HY_EMB, 0:n], start=True, stop=True), [ps1], [w1, zt])
            sin_act(fw, h1, h1[0:64, 0:n], ps1, n, fb1, (s2, s4))
            pe.op(lambda e: e.matmul(ps2[0:64, 0:n], w2[0:64, :], h1[0:64, 0:n], start=True, stop=True), [ps2], [w2, h1])
            sin_act(fw, h2, h2[0:64, 0:n], ps2, n, fb2, (s2, s4))
            pe.op(lambda e: e.matmul(ps3[:, 0:n], w3[0:64, di, :], h2[0:64, 0:n], start=True, stop=True), [ps3], [w3, h2])
            act.op(lambda e: e.activation(tt[:, 0:n], tt[:, 0:n], AF.Exp, scale=ndelta[:, 0:1]), [tt], [tt, ndelta])
            dve.op(lambda e: e.tensor_tensor(kt[:, 0:n], ps3[:, 0:n], tt[:, 0:n], ALU.mult), [kt], [ps3, tt])
            if di == 1 and ti == 0:
                dve.op(lambda e: e.memset(kt[:, 0:1], 0.0), [kt], [])
            dve.op(lambda e: e.tensor_tensor(sqt[:, 0:n], kt[:, 0:n], kt[:, 0:n], ALU.mult), [sqt], [kt])
            dve.op(lambda e: e.reduce_sum(part[:, di * nt + ti:di * nt + ti + 1], sqt[:, 0:n], AX.X), [part], [sqt])
            act.op(lambda e: e.activation(ktb[:, 0:n], kt[:, 0:n], AF.Copy), [ktb], [kt])
            if di == 0:
                sp.dma(KF, KF.t[:, 1 + t0:1 + t0 + n], ktb, ktb[0:64, 0:n])
            elif ti == 0:
                sp.dma(KF, KF.t[:, Lf + 1:Lf + n], ktb, ktb[0:64, 1:n])
            else:
                sp.dma(KF, KF.t[:, Lf + t0:Lf + t0 + n], ktb, ktb[0:64, 0:n])
    dve.op(lambda e: e.reduce_sum(scale[:], part[:], AX.X), [scale], [part])
    dve.op(lambda e: e.tensor_scalar(scale[:], scale[:], EPS, None, ALU.add), [scale], [scale])
    act.op(lambda e: e.activation(scale[:], scale[:], AF.Sqrt), [scale], [scale])
    dve.op(lambda e: e.reciprocal(scale[:], scale[:]), [scale], [scale])
    return scale


def conv3(fw, dst, u, n, sm, part):
    dve = fw.dve
    c = 3 * part
    dve.op(lambda e: e.tensor_scalar(dst[:, 0:n], u[:, 1:n + 1], sm[:, c + 1:c + 2], sm[:, 9 + part:10 + part], ALU.mult, ALU.add), [dst], [u, sm])
    dve.op(lambda e: e.scalar_tensor_tensor(dst[:, 0:n], u[:, 0:n], sm[:, c:c + 1], dst[:, 0:n], ALU.mult, ALU.add), [dst], [u, sm])
    dve.op(lambda e: e.scalar_tensor_tensor(dst[:, 0:n], u[:, 2:n + 2], sm[:, c + 2:c + 3], dst[:, 0:n], ALU.mult, ALU.add), [dst], [u, sm])


def load_u_tile(fw, ut, uT, part, t0, n, Ls):
    sp, dve = fw.sp, fw.dve
    lo = max(t0 - 1, 0)
    hi = min(t0 + n + 1, Ls)
    if t0 == 0:
        dve.op(lambda e: e.memset(ut[:, 0:1], 0.0), [ut], [])
    if t0 + n == Ls:
        dve.op(lambda e: e.memset(ut[:, n + 1:n + 2], 0.0), [ut], [])
    sp.dma(ut, ut[:, lo - (t0 - 1):hi - (t0 - 1)], uT, uT.t[part, :, lo:hi])


def build_k2d(LQ=8192, fw=None, hook_pre=None, hook_step=None, hook_post=None):
    own = fw is None
    if own:
        fw = FW()
    NB = LQ // 128
    NBC = CTX // 128
    TT = min(1024, LQ)
    uT = fw.dram("uT", [3, 128, LQ], F32, "ExternalInput")
    ucT = fw.dram("ucT", [3, 128, CTX], F32, "ExternalInput")
    smd = fw.dram("sm", [128, 16], F32, "ExternalInput")
    w1d = fw.dram("w1", [HY_EMB, 64], F32, "ExternalInput")
    w2d = fw.dram("w2", [64, 64], F32, "ExternalInput")
    w3d = fw.dram("w3", [64, 2, 128], F32, "ExternalInput")
    fbd = fw.dram("fb", [64, 4], F32, "ExternalInput")
    zfL = fw.dram("zfL", [2, HY_EMB, LQ], F32, "ExternalInput")
    t01L = fw.dram("t01L", [2, LQ], F32, "ExternalInput")
    zfC = fw.dram("zfC", [2, HY_EMB, CTX], F32, "ExternalInput")
    t01C = fw.dram("t01C", [2, CTX], F32, "ExternalInput")
    oT = fw.dram("oT", [128, LQ], F32, "ExternalOutput")
    ocT = fw.dram("ocT", [128, CTX], F32, "ExternalOutput")
    KF = fw.dram("KF", [64, 2 * LQ], BF16, "Internal")
    KFC = fw.dram("KFC", [64, 2 * CTX], BF16, "Internal")
    if own:
        fw.engines()
    pe, act, dve, pool, sp = fw.pe, fw.act, fw.dve, fw.pool, fw.sp

    identb = fw.sbuf("identb", [128, 128], BF16)
    identf = fw.sbuf("identf", [128, 128], F32)
    idd = fw.dram("ident", [128, 128], F32, "ExternalInput")
    sp.dma(identf, identf[:], idd, idd[:])
    dve.op(lambda e: e.tensor_copy(identb[:], identf[:]), [identb], [identf])
    antif = fw.sbuf("antif", [128, 128], F32)
    add = fw.dram("anti", [128, 128], F32, "ExternalInput")
    sp.dma(antif, antif[:], add, add[:])
    sm = fw.sbuf("smt", [128, 16], F32)
    sp.dma(sm, sm[:], smd, smd[:])
    w1 = fw.sbuf("w1s", [HY_EMB, 64], F32)
    sp.dma(w1, w1[:], w1d, w1d[:])
    w2 = fw.sbuf("w2s", [64, 64], F32)
    sp.dma(w2, w2[:], w2d, w2d[:])
    w3 = fw.sbuf("w3s", [64, 2, 128], F32)
    sp.dma(w3, w3[:], w3d, w3d[:])
    fb = fw.sbuf("fbs", [64, 4], F32)
    sp.dma(fb, fb[:], fbd, fbd[:])
    fb1 = fw.sbuf("fb1", [64, 4], F32)
    fb2 = fw.sbuf("fb2", [64, 4], F32)
    for dst, bc in ((fb1, 1), (fb2, 2)):
        dve.op(lambda e: e.tensor_scalar(dst[:, 0:1], fb[:, 0:1], 0.5, None, ALU.mult), [dst], [fb])
        dve.op(lambda e: e.tensor_scalar(dst[:, 2:3], fb[:, 0:1], 0.25, None, ALU.mult), [dst], [fb])
        dve.op(lambda e: e.tensor_tensor(dst[:, 1:2], dst[:, 0:1], fb[:, bc:bc + 1], ALU.mult), [dst], [dst, fb])
        dve.op(lambda e: e.tensor_tensor(dst[:, 3:4], dst[:, 2:3], fb[:, bc:bc + 1], ALU.mult), [dst], [dst, fb])
    ndelta = fw.sbuf("ndelta", [128, 1], F32)
    dve.op(lambda e: e.tensor_copy(ndelta[:], sm[:, 13:14]), [ndelta], [sm])
    wts = dict(w1=w1, w2=w2, w3=w3, fb1=fb1, fb2=fb2, ndelta=ndelta)
    scaleL = fw.sbuf("fL_scale", [128, 1], F32)
    scaleC = fw.sbuf("fC_scale", [128, 1], F32)
    fw.push_scope()
    scr = (fw.psum("fg_ps1", [128, 512], F32), fw.psum("fg_ps2", [128, 512], F32), fw.psum("fg_ps3", [128, 512], F32),
           fw.sbuf("fg_zt", [64, 512], F32), fw.sbuf("fg_h1", [64, 512], F32), fw.sbuf("fg_h2", [64, 512], F32),
           fw.sbuf("fg_s2", [64, 512], F32), fw.sbuf("fg_s4", [64, 512], F32), fw.sbuf("fg_tt", [128, 512], F32),
           fw.sbuf("fg_kt", [128, 512], F32), fw.sbuf("fg_ktb", [128, 512], BF16), fw.sbuf("fg_sq", [128, 512], F32))
    filter_gen(fw, LQ, zfL, t01L, wts, KF, "fL", scr, scaleL)
    filter_gen(fw, CTX, zfC, t01C, wts, KFC, "fC", scr, scaleC)
    fw.pop_scope()

    Zt = fw.sbuf("Zt", [128, 64, NB, 2], BF16)
    Ztc = fw.sbuf("Ztc", [128, 64, NBC, 2], BF16)
    fw.push_scope()
    uts = [fw.sbuf("ut%d" % i, [128, TT + 2], F32) for i in range(3)]
    cv = [fw.sbuf("cv%d" % i, [128, TT], F32) for i in range(3)]
    zb = fw.sbuf("zb", [128, TT], BF16)
    pst = [fw.psum("pst%d" % i, [128, 512], BF16) for i in range(2)]

    def z_phase(src, Ls, Ztx, nblk):
        tt_ = min(TT, Ls)
        for t0 in range(0, Ls, tt_):
            for part in range(2):
                load_u_tile(fw, uts[part], src, part, t0, tt_, Ls)
                conv3(fw, cv[part], uts[part], tt_, sm, part)
            dve.op(lambda e: e.tensor_tensor(zb[:, 0:tt_], cv[0][:, 0:tt_], cv[1][:, 0:tt_], ALU.mult), [zb], [cv[0], cv[1]])
            for rb in range(tt_ // 128):
                r = t0 // 128 + rb
                p = pst[r % 2]
                pe.op(lambda e: e.transpose(p[:, 0:128], zb[:, rb * 128:(rb + 1) * 128], identb[:]), [p], [zb, identb])
                dst = Ztx[:, :, r, :].rearrange("p c b -> p b c")
                src_ap = p[:, 0:128].rearrange("p (b c) -> p b c", b=2)
                if r % 2 == 0:
                    dve.op(lambda e: e.tensor_copy(dst, src_ap), [Ztx], [p])
                else:
                    act.op(lambda e: e.activation(dst, src_ap, AF.Copy), [Ztx], [p])

    z_phase(uT, LQ, Zt, NB)
    z_phase(ucT, CTX, Ztc, NBC)
    fw.pop_scope()

    W = (2 * NB - 1) * 128
    X0 = (NB - 1) * 128
    tbs = [fw.sbuf("tb%d" % i, [128, W], BF16) for i in range(2)]
    tbc = fw.sbuf("tbc", [128, 16, 3 * 128], BF16)
    Y = fw.sbuf("Y", [128, NB, 2, 64], F32)
    Yc = fw.sbuf("Yc", [128, NBC, 2, 64], F32)
    psy = [fw.psum("psy%d" % i, [128, 512], F32) for i in range(2)]
    if hook_pre:
        hook_pre()
    kft = KF.t.tensor
    kfct = KFC.t.tensor
    for ch in range(64):
        tb = tbs[ch % 2]
        src = bass.AP(tensor=kft, offset=ch * 2 * LQ + 1, ap=[[1, 128], [1, W]])
        sp.dma(tb, tb[:], KF, src)
        p = psy[ch % 2]
        ds = [0] + [d for d in range(-(NB - 1), NB) if d != 0]
        for k, d in enumerate(ds):
            r0 = max(0, d)
            nb = NB - abs(d)
            pe.op(lambda e: e.matmul(p[:, r0 * 2:(r0 + nb) * 2], tb[:, X0 - 128 * d:X0 - 128 * d + 128],
                                     Zt[:, ch, r0 - d:r0 - d + nb, :].rearrange("p r b -> p (r b)"),
                                     start=(k == 0), stop=(k == len(ds) - 1), skip_group_check=True), [p], [tb, Zt])
        if ch % 2 == 0:
            dve.op(lambda e: e.tensor_copy(Y[:, :, :, ch], p[:, 0:NB * 2].rearrange('p (r b) -> p r b', b=2)), [Y], [p])
        else:
            act.op(lambda e: e.activation(Y[:, :, :, ch], p[:, 0:NB * 2].rearrange('p (r b) -> p r b', b=2), AF.Copy), [Y], [p])
        if hook_step:
            hook_step(ch)
    pc = psy[0]
    for g in range(4):
        src = bass.AP(tensor=kfct, offset=g * 16 * 2 * CTX + 1, ap=[[1, 128], [2 * CTX, 16], [1, 384]])
        sp.dma(tbc, tbc[:], KFC, src)
        for c16 in range(16):
            ch = g * 16 + c16
            o0 = ch * 4
            pe.op(lambda e: e.matmul(pc[:, o0:o0 + 4], tbc[:, c16, 128:256], Ztc[:, ch, :, :].rearrange("p r b -> p (r b)"),
                                     start=True, stop=False, skip_group_check=True), [pc], [tbc, Ztc])
            pe.op(lambda e: e.matmul(pc[:, o0 + 2:o0 + 4], tbc[:, c16, 0:128], Ztc[:, ch, 0, :],
                                     start=False, stop=False, skip_group_check=True), [pc], [tbc, Ztc])
            pe.op(lambda e: e.matmul(pc[:, o0:o0 + 2], tbc[:, c16, 256:384], Ztc[:, ch, 1, :],
                                     start=False, stop=True, skip_group_check=True), [pc], [tbc, Ztc])
    dve.op(lambda e: e.tensor_copy(Yc[:].rearrange("p r b c -> p c r b"), pc[:, 0:256].rearrange("p (c r b) -> p c r b", r=NBC, b=2)), [Yc], [pc])

    if hook_post:
        hook_post()
    fw.push_scope()
    uts = [fw.sbuf("o_ut%d" % i, [128, TT + 2], F32) for i in range(3)]
    cv = [fw.sbuf("o_cv%d" % i, [128, TT], F32) for i in range(3)]
    pso = [fw.psum("pso%d" % i, [128, 512], F32) for i in range(1)]
    ys = fw.sbuf("ys", [128, 512], F32)
    ots = [fw.sbuf("ot%d" % i, [128, 512], F32) for i in range(2)]

    def out_phase(src, Ls, Yx, scale, dstT):
        tt_ = min(TT, Ls)
        cnt = 0
        for t0 in range(0, Ls, tt_):
            for part in range(3):
                load_u_tile(fw, uts[part], src, part, t0, tt_, Ls)
                conv3(fw, cv[part], uts[part], tt_, sm, part)
            dve.op(lambda e: e.tensor_tensor(cv[0][:, 0:tt_], cv[0][:, 0:tt_], cv[1][:, 0:tt_], ALU.mult), [cv[0]], [cv[1]])
            for s0 in range(0, tt_, 512):
                ns = min(512, tt_ - s0)
                p = pso[0]
                for rb in range(ns // 128):
                    r = (t0 + s0) // 128 + rb
                    in_ap = Yx[:, r, :, :].rearrange("p b c -> p (b c)")
                    pe.op(lambda e: e.matmul(p[:, rb * 128:(rb + 1) * 128], in_ap, antif[:], start=True, stop=True), [p], [Yx, antif])
                act.op(lambda e: e.activation(ys[:, 0:ns], p[:, 0:ns], AF.Copy, scale=scale[:, 0:1]), [ys], [p, scale])
                dve.op(lambda e: e.scalar_tensor_tensor(ys[:, 0:ns], cv[0][:, s0:s0 + ns], sm[:, 12:13], ys[:, 0:ns], ALU.mult, ALU.add), [ys], [cv[0], sm])
                ot = ots[cnt % 2]
                cnt += 1
                dve.op(lambda e: e.tensor_tensor(ot[:, 0:ns], ys[:, 0:ns], cv[2][:, s0:s0 + ns], ALU.mult), [ot], [ys, cv[2]])
                sp.dma(dstT, dstT.t[:, t0 + s0:t0 + s0 + ns], ot, ot[:, 0:ns])

    out_phase(uT, LQ, Y, scaleL, oT)
    out_phase(ucT, CTX, Yc, scaleC, ocT)
    sp.wait_buf(oT)
    sp.wait_buf(ocT)
    fw.pop_scope()
    if own:
        fw.close()
    return fw.nc


def hy_feats(L):
    t01 = np.linspace(0.0, 1.0, L, dtype=np.float32)
    bands = (HY_EMB - 1) // 2
    w_ang = (2.0 * math.pi * np.arange(L, dtype=np.float32) / L).astype(np.float32)
    f = np.linspace(1e-4, bands - 1, bands, dtype=np.float32)
    ang = (f[None, :] * w_ang[:, None]).astype(np.float32)
    z = np.concatenate([t01[:, None], np.cos(ang), -np.sin(ang)], axis=-1).astype(np.float32)
    zf = np.stack([z.T, z[::-1].T])
    tt = np.stack([t01, t01[::-1]])
    return np.ascontiguousarray(zf), np.ascontiguousarray(tt)


def hy_ndelta(D_WIDTH=512):
    d = np.linspace(math.log(1e-2) / 0.3, math.log(1e-2) / 1.5, D_WIDTH, dtype=np.float32)
    return -np.abs(d)


def k2d_inputs(core, u_lat, u_ctx, conv_w, conv_b, w1, b1, w2, b2, w3, freq, bias, LQ):
    c0 = 64 * core
    DW = 512
    def pk(u, Ls):
        parts = []
        for part in range(3):
            cols = u[:, :, part * DW + c0: part * DW + c0 + 64]
            parts.append(cols.transpose(0, 2, 1).reshape(128, Ls))
        return np.ascontiguousarray(np.stack(parts))
    sm = np.zeros((128, 16), np.float32)
    for part in range(3):
        for tap in range(3):
            sm[:, part * 3 + tap] = np.tile(conv_w[tap, part * DW + c0: part * DW + c0 + 64], 2)
        sm[:, 9 + part] = np.tile(conv_b[part * DW + c0: part * DW + c0 + 64], 2)
    sm[:, 12] = np.tile(bias[c0:c0 + 64], 2)
    sm[:, 13] = np.tile(hy_ndelta()[c0:c0 + 64], 2)
    w3r = w3.reshape(64, 2, DW)[:, :, c0:c0 + 64]
    w3p = np.ascontiguousarray(np.concatenate([w3r, w3r], axis=2))
    fb = np.zeros((64, 4), np.float32)
    fb[:, 0] = freq; fb[:, 1] = b1; fb[:, 2] = b2
    zfL, t01L = hy_feats(LQ)
    zfC, t01C = hy_feats(CTX)
    return {"uT": pk(u_lat, LQ), "ucT": pk(u_ctx, CTX), "sm": sm, "w1": np.ascontiguousarray(w1), "w2": np.ascontiguousarray(w2),
            "w3": w3p, "fb": fb, "zfL": zfL, "t01L": t01L, "zfC": zfC, "t01C": t01C, "ident": np.eye(128, dtype=np.float32), "anti": np.ascontiguousarray(np.eye(128, dtype=np.float32)[::-1])}


def build_mix(LQ=8192):
    fw = FW()
    fw.engines()
    for pfx, body in (("a_", build_k2a), ("c_", build_k2c)):
        fw.pfx = pfx
        fw.push_scope()
        body(LQ, fw=fw)
        fw.pop_scope()
    st = {}

    def pre():
        fw.pfx = "b_"
        fw.push_scope()
        st["g"] = build_k2b(LQ, fw=fw, defer=True)
        fw.pfx = "d_"

    def step(ch):
        if ch % 3 == 2:
            next(st["g"], None)

    def post():
        for _ in st["g"]:
            pass
        fw.pop_scope()

    fw.pfx = "d_"
    fw.push_scope()
    build_k2d(LQ, fw=fw, hook_pre=pre, hook_step=step, hook_post=post)
    fw.pop_scope()
    fw.pfx = ""
    fw.close()
    return fw.nc


D = 2048
KC = 16
EPS = 1e-6
FH = 5632
HC = FH // 128


def build_k3(TL=2048, TC=64, with_k1=False):
    fw = FW()
    LW = TL + 2
    CW = TC + 2
    TW = LW + CW
    TO = TL + TC
    oT = fw.dram("oT", [KC, 128, TW], F32, "ExternalInput")
    xT = fw.dram("xT", [KC, 128, TW], F32, "ExternalInput")
    vec = fw.dram("vec", [128, 9, KC], F32, "ExternalInput")
    edge = fw.dram("edge", [128, 4], F32, "ExternalInput")
    w_out = fw.dram("w_out", [8, 128, KC, 256], F32, "ExternalInput")
    w_up = fw.dram("w_up", [2 * FH // 256, 128, KC, 256], F32, "ExternalInput")
    cwd = fw.dram("cw", [128, 4, 2 * HC], F32, "ExternalInput")
    w_dn = fw.dram("w_dn", [8, 128, HC, 256], F32, "ExternalInput")
    xo = fw.dram("xo", [KC, 128, TO], F32, "ExternalOutput")
    XN = fw.dram("XN", [KC, 128, TW], F32, "Internal")
    AT = fw.dram("AT", [HC, 128, TO], BF16, "Internal")
    fw.engines()
    pe, act, dve, pool, sp = fw.pe, fw.act, fw.dve, fw.pool, fw.sp
    ones = fw.sbuf("ones", [128, 128], F32)
    dve.op(lambda e: e.memset(ones[:], 1.0), [ones], [])
    vt = fw.sbuf("vt", [128, 9, KC], F32)
    sp.dma(vt, vt[:], vec, vec[:])
    eg = fw.sbuf("eg", [128, 4], F32)
    sp.dma(eg, eg[:], edge, edge[:])
    A_lat = fw.sbuf("A_lat", [128, KC], F32)
    A_ctx = fw.sbuf("A_ctx", [128, KC], F32)
    dve.op(lambda e: e.scalar_tensor_tensor(A_lat[:], vt[:, 3, :], 1.0, vt[:, 2, :], ALU.add, ALU.mult), [A_lat], [vt])
    dve.op(lambda e: e.scalar_tensor_tensor(A_ctx[:], vt[:, 5, :], 1.0, vt[:, 2, :], ALU.add, ALU.mult), [A_ctx], [vt])
    fw.push_scope()
    h2T = fw.sbuf("h2T", [128, KC, TW], BF16)

    fw.push_scope()
    wo = [fw.sbuf("wo%d" % i, [128, KC, 256], BF16) for i in range(8)]
    for i in range(8):
        pool.dma(wo[i], wo[i][:], w_out, w_out.t[i])
    ots = [fw.sbuf("a_ot%d" % i, [128, KC, 256], BF16) for i in range(2)]
    xts = [fw.sbuf("a_xt%d" % i, [128, KC, 256], F32) for i in range(2)]
    sq = fw.sbuf("a_sq", [128, KC, 256], F32)
    rs = fw.sbuf("a_rs", [128, 256], F32)
    psA = [fw.psum("a_ps%d" % i, [128, 512], F32) for i in range(4)]
    psn = fw.psum("a_psn", [128, 512], F32)
    tilesA = [(c0, 256, 0) for c0 in range(0, TL, 256)] + [(TL, 2, 0), (LW, CW, 1)]
    ov = oT.t.rearrange("k p t -> p k t")
    xv = xT.t.rearrange("k p t -> p k t")
    xnv = XN.t.rearrange("k p t -> p k t")
    cnt = 0
    for ti, (c0, n, kind) in enumerate(tilesA):
        ot, xt = ots[ti % 2], xts[ti % 2]
        g1c = 0 if kind == 0 else 1
        A2 = A_lat if kind == 0 else A_ctx
        shc = 4 if kind == 0 else 6
        pool.dma(ot, ot[:, :, 0:n], oT, ov[:, :, c0:c0 + n])
        sp.dma(xt, xt[:, :, 0:n], xT, xv[:, :, c0:c0 + n])
        for ci in range(KC):
            p = psA[cnt % 4]
            cnt += 1
            w = wo[ci // 2]
            h0 = (ci % 2) * 128
            for k in range(KC):
                pe.op(lambda e: e.matmul(p[:, 0:n], w[:, k, h0:h0 + 128], ot[:, k, 0:n], start=(k == 0), stop=(k == KC - 1)), [p], [w, ot])
            dve.op(lambda e: e.scalar_tensor_tensor(xt[:, ci, 0:n], p[:, 0:n], vt[:, g1c, ci:ci + 1], xt[:, ci, 0:n], ALU.mult, ALU.add), [xt], [p, vt])
        sp.dma(XN, xnv[:, :, c0:c0 + n], xt, xt[:, :, 0:n])
        act.op(lambda e: e.activation(sq[:, :, 0:n], xt[:, :, 0:n], AF.Square), [sq], [xt])
        for k in range(KC):
            pe.op(lambda e: e.matmul(psn[:, 0:n], ones[:], sq[:, k, 0:n], start=(k == 0), stop=(k == KC - 1)), [psn], [ones, sq])
        dve.op(lambda e: e.tensor_scalar(rs[:, 0:n], psn[:, 0:n], 1.0 / D, EPS, ALU.mult, ALU.add), [rs], [psn])
        act.op(lambda e: e.activation(rs[:, 0:n], rs[:, 0:n], AF.Sqrt), [rs], [rs])
        dve.op(lambda e: e.reciprocal(rs[:, 0:n], rs[:, 0:n]), [rs], [rs])
        for k in range(KC):
            dve.op(lambda e: e.tensor_tensor(sq[:, k, 0:n], xt[:, k, 0:n], rs[:, 0:n], ALU.mult), [sq], [xt, rs])
        for k in range(KC):
            act.op(lambda e: e.activation(h2T[:, k, c0:c0 + n], sq[:, k, 0:n], AF.Identity, bias=vt[:, shc, k:k + 1], scale=A2[:, k:k + 1]),
                   [h2T], [sq, A2, vt])
    fw.pop_scope()

    fw.push_scope()
    cw = fw.sbuf("cws", [128, 4, 2 * HC], F32)
    sp.dma(cw, cw[:], cwd, cwd[:])
    wgs = [fw.sbuf("b_wg%d" % i, [128, KC, 256], BF16) for i in range(2)]
    wus = [fw.sbuf("b_wu%d" % i, [128, KC, 256], BF16) for i in range(2)]
    psB = [fw.psum("b_ps%d" % i, [128, 512], F32) for i in range(4)]
    ug = [fw.sbuf("b_ug%d" % i, [128, 512], F32) for i in range(2)]
    uu = [fw.sbuf("b_uu%d" % i, [128, 512], F32) for i in range(2)]
    cg = [fw.sbuf("b_cg%d" % i, [128, 512], F32) for i in range(2)]
    cu = [fw.sbuf("b_cu%d" % i, [128, 512], F32) for i in range(2)]
    ab = [fw.sbuf("b_ab%d" % i, [128, 512], BF16) for i in range(2)]
    tilesB = []
    s = 0
    while s + 2 < LW:
        m = min(512, LW - s)
        tilesB.append((s, m, s, 0 if s == 0 else None, 1 if s + m == LW else None))
        s += m - 2
    tilesB.append((LW, CW, TL, 2, 3))

    def conv(dst, u, m, hc):
        dve.op(lambda e: e.tensor_scalar(dst[:, 0:m - 2], u[:, 1:m - 1], cw[:, 1, hc:hc + 1], cw[:, 3, hc:hc + 1], ALU.mult, ALU.add), [dst], [u, cw])
        dve.op(lambda e: e.scalar_tensor_tensor(dst[:, 0:m - 2], u[:, 0:m - 2], cw[:, 0, hc:hc + 1], dst[:, 0:m - 2], ALU.mult, ALU.add), [dst], [u, cw])
        dve.op(lambda e: e.scalar_tensor_tensor(dst[:, 0:m - 2], u[:, 2:m], cw[:, 2, hc:hc + 1], dst[:, 0:m - 2], ALU.mult, ALU.add), [dst], [u, cw])

    cnt = 0
    for j in range(FH // 256):
        wg, wu = wgs[j % 2], wus[j % 2]
        pool.dma(wg, wg[:], w_up, w_up.t[j])
        pool.dma(wu, wu[:], w_up, w_up.t[FH // 256 + j])
        for (s, m, o0, eL, eR) in tilesB:
            for half in range(2):
                hc = 2 * j + half
                h0 = half * 128
                i2 = cnt % 2
                cnt += 1
                pg, pu = psB[(2 * cnt) % 4], psB[(2 * cnt + 1) % 4]
                for k in range(KC):
                    pe.op(lambda e: e.matmul(pg[:, 0:m], wg[:, k, h0:h0 + 128], h2T[:, k, s:s + m], start=(k == 0), stop=(k == KC - 1)), [pg], [wg, h2T])
                for k in range(KC):
                    pe.op(lambda e: e.matmul(pu[:, 0:m], wu[:, k, h0:h0 + 128], h2T[:, k, s:s + m], start=(k == 0), stop=(k == KC - 1)), [pu], [wu, h2T])
                act.op(lambda e: e.activation(ug[i2][:, 0:m], pg[:, 0:m], AF.Copy), [ug[i2]], [pg])
                act.op(lambda e: e.activation(uu[i2][:, 0:m], pu[:, 0:m], AF.Copy), [uu[i2]], [pu])
                for ubuf in (ug[i2], uu[i2]):
                    if eL is not None:
                        dve.op(lambda e: e.tensor_scalar(ubuf[:, 0:1], ubuf[:, 0:1], eg[:, eL:eL + 1], None, ALU.mult), [ubuf], [ubuf, eg])
                    if eR is not None:
                        dve.op(lambda e: e.tensor_scalar(ubuf[:, m - 1:m], ubuf[:, m - 1:m], eg[:, eR:eR + 1], None, ALU.mult), [ubuf], [ubuf, eg])
                conv(cg[i2], ug[i2], m, hc)
                conv(cu[i2], uu[i2], m, HC + hc)
                act.op(lambda e: e.activation(cg[i2][:, 0:m - 2], cg[i2][:, 0:m - 2], AF.Silu), [cg[i2]], [cg[i2]])
                dve.op(lambda e: e.tensor_tensor(ab[i2][:, 0:m - 2], cg[i2][:, 0:m - 2], cu[i2][:, 0:m - 2], ALU.mult), [ab[i2]], [cg[i2], cu[i2]])
                sp.dma(AT, AT.t[hc, :, o0:o0 + m - 2], ab[i2], ab[i2][:, 0:m - 2])
    fw.pop_scope()
    fw.pop_scope()

    fw.push_scope()
    HALF = TO // 3
    at = fw.sbuf("c_at", [128, HC, HALF], BF16)
    wds = [fw.sbuf("c_wd%d" % i, [128, HC, 256], BF16) for i in range(2)]
    psC = [fw.psum("c_ps%d" % i, [128, 512], F32) for i in range(4)]
    xns = [fw.sbuf("c_xn%d" % i, [128, 512], F32) for i in range(3)]
    outs = [fw.sbuf("c_o%d" % i, [128, 512], F32) for i in range(3)]
    atv = AT.t.rearrange("k p t -> p k t")
    cnt = 0
    wcnt = 0
    for hf in range(3):
        h0, h1 = hf * HALF, (hf + 1) * HALF
        sp.dma(at, at[:], AT, atv[:, :, h0:h1])
        subs = []
        o = h0
        while o < h1:
            lim = TL if o < TL else TO
            n = min(512, min(h1, lim) - o)
            subs.append((o, n))
            o += n
        for ct in range(8):
            wd = wds[wcnt % 2]
            wcnt += 1
            pool.dma(wd, wd[:], w_dn, w_dn.t[ct])
            for (o0, n) in subs:
                kind = 0 if o0 < TL else 1
                xcol = o0 + 1 if kind == 0 else o0 + 3
                for h2 in range(2):
                    ci = 2 * ct + h2
                    p = psC[cnt % 4]
                    xn = xns[cnt % 3]
                    ob = outs[cnt % 3]
                    cnt += 1
                    sp.dma(xn, xn[:, 0:n], XN, XN.t[ci, :, xcol:xcol + n])
                    for k in range(HC):
                        pe.op(lambda e: e.matmul(p[:, 0:n], wd[:, k, h2 * 128:h2 * 128 + 128], at[:, k, o0 - h0:o0 - h0 + n],
                                                 start=(k == 0), stop=(k == HC - 1)), [p], [wd, at])
                    dve.op(lambda e: e.scalar_tensor_tensor(ob[:, 0:n], p[:, 0:n], vt[:, 7 + kind, ci:ci + 1], xn[:, 0:n], ALU.mult, ALU.add), [ob], [p, vt, xn])
                    sp.dma(xo, xo.t[ci, :, o0:o0 + n], ob, ob[:, 0:n])
    fw.pop_scope()
    sp.wait_buf(xo)
    if with_k1:
        fw.pfx = "k1_"
        fw.push_scope()
        build_k1(T_lat=TL, T_ctx=TC, fw=fw, xT=xo)
        fw.pop_scope()
        fw.pfx = ""
    fw.close()
    return fw.nc


def tile_w(w):
    K, N = w.shape
    return np.ascontiguousarray(w.reshape(K // 128, 128, N // 256, 256).transpose(2, 1, 0, 3))


def vec_pk(v):
    return np.ascontiguousarray(v.reshape(-1, 128).T)


OFF_QA, OFF_KA, OFF_VA, OFF_QB, OFF_KB, OFF_VB, OFF_POOL, OFF_HY = 0, 512, 1024, 1536, 2048, 2176, 2304, 2816
SEQ = 8192
NCTX = 256
_CORES = list(range(8))
_PROGS = {}
_N = {"launches": 0}


def _prog(name, builder):
    if name not in _PROGS:
        _PROGS[name] = builder()
    return _PROGS[name]


def _run(name, builder, ins):
    nc = _prog(name, builder)
    res = run_bass_kernel_spmd(nc, ins, core_ids=_CORES)
    return res.results


def _build_k3k1():
    return build_k3(with_k1=True)


def _k1_x(XT_lat, XT_ctx, k):
    b, q = divmod(k, 4)
    xT = np.concatenate([XT_lat[b][:, q * 2048:(q + 1) * 2048], XT_ctx[b][:, q * 64:(q + 1) * 64]], axis=1)
    return np.ascontiguousarray(xT.reshape(16, 128, 2112))


def _k1_in(l, k, mod, norm1_g, wt):
    b = k // 4
    m = mod[l].reshape(3, 6, -1)
    vec = np.stack([vec_pk(norm1_g[l]), vec_pk(m[b, 1]), vec_pk(m[b, 0]), vec_pk(m[2, 1]), vec_pk(m[2, 0])], axis=1)
    return {"vec": np.ascontiguousarray(vec), "w": wt}


def _seg(aT, lo, hi):
    F, S = aT.shape
    out = np.zeros((F, hi - lo + 2), np.float32)
    l2, h2 = max(lo - 1, 0), min(hi + 1, S)
    out[:, l2 - (lo - 1):h2 - (lo - 1)] = aT[:, l2:h2]
    return out


def kernel(x, c, ctx, c_ctx, w_mod, b_mod, norm1_g, norm2_g, w_in, w_out, qk_gain, diff_lam, diff_subln,
           win_sink, pool_w, pool_scale, hy_conv_w, hy_conv_b, hy_w1, hy_b1, hy_w2, hy_b2, hy_w3, hy_freq,
           hy_bias, ffn_w_in, ffn_conv_w, ffn_conv_b, ffn_w_out):
    f32 = lambda a: np.ascontiguousarray(np.asarray(a), dtype=np.float32)
    (x, c, ctx, c_ctx, w_mod, b_mod, norm1_g, norm2_g, w_in, w_out, qk_gain, diff_lam, diff_subln, win_sink, pool_w,
     pool_scale, hy_conv_w, hy_conv_b, hy_w1, hy_b1, hy_w2, hy_b2, hy_w3, hy_freq, hy_bias, ffn_w_in, ffn_conv_w,
     ffn_conv_b, ffn_w_out) = [f32(a) for a in (x, c, ctx, c_ctx, w_mod, b_mod, norm1_g, norm2_g, w_in, w_out, qk_gain,
                                                diff_lam, diff_subln, win_sink, pool_w, pool_scale, hy_conv_w, hy_conv_b,
                                                hy_w1, hy_b1, hy_w2, hy_b2, hy_w3, hy_freq, hy_bias, ffn_w_in, ffn_conv_w,
                                                ffn_conv_b, ffn_w_out)]
    B, L, Dm = x.shape
    NL = w_mod.shape[0]
    T = L + NCTX
    r = _run("k0", build_k0, [k0_inputs(k, c, c_ctx, w_mod, b_mod) for k in _CORES])
    mod = np.concatenate([r[k]["mod"] for k in _CORES], axis=2)
    XT_lat = [np.ascontiguousarray(x[b].T) for b in range(B)]
    XT_ctx = [np.ascontiguousarray(ctx[b].T) for b in range(B)]
    CSt = rope_tables(L, NCTX, 2)
    mk = band_masks()
    zfL, t01L = hy_feats(L)
    zfC, t01C = hy_feats(NCTX)
    ident = np.eye(128, dtype=np.float32)
    anti = np.ascontiguousarray(ident[::-1])
    ndel = hy_ndelta()
    for l in range(NL):
        m = mod[l].reshape(3, 6, Dm)
        wt_in = tile_w(w_in[l]) if l == 0 else None
        if l == 0:
            r = _run("k1", build_k1, [dict(xT=_k1_x(XT_lat, XT_ctx, k), **_k1_in(l, k, mod, norm1_g, wt_in)) for k in _CORES])
            pts = [r[k]["pT"].reshape(4352, 2112) for k in _CORES]
            del r
        PT_lat = [np.concatenate([pts[b * 4 + q][:, :2048] for q in range(4)], axis=1) for b in range(B)]
        PT_ctx = [np.concatenate([pts[b * 4 + q][:, 2048:] for q in range(4)], axis=1) for b in range(B)]
        del pts
        OT_lat = [np.zeros((Dm, L), np.float32) for _ in range(B)]
        OT_ctx = [np.zeros((Dm, NCTX), np.float32) for _ in range(B)]
        lambda_init = 0.8 - 0.6 * math.exp(-0.3 * l)
        g0 = np.tile(qk_gain[l, 0], 2)
        g1 = np.tile(qk_gain[l, 1], 2)
        ins_a = []
        for k in _CORES:
            b, h = divmod(k, 4)
            rq = slice(OFF_QA + h * 128, OFF_QA + (h + 1) * 128)
            rk = slice(OFF_KA + h * 128, OFF_KA + (h + 1) * 128)
            rv = slice(OFF_VA + h * 128, OFF_VA + (h + 1) * 128)
            q_full = np.ascontiguousarray(np.concatenate([PT_lat[b][rq], PT_ctx[b][rq]], axis=1))
            k_full = np.ascontiguousarray(np.concatenate([PT_ctx[b][rk], PT_lat[b][rk]], axis=1))
            v_full = np.concatenate([PT_ctx[b][rv], PT_lat[b][rv]], axis=1).T
            sm = np.zeros((128, 8), np.float32)
            sm[:, 0] = g0
            sm[:, 1] = swap_halves(g0[:, None])[:, 0]
            sm[:, 2] = g1
            sm[:, 3] = swap_halves(g1[:, None])[:, 0]
            sm[:, 4] = diff_subln[l]
            sm[:, 5] = lambda_init
            sm[:, 6] = 1.0 - lambda_init
            ins_a.append({"qT": q_full, "qsT": swap_halves(q_full), "kT": k_full, "ksT": swap_halves(k_full),
                        "v": np.ascontiguousarray(v_full.reshape(T // 128, 128, 128).transpose(1, 0, 2)),
                        "CS": CSt, "sm": sm, "lamp": np.ascontiguousarray(diff_lam[l].reshape(1, 256))})
        g2 = np.tile(qk_gain[l, 2], 2)
        g3 = np.tile(qk_gain[l, 3], 2)
        ins_b = []
        for k in _CORES:
            b = k // 4
            h0 = 2 * (k % 4)
            kvh = (k % 4) // 2
            rq = slice(OFF_QB + h0 * 64, OFF_QB + (h0 + 2) * 64)
            rk = slice(OFF_KB + kvh * 64, OFF_KB + (kvh + 1) * 64)
            rv = slice(OFF_VB + kvh * 64, OFF_VB + (kvh + 1) * 64)
            q_full = np.ascontiguousarray(np.concatenate([PT_lat[b][rq], PT_ctx[b][rq]], axis=1))
            k1 = np.concatenate([PT_ctx[b][rk], PT_lat[b][rk]], axis=1)
            k_full = np.ascontiguousarray(np.concatenate([k1, k1], axis=0))
            v_full = np.concatenate([PT_ctx[b][rv], PT_lat[b][rv]], axis=1).T
            sm = np.zeros((128, 8), np.float32)
            sm[:, 0] = g2
            sm[:, 1] = swap_halves(g2[:, None])[:, 0]
            sm[:, 2] = g3
            sm[:, 3] = swap_halves(g3[:, None])[:, 0]
            sm[:, 4] = win_sink[l, h0]
            sm[:, 5] = win_sink[l, h0 + 1]
            ins_b.append({"qT": q_full, "qsT": swap_halves(q_full), "kT": k_full, "ksT": swap_halves(k_full),
                        "v": np.ascontiguousarray(v_full.reshape(T // 128, 128, 64).transpose(1, 0, 2)),
                        "CS": CSt, "sm": sm, "masks": mk})
        ins_c = []
        for k in _CORES:
            b, g = divmod(k, 4)
            ru = slice(OFF_POOL + g * 128, OFF_POOL + (g + 1) * 128)
            sel, ic = pool_consts(g, L)
            ins_c.append({"uT": np.ascontiguousarray(np.concatenate([PT_lat[b][ru], PT_ctx[b][ru]], axis=1)), "wsel": sel, "invc": ic,
                        "wl": np.ascontiguousarray(pool_w[l, g]), "ls": np.ascontiguousarray(pool_scale[l, g * 128:(g + 1) * 128, None])})
        ins_d = []
        for k in _CORES:
            c0 = 64 * k
            uL, uC = [], []
            smh = np.zeros((128, 16), np.float32)
            for part in range(3):
                rr = slice(OFF_HY + part * 512 + c0, OFF_HY + part * 512 + c0 + 64)
                uL.append(np.concatenate([PT_lat[0][rr], PT_lat[1][rr]], axis=0))
                uC.append(np.concatenate([PT_ctx[0][rr], PT_ctx[1][rr]], axis=0))
                for tap in range(3):
                    smh[:, part * 3 + tap] = np.tile(hy_conv_w[l, tap, part * 512 + c0: part * 512 + c0 + 64], 2)
                smh[:, 9 + part] = np.tile(hy_conv_b[l, part * 512 + c0: part * 512 + c0 + 64], 2)
            smh[:, 12] = np.tile(hy_bias[l, c0:c0 + 64], 2)
            smh[:, 13] = np.tile(ndel[c0:c0 + 64], 2)
            w3r = hy_w3[l].reshape(64, 2, 512)[:, :, c0:c0 + 64]
            fb = np.zeros((64, 4), np.float32)
            fb[:, 0] = hy_freq[l]
            fb[:, 1] = hy_b1[l]
            fb[:, 2] = hy_b2[l]
            ins_d.append({"uT": np.ascontiguousarray(np.stack(uL)), "ucT": np.ascontiguousarray(np.stack(uC)), "sm": smh,
                        "w1": np.ascontiguousarray(hy_w1[l]), "w2": np.ascontiguousarray(hy_w2[l]),
                        "w3": np.ascontiguousarray(np.concatenate([w3r, w3r], axis=2)), "fb": fb,
                        "zfL": zfL, "t01L": t01L, "zfC": zfC, "t01C": t01C, "ident": ident, "anti": anti})
        ins = []
        for k in _CORES:
            dct = {}
            for pfx, lst in (("a_", ins_a), ("b_", ins_b), ("c_", ins_c), ("d_", ins_d)):
                for kk, vv in lst[k].items():
                    dct[pfx + kk] = vv
            ins.append(dct)
        del ins_a, ins_b, ins_c, ins_d
        r = _run("mix", build_mix, ins)
        for k in _CORES:
            b, h = divmod(k, 4)
            o = r[k]["a_oT"]
            OT_lat[b][h * 128:(h + 1) * 128] = o[:, :L]
            OT_ctx[b][h * 128:(h + 1) * 128] = o[:, L:]
            h0 = 2 * (k % 4)
            o = r[k]["b_oT"].reshape(128, T)
            OT_lat[b][512 + h0 * 64:512 + (h0 + 2) * 64] = o[:, :L]
            OT_ctx[b][512 + h0 * 64:512 + (h0 + 2) * 64] = o[:, L:]
            g = k % 4
            o = r[k]["c_yT"]
            OT_lat[b][1024 + g * 128:1024 + (g + 1) * 128] = o[:, :L]
            OT_ctx[b][1024 + g * 128:1024 + (g + 1) * 128] = o[:, L:]
            c0 = 64 * k
            o = r[k]["d_oT"]
            oc = r[k]["d_ocT"]
            for bb in range(B):
                OT_lat[bb][1536 + c0:1536 + c0 + 64] = o[bb * 64:(bb + 1) * 64]
                OT_ctx[bb][1536 + c0:1536 + c0 + 64] = oc[bb * 64:(bb + 1) * 64]
        del r, ins
        del PT_lat, PT_ctx
        wo_t = tile_w(w_out[l])
        wt_next = tile_w(w_in[l + 1]) if l + 1 < NL else None
        wu_t = tile_w(ffn_w_in[l])
        wd_t = tile_w(ffn_w_out[l])
        cwp = np.ascontiguousarray(np.stack([vec_pk(ffn_conv_w[l, 0]), vec_pk(ffn_conv_w[l, 1]), vec_pk(ffn_conv_w[l, 2]),
                                             vec_pk(ffn_conv_b[l])], axis=1))
        ins = []
        for k in _CORES:
            b, q = divmod(k, 4)
            ocol = np.concatenate([_seg(OT_lat[b], q * 2048, (q + 1) * 2048), _seg(OT_ctx[b], q * 64, (q + 1) * 64)], axis=1)
            xcol = np.concatenate([_seg(XT_lat[b], q * 2048, (q + 1) * 2048), _seg(XT_ctx[b], q * 64, (q + 1) * 64)], axis=1)
            TW = ocol.shape[1]
            vec = np.stack([vec_pk(m[b, 2]), vec_pk(m[2, 2]), vec_pk(norm2_g[l]), vec_pk(m[b, 4]), vec_pk(m[b, 3]),
                            vec_pk(m[2, 4]), vec_pk(m[2, 3]), vec_pk(m[b, 5]), vec_pk(m[2, 5])], axis=1)
            eg = np.ones((128, 4), np.float32)
            if q == 0:
                eg[:, 0] = 0
                eg[:, 2] = 0
            if q == 3:
                eg[:, 1] = 0
                eg[:, 3] = 0
            dct = {"oT": np.ascontiguousarray(ocol.reshape(16, 128, TW)), "xT": np.ascontiguousarray(xcol.reshape(16, 128, TW)),
                   "vec": np.ascontiguousarray(vec), "edge": eg, "w_out": wo_t, "w_up": wu_t, "cw": cwp, "w_dn": wd_t}
            if l + 1 < NL:
                for kk, vv in _k1_in(l + 1, k, mod, norm1_g, wt_next).items():
                    dct["k1_" + kk] = vv
            ins.append(dct)
        if l + 1 < NL:
            r = _run("k3k1", _build_k3k1, ins)
            pts = [r[k]["k1_pT"].reshape(4352, 2112) for k in _CORES]
        else:
            r = _run("k3", build_k3, ins)
        for k in _CORES:
            b, q = divmod(k, 4)
            o = r[k]["xo"].reshape(Dm, 2112)
            XT_lat[b][:, q * 2048:(q + 1) * 2048] = o[:, :2048]
            XT_ctx[b][:, q * 64:(q + 1) * 64] = o[:, 2048:]
        del r, ins
    return np.ascontiguousarray(np.stack([XT_lat[b].T for b in range(B)])).astype(np.float32)
```

```python
import math
import numpy as np
import time
from contextlib import ExitStack
import concourse.bass as bass
import concourse.mybir as mybir
from concourse.bass_utils import run_bass_kernel_spmd

F32 = mybir.dt.float32
BF16 = mybir.dt.bfloat16
AF = mybir.ActivationFunctionType
ALU = mybir.AluOpType
AX = mybir.AxisListType


class Buf:
    __slots__ = ("name", "t", "w", "r", "sem", "dcount")

    def __init__(self, name, t):
        self.name = name
        self.t = t
        self.w = {}
        self.r = {}
        self.sem = None
        self.dcount = 0

    def __getitem__(self, idx):
        return self.t[idx]


class Eng:
    def __init__(self, fw, name, eng, sem, kind):
        self.fw = fw
        self.name = name
        self.eng = eng
        self.sem = sem
        self.kind = kind
        self.count = 0
        self.seen = {}

    def _wait(self, deps):
        for key, (sem, val) in deps.items():
            if self.seen.get(key, 0) >= val:
                continue
            if sem is self.sem and self.kind == 'pe':
                continue
            self.eng.wait_ge(sem, val)
            self.seen[key] = val

    def _deps(self, outs, ins):
        deps = {}
        for b in ins:
            for k, (s, v) in b.w.items():
                if deps.get(k, (None, 0))[1] < v:
                    deps[k] = (s, v)
        for b in outs:
            for d in (b.w, b.r):
                for k, (s, v) in d.items():
                    if deps.get(k, (None, 0))[1] < v:
                        deps[k] = (s, v)
        return deps

    def op(self, inst_fn, outs, ins):
        self._wait(self._deps(outs, ins))
        inst = inst_fn(self.eng)
        self.count += 1
        inst.then_inc(self.sem, 1)
        key = id(self.sem)
        tok = (self.sem, self.count)
        for b in ins:
            b.r[key] = tok
        for b in outs:
            b.w = {key: tok}
            b.r = {}
        return tok

    def dma(self, out_buf, out_ap, in_buf, in_ap, **kw):
        self._wait(self._deps([out_buf], [in_buf]))
        if out_buf.sem is None:
            out_buf.sem = self.fw.new_sem("d_" + out_buf.name)
        inst = self.eng.dma_start(out=out_ap, in_=in_ap, **kw)
        out_buf.dcount += 16
        inst.then_inc(out_buf.sem, 16)
        key = id(out_buf.sem)
        tok = (out_buf.sem, out_buf.dcount)
        in_buf.r[key] = tok
        out_buf.w = {key: tok}
        out_buf.r = {}
        return tok

    def wait_buf(self, b):
        self._wait(dict(b.w))


class FW:
    def __init__(self, name="k"):
        self.nc = bass.Bass("TRN2", target_bir_lowering=False)
        self.es = ExitStack()
        self.nsem = 0
        self.block = None
        self.all_bufs = []
        self.scopes = []
        self.pfx = ""

    def new_sem(self, name):
        self.nsem += 1
        return self.es.enter_context(self.nc.semaphore(name + "_%d" % self.nsem))

    def dram(self, name, shape, dtype, kind):
        name = self.pfx + name
        t = self.nc.dram_tensor(name, list(shape), dtype, kind=kind)
        b = Buf(name, t.ap())
        self.all_bufs.append(b)
        return b

    def sbuf(self, name, shape, dtype):
        name = self.pfx + name
        t = (self.scopes[-1] if self.scopes else self.es).enter_context(self.nc.sbuf_tensor(name, list(shape), dtype))
        b = Buf(name, t)
        self.all_bufs.append(b)
        return b

    def psum(self, name, shape, dtype=F32):
        name = self.pfx + name
        t = (self.scopes[-1] if self.scopes else self.es).enter_context(self.nc.psum_tensor(name, list(shape), dtype))
        b = Buf(name, t)
        self.all_bufs.append(b)
        return b

    def engines(self):
        nc = self.nc
        self.pe = Eng(self, "pe", nc.tensor, self.new_sem("pe"), 'pe')
        self.act = Eng(self, "act", nc.scalar, self.new_sem("act"), 'act')
        self.dve = Eng(self, "dve", nc.vector, self.new_sem("dve"), 'dve')
        self.pool = Eng(self, "pool", nc.gpsimd, self.new_sem("pool"), 'pool')
        self.sp = Eng(self, "sp", nc.sync, self.new_sem("sp"), 'sp')
        return self.pe, self.act, self.dve, self.pool, self.sp

    def push_scope(self):
        self.scopes.append(ExitStack())

    def pop_scope(self):
        fw_barrier(self)
        self.scopes.pop().close()

    def close(self):
        self.es.close()


def sub_bufs(parent, aps, prefix):
    return [Buf("%s%d" % (prefix, i), ap) for i, ap in enumerate(aps)]


def fw_barrier(fw, bufs=()):
    engs = [fw.pe, fw.act, fw.dve, fw.pool, fw.sp]
    toks = {}
    for e in engs:
        if e.count > 0:
            toks[id(e.sem)] = (e.sem, e.count)
    for b in fw.all_bufs:
        if b.sem is not None and b.dcount > 0:
            toks[id(b.sem)] = (b.sem, b.dcount)
    for e in engs:
        e._wait(dict(toks))


D = 2048
KC = D // 128
EPS = 1e-6


def load_consts(fw, sp):
    ones = fw.sbuf("ones", [128, 128], F32)
    fw.dve.op(lambda e: e.memset(ones[:], 1.0), [ones], [])
    return ones


def norm_mod_phase(fw, xT, hT, tiles, vecs, ones, nm):
    pe, act, dve, pool, sp = fw.pe, fw.act, fw.dve, fw.pool, fw.sp
    xts = [fw.sbuf("%s_xt%d" % (nm, i), [128, KC, 512], F32) for i in range(2)]
    sqs = [fw.sbuf("%s_sq%d" % (nm, i), [128, KC, 512], F32) for i in range(1)]
    rstd = [fw.sbuf("%s_rstd%d" % (nm, i), [128, 512], F32) for i in range(2)]
    ps = [fw.psum("%s_ps%d" % (nm, i), [128, 512], F32) for i in range(2)]
    xv = xT.t.rearrange("k p t -> p k t")
    for i, (t0, n, A, sh) in enumerate(tiles):
        xt, sq, rs, p = xts[i % 2], sqs[0], rstd[i % 2], ps[i % 2]
        sp.dma(xt, xt[:, :, 0:n], xT, xv[:, :, t0:t0 + n])
        act.op(lambda e: e.activation(sq[:, :, 0:n], xt[:, :, 0:n], AF.Square), [sq], [xt])
        for kc in range(KC):
            pe.op(lambda e: e.matmul(p[:, 0:n], ones[:], sq[:, kc, 0:n], start=(kc == 0), stop=(kc == KC - 1)), [p], [ones, sq])
        dve.op(lambda e: e.tensor_scalar(rs[:, 0:n], p[:, 0:n], 1.0 / D, EPS, ALU.mult, ALU.add), [rs], [p])
        act.op(lambda e: e.activation(rs[:, 0:n], rs[:, 0:n], AF.Sqrt), [rs], [rs])
        dve.op(lambda e: e.reciprocal(rs[:, 0:n], rs[:, 0:n]), [rs], [rs])
        for kc in range(KC):
            dve.op(lambda e: e.tensor_tensor(sq[:, kc, 0:n], xt[:, kc, 0:n], rs[:, 0:n], ALU.mult), [sq], [xt, rs])
        for kc in range(KC):
            act.op(lambda e: e.activation(hT[:, kc, t0:t0 + n], sq[:, kc, 0:n], AF.Identity,
                                          bias=sh[:, kc:kc + 1], scale=A[:, kc:kc + 1]), [hT], [sq, A, sh])


def linear_phase(fw, hT, KCn, w, ncoltiles, tiles, epilogue, nm, nbuf=3):
    pe, pool = fw.pe, fw.pool
    wts = [fw.sbuf("%s_w%d" % (nm, i), [128, KCn, 256], BF16) for i in range(nbuf)]
    ps = [fw.psum("%s_lp%d" % (nm, i), [128, 512], F32) for i in range(4)]
    cnt = 0
    for ct in range(ncoltiles):
        wt = wts[ct % nbuf]
        pool.dma(wt, wt[:], w, w.t[ct])
        for ti, (t0, n) in enumerate(tiles):
            for half in range(2):
                p = ps[cnt % 4]
                cnt += 1
                for kc in range(KCn):
                    pe.op(lambda e: e.matmul(p[:, 0:n], wt[:, kc, half * 128:(half + 1) * 128], hT[:, kc, t0:t0 + n],
                                             start=(kc == 0), stop=(kc == KCn - 1)), [p], [wt, hT])
                epilogue(ct * 2 + half, ti, t0, n, p)


def build_k1(T_lat=2048, T_ctx=64, NCOL=4352, fw=None, xT=None):
    own = fw is None
    if own:
        fw = FW()
    T = T_lat + T_ctx
    if xT is None:
        xT = fw.dram("xT", [KC, 128, T], F32, "ExternalInput")
    vec = fw.dram("vec", [128, 5, KC], F32, "ExternalInput")
    w = fw.dram("w", [NCOL // 256, 128, KC, 256], F32, "ExternalInput")
    pT = fw.dram("pT", [NCOL // 128, 128, T], F32, "ExternalOutput")
    if own:
        fw.engines()
    pe, act, dve, pool, sp = fw.pe, fw.act, fw.dve, fw.pool, fw.sp
    ones = load_consts(fw, sp)
    vt = fw.sbuf("vt", [128, 5, KC], F32)
    sp.dma(vt, vt[:], vec, vec[:])
    A_lat = fw.sbuf("A_lat", [128, KC], F32)
    A_ctx = fw.sbuf("A_ctx", [128, KC], F32)
    sh_lat = fw.sbuf("sh_lat", [128, KC], F32)
    sh_ctx = fw.sbuf("sh_ctx", [128, KC], F32)
    dve.op(lambda e: e.scalar_tensor_tensor(A_lat[:], vt[:, 1, :], 1.0, vt[:, 0, :], ALU.add, ALU.mult), [A_lat], [vt])
    dve.op(lambda e: e.scalar_tensor_tensor(A_ctx[:], vt[:, 3, :], 1.0, vt[:, 0, :], ALU.add, ALU.mult), [A_ctx], [vt])
    dve.op(lambda e: e.tensor_copy(sh_lat[:], vt[:, 2, :]), [sh_lat], [vt])
    dve.op(lambda e: e.tensor_copy(sh_ctx[:], vt[:, 4, :]), [sh_ctx], [vt])
    hT = fw.sbuf("hT", [128, KC, T], BF16)
    tiles = [(t0, 512, A_lat, sh_lat) for t0 in range(0, T_lat, 512)]
    if T_ctx:
        tiles.append((T_lat, T_ctx, A_ctx, sh_ctx))
    norm_mod_phase(fw, xT, hT, tiles, None, ones, "n1")
    ots = [fw.sbuf("ot%d" % i, [128, 512], F32) for i in range(4)]
    st = {"i": 0}

    def epi(ci, ti, t0, n, p):
        ot = ots[st["i"] % 4]
        if st["i"] % 2 == 0:
            dve.op(lambda e: e.tensor_copy(ot[:, 0:n], p[:, 0:n]), [ot], [p])
        else:
            act.op(lambda e: e.activation(ot[:, 0:n], p[:, 0:n], AF.Copy), [ot], [p])
        st["i"] += 1
        sp.dma(pT, pT.t[ci, :, t0:t0 + n], ot, ot[:, 0:n])

    linear_phase(fw, hT, KC, w, NCOL // 256, [(t[0], t[1]) for t in tiles], epi, "l1")
    sp.wait_buf(pT)
    if own:
        fw.close()
    return fw.nc


def tile_w(w):
    K, N = w.shape
    return np.ascontiguousarray(w.reshape(K // 128, 128, N // 256, 256).transpose(2, 1, 0, 3))


def vec_pk(v):
    return np.ascontiguousarray(v.reshape(KC, 128).T)


D = 2048
KC = 16
NCOLS = 1536


def build_k0(NL=4):
    fw = FW()
    cT = fw.dram("cT", [128, KC, 3], F32, "ExternalInput")
    w = fw.dram("w", [NL, 3, 128, KC, 512], F32, "ExternalInput")
    bm = fw.dram("bm", [NL, NCOLS], F32, "ExternalInput")
    mod = fw.dram("mod", [NL, 3, NCOLS], F32, "ExternalOutput")
    fw.engines()
    pe, act, dve, pool, sp = fw.pe, fw.act, fw.dve, fw.pool, fw.sp
    ct = fw.sbuf("ct", [128, KC, 3], F32)
    sp.dma(ct, ct[:], cT, cT[:])
    st = fw.sbuf("st", [128, KC, 3], F32)
    act.op(lambda e: e.activation(st[:], ct[:], AF.Silu), [st], [ct])
    wts = [fw.sbuf("wt%d" % i, [128, KC, 512], F32) for i in range(2)]
    bts = [fw.sbuf("bt%d" % i, [3, 512], F32) for i in range(2)]
    ots = [fw.sbuf("ot%d" % i, [3, 512], F32) for i in range(2)]
    ps = [fw.psum("ps%d" % i, [128, 512], F32) for i in range(2)]
    i = 0
    for l in range(NL):
        for t in range(3):
            wt, bt, ot, p = wts[i % 2], bts[i % 2], ots[i % 2], ps[i % 2]
            i += 1
            (sp if i % 2 == 0 else act).dma(wt, wt[:], w, w.t[l, t])
            bsrc = bass.AP(tensor=bm.t.tensor, offset=l * NCOLS + t * 512, ap=[[0, 3], [1, 512]])
            sp.dma(bt, bt[:], bm, bsrc)
            for k in range(KC):
                pe.op(lambda e: e.matmul(p[0:3, :], st[:, k, :], wt[:, k, :], start=(k == 0), stop=(k == KC - 1)), [p], [st, wt])
            dve.op(lambda e: e.tensor_tensor(ot[:], p[0:3, :], bt[:], ALU.add), [ot], [p, bt])
            sp.dma(mod, mod.t[l, :, t * 512:(t + 1) * 512], ot, ot[:])
    sp.wait_buf(mod)
    fw.close()
    return fw.nc


def k0_inputs(core, c, c_ctx, w_mod, b_mod):
    NL = w_mod.shape[0]
    cs = np.stack([c[0], c[1], c_ctx], axis=1)
    cT = np.ascontiguousarray(cs.reshape(KC, 128, 3).transpose(1, 0, 2))
    cols = slice(core * NCOLS, (core + 1) * NCOLS)
    w = w_mod[:, :, cols].reshape(NL, KC, 128, 3, 512).transpose(0, 3, 2, 1, 4)
    return {"cT": cT, "w": np.ascontiguousarray(w), "bm": np.ascontiguousarray(b_mod[:, cols])}


EPS = 1e-6
CTX = 256


def qknorm_rope(fw, xT, xsT, CS, gcol, dst, col_map, tiles, blockones, nm, inv_n):
    pe, act, dve, pool, sp = fw.pe, fw.act, fw.dve, fw.pool, fw.sp
    gbuf, c0 = gcol
    if "qk_scr" not in fw.__dict__:
        fw.qk_scr = dict(
            xt=[fw.sbuf("qk_x%d" % i, [128, 512], F32) for i in range(2)],
            xs=[fw.sbuf("qk_xs%d" % i, [128, 512], F32) for i in range(2)],
            ct=[fw.sbuf("qk_c%d" % i, [128, 512], F32) for i in range(2)],
            st=[fw.sbuf("qk_s%d" % i, [128, 512], F32) for i in range(2)],
            sq=[fw.sbuf("qk_sq%d" % i, [128, 512], F32) for i in range(2)],
            rs=[fw.sbuf("qk_rs%d" % i, [128, 512], F32) for i in range(2)],
            ps=[fw.psum("qk_ps%d" % i, [128, 512], F32) for i in range(2)])
    xt, xs, ct, st, sq, rs, ps = (fw.qk_scr[k] for k in ("xt", "xs", "ct", "st", "sq", "rs", "ps"))
    for i, (x0, tb0, n) in enumerate(tiles):
        j = i % 2
        sp.dma(xt[j], xt[j][:, 0:n], xT, xT.t[:, x0:x0 + n])
        sp.dma(xs[j], xs[j][:, 0:n], xsT, xsT.t[:, x0:x0 + n])
        sp.dma(ct[j], ct[j][:, 0:n], CS, CS.t[0, :, tb0:tb0 + n])
        sp.dma(st[j], st[j][:, 0:n], CS, CS.t[1, :, tb0:tb0 + n])
        act.op(lambda e: e.activation(sq[j][:, 0:n], xt[j][:, 0:n], AF.Square), [sq[j]], [xt[j]])
        pe.op(lambda e: e.matmul(ps[j][:, 0:n], blockones[:], sq[j][:, 0:n], start=True, stop=True), [ps[j]], [blockones, sq[j]])
        dve.op(lambda e: e.tensor_scalar(rs[j][:, 0:n], ps[j][:, 0:n], inv_n, EPS, ALU.mult, ALU.add), [rs[j]], [ps[j]])
        act.op(lambda e: e.activation(rs[j][:, 0:n], rs[j][:, 0:n], AF.Sqrt), [rs[j]], [rs[j]])
        dve.op(lambda e: e.reciprocal(rs[j][:, 0:n], rs[j][:, 0:n]), [rs[j]], [rs[j]])
        dve.op(lambda e: e.scalar_tensor_tensor(ct[j][:, 0:n], xt[j][:, 0:n], gbuf[:, c0:c0 + 1], ct[j][:, 0:n], ALU.mult, ALU.mult),
               [ct[j]], [xt[j], gbuf])
        dve.op(lambda e: e.scalar_tensor_tensor(st[j][:, 0:n], xs[j][:, 0:n], gbuf[:, c0 + 1:c0 + 2], st[j][:, 0:n], ALU.mult, ALU.mult),
                [st[j]], [xs[j], gbuf])
        dve.op(lambda e: e.tensor_tensor(ct[j][:, 0:n], ct[j][:, 0:n], st[j][:, 0:n], ALU.add), [ct[j]], [st[j]])
        dve.op(lambda e: e.tensor_tensor(dst[:, x0:x0 + n], ct[j][:, 0:n], rs[j][:, 0:n], ALU.mult), [dst], [ct[j], rs[j]])


def build_k2a(LQ=8192, fw=None):
    own = fw is None
    if own:
        fw = FW()
    T = LQ + CTX
    NKC = T // 128
    qT = fw.dram("qT", [128, T], F32, "ExternalInput")
    qsT = fw.dram("qsT", [128, T], F32, "ExternalInput")
    kT = fw.dram("kT", [128, T], F32, "ExternalInput")
    ksT = fw.dram("ksT", [128, T], F32, "ExternalInput")
    v = fw.dram("v", [128, NKC, 128], F32, "ExternalInput")
    CS = fw.dram("CS", [2, 128, T], F32, "ExternalInput")
    sm = fw.dram("sm", [128, 8], F32, "ExternalInput")
    lamp = fw.dram("lamp", [1, 256], F32, "ExternalInput")
    oT = fw.dram("oT", [128, T], F32, "ExternalOutput")
    if own:
        fw.engines()
    pe, act, dve, pool, sp = fw.pe, fw.act, fw.dve, fw.pool, fw.sp
    ones = fw.sbuf("ones", [128, 128], F32)
    dve.op(lambda e: e.memset(ones[:], 1.0), [ones], [])
    onesb = fw.sbuf("onesb", [128, 128], BF16)
    dve.op(lambda e: e.memset(onesb[:], 1.0), [onesb], [])
    blockones = fw.sbuf("blockones", [128, 128], F32)
    dve.op(lambda e: e.memset(blockones[:], 0.0), [blockones], [])
    dve.op(lambda e: e.memset(blockones[0:64, 0:64], 1.0), [blockones], [])
    dve.op(lambda e: e.memset(blockones[64:128, 64:128], 1.0), [blockones], [])
    smt = fw.sbuf("smt", [128, 8], F32)
    sp.dma(smt, smt[:], sm, sm[:])
    lt = fw.sbuf("lt", [1, 256], F32)
    sp.dma(lt, lt[:], lamp, lamp[:])
    lw = fw.sbuf("lw", [1, 128], F32)
    dve.op(lambda e: e.tensor_tensor(lw[:, 0:64], lt[:, 0:64], lt[:, 64:128], ALU.mult), [lw], [lt])
    dve.op(lambda e: e.tensor_tensor(lw[:, 64:128], lt[:, 128:192], lt[:, 192:256], ALU.mult), [lw], [lt])
    l2 = fw.sbuf("l2", [1, 4], F32)
    dve.op(lambda e: e.reduce_sum(l2[:, 0:1], lw[:, 0:64], AX.X), [l2], [lw])
    dve.op(lambda e: e.reduce_sum(l2[:, 1:2], lw[:, 64:128], AX.X), [l2], [lw])
    act.op(lambda e: e.activation(l2[:, 0:2], l2[:, 0:2], AF.Exp), [l2], [l2])
    dve.op(lambda e: e.tensor_tensor(l2[:, 2:3], l2[:, 1:2], l2[:, 0:1], ALU.subtract), [l2], [l2])
    dve.op(lambda e: e.tensor_tensor(l2[:, 2:3], l2[:, 2:3], smt[0:1, 5:6], ALU.subtract), [l2], [l2, smt])
    sc = fw.sbuf("sc", [128, 2], F32)
    fw.push_scope()
    psl = fw.psum("psl", [128, 512], F32)
    pe.op(lambda e: e.matmul(psl[:, 0:1], ones[0:1, :], l2[0:1, 2:3], start=True, stop=True), [psl], [ones, l2])
    dve.op(lambda e: e.tensor_copy(sc[:, 0:1], psl[:, 0:1]), [sc], [psl])
    dve.op(lambda e: e.tensor_tensor(sc[:, 1:2], smt[:, 4:5], smt[:, 6:7], ALU.mult), [sc], [smt])
    fw.pop_scope()

    QT = fw.sbuf("QT", [128, T], BF16)
    KT = fw.sbuf("KT", [128, T], BF16)
    V = fw.sbuf("V", [128, NKC, 128], BF16)
    pool.dma(V, V[:], v, v[:])
    qtiles = [(t0, t0, 512) for t0 in range(0, LQ, 512)] + [(LQ, LQ, CTX)]
    ktiles = [(0, LQ, CTX)] + [(CTX + t0, t0, 512) for t0 in range(0, LQ, 512)]
    fw.push_scope()
    qknorm_rope(fw, qT, qsT, CS, (smt, 0), QT, None, qtiles, blockones, "qn", 1.0 / 64)
    qknorm_rope(fw, kT, ksT, CS, (smt, 2), KT, None, ktiles, blockones, "kn", 1.0 / 64)
    del fw.qk_scr
    fw.pop_scope()

    NSB = 2
    NPT = 4
    ps_s = [[fw.psum("ps_s%d_%d" % (m, i), [128, 512], F32) for i in range(NSB)] for m in range(2)]
    ps_o = [fw.psum("ps_o%d" % m, [128, 512], F32) for m in range(2)]
    ps_z = [fw.psum("ps_z%d" % m, [128, 512], F32) for m in range(2)]
    pts = [[fw.sbuf("pt%d_%d" % (m, i), [128, 512], BF16) for i in range(NPT)] for m in range(2)]
    om = [fw.sbuf("om%d" % i, [128, 512], F32) for i in range(2)]
    zacc = [fw.sbuf("zacc%d" % i, [128, 512], F32) for i in range(2)]
    rz = [fw.sbuf("rz%d" % i, [128, 512], F32) for i in range(2)]
    osq = fw.sbuf("osq", [128, 512], F32)
    ors = fw.sbuf("ors", [128, 512], F32)
    ots = [fw.sbuf("ot%d" % i, [128, 512], F32) for i in range(2)]
    for qi, (q0, _, n) in enumerate(qtiles):
        nkc = NKC if q0 < LQ else CTX // 128
        seq = []
        for kc in range(nkc):
            seq.append(("s", kc))
            if kc >= 1:
                seq.append(("av", kc - 1))
        seq.append(("av", nkc - 1))
        for kind, kc in seq:
            if kind == "s":
                for m in range(2):
                    r0 = 64 * m
                    p = ps_s[m][kc % NSB]
                    pe.op(lambda e: e.matmul(p[:, 0:n], KT[r0:r0 + 64, kc * 128:(kc + 1) * 128], QT[r0:r0 + 64, q0:q0 + n],
                                             start=True, stop=True), [p], [KT, QT])
                for m in range(2):
                    p = ps_s[m][kc % NSB]
                    pt = pts[m][kc % NPT]
                    act.op(lambda e: e.activation(pt[:, 0:n], p[:, 0:n], AF.Exp, scale=0.125), [pt], [p])
            else:
                for m in range(2):
                    pt = pts[m][kc % NPT]
                    pe.op(lambda e: e.matmul(ps_o[m][:, 0:n], V[:, kc, :], pt[:, 0:n], start=(kc == 0), stop=(kc == nkc - 1)), [ps_o[m]], [V, pt])
                    if m == 0:
                        pe.op(lambda e: e.matmul(ps_z[m][:, 0:n], onesb[:], pt[:, 0:n], start=(kc == 0), stop=(kc == nkc - 1)), [ps_z[m]], [onesb, pt])
                    elif kc == 0:
                        dve.op(lambda e: e.tensor_copy(zacc[m][:, 0:n], pt[:, 0:n]), [zacc[m]], [pt])
                    else:
                        dve.op(lambda e: e.tensor_tensor(zacc[m][:, 0:n], zacc[m][:, 0:n], pt[:, 0:n], ALU.add), [zacc[m]], [pt])
        for m in range(1, 2):
            pe.op(lambda e: e.matmul(ps_z[m][:, 0:n], ones[:], zacc[m][:, 0:n], start=True, stop=True), [ps_z[m]], [ones, zacc[m]])
        for m in range(2):
            dve.op(lambda e: e.reciprocal(rz[m][:, 0:n], ps_z[m][:, 0:n]), [rz[m]], [ps_z[m]])
            dve.op(lambda e: e.tensor_tensor(om[m][:, 0:n], ps_o[m][:, 0:n], rz[m][:, 0:n], ALU.mult), [om[m]], [ps_o[m], rz[m]])
        dve.op(lambda e: e.scalar_tensor_tensor(om[0][:, 0:n], om[1][:, 0:n], sc[:, 0:1], om[0][:, 0:n], ALU.mult, ALU.add), [om[0]], [om[1], sc])
        act.op(lambda e: e.activation(osq[:, 0:n], om[0][:, 0:n], AF.Square), [osq], [om[0]])
        pl = ps_s[0][(nkc) % NSB]
        pe.op(lambda e: e.matmul(pl[:, 0:n], ones[:], osq[:, 0:n], start=True, stop=True), [pl], [ones, osq])
        dve.op(lambda e: e.tensor_scalar(ors[:, 0:n], pl[:, 0:n], 1.0 / 128, EPS, ALU.mult, ALU.add), [ors], [pl])
        act.op(lambda e: e.activation(ors[:, 0:n], ors[:, 0:n], AF.Sqrt), [ors], [ors])
        dve.op(lambda e: e.reciprocal(ors[:, 0:n], ors[:, 0:n]), [ors], [ors])
        ot = ots[qi % 2]
        dve.op(lambda e: e.scalar_tensor_tensor(ot[:, 0:n], om[0][:, 0:n], sc[:, 1:2], ors[:, 0:n], ALU.mult, ALU.mult), [ot], [om[0], sc, ors])
        sp.dma(oT, oT.t[:, q0:q0 + n], ot, ot[:, 0:n])
    sp.wait_buf(oT)
    if own:
        fw.close()
    return fw.nc


def swap_halves(xT):
    r = xT.reshape(-1, 2, 2, 16, xT.shape[-1])
    return np.ascontiguousarray(r[:, :, ::-1]).reshape(xT.shape)


def rope_tables(L, n_ctx, reps):
    GRID_W = 64
    rows = L // GRID_W
    row = np.repeat(np.arange(rows, dtype=np.float32), GRID_W)
    col = np.tile(np.arange(GRID_W, dtype=np.float32), rows)
    inv = (10000.0 ** (-np.arange(0, 32, 2, dtype=np.float32) / 32)).astype(np.float32)
    ang = np.concatenate([row[:, None] * inv, col[:, None] * inv], axis=-1)
    cos = np.cos(ang).astype(np.float32).reshape(L, 2, 16)
    sin = np.sin(ang).astype(np.float32).reshape(L, 2, 16)
    C = np.ones((2, 2, 16, L + n_ctx), np.float32)
    S = np.zeros((2, 2, 16, L + n_ctx), np.float32)
    C[:, 0, :, :L] = cos.transpose(1, 2, 0)
    C[:, 1, :, :L] = cos.transpose(1, 2, 0)
    S[:, 0, :, :L] = -sin.transpose(1, 2, 0)
    S[:, 1, :, :L] = sin.transpose(1, 2, 0)
    C = np.tile(C.reshape(64, -1), (reps, 1))
    S = np.tile(S.reshape(64, -1), (reps, 1))
    return np.ascontiguousarray(np.stack([C, S]))


def build_k2b(LQ=8192, fw=None, defer=False):
    own = fw is None
    if own:
        fw = FW()
    T = LQ + CTX
    NKC = T // 128
    NB = LQ // 128
    qT = fw.dram("qT", [128, T], F32, "ExternalInput")
    qsT = fw.dram("qsT", [128, T], F32, "ExternalInput")
    kT = fw.dram("kT", [128, T], F32, "ExternalInput")
    ksT = fw.dram("ksT", [128, T], F32, "ExternalInput")
    v = fw.dram("v", [128, NKC, 64], F32, "ExternalInput")
    CS = fw.dram("CS", [2, 128, T], F32, "ExternalInput")
    sm = fw.dram("sm", [128, 8], F32, "ExternalInput")
    masks = fw.dram("masks", [128, 6, 512], F32, "ExternalInput")
    oT = fw.dram("oT", [2, 64, T], F32, "ExternalOutput")
    if own:
        fw.engines()
    pe, act, dve, pool, sp = fw.pe, fw.act, fw.dve, fw.pool, fw.sp
    onesb = fw.sbuf("onesb", [128, 128], BF16)
    dve.op(lambda e: e.memset(onesb[:], 1.0), [onesb], [])
    blockones = fw.sbuf("blockones", [128, 128], F32)
    dve.op(lambda e: e.memset(blockones[:], 0.0), [blockones], [])
    dve.op(lambda e: e.memset(blockones[0:64, 0:64], 1.0), [blockones], [])
    dve.op(lambda e: e.memset(blockones[64:128, 64:128], 1.0), [blockones], [])
    smt = fw.sbuf("smt", [128, 8], F32)
    sp.dma(smt, smt[:], sm, sm[:])
    es = fw.sbuf("es", [128, 2], F32)
    act.op(lambda e: e.activation(es[:], smt[:, 4:6], AF.Exp), [es], [smt])
    mk = fw.sbuf("mk", [128, 6, 512], BF16)
    pool.dma(mk, mk[:], masks, masks[:])
    QT = fw.sbuf("QT", [128, T], BF16)
    KT = fw.sbuf("KT", [128, T], BF16)
    V = fw.sbuf("V", [128, NKC, 64], BF16)
    pool.dma(V, V[:], v, v[:])
    qtiles = [(t0, t0, 512) for t0 in range(0, LQ, 512)] + [(LQ, LQ, CTX)]
    ktiles = [(0, LQ, CTX)] + [(CTX + t0, t0, 512) for t0 in range(0, LQ, 512)]
    fw.push_scope()
    qknorm_rope(fw, qT, qsT, CS, (smt, 0), QT, None, qtiles, blockones, "qn", 1.0 / 64)
    qknorm_rope(fw, kT, ksT, CS, (smt, 2), KT, None, ktiles, blockones, "kn", 1.0 / 64)
    del fw.qk_scr
    fw.pop_scope()

    NSB = 1 if defer else 2
    NPT = 4
    ps_s = [[fw.psum("ps_s%d_%d" % (m, i), [128, 512], F32) for i in range(NSB)] for m in range(2)]
    ps_o = [fw.psum("ps_o%d" % m, [64, 512], F32) for m in range(2)]
    ps_z = [fw.psum("ps_z%d" % m, [64, 512], F32) for m in range(2)]
    pts = [[fw.sbuf("pt%d_%d" % (m, i), [128, 512], BF16) for i in range(NPT)] for m in range(2)]
    rz = [fw.sbuf("rz%d" % m, [64, 512], F32) for m in range(2)]
    ots = [fw.sbuf("ot%d" % i, [64, 512], F32) for i in range(4)]
    def steps():
        cntl = [0]
        for qi, (q0, _, n) in enumerate(qtiles):
            if q0 < LQ:
                n0 = q0 // 128
                chunks = [(0, None), (1, None)]
                for rel in range(-1, 5):
                    j = n0 + rel
                    if 0 <= j < NB:
                        chunks.append((CTX // 128 + j, rel + 1))
            else:
                chunks = [(0, None), (1, None)]
            nch = len(chunks)
            seq = []
            for ci in range(nch):
                seq.append(("s", ci))
                if ci >= 1:
                    seq.append(("av", ci - 1))
            seq.append(("av", nch - 1))
            for kind, ci in seq:
                kc, mi = chunks[ci]
                if kind == "s":
                    for m in range(2):
                        r0 = 64 * m
                        p = ps_s[m][ci % NSB]
                        pe.op(lambda e: e.matmul(p[:, 0:n], KT[r0:r0 + 64, kc * 128:(kc + 1) * 128], QT[r0:r0 + 64, q0:q0 + n],
                                                 start=True, stop=True), [p], [KT, QT])
                    for m in range(2):
                        p = ps_s[m][ci % NSB]
                        pt = pts[m][ci % NPT]
                        act.op(lambda e: e.activation(pt[:, 0:n], p[:, 0:n], AF.Exp, scale=0.125), [pt], [p])
                        if mi is not None:
                            (dve if m == 0 else pool).op(lambda e: e.tensor_tensor(pt[:, 0:n], pt[:, 0:n], mk[:, mi, 0:n], ALU.mult), [pt], [pt, mk])
                else:
                    last = ci == nch - 1
                    for m in range(2):
                        pt = pts[m][ci % NPT]
                        pe.op(lambda e: e.matmul(ps_o[m][:, 0:n], V[:, kc, :], pt[:, 0:n], start=(ci == 0), stop=last), [ps_o[m]], [V, pt])
                        pe.op(lambda e: e.matmul(ps_z[m][:, 0:n], onesb[:, 0:64], pt[:, 0:n], start=(ci == 0), stop=last), [ps_z[m]], [onesb, pt])
            for m in range(2):
                dve.op(lambda e: e.tensor_scalar(rz[m][:, 0:n], ps_z[m][:, 0:n], es[0:64, m:m + 1], None, ALU.add), [rz[m]], [ps_z[m], es])
                dve.op(lambda e: e.reciprocal(rz[m][:, 0:n], rz[m][:, 0:n]), [rz[m]], [rz[m]])
                ot = ots[cntl[0] % 4]
                cntl[0] += 1
                dve.op(lambda e: e.tensor_tensor(ot[:, 0:n], ps_o[m][:, 0:n], rz[m][:, 0:n], ALU.mult), [ot], [ps_o[m], rz[m]])
                (pool if defer else sp).dma(oT, oT.t[m, :, q0:q0 + n], ot, ot[:, 0:n])
            yield qi
        sp.wait_buf(oT)

    if defer:
        return steps()
    for _ in steps():
        pass
    if own:
        fw.close()
    return fw.nc


def band_masks():
    ki = np.arange(128)[:, None, None]
    rel = np.arange(-1, 5)[None, :, None]
    qq = np.arange(512)[None, None, :]
    return (np.abs(128 * rel + ki - qq) <= 128).astype(np.float32)


CTX = 256
POOL_WINDOWS = (2, 4, 8, 16)


def build_k2c(LQ=8192, fw=None):
    own = fw is None
    if own:
        fw = FW()
    T = LQ + CTX
    PADL = 8
    uT = fw.dram("uT", [128, T], F32, "ExternalInput")
    wsel = fw.dram("wsel", [128, 16], F32, "ExternalInput")
    invc = fw.dram("invc", [128, T], F32, "ExternalInput")
    wl = fw.dram("wl", [128, 128], F32, "ExternalInput")
    ls = fw.dram("ls", [128, 1], F32, "ExternalInput")
    yT = fw.dram("yT", [128, T], F32, "ExternalOutput")
    if own:
        fw.engines()
    pe, act, dve, pool, sp = fw.pe, fw.act, fw.dve, fw.pool, fw.sp
    wst = fw.sbuf("wst", [128, 16], F32)
    sp.dma(wst, wst[:], wsel, wsel[:])
    lst = fw.sbuf("lst", [128, 1], F32)
    sp.dma(lst, lst[:], ls, ls[:])
    wlt = fw.sbuf("wlt", [128, 128], BF16)
    pool.dma(wlt, wlt[:], wl, wl[:])
    TT = 2048
    ut = fw.sbuf("ut", [128, TT + 16], F32)
    ic = fw.sbuf("ic", [128, TT], F32)
    acc = fw.sbuf("acc", [128, TT], F32)
    db = fw.sbuf("db", [128, TT], BF16)
    ps = [fw.psum("ps%d" % i, [128, 512], F32) for i in range(2)]
    ots = [fw.sbuf("ot%d" % i, [128, 512], F32) for i in range(2)]
    cnt = 0
    for (s0, Ls) in ((0, LQ), (LQ, CTX)):
        tt_ = min(TT, Ls)
        for t0 in range(0, Ls, tt_):
            lo = max(t0 - 8, 0)
            hi = min(t0 + tt_ + 8, Ls)
            if lo > t0 - 8:
                dve.op(lambda e: e.memset(ut[:, 0:8], 0.0), [ut], [])
            if hi < t0 + tt_ + 8:
                dve.op(lambda e: e.memset(ut[:, tt_ + 8:tt_ + 16], 0.0), [ut], [])
            sp.dma(ut, ut[:, lo - (t0 - 8):hi - (t0 - 8)], uT, uT.t[:, s0 + lo:s0 + hi])
            sp.dma(ic, ic[:, 0:tt_], invc, invc.t[:, s0 + t0:s0 + t0 + tt_])
            dve.op(lambda e: e.tensor_scalar(acc[:, 0:tt_], ut[:, 0:tt_], wst[:, 0:1], None, ALU.mult), [acc], [ut, wst])
            for k in range(1, 16):
                dve.op(lambda e: e.scalar_tensor_tensor(acc[:, 0:tt_], ut[:, k:k + tt_], wst[:, k:k + 1], acc[:, 0:tt_], ALU.mult, ALU.add), [acc], [ut, wst])
            dve.op(lambda e: e.tensor_tensor(acc[:, 0:tt_], acc[:, 0:tt_], ic[:, 0:tt_], ALU.mult), [acc], [ic])
            dve.op(lambda e: e.tensor_tensor(db[:, 0:tt_], acc[:, 0:tt_], ut[:, 8:8 + tt_], ALU.subtract), [db], [acc, ut])
            for c0 in range(0, tt_, 512):
                n = min(512, tt_ - c0)
                p = ps[cnt % 2]
                ot = ots[cnt % 2]
                cnt += 1
                pe.op(lambda e: e.matmul(p[:, 0:n], wlt[:], db[:, c0:c0 + n], start=True, stop=True), [p], [wlt, db])
                act.op(lambda e: e.activation(ot[:, 0:n], p[:, 0:n], AF.Copy, scale=lst[:, 0:1]), [ot], [p, lst])
                sp.dma(yT, yT.t[:, s0 + t0 + c0:s0 + t0 + c0 + n], ot, ot[:, 0:n])
    sp.wait_buf(yT)
    if own:
        fw.close()
    return fw.nc


def pool_consts(g, LQ):
    w = POOL_WINDOWS[g]
    lo = w // 2
    hi = w - 1 - lo
    sel = np.zeros(16, np.float32)
    for s in range(-lo, hi + 1):
        sel[s + 8] = 1.0
    outs = []
    for Ls in (LQ, CTX):
        t = np.arange(Ls)
        start = np.clip(t - lo, 0, Ls)
        end = np.clip(t + hi + 1, 0, Ls)
        outs.append((1.0 / (end - start).astype(np.float32)).astype(np.float32))
    ic = np.concatenate(outs)
    return np.tile(sel[None], (128, 1)), np.ascontiguousarray(np.tile(ic[None], (128, 1)))


EPS = 1e-6
CTX = 256
HY_EMB = 33


def sin_act(fw, out_buf, out_ap, p, n, fb, scr):
    act, dve = fw.act, fw.dve
    s2, s4 = scr
    P = 64
    act.op(lambda e: e.activation(s2[0:P, 0:n], p[0:P, 0:n], AF.Sin, bias=fb[0:P, 1:2], scale=fb[0:P, 0:1]), [s2], [p, fb])
    act.op(lambda e: e.activation(s4[0:P, 0:n], p[0:P, 0:n], AF.Sin, bias=fb[0:P, 3:4], scale=fb[0:P, 2:3]), [s4], [p, fb])
    dve.op(lambda e: e.tensor_tensor(s4[0:P, 0:n], s4[0:P, 0:n], s4[0:P, 0:n], ALU.mult), [s4], [s4])
    dve.op(lambda e: e.tensor_scalar(s4[0:P, 0:n], s4[0:P, 0:n], -2.0, 1.0, ALU.mult, ALU.add), [s4], [s4])
    dve.op(lambda e: e.scalar_tensor_tensor(out_ap, s2[0:P, 0:n], 2.0, s4[0:P, 0:n], ALU.mult, ALU.mult), [out_buf], [s2, s4])


def filter_gen(fw, Lf, zf, t01, wts, KF, nm, scr, scale):
    pe, act, dve, pool, sp = fw.pe, fw.act, fw.dve, fw.pool, fw.sp
    w1, w2, w3, fb1, fb2, ndelta = wts["w1"], wts["w2"], wts["w3"], wts["fb1"], wts["fb2"], wts["ndelta"]
    ps1, ps2, ps3, zt, h1, h2, s2, s4, tt, kt, ktb, sqt = scr
    nt = (Lf + 511) // 512
    part = fw.sbuf(nm + "_part", [128, 2 * nt], F32)
    dve.op(lambda e: e.memset(part[:], 0.0), [part], [])
    for di in range(2):
        for ti in range(nt):
            t0 = ti * 512
            n = min(512, Lf - t0)
            sp.dma(zt, zt[0:HY_EMB, 0:n], zf, zf.t[1 - di, :, t0:t0 + n])
            tsrc = bass.AP(tensor=t01.t.tensor, offset=(1 - di) * Lf + t0, ap=[[0, 128], [1, n]])
            sp.dma(tt, tt[:, 0:n], t01, tsrc)
            pe.op(lambda e: e.matmul(ps1[0:64, 0:n], w1[0:HY_EMB, :], zt[0:HY_EMB, 0:n], start=True, stop=True), [ps1], [w1, zt])
            sin_act(fw, h1, h1[0:64, 0:n], ps1, n, fb1, (s2, s4))
            pe.op(lambda e: e.matmul(ps2[0:64, 0:n], w2[0:64, :], h1[0:64, 0:n], start=True, stop=True), [ps2], [w2, h1])
            sin_act(fw, h2, h2[0:64, 0:n], ps2, n, fb2, (s2, s4))
            pe.op(lambda e: e.matmul(ps3[:, 0:n], w3[0:64, di, :], h2[0:64, 0:n], start=True, stop=True), [ps3], [w3, h2])
            act.op(lambda e: e.activation(tt[:, 0:n], tt[:, 0:n], AF.Exp, scale=ndelta[:, 0:1]), [tt], [tt, ndelta])
            dve.op(lambda e: e.tensor_tensor(kt[:, 0:n], ps3[:, 0:n], tt[:, 0:n], ALU.mult), [kt], [ps3, tt])
            if di == 1 and ti == 0:
                dve.op(lambda e: e.memset(kt[:, 0:1], 0.0), [kt], [])
            dve.op(lambda e: e.tensor_tensor(sqt[:, 0:n], kt[:, 0:n], kt[:, 0:n], ALU.mult), [sqt], [kt])
            dve.op(lambda e: e.reduce_sum(part[:, di * nt + ti:di * nt + ti + 1], sqt[:, 0:n], AX.X), [part], [sqt])
            act.op(lambda e: e.activation(ktb[:, 0:n], kt[:, 0:n], AF.Copy), [ktb], [kt])
            if di == 0:
                sp.dma(KF, KF.t[:, 1 + t0:1 + t0 + n], ktb, ktb[0:64, 0:n])
            elif ti == 0:
                sp.dma(KF, KF.t[:, Lf + 1:Lf + n], ktb, ktb[0:64, 1:n])
            else:
                sp.dma(KF, KF.t[:, Lf + t0:Lf + t0 + n], ktb, ktb[0:64, 0:n])
    dve.op(lambda e: e.reduce_sum(scale[:], part[:], AX.X), [scale], [part])
    dve.op(lambda e: e.tensor_scalar(scale[:], scale[:], EPS, None, ALU.add), [scale], [scale])
    act.op(lambda e: e.activation(scale[:], scale[:], AF.Sqrt), [scale], [scale])
    dve.op(lambda e: e.reciprocal(scale[:], scale[:]), [scale], [scale])
    return scale


def conv3(fw, dst, u, n, sm, part):
    dve = fw.dve
    c = 3 * part
    dve.op(lambda e: e.tensor_scalar(dst[:, 0:n], u[:, 1:n + 1], sm[:, c + 1:c + 2], sm[:, 9 + part:10 + part], ALU.mult, ALU.add), [dst], [u, sm])
    dve.op(lambda e: e.scalar_tensor_tensor(dst[:, 0:n], u[:, 0:n], sm[:, c:c + 1], dst[:, 0:n], ALU.mult, ALU.add), [dst], [u, sm])
    dve.op(lambda e: e.scalar_tensor_tensor(dst[:, 0:n], u[:, 2:n + 2], sm[:, c + 2:c + 3], dst[:, 0:n], ALU.mult, ALU.add), [dst], [u, sm])


def load_u_tile(fw, ut, uT, part, t0, n, Ls):
    sp, dve = fw.sp, fw.dve
    lo = max(t0 - 1, 0)
    hi = min(t0 + n + 1, Ls)
    if t0 == 0:
        dve.op(lambda e: e.memset(ut[:, 0:1], 0.0), [ut], [])
    if t0 + n == Ls:
        dve.op(lambda e: e.memset(ut[:, n + 1:n + 2], 0.0), [ut], [])
    sp.dma(ut, ut[:, lo - (t0 - 1):hi - (t0 - 1)], uT, uT.t[part, :, lo:hi])


def build_k2d(LQ=8192, fw=None, hook_pre=None, hook_step=None, hook_post=None):
    own = fw is None
    if own:
        fw = FW()
    NB = LQ // 128
    NBC = CTX // 128
    TT = min(1024, LQ)
    uT = fw.dram("uT", [3, 128, LQ], F32, "ExternalInput")
    ucT = fw.dram("ucT", [3, 128, CTX], F32, "ExternalInput")
    smd = fw.dram("sm", [128, 16], F32, "ExternalInput")
    w1d = fw.dram("w1", [HY_EMB, 64], F32, "ExternalInput")
    w2d = fw.dram("w2", [64, 64], F32, "ExternalInput")
    w3d = fw.dram("w3", [64, 2, 128], F32, "ExternalInput")
    fbd = fw.dram("fb", [64, 4], F32, "ExternalInput")
    zfL = fw.dram("zfL", [2, HY_EMB, LQ], F32, "ExternalInput")
    t01L = fw.dram("t01L", [2, LQ], F32, "ExternalInput")
    zfC = fw.dram("zfC", [2, HY_EMB, CTX], F32, "ExternalInput")
    t01C = fw.dram("t01C", [2, CTX], F32, "ExternalInput")
    oT = fw.dram("oT", [128, LQ], F32, "ExternalOutput")
    ocT = fw.dram("ocT", [128, CTX], F32, "ExternalOutput")
    KF = fw.dram("KF", [64, 2 * LQ], BF16, "Internal")
    KFC = fw.dram("KFC", [64, 2 * CTX], BF16, "Internal")
    if own:
        fw.engines()
    pe, act, dve, pool, sp = fw.pe, fw.act, fw.dve, fw.pool, fw.sp

    identb = fw.sbuf("identb", [128, 128], BF16)
    identf = fw.sbuf("identf", [128, 128], F32)
    idd = fw.dram("ident", [128, 128], F32, "ExternalInput")
    sp.dma(identf, identf[:], idd, idd[:])
    dve.op(lambda e: e.tensor_copy(identb[:], identf[:]), [identb], [identf])
    antif = fw.sbuf("antif", [128, 128], F32)
    add = fw.dram("anti", [128, 128], F32, "ExternalInput")
    sp.dma(antif, antif[:], add, add[:])
    sm = fw.sbuf("smt", [128, 16], F32)
    sp.dma(sm, sm[:], smd, smd[:])
    w1 = fw.sbuf("w1s", [HY_EMB, 64], F32)
    sp.dma(w1, w1[:], w1d, w1d[:])
    w2 = fw.sbuf("w2s", [64, 64], F32)
    sp.dma(w2, w2[:], w2d, w2d[:])
    w3 = fw.sbuf("w3s", [64, 2, 128], F32)
    sp.dma(w3, w3[:], w3d, w3d[:])
    fb = fw.sbuf("fbs", [64, 4], F32)
    sp.dma(fb, fb[:], fbd, fbd[:])
    fb1 = fw.sbuf("fb1", [64, 4], F32)
    fb2 = fw.sbuf("fb2", [64, 4], F32)
    for dst, bc in ((fb1, 1), (fb2, 2)):
        dve.op(lambda e: e.tensor_scalar(dst[:, 0:1], fb[:, 0:1], 0.5, None, ALU.mult), [dst], [fb])
        dve.op(lambda e: e.tensor_scalar(dst[:, 2:3], fb[:, 0:1], 0.25, None, ALU.mult), [dst], [fb])
        dve.op(lambda e: e.tensor_tensor(dst[:, 1:2], dst[:, 0:1], fb[:, bc:bc + 1], ALU.mult), [dst], [dst, fb])
        dve.op(lambda e: e.tensor_tensor(dst[:, 3:4], dst[:, 2:3], fb[:, bc:bc + 1], ALU.mult), [dst], [dst, fb])
    ndelta = fw.sbuf("ndelta", [128, 1], F32)
    dve.op(lambda e: e.tensor_copy(ndelta[:], sm[:, 13:14]), [ndelta], [sm])
    wts = dict(w1=w1, w2=w2, w3=w3, fb1=fb1, fb2=fb2, ndelta=ndelta)
    scaleL = fw.sbuf("fL_scale", [128, 1], F32)
    scaleC = fw.sbuf("fC_scale", [128, 1], F32)
    fw.push_scope()
    scr = (fw.psum("fg_ps1", [128, 512], F32), fw.psum("fg_ps2", [128, 512], F32), fw.psum("fg_ps3", [128, 512], F32),
           fw.sbuf("fg_zt", [64, 512], F32), fw.sbuf("fg_h1", [64, 512], F32), fw.sbuf("fg_h2", [64, 512], F32),
           fw.sbuf("fg_s2", [64, 512], F32), fw.sbuf("fg_s4", [64, 512], F32), fw.sbuf("fg_tt", [128, 512], F32),
           fw.sbuf("fg_kt", [128, 512], F32), fw.sbuf("fg_ktb", [128, 512], BF16), fw.sbuf("fg_sq", [128, 512], F32))
    filter_gen(fw, LQ, zfL, t01L, wts, KF, "fL", scr, scaleL)
    filter_gen(fw, CTX, zfC, t01C, wts, KFC, "fC", scr, scaleC)
    fw.pop_scope()

    Zt = fw.sbuf("Zt", [128, 64, NB, 2], BF16)
    Ztc = fw.sbuf("Ztc", [128, 64, NBC, 2], BF16)
    fw.push_scope()
    uts = [fw.sbuf("ut%d" % i, [128, TT + 2], F32) for i in range(3)]
    cv = [fw.sbuf("cv%d" % i, [128, TT], F32) for i in range(3)]
    zb = fw.sbuf("zb", [128, TT], BF16)
    pst = [fw.psum("pst%d" % i, [128, 512], BF16) for i in range(2)]

    def z_phase(src, Ls, Ztx, nblk):
        tt_ = min(TT, Ls)
        for t0 in range(0, Ls, tt_):
            for part in range(2):
                load_u_tile(fw, uts[part], src, part, t0, tt_, Ls)
                conv3(fw, cv[part], uts[part], tt_, sm, part)
            dve.op(lambda e: e.tensor_tensor(zb[:, 0:tt_], cv[0][:, 0:tt_], cv[1][:, 0:tt_], ALU.mult), [zb], [cv[0], cv[1]])
            for rb in range(tt_ // 128):
                r = t0 // 128 + rb
                p = pst[r % 2]
                pe.op(lambda e: e.transpose(p[:, 0:128], zb[:, rb * 128:(rb + 1) * 128], identb[:]), [p], [zb, identb])
                dst = Ztx[:, :, r, :].rearrange("p c b -> p b c")
                src_ap = p[:, 0:128].rearrange("p (b c) -> p b c", b=2)
                if r % 2 == 0:
                    dve.op(lambda e: e.tensor_copy(dst, src_ap), [Ztx], [p])
                else:
                    act.op(lambda e: e.activation(dst, src_ap, AF.Copy), [Ztx], [p])

    z_phase(uT, LQ, Zt, NB)
    z_phase(ucT, CTX, Ztc, NBC)
    fw.pop_scope()

    W = (2 * NB - 1) * 128
    X0 = (NB - 1) * 128
    tbs = [fw.sbuf("tb%d" % i, [128, W], BF16) for i in range(2)]
    tbc = fw.sbuf("tbc", [128, 16, 3 * 128], BF16)
    Y = fw.sbuf("Y", [128, NB, 2, 64], F32)
    Yc = fw.sbuf("Yc", [128, NBC, 2, 64], F32)
    psy = [fw.psum("psy%d" % i, [128, 512], F32) for i in range(2)]
    if hook_pre:
        hook_pre()
    kft = KF.t.tensor
    kfct = KFC.t.tensor
    for ch in range(64):
        tb = tbs[ch % 2]
        src = bass.AP(tensor=kft, offset=ch * 2 * LQ + 1, ap=[[1, 128], [1, W]])
        sp.dma(tb, tb[:], KF, src)
        p = psy[ch % 2]
        ds = [0] + [d for d in range(-(NB - 1), NB) if d != 0]
        for k, d in enumerate(ds):
            r0 = max(0, d)
            nb = NB - abs(d)
            pe.op(lambda e: e.matmul(p[:, r0 * 2:(r0 + nb) * 2], tb[:, X0 - 128 * d:X0 - 128 * d + 128],
                                     Zt[:, ch, r0 - d:r0 - d + nb, :].rearrange("p r b -> p (r b)"),
                                     start=(k == 0), stop=(k == len(ds) - 1), skip_group_check=True), [p], [tb, Zt])
        if ch % 2 == 0:
            dve.op(lambda e: e.tensor_copy(Y[:, :, :, ch], p[:, 0:NB * 2].rearrange('p (r b) -> p r b', b=2)), [Y], [p])
        else:
            act.op(lambda e: e.activation(Y[:, :, :, ch], p[:, 0:NB * 2].rearrange('p (r b) -> p r b', b=2), AF.Copy), [Y], [p])
        if hook_step:
            hook_step(ch)
    pc = psy[0]
    for g in range(4):
        src = bass.AP(tensor=kfct, offset=g * 16 * 2 * CTX + 1, ap=[[1, 128], [2 * CTX, 16], [1, 384]])
        sp.dma(tbc, tbc[:], KFC, src)
        for c16 in range(16):
            ch = g * 16 + c16
            o0 = ch * 4
            pe.op(lambda e: e.matmul(pc[:, o0:o0 + 4], tbc[:, c16, 128:256], Ztc[:, ch, :, :].rearrange("p r b -> p (r b)"),
                                     start=True, stop=False, skip_group_check=True), [pc], [tbc, Ztc])
            pe.op(lambda e: e.matmul(pc[:, o0 + 2:o0 + 4], tbc[:, c16, 0:128], Ztc[:, ch, 0, :],
                                     start=False, stop=False, skip_group_check=True), [pc], [tbc, Ztc])
            pe.op(lambda e: e.matmul(pc[:, o0:o0 + 2], tbc[:, c16, 256:384], Ztc[:, ch, 1, :],
                                     start=False, stop=True, skip_group_check=True), [pc], [tbc, Ztc])
    dve.op(lambda e: e.tensor_copy(Yc[:].rearrange("p r b c -> p c r b"), pc[:, 0:256].rearrange("p (c r b) -> p c r b", r=NBC, b=2)), [Yc], [pc])

    if hook_post:
        hook_post()
    fw.push_scope()
    uts = [fw.sbuf("o_ut%d" % i, [128, TT + 2], F32) for i in range(3)]
    cv = [fw.sbuf("o_cv%d" % i, [128, TT], F32) for i in range(3)]
    pso = [fw.psum("pso%d" % i, [128, 512], F32) for i in range(1)]
    ys = fw.sbuf("ys", [128, 512], F32)
    ots = [fw.sbuf("ot%d" % i, [128, 512], F32) for i in range(2)]

    def out_phase(src, Ls, Yx, scale, dstT):
        tt_ = min(TT, Ls)
        cnt = 0
        for t0 in range(0, Ls, tt_):
            for part in range(3):
                load_u_tile(fw, uts[part], src, part, t0, tt_, Ls)
                conv3(fw, cv[part], uts[part], tt_, sm, part)
            dve.op(lambda e: e.tensor_tensor(cv[0][:, 0:tt_], cv[0][:, 0:tt_], cv[1][:, 0:tt_], ALU.mult), [cv[0]], [cv[1]])
            for s0 in range(0, tt_, 512):
                ns = min(512, tt_ - s0)
                p = pso[0]
                for rb in range(ns // 128):
                    r = (t0 + s0) // 128 + rb
                    in_ap = Yx[:, r, :, :].rearrange("p b c -> p (b c)")
                    pe.op(lambda e: e.matmul(p[:, rb * 128:(rb + 1) * 128], in_ap, antif[:], start=True, stop=True), [p], [Yx, antif])
                act.op(lambda e: e.activation(ys[:, 0:ns], p[:, 0:ns], AF.Copy, scale=scale[:, 0:1]), [ys], [p, scale])
                dve.op(lambda e: e.scalar_tensor_tensor(ys[:, 0:ns], cv[0][:, s0:s0 + ns], sm[:, 12:13], ys[:, 0:ns], ALU.mult, ALU.add), [ys], [cv[0], sm])
                ot = ots[cnt % 2]
                cnt += 1
                dve.op(lambda e: e.tensor_tensor(ot[:, 0:ns], ys[:, 0:ns], cv[2][:, s0:s0 + ns], ALU.mult), [ot], [ys, cv[2]])
                sp.dma(dstT, dstT.t[:, t0 + s0:t0 + s0 + ns], ot, ot[:, 0:ns])

    out_phase(uT, LQ, Y, scaleL, oT)
    out_phase(ucT, CTX, Yc, scaleC, ocT)
    sp.wait_buf(oT)
    sp.wait_buf(ocT)
    fw.pop_scope()
    if own:
        fw.close()
    return fw.nc


def hy_feats(L):
    t01 = np.linspace(0.0, 1.0, L, dtype=np.float32)
    bands = (HY_EMB - 1) // 2
    w_ang = (2.0 * math.pi * np.arange(L, dtype=np.float32) / L).astype(np.float32)
    f = np.linspace(1e-4, bands - 1, bands, dtype=np.float32)
    ang = (f[None, :] * w_ang[:, None]).astype(np.float32)
    z = np.concatenate([t01[:, None], np.cos(ang), -np.sin(ang)], axis=-1).astype(np.float32)
    zf = np.stack([z.T, z[::-1].T])
    tt = np.stack([t01, t01[::-1]])
    return np.ascontiguousarray(zf), np.ascontiguousarray(tt)


def hy_ndelta(D_WIDTH=512):
    d = np.linspace(math.log(1e-2) / 0.3, math.log(1e-2) / 1.5, D_WIDTH, dtype=np.float32)
    return -np.abs(d)


def k2d_inputs(core, u_lat, u_ctx, conv_w, conv_b, w1, b1, w2, b2, w3, freq, bias, LQ):
    c0 = 64 * core
    DW = 512
    def pk(u, Ls):
        parts = []
        for part in range(3):
            cols = u[:, :, part * DW + c0: part * DW + c0 + 64]
            parts.append(cols.transpose(0, 2, 1).reshape(128, Ls))
        return np.ascontiguousarray(np.stack(parts))
    sm = np.zeros((128, 16), np.float32)
    for part in range(3):
        for tap in range(3):
            sm[:, part * 3 + tap] = np.tile(conv_w[tap, part * DW + c0: part * DW + c0 + 64], 2)
        sm[:, 9 + part] = np.tile(conv_b[part * DW + c0: part * DW + c0 + 64], 2)
    sm[:, 12] = np.tile(bias[c0:c0 + 64], 2)
    sm[:, 13] = np.tile(hy_ndelta()[c0:c0 + 64], 2)
    w3r = w3.reshape(64, 2, DW)[:, :, c0:c0 + 64]
    w3p = np.ascontiguousarray(np.concatenate([w3r, w3r], axis=2))
    fb = np.zeros((64, 4), np.float32)
    fb[:, 0] = freq; fb[:, 1] = b1; fb[:, 2] = b2
    zfL, t01L = hy_feats(LQ)
    zfC, t01C = hy_feats(CTX)
    return {"uT": pk(u_lat, LQ), "ucT": pk(u_ctx, CTX), "sm": sm, "w1": np.ascontiguousarray(w1), "w2": np.ascontiguousarray(w2),
            "w3": w3p, "fb": fb, "zfL": zfL, "t01L": t01L, "zfC": zfC, "t01C": t01C, "ident": np.eye(128, dtype=np.float32), "anti": np.ascontiguousarray(np.eye(128, dtype=np.float32)[::-1])}


def build_mix(LQ=8192):
    fw = FW()
    fw.engines()
    for pfx, body in (("a_", build_k2a), ("c_", build_k2c)):
        fw.pfx = pfx
        fw.push_scope()
        body(LQ, fw=fw)
        fw.pop_scope()
    st = {}

    def pre():
        fw.pfx = "b_"
        fw.push_scope()
        st["g"] = build_k2b(LQ, fw=fw, defer=True)
        fw.pfx = "d_"

    def step(ch):
        if ch % 3 == 2:
            next(st["g"], None)

    def post():
        for _ in st["g"]:
            pass
        fw.pop_scope()

    fw.pfx = "d_"
    fw.push_scope()
    build_k2d(LQ, fw=fw, hook_pre=pre, hook_step=step, hook_post=post)
    fw.pop_scope()
    fw.pfx = ""
    fw.close()
    return fw.nc


D = 2048
KC = 16
EPS = 1e-6
FH = 5632
HC = FH // 128


def build_k3(TL=2048, TC=64, with_k1=False):
    fw = FW()
    LW = TL + 2
    CW = TC + 2
    TW = LW + CW
    TO = TL + TC
    oT = fw.dram("oT", [KC, 128, TW], F32, "ExternalInput")
    xT = fw.dram("xT", [KC, 128, TW], F32, "ExternalInput")
    vec = fw.dram("vec", [128, 9, KC], F32, "ExternalInput")
    edge = fw.dram("edge", [128, 4], F32, "ExternalInput")
    w_out = fw.dram("w_out", [8, 128, KC, 256], F32, "ExternalInput")
    w_up = fw.dram("w_up", [2 * FH // 256, 128, KC, 256], F32, "ExternalInput")
    cwd = fw.dram("cw", [128, 4, 2 * HC], F32, "ExternalInput")
    w_dn = fw.dram("w_dn", [8, 128, HC, 256], F32, "ExternalInput")
    xo = fw.dram("xo", [KC, 128, TO], F32, "ExternalOutput")
    XN = fw.dram("XN", [KC, 128, TW], F32, "Internal")
    AT = fw.dram("AT", [HC, 128, TO], BF16, "Internal")
    fw.engines()
    pe, act, dve, pool, sp = fw.pe, fw.act, fw.dve, fw.pool, fw.sp
    ones = fw.sbuf("ones", [128, 128], F32)
    dve.op(lambda e: e.memset(ones[:], 1.0), [ones], [])
    vt = fw.sbuf("vt", [128, 9, KC], F32)
    sp.dma(vt, vt[:], vec, vec[:])
    eg = fw.sbuf("eg", [128, 4], F32)
    sp.dma(eg, eg[:], edge, edge[:])
    A_lat = fw.sbuf("A_lat", [128, KC], F32)
    A_ctx = fw.sbuf("A_ctx", [128, KC], F32)
    dve.op(lambda e: e.scalar_tensor_tensor(A_lat[:], vt[:, 3, :], 1.0, vt[:, 2, :], ALU.add, ALU.mult), [A_lat], [vt])
    dve.op(lambda e: e.scalar_tensor_tensor(A_ctx[:], vt[:, 5, :], 1.0, vt[:, 2, :], ALU.add, ALU.mult), [A_ctx], [vt])
    fw.push_scope()
    h2T = fw.sbuf("h2T", [128, KC, TW], BF16)

    fw.push_scope()
    wo = [fw.sbuf("wo%d" % i, [128, KC, 256], BF16) for i in range(8)]
    for i in range(8):
        pool.dma(wo[i], wo[i][:], w_out, w_out.t[i])
    ots = [fw.sbuf("a_ot%d" % i, [128, KC, 256], BF16) for i in range(2)]
    xts = [fw.sbuf("a_xt%d" % i, [128, KC, 256], F32) for i in range(2)]
    sq = fw.sbuf("a_sq", [128, KC, 256], F32)
    rs = fw.sbuf("a_rs", [128, 256], F32)
    psA = [fw.psum("a_ps%d" % i, [128, 512], F32) for i in range(4)]
    psn = fw.psum("a_psn", [128, 512], F32)
    tilesA = [(c0, 256, 0) for c0 in range(0, TL, 256)] + [(TL, 2, 0), (LW, CW, 1)]
    ov = oT.t.rearrange("k p t -> p k t")
    xv = xT.t.rearrange("k p t -> p k t")
    xnv = XN.t.rearrange("k p t -> p k t")
    cnt = 0
    for ti, (c0, n, kind) in enumerate(tilesA):
        ot, xt = ots[ti % 2], xts[ti % 2]
        g1c = 0 if kind == 0 else 1
        A2 = A_lat if kind == 0 else A_ctx
        shc = 4 if kind == 0 else 6
        pool.dma(ot, ot[:, :, 0:n], oT, ov[:, :, c0:c0 + n])
        sp.dma(xt, xt[:, :, 0:n], xT, xv[:, :, c0:c0 + n])
        for ci in range(KC):
            p = psA[cnt % 4]
            cnt += 1
            w = wo[ci // 2]
            h0 = (ci % 2) * 128
            for k in range(KC):
                pe.op(lambda e: e.matmul(p[:, 0:n], w[:, k, h0:h0 + 128], ot[:, k, 0:n], start=(k == 0), stop=(k == KC - 1)), [p], [w, ot])
            dve.op(lambda e: e.scalar_tensor_tensor(xt[:, ci, 0:n], p[:, 0:n], vt[:, g1c, ci:ci + 1], xt[:, ci, 0:n], ALU.mult, ALU.add), [xt], [p, vt])
        sp.dma(XN, xnv[:, :, c0:c0 + n], xt, xt[:, :, 0:n])
        act.op(lambda e: e.activation(sq[:, :, 0:n], xt[:, :, 0:n], AF.Square), [sq], [xt])
        for k in range(KC):
            pe.op(lambda e: e.matmul(psn[:, 0:n], ones[:], sq[:, k, 0:n], start=(k == 0), stop=(k == KC - 1)), [psn], [ones, sq])
        dve.op(lambda e: e.tensor_scalar(rs[:, 0:n], psn[:, 0:n], 1.0 / D, EPS, ALU.mult, ALU.add), [rs], [psn])
        act.op(lambda e: e.activation(rs[:, 0:n], rs[:, 0:n], AF.Sqrt), [rs], [rs])
        dve.op(lambda e: e.reciprocal(rs[:, 0:n], rs[:, 0:n]), [rs], [rs])
        for k in range(KC):
            dve.op(lambda e: e.tensor_tensor(sq[:, k, 0:n], xt[:, k, 0:n], rs[:, 0:n], ALU.mult), [sq], [xt, rs])
        for k in range(KC):
            act.op(lambda e: e.activation(h2T[:, k, c0:c0 + n], sq[:, k, 0:n], AF.Identity, bias=vt[:, shc, k:k + 1], scale=A2[:, k:k + 1]),
                   [h2T], [sq, A2, vt])
    fw.pop_scope()

    fw.push_scope()
    cw = fw.sbuf("cws", [128, 4, 2 * HC], F32)
    sp.dma(cw, cw[:], cwd, cwd[:])
    wgs = [fw.sbuf("b_wg%d" % i, [128, KC, 256], BF16) for i in range(2)]
    wus = [fw.sbuf("b_wu%d" % i, [128, KC, 256], BF16) for i in range(2)]
    psB = [fw.psum("b_ps%d" % i, [128, 512], F32) for i in range(4)]
    ug = [fw.sbuf("b_ug%d" % i, [128, 512], F32) for i in range(2)]
    uu = [fw.sbuf("b_uu%d" % i, [128, 512], F32) for i in range(2)]
    cg = [fw.sbuf("b_cg%d" % i, [128, 512], F32) for i in range(2)]
    cu = [fw.sbuf("b_cu%d" % i, [128, 512], F32) for i in range(2)]
    ab = [fw.sbuf("b_ab%d" % i, [128, 512], BF16) for i in range(2)]
    tilesB = []
    s = 0
    while s + 2 < LW:
        m = min(512, LW - s)
        tilesB.append((s, m, s, 0 if s == 0 else None, 1 if s + m == LW else None))
        s += m - 2
    tilesB.append((LW, CW, TL, 2, 3))

    def conv(dst, u, m, hc):
        dve.op(lambda e: e.tensor_scalar(dst[:, 0:m - 2], u[:, 1:m - 1], cw[:, 1, hc:hc + 1], cw[:, 3, hc:hc + 1], ALU.mult, ALU.add), [dst], [u, cw])
        dve.op(lambda e: e.scalar_tensor_tensor(dst[:, 0:m - 2], u[:, 0:m - 2], cw[:, 0, hc:hc + 1], dst[:, 0:m - 2], ALU.mult, ALU.add), [dst], [u, cw])
        dve.op(lambda e: e.scalar_tensor_tensor(dst[:, 0:m - 2], u[:, 2:m], cw[:, 2, hc:hc + 1], dst[:, 0:m - 2], ALU.mult, ALU.add), [dst], [u, cw])

    cnt = 0
    for j in range(FH // 256):
        wg, wu = wgs[j % 2], wus[j % 2]
        pool.dma(wg, wg[:], w_up, w_up.t[j])
        pool.dma(wu, wu[:], w_up, w_up.t[FH // 256 + j])
        for (s, m, o0, eL, eR) in tilesB:
            for half in range(2):
                hc = 2 * j + half
                h0 = half * 128
                i2 = cnt % 2
                cnt += 1
                pg, pu = psB[(2 * cnt) % 4], psB[(2 * cnt + 1) % 4]
                for k in range(KC):
                    pe.op(lambda e: e.matmul(pg[:, 0:m], wg[:, k, h0:h0 + 128], h2T[:, k, s:s + m], start=(k == 0), stop=(k == KC - 1)), [pg], [wg, h2T])
                for k in range(KC):
                    pe.op(lambda e: e.matmul(pu[:, 0:m], wu[:, k, h0:h0 + 128], h2T[:, k, s:s + m], start=(k == 0), stop=(k == KC - 1)), [pu], [wu, h2T])
                act.op(lambda e: e.activation(ug[i2][:, 0:m], pg[:, 0:m], AF.Copy), [ug[i2]], [pg])
                act.op(lambda e: e.activation(uu[i2][:, 0:m], pu[:, 0:m], AF.Copy), [uu[i2]], [pu])
                for ubuf in (ug[i2], uu[i2]):
                    if eL is not None:
                        dve.op(lambda e: e.tensor_scalar(ubuf[:, 0:1], ubuf[:, 0:1], eg[:, eL:eL + 1], None, ALU.mult), [ubuf], [ubuf, eg])
                    if eR is not None:
                        dve.op(lambda e: e.tensor_scalar(ubuf[:, m - 1:m], ubuf[:, m - 1:m], eg[:, eR:eR + 1], None, ALU.mult), [ubuf], [ubuf, eg])
                conv(cg[i2], ug[i2], m, hc)
                conv(cu[i2], uu[i2], m, HC + hc)
                act.op(lambda e: e.activation(cg[i2][:, 0:m - 2], cg[i2][:, 0:m - 2], AF.Silu), [cg[i2]], [cg[i2]])
                dve.op(lambda e: e.tensor_tensor(ab[i2][:, 0:m - 2], cg[i2][:, 0:m - 2], cu[i2][:, 0:m - 2], ALU.mult), [ab[i2]], [cg[i2], cu[i2]])
                sp.dma(AT, AT.t[hc, :, o0:o0 + m - 2], ab[i2], ab[i2][:, 0:m - 2])
    fw.pop_scope()
    fw.pop_scope()

    fw.push_scope()
    HALF = TO // 3
    at = fw.sbuf("c_at", [128, HC, HALF], BF16)
    wds = [fw.sbuf("c_wd%d" % i, [128, HC, 256], BF16) for i in range(2)]
    psC = [fw.psum("c_ps%d" % i, [128, 512], F32) for i in range(4)]
    xns = [fw.sbuf("c_xn%d" % i, [128, 512], F32) for i in range(3)]
    outs = [fw.sbuf("c_o%d" % i, [128, 512], F32) for i in range(3)]
    atv = AT.t.rearrange("k p t -> p k t")
    cnt = 0
    wcnt = 0
    for hf in range(3):
        h0, h1 = hf * HALF, (hf + 1) * HALF
        sp.dma(at, at[:], AT, atv[:, :, h0:h1])
        subs = []
        o = h0
        while o < h1:
            lim = TL if o < TL else TO
            n = min(512, min(h1, lim) - o)
            subs.append((o, n))
            o += n
        for ct in range(8):
            wd = wds[wcnt % 2]
            wcnt += 1
            pool.dma(wd, wd[:], w_dn, w_dn.t[ct])
            for (o0, n) in subs:
                kind = 0 if o0 < TL else 1
                xcol = o0 + 1 if kind == 0 else o0 + 3
                for h2 in range(2):
                    ci = 2 * ct + h2
                    p = psC[cnt % 4]
                    xn = xns[cnt % 3]
                    ob = outs[cnt % 3]
                    cnt += 1
                    sp.dma(xn, xn[:, 0:n], XN, XN.t[ci, :, xcol:xcol + n])
                    for k in range(HC):
                        pe.op(lambda e: e.matmul(p[:, 0:n], wd[:, k, h2 * 128:h2 * 128 + 128], at[:, k, o0 - h0:o0 - h0 + n],
                                                 start=(k == 0), stop=(k == HC - 1)), [p], [wd, at])
                    dve.op(lambda e: e.scalar_tensor_tensor(ob[:, 0:n], p[:, 0:n], vt[:, 7 + kind, ci:ci + 1], xn[:, 0:n], ALU.mult, ALU.add), [ob], [p, vt, xn])
                    sp.dma(xo, xo.t[ci, :, o0:o0 + n], ob, ob[:, 0:n])
    fw.pop_scope()
    sp.wait_buf(xo)
    if with_k1:
        fw.pfx = "k1_"
        fw.push_scope()
        build_k1(T_lat=TL, T_ctx=TC, fw=fw, xT=xo)
        fw.pop_scope()
        fw.pfx = ""
    fw.close()
    return fw.nc


def tile_w(w):
    K, N = w.shape
    return np.ascontiguousarray(w.reshape(K // 128, 128, N // 256, 256).transpose(2, 1, 0, 3))


def vec_pk(v):
    return np.ascontiguousarray(v.reshape(-1, 128).T)


OFF_QA, OFF_KA, OFF_VA, OFF_QB, OFF_KB, OFF_VB, OFF_POOL, OFF_HY = 0, 512, 1024, 1536, 2048, 2176, 2304, 2816
SEQ = 8192
NCTX = 256
_CORES = list(range(8))
_PROGS = {}
_N = {"launches": 0}


def _prog(name, builder):
    if name not in _PROGS:
        _PROGS[name] = builder()
    return _PROGS[name]


def _run(name, builder, ins):
    nc = _prog(name, builder)
    res = run_bass_kernel_spmd(nc, ins, core_ids=_CORES)
    return res.results


def _build_k3k1():
    return build_k3(with_k1=True)


def _k1_x(XT_lat, XT_ctx, k):
    b, q = divmod(k, 4)
    xT = np.concatenate([XT_lat[b][:, q * 2048:(q + 1) * 2048], XT_ctx[b][:, q * 64:(q + 1) * 64]], axis=1)
    return np.ascontiguousarray(xT.reshape(16, 128, 2112))


def _k1_in(l, k, mod, norm1_g, wt):
    b = k // 4
    m = mod[l].reshape(3, 6, -1)
    vec = np.stack([vec_pk(norm1_g[l]), vec_pk(m[b, 1]), vec_pk(m[b, 0]), vec_pk(m[2, 1]), vec_pk(m[2, 0])], axis=1)
    return {"vec": np.ascontiguousarray(vec), "w": wt}


def _seg(aT, lo, hi):
    F, S = aT.shape
    out = np.zeros((F, hi - lo + 2), np.float32)
    l2, h2 = max(lo - 1, 0), min(hi + 1, S)
    out[:, l2 - (lo - 1):h2 - (lo - 1)] = aT[:, l2:h2]
    return out


def kernel(x, c, ctx, c_ctx, w_mod, b_mod, norm1_g, norm2_g, w_in, w_out, qk_gain, diff_lam, diff_subln,
           win_sink, pool_w, pool_scale, hy_conv_w, hy_conv_b, hy_w1, hy_b1, hy_w2, hy_b2, hy_w3, hy_freq,
           hy_bias, ffn_w_in, ffn_conv_w, ffn_conv_b, ffn_w_out):
    f32 = lambda a: np.ascontiguousarray(np.asarray(a), dtype=np.float32)
    (x, c, ctx, c_ctx, w_mod, b_mod, norm1_g, norm2_g, w_in, w_out, qk_gain, diff_lam, diff_subln, win_sink, pool_w,
     pool_scale, hy_conv_w, hy_conv_b, hy_w1, hy_b1, hy_w2, hy_b2, hy_w3, hy_freq, hy_bias, ffn_w_in, ffn_conv_w,
     ffn_conv_b, ffn_w_out) = [f32(a) for a in (x, c, ctx, c_ctx, w_mod, b_mod, norm1_g, norm2_g, w_in, w_out, qk_gain,
                                                diff_lam, diff_subln, win_sink, pool_w, pool_scale, hy_conv_w, hy_conv_b,
                                                hy_w1, hy_b1, hy_w2, hy_b2, hy_w3, hy_freq, hy_bias, ffn_w_in, ffn_conv_w,
                                                ffn_conv_b, ffn_w_out)]
    B, L, Dm = x.shape
    NL = w_mod.shape[0]
    T = L + NCTX
    r = _run("k0", build_k0, [k0_inputs(k, c, c_ctx, w_mod, b_mod) for k in _CORES])
    mod = np.concatenate([r[k]["mod"] for k in _CORES], axis=2)
    XT_lat = [np.ascontiguousarray(x[b].T) for b in range(B)]
    XT_ctx = [np.ascontiguousarray(ctx[b].T) for b in range(B)]
    CSt = rope_tables(L, NCTX, 2)
    mk = band_masks()
    zfL, t01L = hy_feats(L)
    zfC, t01C = hy_feats(NCTX)
    ident = np.eye(128, dtype=np.float32)
    anti = np.ascontiguousarray(ident[::-1])
    ndel = hy_ndelta()
    for l in range(NL):
        m = mod[l].reshape(3, 6, Dm)
        wt_in = tile_w(w_in[l]) if l == 0 else None
        if l == 0:
            r = _run("k1", build_k1, [dict(xT=_k1_x(XT_lat, XT_ctx, k), **_k1_in(l, k, mod, norm1_g, wt_in)) for k in _CORES])
            pts = [r[k]["pT"].reshape(4352, 2112) for k in _CORES]
            del r
        PT_lat = [np.concatenate([pts[b * 4 + q][:, :2048] for q in range(4)], axis=1) for b in range(B)]
        PT_ctx = [np.concatenate([pts[b * 4 + q][:, 2048:] for q in range(4)], axis=1) for b in range(B)]
        del pts
        OT_lat = [np.zeros((Dm, L), np.float32) for _ in range(B)]
        OT_ctx = [np.zeros((Dm, NCTX), np.float32) for _ in range(B)]
        lambda_init = 0.8 - 0.6 * math.exp(-0.3 * l)
        g0 = np.tile(qk_gain[l, 0], 2)
        g1 = np.tile(qk_gain[l, 1], 2)
        ins_a = []
        for k in _CORES:
            b, h = divmod(k, 4)
            rq = slice(OFF_QA + h * 128, OFF_QA + (h + 1) * 128)
            rk = slice(OFF_KA + h * 128, OFF_KA + (h + 1) * 128)
            rv = slice(OFF_VA + h * 128, OFF_VA + (h + 1) * 128)
            q_full = np.ascontiguousarray(np.concatenate([PT_lat[b][rq], PT_ctx[b][rq]], axis=1))
            k_full = np.ascontiguousarray(np.concatenate([PT_ctx[b][rk], PT_lat[b][rk]], axis=1))
            v_full = np.concatenate([PT_ctx[b][rv], PT_lat[b][rv]], axis=1).T
            sm = np.zeros((128, 8), np.float32)
            sm[:, 0] = g0
            sm[:, 1] = swap_halves(g0[:, None])[:, 0]
            sm[:, 2] = g1
            sm[:, 3] = swap_halves(g1[:, None])[:, 0]
            sm[:, 4] = diff_subln[l]
            sm[:, 5] = lambda_init
            sm[:, 6] = 1.0 - lambda_init
            ins_a.append({"qT": q_full, "qsT": swap_halves(q_full), "kT": k_full, "ksT": swap_halves(k_full),
                        "v": np.ascontiguousarray(v_full.reshape(T // 128, 128, 128).transpose(1, 0, 2)),
                        "CS": CSt, "sm": sm, "lamp": np.ascontiguousarray(diff_lam[l].reshape(1, 256))})
        g2 = np.tile(qk_gain[l, 2], 2)
        g3 = np.tile(qk_gain[l, 3], 2)
        ins_b = []
        for k in _CORES:
            b = k // 4
            h0 = 2 * (k % 4)
            kvh = (k % 4) // 2
            rq = slice(OFF_QB + h0 * 64, OFF_QB + (h0 + 2) * 64)
            rk = slice(OFF_KB + kvh * 64, OFF_KB + (kvh + 1) * 64)
            rv = slice(OFF_VB + kvh * 64, OFF_VB + (kvh + 1) * 64)
            q_full = np.ascontiguousarray(np.concatenate([PT_lat[b][rq], PT_ctx[b][rq]], axis=1))
            k1 = np.concatenate([PT_ctx[b][rk], PT_lat[b][rk]], axis=1)
            k_full = np.ascontiguousarray(np.concatenate([k1, k1], axis=0))
            v_full = np.concatenate([PT_ctx[b][rv], PT_lat[b][rv]], axis=1).T
            sm = np.zeros((128, 8), np.float32)
            sm[:, 0] = g2
            sm[:, 1] = swap_halves(g2[:, None])[:, 0]
            sm[:, 2] = g3
            sm[:, 3] = swap_halves(g3[:, None])[:, 0]
            sm[:, 4] = win_sink[l, h0]
            sm[:, 5] = win_sink[l, h0 + 1]
            ins_b.append({"qT": q_full, "qsT": swap_halves(q_full), "kT": k_full, "ksT": swap_halves(k_full),
                        "v": np.ascontiguousarray(v_full.reshape(T // 128, 128, 64).transpose(1, 0, 2)),
                        "CS": CSt, "sm": sm, "masks": mk})
        ins_c = []
        for k in _CORES:
            b, g = divmod(k, 4)
            ru = slice(OFF_POOL + g * 128, OFF_POOL + (g + 1) * 128)
            sel, ic = pool_consts(g, L)
            ins_c.append({"uT": np.ascontiguousarray(np.concatenate([PT_lat[b][ru], PT_ctx[b][ru]], axis=1)), "wsel": sel, "invc": ic,
                        "wl": np.ascontiguousarray(pool_w[l, g]), "ls": np.ascontiguousarray(pool_scale[l, g * 128:(g + 1) * 128, None])})
        ins_d = []
        for k in _CORES:
            c0 = 64 * k
            uL, uC = [], []
            smh = np.zeros((128, 16), np.float32)
            for part in range(3):
                rr = slice(OFF_HY + part * 512 + c0, OFF_HY + part * 512 + c0 + 64)
                uL.append(np.concatenate([PT_lat[0][rr], PT_lat[1][rr]], axis=0))
                uC.append(np.concatenate([PT_ctx[0][rr], PT_ctx[1][rr]], axis=0))
                for tap in range(3):
                    smh[:, part * 3 + tap] = np.tile(hy_conv_w[l, tap, part * 512 + c0: part * 512 + c0 + 64], 2)
                smh[:, 9 + part] = np.tile(hy_conv_b[l, part * 512 + c0: part * 512 + c0 + 64], 2)
            smh[:, 12] = np.tile(hy_bias[l, c0:c0 + 64], 2)
            smh[:, 13] = np.tile(ndel[c0:c0 + 64], 2)
            w3r = hy_w3[l].reshape(64, 2, 512)[:, :, c0:c0 + 64]
            fb = np.zeros((64, 4), np.float32)
            fb[:, 0] = hy_freq[l]
            fb[:, 1] = hy_b1[l]
            fb[:, 2] = hy_b2[l]
            ins_d.append({"uT": np.ascontiguousarray(np.stack(uL)), "ucT": np.ascontiguousarray(np.stack(uC)), "sm": smh,
                        "w1": np.ascontiguousarray(hy_w1[l]), "w2": np.ascontiguousarray(hy_w2[l]),
                        "w3": np.ascontiguousarray(np.concatenate([w3r, w3r], axis=2)), "fb": fb,
                        "zfL": zfL, "t01L": t01L, "zfC": zfC, "t01C": t01C, "ident": ident, "anti": anti})
        ins = []
        for k in _CORES:
            dct = {}
            for pfx, lst in (("a_", ins_a), ("b_", ins_b), ("c_", ins_c), ("d_", ins_d)):
                for kk, vv in lst[k].items():
                    dct[pfx + kk] = vv
            ins.append(dct)
        del ins_a, ins_b, ins_c, ins_d
        r = _run("mix", build_mix, ins)
        for k in _CORES:
            b, h = divmod(k, 4)
            o = r[k]["a_oT"]
            OT_lat[b][h * 128:(h + 1) * 128] = o[:, :L]
            OT_ctx[b][h * 128:(h + 1) * 128] = o[:, L:]
            h0 = 2 * (k % 4)
            o = r[k]["b_oT"].reshape(128, T)
            OT_lat[b][512 + h0 * 64:512 + (h0 + 2) * 64] = o[:, :L]
            OT_ctx[b][512 + h0 * 64:512 + (h0 + 2) * 64] = o[:, L:]
            g = k % 4
            o = r[k]["c_yT"]
            OT_lat[b][1024 + g * 128:1024 + (g + 1) * 128] = o[:, :L]
            OT_ctx[b][1024 + g * 128:1024 + (g + 1) * 128] = o[:, L:]
            c0 = 64 * k
            o = r[k]["d_oT"]
            oc = r[k]["d_ocT"]
            for bb in range(B):
                OT_lat[bb][1536 + c0:1536 + c0 + 64] = o[bb * 64:(bb + 1) * 64]
                OT_ctx[bb][1536 + c0:1536 + c0 + 64] = oc[bb * 64:(bb + 1) * 64]
        del r, ins
        del PT_lat, PT_ctx
        wo_t = tile_w(w_out[l])
        wt_next = tile_w(w_in[l + 1]) if l + 1 < NL else None
        wu_t = tile_w(ffn_w_in[l])
        wd_t = tile_w(ffn_w_out[l])
        cwp = np.ascontiguousarray(np.stack([vec_pk(ffn_conv_w[l, 0]), vec_pk(ffn_conv_w[l, 1]), vec_pk(ffn_conv_w[l, 2]),
                                             vec_pk(ffn_conv_b[l])], axis=1))
        ins = []
        for k in _CORES:
            b, q = divmod(k, 4)
            ocol = np.concatenate([_seg(OT_lat[b], q * 2048, (q + 1) * 2048), _seg(OT_ctx[b], q * 64, (q + 1) * 64)], axis=1)
            xcol = np.concatenate([_seg(XT_lat[b], q * 2048, (q + 1) * 2048), _seg(XT_ctx[b], q * 64, (q + 1) * 64)], axis=1)
            TW = ocol.shape[1]
            vec = np.stack([vec_pk(m[b, 2]), vec_pk(m[2, 2]), vec_pk(norm2_g[l]), vec_pk(m[b, 4]), vec_pk(m[b, 3]),
                            vec_pk(m[2, 4]), vec_pk(m[2, 3]), vec_pk(m[b, 5]), vec_pk(m[2, 5])], axis=1)
            eg = np.ones((128, 4), np.float32)
            if q == 0:
                eg[:, 0] = 0
                eg[:, 2] = 0
            if q == 3:
                eg[:, 1] = 0
                eg[:, 3] = 0
            dct = {"oT": np.ascontiguousarray(ocol.reshape(16, 128, TW)), "xT": np.ascontiguousarray(xcol.reshape(16, 128, TW)),
                   "vec": np.ascontiguousarray(vec), "edge": eg, "w_out": wo_t, "w_up": wu_t, "cw": cwp, "w_dn": wd_t}
            if l + 1 < NL:
                for kk, vv in _k1_in(l + 1, k, mod, norm1_g, wt_next).items():
                    dct["k1_" + kk] = vv
            ins.append(dct)
        if l + 1 < NL:
            r = _run("k3k1", _build_k3k1, ins)
            pts = [r[k]["k1_pT"].reshape(4352, 2112) for k in _CORES]
        else:
            r = _run("k3", build_k3, ins)
        for k in _CORES:
            b, q = divmod(k, 4)
            o = r[k]["xo"].reshape(Dm, 2112)
            XT_lat[b][:, q * 2048:(q + 1) * 2048] = o[:, :2048]
            XT_ctx[b][:, q * 64:(q + 1) * 64] = o[:, 2048:]
        del r, ins
    return np.ascontiguousarray(np.stack([XT_lat[b].T for b in range(B)])).astype(np.float32)
```

```python
import math
import numpy as np
import time
from contextlib import ExitStack
import concourse.bass as bass
import concourse.mybir as mybir
from concourse.bass_utils import run_bass_kernel_spmd

F32 = mybir.dt.float32
BF16 = mybir.dt.bfloat16
AF = mybir.ActivationFunctionType
ALU = mybir.AluOpType
AX = mybir.AxisListType


class Buf:
    __slots__ = ("name", "t", "w", "r", "sem", "dcount")

    def __init__(self, name, t):
        self.name = name
        self.t = t
        self.w = {}
        self.r = {}
        self.sem = None
        self.dcount = 0

    def __getitem__(self, idx):
        return self.t[idx]


class Eng:
    def __init__(self, fw, name, eng, sem, kind):
        self.fw = fw
        self.name = name
        self.eng = eng
        self.sem = sem
        self.kind = kind
        self.count = 0
        self.seen = {}

    def _wait(self, deps):
        for key, (sem, val) in deps.items():
            if self.seen.get(key, 0) >= val:
                continue
            if sem is self.sem and self.kind == 'pe':
                continue
            self.eng.wait_ge(sem, val)
            self.seen[key] = val

    def _deps(self, outs, ins):
        deps = {}
        for b in ins:
            for k, (s, v) in b.w.items():
                if deps.get(k, (None, 0))[1] < v:
                    deps[k] = (s, v)
        for b in outs:
            for d in (b.w, b.r):
                for k, (s, v) in d.items():
                    if deps.get(k, (None, 0))[1] < v:
                        deps[k] = (s, v)
        return deps

    def op(self, inst_fn, outs, ins):
        self._wait(self._deps(outs, ins))
        inst = inst_fn(self.eng)
        self.count += 1
        inst.then_inc(self.sem, 1)
        key = id(self.sem)
        tok = (self.sem, self.count)
        for b in ins:
            b.r[key] = tok
        for b in outs:
            b.w = {key: tok}
            b.r = {}
        return tok

    def dma(self, out_buf, out_ap, in_buf, in_ap, **kw):
        self._wait(self._deps([out_buf], [in_buf]))
        if out_buf.sem is None:
            out_buf.sem = self.fw.new_sem("d_" + out_buf.name)
        inst = self.eng.dma_start(out=out_ap, in_=in_ap, **kw)
        out_buf.dcount += 16
        inst.then_inc(out_buf.sem, 16)
        key = id(out_buf.sem)
        tok = (out_buf.sem, out_buf.dcount)
        in_buf.r[key] = tok
        out_buf.w = {key: tok}
        out_buf.r = {}
        return tok

    def wait_buf(self, b):
        self._wait(dict(b.w))


class FW:
    def __init__(self, name="k"):
        self.nc = bass.Bass("TRN2", target_bir_lowering=False)
        self.es = ExitStack()
        self.nsem = 0
        self.block = None
        self.all_bufs = []
        self.scopes = []
        self.pfx = ""

    def new_sem(self, name):
        self.nsem += 1
        return self.es.enter_context(self.nc.semaphore(name + "_%d" % self.nsem))

    def dram(self, name, shape, dtype, kind):
        name = self.pfx + name
        t = self.nc.dram_tensor(name, list(shape), dtype, kind=kind)
        b = Buf(name, t.ap())
        self.all_bufs.append(b)
        return b

    def sbuf(self, name, shape, dtype):
        name = self.pfx + name
        t = (self.scopes[-1] if self.scopes else self.es).enter_context(self.nc.sbuf_tensor(name, list(shape), dtype))
        b = Buf(name, t)
        self.all_bufs.append(b)
        return b

    def psum(self, name, shape, dtype=F32):
        name = self.pfx + name
        t = (self.scopes[-1] if self.scopes else self.es).enter_context(self.nc.psum_tensor(name, list(shape), dtype))
        b = Buf(name, t)
        self.all_bufs.append(b)
        return b

    def engines(self):
        nc = self.nc
        self.pe = Eng(self, "pe", nc.tensor, self.new_sem("pe"), 'pe')
        self.act = Eng(self, "act", nc.scalar, self.new_sem("act"), 'act')
        self.dve = Eng(self, "dve", nc.vector, self.new_sem("dve"), 'dve')
        self.pool = Eng(self, "pool", nc.gpsimd, self.new_sem("pool"), 'pool')
        self.sp = Eng(self, "sp", nc.sync, self.new_sem("sp"), 'sp')
        return self.pe, self.act, self.dve, self.pool, self.sp

    def push_scope(self):
        self.scopes.append(ExitStack())

    def pop_scope(self):
        fw_barrier(self)
        self.scopes.pop().close()

    def close(self):
        self.es.close()


def sub_bufs(parent, aps, prefix):
    return [Buf("%s%d" % (prefix, i), ap) for i, ap in enumerate(aps)]


def fw_barrier(fw, bufs=()):
    engs = [fw.pe, fw.act, fw.dve, fw.pool, fw.sp]
    toks = {}
    for e in engs:
        if e.count > 0:
            toks[id(e.sem)] = (e.sem, e.count)
    for b in fw.all_bufs:
        if b.sem is not None and b.dcount > 0:
            toks[id(b.sem)] = (b.sem, b.dcount)
    for e in engs:
        e._wait(dict(toks))


D = 2048
KC = D // 128
EPS = 1e-6


def load_consts(fw, sp):
    ones = fw.sbuf("ones", [128, 128], F32)
    fw.dve.op(lambda e: e.memset(ones[:], 1.0), [ones], [])
    return ones


def norm_mod_phase(fw, xT, hT, tiles, vecs, ones, nm):
    pe, act, dve, pool, sp = fw.pe, fw.act, fw.dve, fw.pool, fw.sp
    xts = [fw.sbuf("%s_xt%d" % (nm, i), [128, KC, 512], F32) for i in range(2)]
    sqs = [fw.sbuf("%s_sq%d" % (nm, i), [128, KC, 512], F32) for i in range(1)]
    rstd = [fw.sbuf("%s_rstd%d" % (nm, i), [128, 512], F32) for i in range(2)]
    ps = [fw.psum("%s_ps%d" % (nm, i), [128, 512], F32) for i in range(2)]
    xv = xT.t.rearrange("k p t -> p k t")
    for i, (t0, n, A, sh) in enumerate(tiles):
        xt, sq, rs, p = xts[i % 2], sqs[0], rstd[i % 2], ps[i % 2]
        sp.dma(xt, xt[:, :, 0:n], xT, xv[:, :, t0:t0 + n])
        act.op(lambda e: e.activation(sq[:, :, 0:n], xt[:, :, 0:n], AF.Square), [sq], [xt])
        for kc in range(KC):
            pe.op(lambda e: e.matmul(p[:, 0:n], ones[:], sq[:, kc, 0:n], start=(kc == 0), stop=(kc == KC - 1)), [p], [ones, sq])
        dve.op(lambda e: e.tensor_scalar(rs[:, 0:n], p[:, 0:n], 1.0 / D, EPS, ALU.mult, ALU.add), [rs], [p])
        act.op(lambda e: e.activation(rs[:, 0:n], rs[:, 0:n], AF.Sqrt), [rs], [rs])
        dve.op(lambda e: e.reciprocal(rs[:, 0:n], rs[:, 0:n]), [rs], [rs])
        for kc in range(KC):
            dve.op(lambda e: e.tensor_tensor(sq[:, kc, 0:n], xt[:, kc, 0:n], rs[:, 0:n], ALU.mult), [sq], [xt, rs])
        for kc in range(KC):
            act.op(lambda e: e.activation(hT[:, kc, t0:t0 + n], sq[:, kc, 0:n], AF.Identity,
                                          bias=sh[:, kc:kc + 1], scale=A[:, kc:kc + 1]), [hT], [sq, A, sh])


def linear_phase(fw, hT, KCn, w, ncoltiles, tiles, epilogue, nm, nbuf=3):
    pe, pool = fw.pe, fw.pool
    wts = [fw.sbuf("%s_w%d" % (nm, i), [128, KCn, 256], BF16) for i in range(nbuf)]
    ps = [fw.psum("%s_lp%d" % (nm, i), [128, 512], F32) for i in range(4)]
    cnt = 0
    for ct in range(ncoltiles):
        wt = wts[ct % nbuf]
        pool.dma(wt, wt[:], w, w.t[ct])
        for ti, (t0, n) in enumerate(tiles):
            for half in range(2):
                p = ps[cnt % 4]
                cnt += 1
                for kc in range(KCn):
                    pe.op(lambda e: e.matmul(p[:, 0:n], wt[:, kc, half * 128:(half + 1) * 128], hT[:, kc, t0:t0 + n],
                                             start=(kc == 0), stop=(kc == KCn - 1)), [p], [wt, hT])
                epilogue(ct * 2 + half, ti, t0, n, p)


def build_k1(T_lat=2048, T_ctx=64, NCOL=4352, fw=None, xT=None):
    own = fw is None
    if own:
        fw = FW()
    T = T_lat + T_ctx
    if xT is None:
        xT = fw.dram("xT", [KC, 128, T], F32, "ExternalInput")
    vec = fw.dram("vec", [128, 5, KC], F32, "ExternalInput")
    w = fw.dram("w", [NCOL // 256, 128, KC, 256], F32, "ExternalInput")
    pT = fw.dram("pT", [NCOL // 128, 128, T], F32, "ExternalOutput")
    if own:
        fw.engines()
    pe, act, dve, pool, sp = fw.pe, fw.act, fw.dve, fw.pool, fw.sp
    ones = load_consts(fw, sp)
    vt = fw.sbuf("vt", [128, 5, KC], F32)
    sp.dma(vt, vt[:], vec, vec[:])
    A_lat = fw.sbuf("A_lat", [128, KC], F32)
    A_ctx = fw.sbuf("A_ctx", [128, KC], F32)
    sh_lat = fw.sbuf("sh_lat", [128, KC], F32)
    sh_ctx = fw.sbuf("sh_ctx", [128, KC], F32)
    dve.op(lambda e: e.scalar_tensor_tensor(A_lat[:], vt[:, 1, :], 1.0, vt[:, 0, :], ALU.add, ALU.mult), [A_lat], [vt])
    dve.op(lambda e: e.scalar_tensor_tensor(A_ctx[:], vt[:, 3, :], 1.0, vt[:, 0, :], ALU.add, ALU.mult), [A_ctx], [vt])
    dve.op(lambda e: e.tensor_copy(sh_lat[:], vt[:, 2, :]), [sh_lat], [vt])
    dve.op(lambda e: e.tensor_copy(sh_ctx[:], vt[:, 4, :]), [sh_ctx], [vt])
    hT = fw.sbuf("hT", [128, KC, T], BF16)
    tiles = [(t0, 512, A_lat, sh_lat) for t0 in range(0, T_lat, 512)]
    if T_ctx:
        tiles.append((T_lat, T_ctx, A_ctx, sh_ctx))
    norm_mod_phase(fw, xT, hT, tiles, None, ones, "n1")
    ots = [fw.sbuf("ot%d" % i, [128, 512], F32) for i in range(4)]
    st = {"i": 0}

    def epi(ci, ti, t0, n, p):
        ot = ots[st["i"] % 4]
        if st["i"] % 2 == 0:
            dve.op(lambda e: e.tensor_copy(ot[:, 0:n], p[:, 0:n]), [ot], [p])
        else:
            act.op(lambda e: e.activation(ot[:, 0:n], p[:, 0:n], AF.Copy), [ot], [p])
        st["i"] += 1
        sp.dma(pT, pT.t[ci, :, t0:t0 + n], ot, ot[:, 0:n])

    linear_phase(fw, hT, KC, w, NCOL // 256, [(t[0], t[1]) for t in tiles], epi, "l1")
    sp.wait_buf(pT)
    if own:
        fw.close()
    return fw.nc


def tile_w(w):
    K, N = w.shape
    return np.ascontiguousarray(w.reshape(K // 128, 128, N // 256, 256).transpose(2, 1, 0, 3))


def vec_pk(v):
    return np.ascontiguousarray(v.reshape(KC, 128).T)


D = 2048
KC = 16
NCOLS = 1536


def build_k0(NL=4):
    fw = FW()
    cT = fw.dram("cT", [128, KC, 3], F32, "ExternalInput")
    w = fw.dram("w", [NL, 3, 128, KC, 512], F32, "ExternalInput")
    bm = fw.dram("bm", [NL, NCOLS], F32, "ExternalInput")
    mod = fw.dram("mod", [NL, 3, NCOLS], F32, "ExternalOutput")
    fw.engines()
    pe, act, dve, pool, sp = fw.pe, fw.act, fw.dve, fw.pool, fw.sp
    ct = fw.sbuf("ct", [128, KC, 3], F32)
    sp.dma(ct, ct[:], cT, cT[:])
    st = fw.sbuf("st", [128, KC, 3], F32)
    act.op(lambda e: e.activation(st[:], ct[:], AF.Silu), [st], [ct])
    wts = [fw.sbuf("wt%d" % i, [128, KC, 512], F32) for i in range(2)]
    bts = [fw.sbuf("bt%d" % i, [3, 512], F32) for i in range(2)]
    ots = [fw.sbuf("ot%d" % i, [3, 512], F32) for i in range(2)]
    ps = [fw.psum("ps%d" % i, [128, 512], F32) for i in range(2)]
    i = 0
    for l in range(NL):
        for t in range(3):
            wt, bt, ot, p = wts[i % 2], bts[i % 2], ots[i % 2], ps[i % 2]
            i += 1
            (sp if i % 2 == 0 else act).dma(wt, wt[:], w, w.t[l, t])
            bsrc = bass.AP(tensor=bm.t.tensor, offset=l * NCOLS + t * 512, ap=[[0, 3], [1, 512]])
            sp.dma(bt, bt[:], bm, bsrc)
            for k in range(KC):
                pe.op(lambda e: e.matmul(p[0:3, :], st[:, k, :], wt[:, k, :], start=(k == 0), stop=(k == KC - 1)), [p], [st, wt])
            dve.op(lambda e: e.tensor_tensor(ot[:], p[0:3, :], bt[:], ALU.add), [ot], [p, bt])
            sp.dma(mod, mod.t[l, :, t * 512:(t + 1) * 512], ot, ot[:])
    sp.wait_buf(mod)
    fw.close()
    return fw.nc


def k0_inputs(core, c, c_ctx, w_mod, b_mod):
    NL = w_mod.shape[0]
    cs = np.stack([c[0], c[1], c_ctx], axis=1)
    cT = np.ascontiguousarray(cs.reshape(KC, 128, 3).transpose(1, 0, 2))
    cols = slice(core * NCOLS, (core + 1) * NCOLS)
    w = w_mod[:, :, cols].reshape(NL, KC, 128, 3, 512).transpose(0, 3, 2, 1, 4)
    return {"cT": cT, "w": np.ascontiguousarray(w), "bm": np.ascontiguousarray(b_mod[:, cols])}


EPS = 1e-6
CTX = 256


def qknorm_rope(fw, xT, xsT, CS, gcol, dst, col_map, tiles, blockones, nm, inv_n):
    pe, act, dve, pool, sp = fw.pe, fw.act, fw.dve, fw.pool, fw.sp
    gbuf, c0 = gcol
    if "qk_scr" not in fw.__dict__:
        fw.qk_scr = dict(
            xt=[fw.sbuf("qk_x%d" % i, [128, 512], F32) for i in range(2)],
            xs=[fw.sbuf("qk_xs%d" % i, [128, 512], F32) for i in range(2)],
            ct=[fw.sbuf("qk_c%d" % i, [128, 512], F32) for i in range(2)],
            st=[fw.sbuf("qk_s%d" % i, [128, 512], F32) for i in range(2)],
            sq=[fw.sbuf("qk_sq%d" % i, [128, 512], F32) for i in range(2)],
            rs=[fw.sbuf("qk_rs%d" % i, [128, 512], F32) for i in range(2)],
            ps=[fw.psum("qk_ps%d" % i, [128, 512], F32) for i in range(2)])
    xt, xs, ct, st, sq, rs, ps = (fw.qk_scr[k] for k in ("xt", "xs", "ct", "st", "sq", "rs", "ps"))
    for i, (x0, tb0, n) in enumerate(tiles):
        j = i % 2
        sp.dma(xt[j], xt[j][:, 0:n], xT, xT.t[:, x0:x0 + n])
        sp.dma(xs[j], xs[j][:, 0:n], xsT, xsT.t[:, x0:x0 + n])
        sp.dma(ct[j], ct[j][:, 0:n], CS, CS.t[0, :, tb0:tb0 + n])
        sp.dma(st[j], st[j][:, 0:n], CS, CS.t[1, :, tb0:tb0 + n])
        act.op(lambda e: e.activation(sq[j][:, 0:n], xt[j][:, 0:n], AF.Square), [sq[j]], [xt[j]])
        pe.op(lambda e: e.matmul(ps[j][:, 0:n], blockones[:], sq[j][:, 0:n], start=True, stop=True), [ps[j]], [blockones, sq[j]])
        dve.op(lambda e: e.tensor_scalar(rs[j][:, 0:n], ps[j][:, 0:n], inv_n, EPS, ALU.mult, ALU.add), [rs[j]], [ps[j]])
        act.op(lambda e: e.activation(rs[j][:, 0:n], rs[j][:, 0:n], AF.Sqrt), [rs[j]], [rs[j]])
        dve.op(lambda e: e.reciprocal(rs[j][:, 0:n], rs[j][:, 0:n]), [rs[j]], [rs[j]])
        dve.op(lambda e: e.scalar_tensor_tensor(ct[j][:, 0:n], xt[j][:, 0:n], gbuf[:, c0:c0 + 1], ct[j][:, 0:n], ALU.mult, ALU.mult),
               [ct[j]], [xt[j], gbuf])
        dve.op(lambda e: e.scalar_tensor_tensor(st[j][:, 0:n], xs[j][:, 0:n], gbuf[:, c0 + 1:c0 + 2], st[j][:, 0:n], ALU.mult, ALU.mult),
                [st[j]], [xs[j], gbuf])
        dve.op(lambda e: e.tensor_tensor(ct[j][:, 0:n], ct[j][:, 0:n], st[j][:, 0:n], ALU.add), [ct[j]], [st[j]])
        dve.op(lambda e: e.tensor_tensor(dst[:, x0:x0 + n], ct[j][:, 0:n], rs[j][:, 0:n], ALU.mult), [dst], [ct[j], rs[j]])


def build_k2a(LQ=8192, fw=None):
    own = fw is None
    if own:
        fw = FW()
    T = LQ + CTX
    NKC = T // 128
    qT = fw.dram("qT", [128, T], F32, "ExternalInput")
    qsT = fw.dram("qsT", [128, T], F32, "ExternalInput")
    kT = fw.dram("kT", [128, T], F32, "ExternalInput")
    ksT = fw.dram("ksT", [128, T], F32, "ExternalInput")
    v = fw.dram("v", [128, NKC, 128], F32, "ExternalInput")
    CS = fw.dram("CS", [2, 128, T], F32, "ExternalInput")
    sm = fw.dram("sm", [128, 8], F32, "ExternalInput")
    lamp = fw.dram("lamp", [1, 256], F32, "ExternalInput")
    oT = fw.dram("oT", [128, T], F32, "ExternalOutput")
    if own:
        fw.engines()
    pe, act, dve, pool, sp = fw.pe, fw.act, fw.dve, fw.pool, fw.sp
    ones = fw.sbuf("ones", [128, 128], F32)
    dve.op(lambda e: e.memset(ones[:], 1.0), [ones], [])
    onesb = fw.sbuf("onesb", [128, 128], BF16)
    dve.op(lambda e: e.memset(onesb[:], 1.0), [onesb], [])
    blockones = fw.sbuf("blockones", [128, 128], F32)
    dve.op(lambda e: e.memset(blockones[:], 0.0), [blockones], [])
    dve.op(lambda e: e.memset(blockones[0:64, 0:64], 1.0), [blockones], [])
    dve.op(lambda e: e.memset(blockones[64:128, 64:128], 1.0), [blockones], [])
    smt = fw.sbuf("smt", [128, 8], F32)
    sp.dma(smt, smt[:], sm, sm[:])
    lt = fw.sbuf("lt", [1, 256], F32)
    sp.dma(lt, lt[:], lamp, lamp[:])
    lw = fw.sbuf("lw", [1, 128], F32)
    dve.op(lambda e: e.tensor_tensor(lw[:, 0:64], lt[:, 0:64], lt[:, 64:128], ALU.mult), [lw], [lt])
    dve.op(lambda e: e.tensor_tensor(lw[:, 64:128], lt[:, 128:192], lt[:, 192:256], ALU.mult), [lw], [lt])
    l2 = fw.sbuf("l2", [1, 4], F32)
    dve.op(lambda e: e.reduce_sum(l2[:, 0:1], lw[:, 0:64], AX.X), [l2], [lw])
    dve.op(lambda e: e.reduce_sum(l2[:, 1:2], lw[:, 64:128], AX.X), [l2], [lw])
    act.op(lambda e: e.activation(l2[:, 0:2], l2[:, 0:2], AF.Exp), [l2], [l2])
    dve.op(lambda e: e.tensor_tensor(l2[:, 2:3], l2[:, 1:2], l2[:, 0:1], ALU.subtract), [l2], [l2])
    dve.op(lambda e: e.tensor_tensor(l2[:, 2:3], l2[:, 2:3], smt[0:1, 5:6], ALU.subtract), [l2], [l2, smt])
    sc = fw.sbuf("sc", [128, 2], F32)
    fw.push_scope()
    psl = fw.psum("psl", [128, 512], F32)
    pe.op(lambda e: e.matmul(psl[:, 0:1], ones[0:1, :], l2[0:1, 2:3], start=True, stop=True), [psl], [ones, l2])
    dve.op(lambda e: e.tensor_copy(sc[:, 0:1], psl[:, 0:1]), [sc], [psl])
    dve.op(lambda e: e.tensor_tensor(sc[:, 1:2], smt[:, 4:5], smt[:, 6:7], ALU.mult), [sc], [smt])
    fw.pop_scope()

    QT = fw.sbuf("QT", [128, T], BF16)
    KT = fw.sbuf("KT", [128, T], BF16)
    V = fw.sbuf("V", [128, NKC, 128], BF16)
    pool.dma(V, V[:], v, v[:])
    qtiles = [(t0, t0, 512) for t0 in range(0, LQ, 512)] + [(LQ, LQ, CTX)]
    ktiles = [(0, LQ, CTX)] + [(CTX + t0, t0, 512) for t0 in range(0, LQ, 512)]
    fw.push_scope()
    qknorm_rope(fw, qT, qsT, CS, (smt, 0), QT, None, qtiles, blockones, "qn", 1.0 / 64)
    qknorm_rope(fw, kT, ksT, CS, (smt, 2), KT, None, ktiles, blockones, "kn", 1.0 / 64)
    del fw.qk_scr
    fw.pop_scope()

    NSB = 2
    NPT = 4
    ps_s = [[fw.psum("ps_s%d_%d" % (m, i), [128, 512], F32) for i in range(NSB)] for m in range(2)]
    ps_o = [fw.psum("ps_o%d" % m, [128, 512], F32) for m in range(2)]
    ps_z = [fw.psum("ps_z%d" % m, [128, 512], F32) for m in range(2)]
    pts = [[fw.sbuf("pt%d_%d" % (m, i), [128, 512], BF16) for i in range(NPT)] for m in range(2)]
    om = [fw.sbuf("om%d" % i, [128, 512], F32) for i in range(2)]
    zacc = [fw.sbuf("zacc%d" % i, [128, 512], F32) for i in range(2)]
    rz = [fw.sbuf("rz%d" % i, [128, 512], F32) for i in range(2)]
    osq = fw.sbuf("osq", [128, 512], F32)
    ors = fw.sbuf("ors", [128, 512], F32)
    ots = [fw.sbuf("ot%d" % i, [128, 512], F32) for i in range(2)]
    for qi, (q0, _, n) in enumerate(qtiles):
        nkc = NKC if q0 < LQ else CTX // 128
        seq = []
        for kc in range(nkc):
            seq.append(("s", kc))
            if kc >= 1:
                seq.append(("av", kc - 1))
        seq.append(("av", nkc - 1))
        for kind, kc in seq:
            if kind == "s":
                for m in range(2):
                    r0 = 64 * m
                    p = ps_s[m][kc % NSB]
                    pe.op(lambda e: e.matmul(p[:, 0:n], KT[r0:r0 + 64, kc * 128:(kc + 1) * 128], QT[r0:r0 + 64, q0:q0 + n],
                                             start=True, stop=True), [p], [KT, QT])
                for m in range(2):
                    p = ps_s[m][kc % NSB]
                    pt = pts[m][kc % NPT]
                    act.op(lambda e: e.activation(pt[:, 0:n], p[:, 0:n], AF.Exp, scale=0.125), [pt], [p])
            else:
                for m in range(2):
                    pt = pts[m][kc % NPT]
                    pe.op(lambda e: e.matmul(ps_o[m][:, 0:n], V[:, kc, :], pt[:, 0:n], start=(kc == 0), stop=(kc == nkc - 1)), [ps_o[m]], [V, pt])
                    if m == 0:
                        pe.op(lambda e: e.matmul(ps_z[m][:, 0:n], onesb[:], pt[:, 0:n], start=(kc == 0), stop=(kc == nkc - 1)), [ps_z[m]], [onesb, pt])
                    elif kc == 0:
                        dve.op(lambda e: e.tensor_copy(zacc[m][:, 0:n], pt[:, 0:n]), [zacc[m]], [pt])
                    else:
                        dve.op(lambda e: e.tensor_tensor(zacc[m][:, 0:n], zacc[m][:, 0:n], pt[:, 0:n], ALU.add), [zacc[m]], [pt])
        for m in range(1, 2):
            pe.op(lambda e: e.matmul(ps_z[m][:, 0:n], ones[:], zacc[m][:, 0:n], start=True, stop=True), [ps_z[m]], [ones, zacc[m]])
        for m in range(2):
            dve.op(lambda e: e.reciprocal(rz[m][:, 0:n], ps_z[m][:, 0:n]), [rz[m]], [ps_z[m]])
            dve.op(lambda e: e.tensor_tensor(om[m][:, 0:n], ps_o[m][:, 0:n], rz[m][:, 0:n], ALU.mult), [om[m]], [ps_o[m], rz[m]])
        dve.op(lambda e: e.scalar_tensor_tensor(om[0][:, 0:n], om[1][:, 0:n], sc[:, 0:1], om[0][:, 0:n], ALU.mult, ALU.add), [om[0]], [om[1], sc])
        act.op(lambda e: e.activation(osq[:, 0:n], om[0][:, 0:n], AF.Square), [osq], [om[0]])
        pl = ps_s[0][(nkc) % NSB]
        pe.op(lambda e: e.matmul(pl[:, 0:n], ones[:], osq[:, 0:n], start=True, stop=True), [pl], [ones, osq])
        dve.op(lambda e: e.tensor_scalar(ors[:, 0:n], pl[:, 0:n], 1.0 / 128, EPS, ALU.mult, ALU.add), [ors], [pl])
        act.op(lambda e: e.activation(ors[:, 0:n], ors[:, 0:n], AF.Sqrt), [ors], [ors])
        dve.op(lambda e: e.reciprocal(ors[:, 0:n], ors[:, 0:n]), [ors], [ors])
        ot = ots[qi % 2]
        dve.op(lambda e: e.scalar_tensor_tensor(ot[:, 0:n], om[0][:, 0:n], sc[:, 1:2], ors[:, 0:n], ALU.mult, ALU.mult), [ot], [om[0], sc, ors])
        sp.dma(oT, oT.t[:, q0:q0 + n], ot, ot[:, 0:n])
    sp.wait_buf(oT)
    if own:
        fw.close()
    return fw.nc


def swap_halves(xT):
    r = xT.reshape(-1, 2, 2, 16, xT.shape[-1])
    return np.ascontiguousarray(r[:, :, ::-1]).reshape(xT.shape)


def rope_tables(L, n_ctx, reps):
    GRID_W = 64
    rows = L // GRID_W
    row = np.repeat(np.arange(rows, dtype=np.float32), GRID_W)
    col = np.tile(np.arange(GRID_W, dtype=np.float32), rows)
    inv = (10000.0 ** (-np.arange(0, 32, 2, dtype=np.float32) / 32)).astype(np.float32)
    ang = np.concatenate([row[:, None] * inv, col[:, None] * inv], axis=-1)
    cos = np.cos(ang).astype(np.float32).reshape(L, 2, 16)
    sin = np.sin(ang).astype(np.float32).reshape(L, 2, 16)
    C = np.ones((2, 2, 16, L + n_ctx), np.float32)
    S = np.zeros((2, 2, 16, L + n_ctx), np.float32)
    C[:, 0, :, :L] = cos.transpose(1, 2, 0)
    C[:, 1, :, :L] = cos.transpose(1, 2, 0)
    S[:, 0, :, :L] = -sin.transpose(1, 2, 0)
    S[:, 1, :, :L] = sin.transpose(1, 2, 0)
    C = np.tile(C.reshape(64, -1), (reps, 1))
    S = np.tile(S.reshape(64, -1), (reps, 1))
    return np.ascontiguousarray(np.stack([C, S]))


def build_k2b(LQ=8192, fw=None, defer=False):
    own = fw is None
    if own:
        fw = FW()
    T = LQ + CTX
    NKC = T // 128
    NB = LQ // 128
    qT = fw.dram("qT", [128, T], F32, "ExternalInput")
    qsT = fw.dram("qsT", [128, T], F32, "ExternalInput")
    kT = fw.dram("kT", [128, T], F32, "ExternalInput")
    ksT = fw.dram("ksT", [128, T], F32, "ExternalInput")
    v = fw.dram("v", [128, NKC, 64], F32, "ExternalInput")
    CS = fw.dram("CS", [2, 128, T], F32, "ExternalInput")
    sm = fw.dram("sm", [128, 8], F32, "ExternalInput")
    masks = fw.dram("masks", [128, 6, 512], F32, "ExternalInput")
    oT = fw.dram("oT", [2, 64, T], F32, "ExternalOutput")
    if own:
        fw.engines()
    pe, act, dve, pool, sp = fw.pe, fw.act, fw.dve, fw.pool, fw.sp
    onesb = fw.sbuf("onesb", [128, 128], BF16)
    dve.op(lambda e: e.memset(onesb[:], 1.0), [onesb], [])
    blockones = fw.sbuf("blockones", [128, 128], F32)
    dve.op(lambda e: e.memset(blockones[:], 0.0), [blockones], [])
    dve.op(lambda e: e.memset(blockones[0:64, 0:64], 1.0), [blockones], [])
    dve.op(lambda e: e.memset(blockones[64:128, 64:128], 1.0), [blockones], [])
    smt = fw.sbuf("smt", [128, 8], F32)
    sp.dma(smt, smt[:], sm, sm[:])
    es = fw.sbuf("es", [128, 2], F32)
    act.op(lambda e: e.activation(es[:], smt[:, 4:6], AF.Exp), [es], [smt])
    mk = fw.sbuf("mk", [128, 6, 512], BF16)
    pool.dma(mk, mk[:], masks, masks[:])
    QT = fw.sbuf("QT", [128, T], BF16)
    KT = fw.sbuf("KT", [128, T], BF16)
    V = fw.sbuf("V", [128, NKC, 64], BF16)
    pool.dma(V, V[:], v, v[:])
    qtiles = [(t0, t0, 512) for t0 in range(0, LQ, 512)] + [(LQ, LQ, CTX)]
    ktiles = [(0, LQ, CTX)] + [(CTX + t0, t0, 512) for t0 in range(0, LQ, 512)]
    fw.push_scope()
    qknorm_rope(fw, qT, qsT, CS, (smt, 0), QT, None, qtiles, blockones, "qn", 1.0 / 64)
    qknorm_rope(fw, kT, ksT, CS, (smt, 2), KT, None, ktiles, blockones, "kn", 1.0 / 64)
    del fw.qk_scr
    fw.pop_scope()

    NSB = 1 if defer else 2
    NPT = 4
    ps_s = [[fw.psum("ps_s%d_%d" % (m, i), [128, 512], F32) for i in range(NSB)] for m in range(2)]
    ps_o = [fw.psum("ps_o%d" % m, [64, 512], F32) for m in range(2)]
    ps_z = [fw.psum("ps_z%d" % m, [64, 512], F32) for m in range(2)]
    pts = [[fw.sbuf("pt%d_%d" % (m, i), [128, 512], BF16) for i in range(NPT)] for m in range(2)]
    rz = [fw.sbuf("rz%d" % m, [64, 512], F32) for m in range(2)]
    ots = [fw.sbuf("ot%d" % i, [64, 512], F32) for i in range(4)]
    def steps():
        cntl = [0]
        for qi, (q0, _, n) in enumerate(qtiles):
            if q0 < LQ:
                n0 = q0 // 128
                chunks = [(0, None), (1, None)]
                for rel in range(-1, 5):
                    j = n0 + rel
                    if 0 <= j < NB:
                        chunks.append((CTX // 128 + j, rel + 1))
            else:
                chunks = [(0, None), (1, None)]
            nch = len(chunks)
            seq = []
            for ci in range(nch):
                seq.append(("s", ci))
                if ci >= 1:
                    seq.append(("av", ci - 1))
            seq.append(("av", nch - 1))
            for kind, ci in seq:
                kc, mi = chunks[ci]
                if kind == "s":
                    for m in range(2):
                        r0 = 64 * m
                        p = ps_s[m][ci % NSB]
                        pe.op(lambda e: e.matmul(p[:, 0:n], KT[r0:r0 + 64, kc * 128:(kc + 1) * 128], QT[r0:r0 + 64, q0:q0 + n],
                                                 start=True, stop=True), [p], [KT, QT])
                    for m in range(2):
                        p = ps_s[m][ci % NSB]
                        pt = pts[m][ci % NPT]
                        act.op(lambda e: e.activation(pt[:, 0:n], p[:, 0:n], AF.Exp, scale=0.125), [pt], [p])
                        if mi is not None:
                            (dve if m == 0 else pool).op(lambda e: e.tensor_tensor(pt[:, 0:n], pt[:, 0:n], mk[:, mi, 0:n], ALU.mult), [pt], [pt, mk])
                else:
                    last = ci == nch - 1
                    for m in range(2):
                        pt = pts[m][ci % NPT]
                        pe.op(lambda e: e.matmul(ps_o[m][:, 0:n], V[:, kc, :], pt[:, 0:n], start=(ci == 0), stop=last), [ps_o[m]], [V, pt])
                        pe.op(lambda e: e.matmul(ps_z[m][:, 0:n], onesb[:, 0:64], pt[:, 0:n], start=(ci == 0), stop=last), [ps_z[m]], [onesb, pt])
            for m in range(2):
                dve.op(lambda e: e.tensor_scalar(rz[m][:, 0:n], ps_z[m][:, 0:n], es[0:64, m:m + 1], None, ALU.add), [rz[m]], [ps_z[m], es])
                dve.op(lambda e: e.reciprocal(rz[m][:, 0:n], rz[m][:, 0:n]), [rz[m]], [rz[m]])
                ot = ots[cntl[0] % 4]
                cntl[0] += 1
                dve.op(lambda e: e.tensor_tensor(ot[:, 0:n], ps_o[m][:, 0:n], rz[m][:, 0:n], ALU.mult), [ot], [ps_o[m], rz[m]])
                (pool if defer else sp).dma(oT, oT.t[m, :, q0:q0 + n], ot, ot[:, 0:n])
            yield qi
        sp.wait_buf(oT)

    if defer:
        return steps()
    for _ in steps():
        pass
    if own:
        fw.close()
    return fw.nc


def band_masks():
    ki = np.arange(128)[:, None, None]
    rel = np.arange(-1, 5)[None, :, None]
    qq = np.arange(512)[None, None, :]
    return (np.abs(128 * rel + ki - qq) <= 128).astype(np.float32)


CTX = 256
POOL_WINDOWS = (2, 4, 8, 16)


def build_k2c(LQ=8192, fw=None):
    own = fw is None
    if own:
        fw = FW()
    T = LQ + CTX
    PADL = 8
    uT = fw.dram("uT", [128, T], F32, "ExternalInput")
    wsel = fw.dram("wsel", [128, 16], F32, "ExternalInput")
    invc = fw.dram("invc", [128, T], F32, "ExternalInput")
    wl = fw.dram("wl", [128, 128], F32, "ExternalInput")
    ls = fw.dram("ls", [128, 1], F32, "ExternalInput")
    yT = fw.dram("yT", [128, T], F32, "ExternalOutput")
    if own:
        fw.engines()
    pe, act, dve, pool, sp = fw.pe, fw.act, fw.dve, fw.pool, fw.sp
    wst = fw.sbuf("wst", [128, 16], F32)
    sp.dma(wst, wst[:], wsel, wsel[:])
    lst = fw.sbuf("lst", [128, 1], F32)
    sp.dma(lst, lst[:], ls, ls[:])
    wlt = fw.sbuf("wlt", [128, 128], BF16)
    pool.dma(wlt, wlt[:], wl, wl[:])
    TT = 2048
    ut = fw.sbuf("ut", [128, TT + 16], F32)
    ic = fw.sbuf("ic", [128, TT], F32)
    acc = fw.sbuf("acc", [128, TT], F32)
    db = fw.sbuf("db", [128, TT], BF16)
    ps = [fw.psum("ps%d" % i, [128, 512], F32) for i in range(2)]
    ots = [fw.sbuf("ot%d" % i, [128, 512], F32) for i in range(2)]
    cnt = 0
    for (s0, Ls) in ((0, LQ), (LQ, CTX)):
        tt_ = min(TT, Ls)
        for t0 in range(0, Ls, tt_):
            lo = max(t0 - 8, 0)
            hi = min(t0 + tt_ + 8, Ls)
            if lo > t0 - 8:
                dve.op(lambda e: e.memset(ut[:, 0:8], 0.0), [ut], [])
            if hi < t0 + tt_ + 8:
                dve.op(lambda e: e.memset(ut[:, tt_ + 8:tt_ + 16], 0.0), [ut], [])
            sp.dma(ut, ut[:, lo - (t0 - 8):hi - (t0 - 8)], uT, uT.t[:, s0 + lo:s0 + hi])
            sp.dma(ic, ic[:, 0:tt_], invc, invc.t[:, s0 + t0:s0 + t0 + tt_])
            dve.op(lambda e: e.tensor_scalar(acc[:, 0:tt_], ut[:, 0:tt_], wst[:, 0:1], None, ALU.mult), [acc], [ut, wst])
            for k in range(1, 16):
                dve.op(lambda e: e.scalar_tensor_tensor(acc[:, 0:tt_], ut[:, k:k + tt_], wst[:, k:k + 1], acc[:, 0:tt_], ALU.mult, ALU.add), [acc], [ut, wst])
            dve.op(lambda e: e.tensor_tensor(acc[:, 0:tt_], acc[:, 0:tt_], ic[:, 0:tt_], ALU.mult), [acc], [ic])
            dve.op(lambda e: e.tensor_tensor(db[:, 0:tt_], acc[:, 0:tt_], ut[:, 8:8 + tt_], ALU.subtract), [db], [acc, ut])
            for c0 in range(0, tt_, 512):
                n = min(512, tt_ - c0)
                p = ps[cnt % 2]
                ot = ots[cnt % 2]
                cnt += 1
                pe.op(lambda e: e.matmul(p[:, 0:n], wlt[:], db[:, c0:c0 + n], start=True, stop=True), [p], [wlt, db])
                act.op(lambda e: e.activation(ot[:, 0:n], p[:, 0:n], AF.Copy, scale=lst[:, 0:1]), [ot], [p, lst])
                sp.dma(yT, yT.t[:, s0 + t0 + c0:s0 + t0 + c0 + n], ot, ot[:, 0:n])
    sp.wait_buf(yT)
    if own:
        fw.close()
    return fw.nc


def pool_consts(g, LQ):
    w = POOL_WINDOWS[g]
    lo = w // 2
    hi = w - 1 - lo
    sel = np.zeros(16, np.float32)
    for s in range(-lo, hi + 1):
        sel[s + 8] = 1.0
    outs = []
    for Ls in (LQ, CTX):
        t = np.arange(Ls)
        start = np.clip(t - lo, 0, Ls)
        end = np.clip(t + hi + 1, 0, Ls)
        outs.append((1.0 / (end - start).astype(np.float32)).astype(np.float32))
    ic = np.concatenate(outs)
    return np.tile(sel[None], (128, 1)), np.ascontiguousarray(np.tile(ic[None], (128, 1)))


EPS = 1e-6
CTX = 256
HY_EMB = 33


def sin_act(fw, out_buf, out_ap, p, n, fb, scr):
    act, dve = fw.act, fw.dve
    s2, s4 = scr
    P = 64
    act.op(lambda e: e.activation(s2[0:P, 0:n], p[0:P, 0:n], AF.Sin, bias=fb[0:P, 1:2], scale=fb[0:P, 0:1]), [s2], [p, fb])
    act.op(lambda e: e.activation(s4[0:P, 0:n], p[0:P, 0:n], AF.Sin, bias=fb[0:P, 3:4], scale=fb[0:P, 2:3]), [s4], [p, fb])
    dve.op(lambda e: e.tensor_tensor(s4[0:P, 0:n], s4[0:P, 0:n], s4[0:P, 0:n], ALU.mult), [s4], [s4])
    dve.op(lambda e: e.tensor_scalar(s4[0:P, 0:n], s4[0:P, 0:n], -2.0, 1.0, ALU.mult, ALU.add), [s4], [s4])
    dve.op(lambda e: e.scalar_tensor_tensor(out_ap, s2[0:P, 0:n], 2.0, s4[0:P, 0:n], ALU.mult, ALU.mult), [out_buf], [s2, s4])


def filter_gen(fw, Lf, zf, t01, wts, KF, nm, scr, scale):
    pe, act, dve, pool, sp = fw.pe, fw.act, fw.dve, fw.pool, fw.sp
    w1, w2, w3, fb1, fb2, ndelta = wts["w1"], wts["w2"], wts["w3"], wts["fb1"], wts["fb2"], wts["ndelta"]
    nt = (Lf + 511) // 512
    part = fw.sbuf(nm + "_part", [128, 2 * nt], F32)
    dve.op(lambda e: e.memset(part[:], 0.0), [part], [])
    for di in range(2):
        for ti in range(nt):
            ps1, ps2, ps3, zt, h1, h2, s2, s4, tt, kt, ktb, sqt = scr[(di * nt + ti) % 2]
            t0 = ti * 512
            n = min(512, Lf - t0)
            sp.dma(zt, zt[0:HY_EMB, 0:n], zf, zf.t[1 - di, :, t0:t0 + n])
            tsrc = bass.AP(tensor=t01.t.tensor, offset=(1 - di) * Lf + t0, ap=[[0, 128], [1, n]])
            sp.dma(tt, tt[:, 0:n], t01, tsrc)
            pe.op(lambda e: e.matmul(ps1[0:64, 0:n], w1[0:HY_EMB, :], zt[0:HY_EMB, 0:n], start=True, stop=True), [ps1], [w1, zt])
            sin_act(fw, h1, h1[0:64, 0:n], ps1, n, fb1, (s2, s4))
            pe.op(lambda e: e.matmul(ps2[0:64, 0:n], w2[0:64, :], h1[0:64, 0:n], start=True, stop=True), [ps2], [w2, h1])
            sin_act(fw, h2, h2[0:64, 0:n], ps2, n, fb2, (s2, s4))
            pe.op(lambda e: e.matmul(ps3[:, 0:n], w3[0:64, di, :], h2[0:64, 0:n], start=True, stop=True), [ps3], [w3, h2])
            act.op(lambda e: e.activation(tt[:, 0:n], tt[:, 0:n], AF.Exp, scale=ndelta[:, 0:1]), [tt], [tt, ndelta])
            dve.op(lambda e: e.tensor_tensor(kt[:, 0:n], ps3[:, 0:n], tt[:, 0:n], ALU.mult), [kt], [ps3, tt])
            if di == 1 and ti == 0:
                dve.op(lambda e: e.memset(kt[:, 0:1], 0.0), [kt], [])
            dve.op(lambda e: e.tensor_tensor(sqt[:, 0:n], kt[:, 0:n], kt[:, 0:n], ALU.mult), [sqt], [kt])
            dve.op(lambda e: e.reduce_sum(part[:, di * nt + ti:di * nt + ti + 1], sqt[:, 0:n], AX.X), [part], [sqt])
            act.op(lambda e: e.activation(ktb[:, 0:n], kt[:, 0:n], AF.Copy), [ktb], [kt])
            if di == 0:
                sp.dma(KF, KF.t[:, 1 + t0:1 + t0 + n], ktb, ktb[0:64, 0:n])
            elif ti == 0:
                sp.dma(KF, KF.t[:, Lf + 1:Lf + n], ktb, ktb[0:64, 1:n])
            else:
                sp.dma(KF, KF.t[:, Lf + t0:Lf + t0 + n], ktb, ktb[0:64, 0:n])
    dve.op(lambda e: e.reduce_sum(scale[:], part[:], AX.X), [scale], [part])
    dve.op(lambda e: e.tensor_scalar(scale[:], scale[:], EPS, None, ALU.add), [scale], [scale])
    act.op(lambda e: e.activation(scale[:], scale[:], AF.Sqrt), [scale], [scale])
    dve.op(lambda e: e.reciprocal(scale[:], scale[:]), [scale], [scale])
    return scale


def conv3(fw, dst, u, n, sm, part):
    dve = fw.dve
    c = 3 * part
    dve.op(lambda e: e.tensor_scalar(dst[:, 0:n], u[:, 1:n + 1], sm[:, c + 1:c + 2], sm[:, 9 + part:10 + part], ALU.mult, ALU.add), [dst], [u, sm])
    dve.op(lambda e: e.scalar_tensor_tensor(dst[:, 0:n], u[:, 0:n], sm[:, c:c + 1], dst[:, 0:n], ALU.mult, ALU.add), [dst], [u, sm])
    dve.op(lambda e: e.scalar_tensor_tensor(dst[:, 0:n], u[:, 2:n + 2], sm[:, c + 2:c + 3], dst[:, 0:n], ALU.mult, ALU.add), [dst], [u, sm])


def load_u_tile(fw, ut, uT, part, t0, n, Ls):
    sp, dve = fw.sp, fw.dve
    lo = max(t0 - 1, 0)
    hi = min(t0 + n + 1, Ls)
    if t0 == 0:
        dve.op(lambda e: e.memset(ut[:, 0:1], 0.0), [ut], [])
    if t0 + n == Ls:
        dve.op(lambda e: e.memset(ut[:, n + 1:n + 2], 0.0), [ut], [])
    sp.dma(ut, ut[:, lo - (t0 - 1):hi - (t0 - 1)], uT, uT.t[part, :, lo:hi])


def build_k2d(LQ=8192, fw=None, hook_pre=None, hook_step=None, hook_post=None):
    own = fw is None
    if own:
        fw = FW()
    NB = LQ // 128
    NBC = CTX // 128
    TT = min(1024, LQ)
    uT = fw.dram("uT", [3, 128, LQ], F32, "ExternalInput")
    ucT = fw.dram("ucT", [3, 128, CTX], F32, "ExternalInput")
    smd = fw.dram("sm", [128, 16], F32, "ExternalInput")
    w1d = fw.dram("w1", [HY_EMB, 64], F32, "ExternalInput")
    w2d = fw.dram("w2", [64, 64], F32, "ExternalInput")
    w3d = fw.dram("w3", [64, 2, 128], F32, "ExternalInput")
    fbd = fw.dram("fb", [64, 4], F32, "ExternalInput")
    zfL = fw.dram("zfL", [2, HY_EMB, LQ], F32, "ExternalInput")
    t01L = fw.dram("t01L", [2, LQ], F32, "ExternalInput")
    zfC = fw.dram("zfC", [2, HY_EMB, CTX], F32, "ExternalInput")
    t01C = fw.dram("t01C", [2, CTX], F32, "ExternalInput")
    oT = fw.dram("oT", [128, LQ], F32, "ExternalOutput")
    ocT = fw.dram("ocT", [128, CTX], F32, "ExternalOutput")
    KF = fw.dram("KF", [64, 2 * LQ], BF16, "Internal")
    KFC = fw.dram("KFC", [64, 2 * CTX], BF16, "Internal")
    if own:
        fw.engines()
    pe, act, dve, pool, sp = fw.pe, fw.act, fw.dve, fw.pool, fw.sp

    identb = fw.sbuf("identb", [128, 128], BF16)
    identf = fw.sbuf("identf", [128, 128], F32)
    idd = fw.dram("ident", [128, 128], F32, "ExternalInput")
    sp.dma(identf, identf[:], idd, idd[:])
    dve.op(lambda e: e.tensor_copy(identb[:], identf[:]), [identb], [identf])
    antif = fw.sbuf("antif", [128, 128], F32)
    add = fw.dram("anti", [128, 128], F32, "ExternalInput")
    sp.dma(antif, antif[:], add, add[:])
    sm = fw.sbuf("smt", [128, 16], F32)
    sp.dma(sm, sm[:], smd, smd[:])
    w1 = fw.sbuf("w1s", [HY_EMB, 64], F32)
    sp.dma(w1, w1[:], w1d, w1d[:])
    w2 = fw.sbuf("w2s", [64, 64], F32)
    sp.dma(w2, w2[:], w2d, w2d[:])
    w3 = fw.sbuf("w3s", [64, 2, 128], F32)
    sp.dma(w3, w3[:], w3d, w3d[:])
    fb = fw.sbuf("fbs", [64, 4], F32)
    sp.dma(fb, fb[:], fbd, fbd[:])
    fb1 = fw.sbuf("fb1", [64, 4], F32)
    fb2 = fw.sbuf("fb2", [64, 4], F32)
    for dst, bc in ((fb1, 1), (fb2, 2)):
        dve.op(lambda e: e.tensor_scalar(dst[:, 0:1], fb[:, 0:1], 0.5, None, ALU.mult), [dst], [fb])
        dve.op(lambda e: e.tensor_scalar(dst[:, 2:3], fb[:, 0:1], 0.25, None, ALU.mult), [dst], [fb])
        dve.op(lambda e: e.tensor_tensor(dst[:, 1:2], dst[:, 0:1], fb[:, bc:bc + 1], ALU.mult), [dst], [dst, fb])
        dve.op(lambda e: e.tensor_tensor(dst[:, 3:4], dst[:, 2:3], fb[:, bc:bc + 1], ALU.mult), [dst], [dst, fb])
    ndelta = fw.sbuf("ndelta", [128, 1], F32)
    dve.op(lambda e: e.tensor_copy(ndelta[:], sm[:, 13:14]), [ndelta], [sm])
    wts = dict(w1=w1, w2=w2, w3=w3, fb1=fb1, fb2=fb2, ndelta=ndelta)
    scaleL = fw.sbuf("fL_scale", [128, 1], F32)
    scaleC = fw.sbuf("fC_scale", [128, 1], F32)
    fw.push_scope()
    scr = [(fw.psum("fg_ps1_%d" % i, [128, 512], F32), fw.psum("fg_ps2_%d" % i, [128, 512], F32), fw.psum("fg_ps3_%d" % i, [128, 512], F32),
            fw.sbuf("fg_zt%d" % i, [64, 512], F32), fw.sbuf("fg_h1_%d" % i, [64, 512], F32), fw.sbuf("fg_h2_%d" % i, [64, 512], F32),
            fw.sbuf("fg_s2_%d" % i, [64, 512], F32), fw.sbuf("fg_s4_%d" % i, [64, 512], F32), fw.sbuf("fg_tt%d" % i, [128, 512], F32),
            fw.sbuf("fg_kt%d" % i, [128, 512], F32), fw.sbuf("fg_ktb%d" % i, [128, 512], BF16), fw.sbuf("fg_sq%d" % i, [128, 512], F32))
           for i in range(2)]
    filter_gen(fw, LQ, zfL, t01L, wts, KF, "fL", scr, scaleL)
    filter_gen(fw, CTX, zfC, t01C, wts, KFC, "fC", scr, scaleC)
    fw.pop_scope()

    Zt = fw.sbuf("Zt", [128, 64, NB, 2], BF16)
    Ztc = fw.sbuf("Ztc", [128, 64, NBC, 2], BF16)
    fw.push_scope()
    uts = [fw.sbuf("ut%d" % i, [128, TT + 2], F32) for i in range(3)]
    cv = [fw.sbuf("cv%d" % i, [128, TT], F32) for i in range(3)]
    zb = fw.sbuf("zb", [128, TT], BF16)
    pst = [fw.psum("pst%d" % i, [128, 512], BF16) for i in range(2)]

    def z_phase(src, Ls, Ztx, nblk):
        tt_ = min(TT, Ls)
        for t0 in range(0, Ls, tt_):
            for part in range(2):
                load_u_tile(fw, uts[part], src, part, t0, tt_, Ls)
                conv3(fw, cv[part], uts[part], tt_, sm, part)
            dve.op(lambda e: e.tensor_tensor(zb[:, 0:tt_], cv[0][:, 0:tt_], cv[1][:, 0:tt_], ALU.mult), [zb], [cv[0], cv[1]])
            for rb in range(tt_ // 128):
                r = t0 // 128 + rb
                p = pst[r % 2]
                pe.op(lambda e: e.transpose(p[:, 0:128], zb[:, rb * 128:(rb + 1) * 128], identb[:]), [p], [zb, identb])
                dst = Ztx[:, :, r, :].rearrange("p c b -> p b c")
                src_ap = p[:, 0:128].rearrange("p (b c) -> p b c", b=2)
                if r % 2 == 0:
                    dve.op(lambda e: e.tensor_copy(dst, src_ap), [Ztx], [p])
                else:
                    act.op(lambda e: e.activation(dst, src_ap, AF.Copy), [Ztx], [p])

    z_phase(uT, LQ, Zt, NB)
    z_phase(ucT, CTX, Ztc, NBC)
    fw.pop_scope()

    W = (2 * NB - 1) * 128
    X0 = (NB - 1) * 128
    tbs = [fw.sbuf("tb%d" % i, [128, W], BF16) for i in range(2)]
    tbc = fw.sbuf("tbc", [128, 16, 3 * 128], BF16)
    Y = fw.sbuf("Y", [128, NB, 2, 64], F32)
    Yc = fw.sbuf("Yc", [128, NBC, 2, 64], F32)
    psy = [fw.psum("psy%d" % i, [128, 512], F32) for i in range(2)]
    if hook_pre:
        hook_pre()
    kft = KF.t.tensor
    kfct = KFC.t.tensor
    for ch in range(64):
        tb = tbs[ch % 2]
        src = bass.AP(tensor=kft, offset=ch * 2 * LQ + 1, ap=[[1, 128], [1, W]])
        sp.dma(tb, tb[:], KF, src)
        p = psy[ch % 2]
        ds = [0] + [d for d in range(-(NB - 1), NB) if d != 0]
        for k, d in enumerate(ds):
            r0 = max(0, d)
            nb = NB - abs(d)
            pe.op(lambda e: e.matmul(p[:, r0 * 2:(r0 + nb) * 2], tb[:, X0 - 128 * d:X0 - 128 * d + 128],
                                     Zt[:, ch, r0 - d:r0 - d + nb, :].rearrange("p r b -> p (r b)"),
                                     start=(k == 0), stop=(k == len(ds) - 1), skip_group_check=True), [p], [tb, Zt])
        if ch % 2 == 0:
            dve.op(lambda e: e.tensor_copy(Y[:, :, :, ch], p[:, 0:NB * 2].rearrange('p (r b) -> p r b', b=2)), [Y], [p])
        else:
            act.op(lambda e: e.activation(Y[:, :, :, ch], p[:, 0:NB * 2].rearrange('p (r b) -> p r b', b=2), AF.Copy), [Y], [p])
        if hook_step:
            hook_step(ch)
    pc = psy[0]
    for g in range(4):
        src = bass.AP(tensor=kfct, offset=g * 16 * 2 * CTX + 1, ap=[[1, 128], [2 * CTX, 16], [1, 384]])
        sp.dma(tbc, tbc[:], KFC, src)
        for c16 in range(16):
            ch = g * 16 + c16
            o0 = ch * 4
            pe.op(lambda e: e.matmul(pc[:, o0:o0 + 4], tbc[:, c16, 128:256], Ztc[:, ch, :, :].rearrange("p r b -> p (r b)"),
                                     start=True, stop=False, skip_group_check=True), [pc], [tbc, Ztc])
            pe.op(lambda e: e.matmul(pc[:, o0 + 2:o0 + 4], tbc[:, c16, 0:128], Ztc[:, ch, 0, :],
                                     start=False, stop=False, skip_group_check=True), [pc], [tbc, Ztc])
            pe.op(lambda e: e.matmul(pc[:, o0:o0 + 2], tbc[:, c16, 256:384], Ztc[:, ch, 1, :],
                                     start=False, stop=True, skip_group_check=True), [pc], [tbc, Ztc])
    dve.op(lambda e: e.tensor_copy(Yc[:].rearrange("p r b c -> p c r b"), pc[:, 0:256].rearrange("p (c r b) -> p c r b", r=NBC, b=2)), [Yc], [pc])

    if hook_post:
        hook_post()
    fw.push_scope()
    uts = [fw.sbuf("o_ut%d" % i, [128, TT + 2], F32) for i in range(3)]
    cv = [fw.sbuf("o_cv%d" % i, [128, TT], F32) for i in range(3)]
    pso = [fw.psum("pso%d" % i, [128, 512], F32) for i in range(1)]
    ys = fw.sbuf("ys", [128, 512], F32)
    ots = [fw.sbuf("ot%d" % i, [128, 512], F32) for i in range(2)]

    def out_phase(src, Ls, Yx, scale, dstT):
        tt_ = min(TT, Ls)
        cnt = 0
        for t0 in range(0, Ls, tt_):
            for part in range(3):
                load_u_tile(fw, uts[part], src, part, t0, tt_, Ls)
                conv3(fw, cv[part], uts[part], tt_, sm, part)
            dve.op(lambda e: e.tensor_tensor(cv[0][:, 0:tt_], cv[0][:, 0:tt_], cv[1][:, 0:tt_], ALU.mult), [cv[0]], [cv[1]])
            for s0 in range(0, tt_, 512):
                ns = min(512, tt_ - s0)
                p = pso[0]
                for rb in range(ns // 128):
                    r = (t0 + s0) // 128 + rb
                    in_ap = Yx[:, r, :, :].rearrange("p b c -> p (b c)")
                    pe.op(lambda e: e.matmul(p[:, rb * 128:(rb + 1) * 128], in_ap, antif[:], start=True, stop=True), [p], [Yx, antif])
                act.op(lambda e: e.activation(ys[:, 0:ns], p[:, 0:ns], AF.Copy, scale=scale[:, 0:1]), [ys], [p, scale])
                dve.op(lambda e: e.scalar_tensor_tensor(ys[:, 0:ns], cv[0][:, s0:s0 + ns], sm[:, 12:13], ys[:, 0:ns], ALU.mult, ALU.add), [ys], [cv[0], sm])
                ot = ots[cnt % 2]
                cnt += 1
                dve.op(lambda e: e.tensor_tensor(ot[:, 0:ns], ys[:, 0:ns], cv[2][:, s0:s0 + ns], ALU.mult), [ot], [ys, cv[2]])
                sp.dma(dstT, dstT.t[:, t0 + s0:t0 + s0 + ns], ot, ot[:, 0:ns])

    out_phase(uT, LQ, Y, scaleL, oT)
    out_phase(ucT, CTX, Yc, scaleC, ocT)
    sp.wait_buf(oT)
    sp.wait_buf(ocT)
    fw.pop_scope()
    if own:
        fw.close()
    return fw.nc


def hy_feats(L):
    t01 = np.linspace(0.0, 1.0, L, dtype=np.float32)
    bands = (HY_EMB - 1) // 2
    w_ang = (2.0 * math.pi * np.arange(L, dtype=np.float32) / L).astype(np.float32)
    f = np.linspace(1e-4, bands - 1, bands, dtype=np.float32)
    ang = (f[None, :] * w_ang[:, None]).astype(np.float32)
    z = np.concatenate([t01[:, None], np.cos(ang), -np.sin(ang)], axis=-1).astype(np.float32)
    zf = np.stack([z.T, z[::-1].T])
    tt = np.stack([t01, t01[::-1]])
    return np.ascontiguousarray(zf), np.ascontiguousarray(tt)


def hy_ndelta(D_WIDTH=512):
    d = np.linspace(math.log(1e-2) / 0.3, math.log(1e-2) / 1.5, D_WIDTH, dtype=np.float32)
    return -np.abs(d)


def k2d_inputs(core, u_lat, u_ctx, conv_w, conv_b, w1, b1, w2, b2, w3, freq, bias, LQ):
    c0 = 64 * core
    DW = 512
    def pk(u, Ls):
        parts = []
        for part in range(3):
            cols = u[:, :, part * DW + c0: part * DW + c0 + 64]
            parts.append(cols.transpose(0, 2, 1).reshape(128, Ls))
        return np.ascontiguousarray(np.stack(parts))
    sm = np.zeros((128, 16), np.float32)
    for part in range(3):
        for tap in range(3):
            sm[:, part * 3 + tap] = np.tile(conv_w[tap, part * DW + c0: part * DW + c0 + 64], 2)
        sm[:, 9 + part] = np.tile(conv_b[part * DW + c0: part * DW + c0 + 64], 2)
    sm[:, 12] = np.tile(bias[c0:c0 + 64], 2)
    sm[:, 13] = np.tile(hy_ndelta()[c0:c0 + 64], 2)
    w3r = w3.reshape(64, 2, DW)[:, :, c0:c0 + 64]
    w3p = np.ascontiguousarray(np.concatenate([w3r, w3r], axis=2))
    fb = np.zeros((64, 4), np.float32)
    fb[:, 0] = freq; fb[:, 1] = b1; fb[:, 2] = b2
    zfL, t01L = hy_feats(LQ)
    zfC, t01C = hy_feats(CTX)
    return {"uT": pk(u_lat, LQ), "ucT": pk(u_ctx, CTX), "sm": sm, "w1": np.ascontiguousarray(w1), "w2": np.ascontiguousarray(w2),
            "w3": w3p, "fb": fb, "zfL": zfL, "t01L": t01L, "zfC": zfC, "t01C": t01C, "ident": np.eye(128, dtype=np.float32), "anti": np.ascontiguousarray(np.eye(128, dtype=np.float32)[::-1])}


def build_mix(LQ=8192):
    fw = FW()
    fw.engines()
    for pfx, body in (("a_", build_k2a), ("c_", build_k2c)):
        fw.pfx = pfx
        fw.push_scope()
        body(LQ, fw=fw)
        fw.pop_scope()
    st = {}

    def pre():
        fw.pfx = "b_"
        fw.push_scope()
        st["g"] = build_k2b(LQ, fw=fw, defer=True)
        fw.pfx = "d_"

    def step(ch):
        if ch % 3 == 2:
            next(st["g"], None)

    def post():
        for _ in st["g"]:
            pass
        fw.pop_scope()

    fw.pfx = "d_"
    fw.push_scope()
    build_k2d(LQ, fw=fw, hook_pre=pre, hook_step=step, hook_post=post)
    fw.pop_scope()
    fw.pfx = ""
    fw.close()
    return fw.nc


D = 2048
KC = 16
EPS = 1e-6
FH = 5632
HC = FH // 128


def build_k3(TL=2048, TC=64, with_k1=False):
    fw = FW()
    LW = TL + 2
    CW = TC + 2
    TW = LW + CW
    TO = TL + TC
    oT = fw.dram("oT", [KC, 128, TW], F32, "ExternalInput")
    xT = fw.dram("xT", [KC, 128, TW], F32, "ExternalInput")
    vec = fw.dram("vec", [128, 9, KC], F32, "ExternalInput")
    edge = fw.dram("edge", [128, 4], F32, "ExternalInput")
    w_out = fw.dram("w_out", [8, 128, KC, 256], F32, "ExternalInput")
    w_up = fw.dram("w_up", [2 * FH // 256, 128, KC, 256], F32, "ExternalInput")
    cwd = fw.dram("cw", [128, 4, 2 * HC], F32, "ExternalInput")
    w_dn = fw.dram("w_dn", [8, 128, HC, 256], F32, "ExternalInput")
    xo = fw.dram("xo", [KC, 128, TO], F32, "ExternalOutput")
    XN = fw.dram("XN", [KC, 128, TW], F32, "Internal")
    AT = fw.dram("AT", [HC, 128, TO], BF16, "Internal")
    fw.engines()
    pe, act, dve, pool, sp = fw.pe, fw.act, fw.dve, fw.pool, fw.sp
    ones = fw.sbuf("ones", [128, 128], F32)
    dve.op(lambda e: e.memset(ones[:], 1.0), [ones], [])
    vt = fw.sbuf("vt", [128, 9, KC], F32)
    sp.dma(vt, vt[:], vec, vec[:])
    eg = fw.sbuf("eg", [128, 4], F32)
    sp.dma(eg, eg[:], edge, edge[:])
    A_lat = fw.sbuf("A_lat", [128, KC], F32)
    A_ctx = fw.sbuf("A_ctx", [128, KC], F32)
    dve.op(lambda e: e.scalar_tensor_tensor(A_lat[:], vt[:, 3, :], 1.0, vt[:, 2, :], ALU.add, ALU.mult), [A_lat], [vt])
    dve.op(lambda e: e.scalar_tensor_tensor(A_ctx[:], vt[:, 5, :], 1.0, vt[:, 2, :], ALU.add, ALU.mult), [A_ctx], [vt])
    fw.push_scope()
    h2T = fw.sbuf("h2T", [128, KC, TW], BF16)

    fw.push_scope()
    wo = [fw.sbuf("wo%d" % i, [128, KC, 256], BF16) for i in range(8)]
    for i in range(8):
        pool.dma(wo[i], wo[i][:], w_out, w_out.t[i])
    ots = [fw.sbuf("a_ot%d" % i, [128, KC, 256], BF16) for i in range(2)]
    xts = [fw.sbuf("a_xt%d" % i, [128, KC, 256], F32) for i in range(2)]
    sq = fw.sbuf("a_sq", [128, KC, 256], F32)
    rs = fw.sbuf("a_rs", [128, 256], F32)
    psA = [fw.psum("a_ps%d" % i, [128, 512], F32) for i in range(4)]
    psn = fw.psum("a_psn", [128, 512], F32)
    tilesA = [(c0, 256, 0) for c0 in range(0, TL, 256)] + [(TL, 2, 0), (LW, CW, 1)]
    ov = oT.t.rearrange("k p t -> p k t")
    xv = xT.t.rearrange("k p t -> p k t")
    xnv = XN.t.rearrange("k p t -> p k t")
    cnt = 0
    for ti, (c0, n, kind) in enumerate(tilesA):
        ot, xt = ots[ti % 2], xts[ti % 2]
        g1c = 0 if kind == 0 else 1
        A2 = A_lat if kind == 0 else A_ctx
        shc = 4 if kind == 0 else 6
        pool.dma(ot, ot[:, :, 0:n], oT, ov[:, :, c0:c0 + n])
        sp.dma(xt, xt[:, :, 0:n], xT, xv[:, :, c0:c0 + n])
        for ci in range(KC):
            p = psA[cnt % 4]
            cnt += 1
            w = wo[ci // 2]
            h0 = (ci % 2) * 128
            for k in range(KC):
                pe.op(lambda e: e.matmul(p[:, 0:n], w[:, k, h0:h0 + 128], ot[:, k, 0:n], start=(k == 0), stop=(k == KC - 1)), [p], [w, ot])
            dve.op(lambda e: e.scalar_tensor_tensor(xt[:, ci, 0:n], p[:, 0:n], vt[:, g1c, ci:ci + 1], xt[:, ci, 0:n], ALU.mult, ALU.add), [xt], [p, vt])
        sp.dma(XN, xnv[:, :, c0:c0 + n], xt, xt[:, :, 0:n])
        act.op(lambda e: e.activation(sq[:, :, 0:n], xt[:, :, 0:n], AF.Square), [sq], [xt])
        for k in range(KC):
            pe.op(lambda e: e.matmul(psn[:, 0:n], ones[:], sq[:, k, 0:n], start=(k == 0), stop=(k == KC - 1)), [psn], [ones, sq])
        dve.op(lambda e: e.tensor_scalar(rs[:, 0:n], psn[:, 0:n], 1.0 / D, EPS, ALU.mult, ALU.add), [rs], [psn])
        act.op(lambda e: e.activation(rs[:, 0:n], rs[:, 0:n], AF.Sqrt), [rs], [rs])
        dve.op(lambda e: e.reciprocal(rs[:, 0:n], rs[:, 0:n]), [rs], [rs])
        for k in range(KC):
            dve.op(lambda e: e.tensor_tensor(sq[:, k, 0:n], xt[:, k, 0:n], rs[:, 0:n], ALU.mult), [sq], [xt, rs])
        for k in range(KC):
            act.op(lambda e: e.activation(h2T[:, k, c0:c0 + n], sq[:, k, 0:n], AF.Identity, bias=vt[:, shc, k:k + 1], scale=A2[:, k:k + 1]),
                   [h2T], [sq, A2, vt])
    fw.pop_scope()

    fw.push_scope()
    cw = fw.sbuf("cws", [128, 4, 2 * HC], F32)
    sp.dma(cw, cw[:], cwd, cwd[:])
    wgs = [fw.sbuf("b_wg%d" % i, [128, KC, 256], BF16) for i in range(2)]
    wus = [fw.sbuf("b_wu%d" % i, [128, KC, 256], BF16) for i in range(2)]
    psB = [fw.psum("b_ps%d" % i, [128, 512], F32) for i in range(4)]
    ug = [fw.sbuf("b_ug%d" % i, [128, 512], F32) for i in range(2)]
    uu = [fw.sbuf("b_uu%d" % i, [128, 512], F32) for i in range(2)]
    cg = [fw.sbuf("b_cg%d" % i, [128, 512], F32) for i in range(2)]
    cu = [fw.sbuf("b_cu%d" % i, [128, 512], F32) for i in range(2)]
    ab = [fw.sbuf("b_ab%d" % i, [128, 512], BF16) for i in range(2)]
    tilesB = []
    s = 0
    while s + 2 < LW:
        m = min(512, LW - s)
        tilesB.append((s, m, s, 0 if s == 0 else None, 1 if s + m == LW else None))
        s += m - 2
    tilesB.append((LW, CW, TL, 2, 3))

    def conv(dst, u, m, hc):
        dve.op(lambda e: e.tensor_scalar(dst[:, 0:m - 2], u[:, 1:m - 1], cw[:, 1, hc:hc + 1], cw[:, 3, hc:hc + 1], ALU.mult, ALU.add), [dst], [u, cw])
        dve.op(lambda e: e.scalar_tensor_tensor(dst[:, 0:m - 2], u[:, 0:m - 2], cw[:, 0, hc:hc + 1], dst[:, 0:m - 2], ALU.mult, ALU.add), [dst], [u, cw])
        dve.op(lambda e: e.scalar_tensor_tensor(dst[:, 0:m - 2], u[:, 2:m], cw[:, 2, hc:hc + 1], dst[:, 0:m - 2], ALU.mult, ALU.add), [dst], [u, cw])

    cnt = 0
    for j in range(FH // 256):
        wg, wu = wgs[j % 2], wus[j % 2]
        pool.dma(wg, wg[:], w_up, w_up.t[j])
        pool.dma(wu, wu[:], w_up, w_up.t[FH // 256 + j])
        for (s, m, o0, eL, eR) in tilesB:
            for half in range(2):
                hc = 2 * j + half
                h0 = half * 128
                i2 = cnt % 2
                cnt += 1
                pg, pu = psB[(2 * cnt) % 4], psB[(2 * cnt + 1) % 4]
                for k in range(KC):
                    pe.op(lambda e: e.matmul(pg[:, 0:m], wg[:, k, h0:h0 + 128], h2T[:, k, s:s + m], start=(k == 0), stop=(k == KC - 1)), [pg], [wg, h2T])
                for k in range(KC):
                    pe.op(lambda e: e.matmul(pu[:, 0:m], wu[:, k, h0:h0 + 128], h2T[:, k, s:s + m], start=(k == 0), stop=(k == KC - 1)), [pu], [wu, h2T])
                act.op(lambda e: e.activation(ug[i2][:, 0:m], pg[:, 0:m], AF.Copy), [ug[i2]], [pg])
                act.op(lambda e: e.activation(uu[i2][:, 0:m], pu[:, 0:m], AF.Copy), [uu[i2]], [pu])
                for ubuf in (ug[i2], uu[i2]):
                    if eL is not None:
                        dve.op(lambda e: e.tensor_scalar(ubuf[:, 0:1], ubuf[:, 0:1], eg[:, eL:eL + 1], None, ALU.mult), [ubuf], [ubuf, eg])
                    if eR is not None:
                        dve.op(lambda e: e.tensor_scalar(ubuf[:, m - 1:m], ubuf[:, m - 1:m], eg[:, eR:eR + 1], None, ALU.mult), [ubuf], [ubuf, eg])
                conv(cg[i2], ug[i2], m, hc)
                conv(cu[i2], uu[i2], m, HC + hc)
                act.op(lambda e: e.activation(cg[i2][:, 0:m - 2], cg[i2][:, 0:m - 2], AF.Silu), [cg[i2]], [cg[i2]])
                dve.op(lambda e: e.tensor_tensor(ab[i2][:, 0:m - 2], cg[i2][:, 0:m - 2], cu[i2][:, 0:m - 2], ALU.mult), [ab[i2]], [cg[i2], cu[i2]])
                sp.dma(AT, AT.t[hc, :, o0:o0 + m - 2], ab[i2], ab[i2][:, 0:m - 2])
    fw.pop_scope()
    fw.pop_scope()

    fw.push_scope()
    HALF = TO // 3
    at = fw.sbuf("c_at", [128, HC, HALF], BF16)
    wds = [fw.sbuf("c_wd%d" % i, [128, HC, 256], BF16) for i in range(2)]
    psC = [fw.psum("c_ps%d" % i, [128, 512], F32) for i in range(4)]
    xns = [fw.sbuf("c_xn%d" % i, [128, 512], F32) for i in range(3)]
    outs = [fw.sbuf("c_o%d" % i, [128, 512], F32) for i in range(3)]
    atv = AT.t.rearrange("k p t -> p k t")
    cnt = 0
    wcnt = 0
    for hf in range(3):
        h0, h1 = hf * HALF, (hf + 1) * HALF
        sp.dma(at, at[:], AT, atv[:, :, h0:h1])
        subs = []
        o = h0
        while o < h1:
            lim = TL if o < TL else TO
            n = min(512, min(h1, lim) - o)
            subs.append((o, n))
            o += n
        for ct in range(8):
            wd = wds[wcnt % 2]
            wcnt += 1
            pool.dma(wd, wd[:], w_dn, w_dn.t[ct])
            for (o0, n) in subs:
                kind = 0 if o0 < TL else 1
                xcol = o0 + 1 if kind == 0 else o0 + 3
                for h2 in range(2):
                    ci = 2 * ct + h2
                    p = psC[cnt % 4]
                    xn = xns[cnt % 3]
                    ob = outs[cnt % 3]
                    cnt += 1
                    sp.dma(xn, xn[:, 0:n], XN, XN.t[ci, :, xcol:xcol + n])
                    for k in range(HC):
                        pe.op(lambda e: e.matmul(p[:, 0:n], wd[:, k, h2 * 128:h2 * 128 + 128], at[:, k, o0 - h0:o0 - h0 + n],
                                                 start=(k == 0), stop=(k == HC - 1)), [p], [wd, at])
                    dve.op(lambda e: e.scalar_tensor_tensor(ob[:, 0:n], p[:, 0:n], vt[:, 7 + kind, ci:ci + 1], xn[:, 0:n], ALU.mult, ALU.add), [ob], [p, vt, xn])
                    sp.dma(xo, xo.t[ci, :, o0:o0 + n], ob, ob[:, 0:n])
    fw.pop_scope()
    sp.wait_buf(xo)
    if with_k1:
        fw.pfx = "k1_"
        fw.push_scope()
        build_k1(T_lat=TL, T_ctx=TC, fw=fw, xT=xo)
        fw.pop_scope()
        fw.pfx = ""
    fw.close()
    return fw.nc


def tile_w(w):
    K, N = w.shape
    return np.ascontiguousarray(w.reshape(K // 128, 128, N // 256, 256).transpose(2, 1, 0, 3))


def vec_pk(v):
    return np.ascontiguousarray(v.reshape(-1, 128).T)


OFF_QA, OFF_KA, OFF_VA, OFF_QB, OFF_KB, OFF_VB, OFF_POOL, OFF_HY = 0, 512, 1024, 1536, 2048, 2176, 2304, 2816
SEQ = 8192
NCTX = 256
_CORES = list(range(8))
_PROGS = {}
_N = {"launches": 0}


def _prog(name, builder):
    if name not in _PROGS:
        _PROGS[name] = builder()
    return _PROGS[name]


def _run(name, builder, ins):
    nc = _prog(name, builder)
    res = run_bass_kernel_spmd(nc, ins, core_ids=_CORES)
    return res.results


def _build_k3k1():
    return build_k3(with_k1=True)


def _k1_x(XT_lat, XT_ctx, k):
    b, q = divmod(k, 4)
    xT = np.concatenate([XT_lat[b][:, q * 2048:(q + 1) * 2048], XT_ctx[b][:, q * 64:(q + 1) * 64]], axis=1)
    return np.ascontiguousarray(xT.reshape(16, 128, 2112))


def _k1_in(l, k, mod, norm1_g, wt):
    b = k // 4
    m = mod[l].reshape(3, 6, -1)
    vec = np.stack([vec_pk(norm1_g[l]), vec_pk(m[b, 1]), vec_pk(m[b, 0]), vec_pk(m[2, 1]), vec_pk(m[2, 0])], axis=1)
    return {"vec": np.ascontiguousarray(vec), "w": wt}


def _seg(aT, lo, hi):
    F, S = aT.shape
    out = np.zeros((F, hi - lo + 2), np.float32)
    l2, h2 = max(lo - 1, 0), min(hi + 1, S)
    out[:, l2 - (lo - 1):h2 - (lo - 1)] = aT[:, l2:h2]
    return out


def kernel(x, c, ctx, c_ctx, w_mod, b_mod, norm1_g, norm2_g, w_in, w_out, qk_gain, diff_lam, diff_subln,
           win_sink, pool_w, pool_scale, hy_conv_w, hy_conv_b, hy_w1, hy_b1, hy_w2, hy_b2, hy_w3, hy_freq,
           hy_bias, ffn_w_in, ffn_conv_w, ffn_conv_b, ffn_w_out):
    f32 = lambda a: np.ascontiguousarray(np.asarray(a), dtype=np.float32)
    (x, c, ctx, c_ctx, w_mod, b_mod, norm1_g, norm2_g, w_in, w_out, qk_gain, diff_lam, diff_subln, win_sink, pool_w,
     pool_scale, hy_conv_w, hy_conv_b, hy_w1, hy_b1, hy_w2, hy_b2, hy_w3, hy_freq, hy_bias, ffn_w_in, ffn_conv_w,
     ffn_conv_b, ffn_w_out) = [f32(a) for a in (x, c, ctx, c_ctx, w_mod, b_mod, norm1_g, norm2_g, w_in, w_out, qk_gain,
                                                diff_lam, diff_subln, win_sink, pool_w, pool_scale, hy_conv_w, hy_conv_b,
                                                hy_w1, hy_b1, hy_w2, hy_b2, hy_w3, hy_freq, hy_bias, ffn_w_in, ffn_conv_w,
                                                ffn_conv_b, ffn_w_out)]
    B, L, Dm = x.shape
    NL = w_mod.shape[0]
    T = L + NCTX
    r = _run("k0", build_k0, [k0_inputs(k, c, c_ctx, w_mod, b_mod) for k in _CORES])
    mod = np.concatenate([r[k]["mod"] for k in _CORES], axis=2)
    XT_lat = [np.ascontiguousarray(x[b].T) for b in range(B)]
    XT_ctx = [np.ascontiguousarray(ctx[b].T) for b in range(B)]
    CSt = rope_tables(L, NCTX, 2)
    mk = band_masks()
    zfL, t01L = hy_feats(L)
    zfC, t01C = hy_feats(NCTX)
    ident = np.eye(128, dtype=np.float32)
    anti = np.ascontiguousarray(ident[::-1])
    ndel = hy_ndelta()
    for l in range(NL):
        m = mod[l].reshape(3, 6, Dm)
        wt_in = tile_w(w_in[l]) if l == 0 else None
        if l == 0:
            r = _run("k1", build_k1, [dict(xT=_k1_x(XT_lat, XT_ctx, k), **_k1_in(l, k, mod, norm1_g, wt_in)) for k in _CORES])
            pts = [r[k]["pT"].reshape(4352, 2112) for k in _CORES]
            del r
        PT_lat = [np.concatenate([pts[b * 4 + q][:, :2048] for q in range(4)], axis=1) for b in range(B)]
        PT_ctx = [np.concatenate([pts[b * 4 + q][:, 2048:] for q in range(4)], axis=1) for b in range(B)]
        del pts
        OT_lat = [np.zeros((Dm, L), np.float32) for _ in range(B)]
        OT_ctx = [np.zeros((Dm, NCTX), np.float32) for _ in range(B)]
        lambda_init = 0.8 - 0.6 * math.exp(-0.3 * l)
        g0 = np.tile(qk_gain[l, 0], 2)
        g1 = np.tile(qk_gain[l, 1], 2)
        ins_a = []
        for k in _CORES:
            b, h = divmod(k, 4)
            rq = slice(OFF_QA + h * 128, OFF_QA + (h + 1) * 128)
            rk = slice(OFF_KA + h * 128, OFF_KA + (h + 1) * 128)
            rv = slice(OFF_VA + h * 128, OFF_VA + (h + 1) * 128)
            q_full = np.ascontiguousarray(np.concatenate([PT_lat[b][rq], PT_ctx[b][rq]], axis=1))
            k_full = np.ascontiguousarray(np.concatenate([PT_ctx[b][rk], PT_lat[b][rk]], axis=1))
            v_full = np.concatenate([PT_ctx[b][rv], PT_lat[b][rv]], axis=1).T
            sm = np.zeros((128, 8), np.float32)
            sm[:, 0] = g0
            sm[:, 1] = swap_halves(g0[:, None])[:, 0]
            sm[:, 2] = g1
            sm[:, 3] = swap_halves(g1[:, None])[:, 0]
            sm[:, 4] = diff_subln[l]
            sm[:, 5] = lambda_init
            sm[:, 6] = 1.0 - lambda_init
            ins_a.append({"qT": q_full, "qsT": swap_halves(q_full), "kT": k_full, "ksT": swap_halves(k_full),
                        "v": np.ascontiguousarray(v_full.reshape(T // 128, 128, 128).transpose(1, 0, 2)),
                        "CS": CSt, "sm": sm, "lamp": np.ascontiguousarray(diff_lam[l].reshape(1, 256))})
        g2 = np.tile(qk_gain[l, 2], 2)
        g3 = np.tile(qk_gain[l, 3], 2)
        ins_b = []
        for k in _CORES:
            b = k // 4
            h0 = 2 * (k % 4)
            kvh = (k % 4) // 2
            rq = slice(OFF_QB + h0 * 64, OFF_QB + (h0 + 2) * 64)
            rk = slice(OFF_KB + kvh * 64, OFF_KB + (kvh + 1) * 64)
            rv = slice(OFF_VB + kvh * 64, OFF_VB + (kvh + 1) * 64)
            q_full = np.ascontiguousarray(np.concatenate([PT_lat[b][rq], PT_ctx[b][rq]], axis=1))
            k1 = np.concatenate([PT_ctx[b][rk], PT_lat[b][rk]], axis=1)
            k_full = np.ascontiguousarray(np.concatenate([k1, k1], axis=0))
            v_full = np.concatenate([PT_ctx[b][rv], PT_lat[b][rv]], axis=1).T
            sm = np.zeros((128, 8), np.float32)
            sm[:, 0] = g2
            sm[:, 1] = swap_halves(g2[:, None])[:, 0]
            sm[:, 2] = g3
            sm[:, 3] = swap_halves(g3[:, None])[:, 0]
            sm[:, 4] = win_sink[l, h0]
            sm[:, 5] = win_sink[l, h0 + 1]
            ins_b.append({"qT": q_full, "qsT": swap_halves(q_full), "kT": k_full, "ksT": swap_halves(k_full),
                        "v": np.ascontiguousarray(v_full.reshape(T // 128, 128, 64).transpose(1, 0, 2)),
                        "CS": CSt, "sm": sm, "masks": mk})
        ins_c = []
        for k in _CORES:
            b, g = divmod(k, 4)
            ru = slice(OFF_POOL + g * 128, OFF_POOL + (g + 1) * 128)
            sel, ic = pool_consts(g, L)
            ins_c.append({"uT": np.ascontiguousarray(np.concatenate([PT_lat[b][ru], PT_ctx[b][ru]], axis=1)), "wsel": sel, "invc": ic,
                        "wl": np.ascontiguousarray(pool_w[l, g]), "ls": np.ascontiguousarray(pool_scale[l, g * 128:(g + 1) * 128, None])})
        ins_d = []
        for k in _CORES:
            c0 = 64 * k
            uL, uC = [], []
            smh = np.zeros((128, 16), np.float32)
            for part in range(3):
                rr = slice(OFF_HY + part * 512 + c0, OFF_HY + part * 512 + c0 + 64)
                uL.append(np.concatenate([PT_lat[0][rr], PT_lat[1][rr]], axis=0))
                uC.append(np.concatenate([PT_ctx[0][rr], PT_ctx[1][rr]], axis=0))
                for tap in range(3):
                    smh[:, part * 3 + tap] = np.tile(hy_conv_w[l, tap, part * 512 + c0: part * 512 + c0 + 64], 2)
                smh[:, 9 + part] = np.tile(hy_conv_b[l, part * 512 + c0: part * 512 + c0 + 64], 2)
            smh[:, 12] = np.tile(hy_bias[l, c0:c0 + 64], 2)
            smh[:, 13] = np.tile(ndel[c0:c0 + 64], 2)
            w3r = hy_w3[l].reshape(64, 2, 512)[:, :, c0:c0 + 64]
            fb = np.zeros((64, 4), np.float32)
            fb[:, 0] = hy_freq[l]
            fb[:, 1] = hy_b1[l]
            fb[:, 2] = hy_b2[l]
            ins_d.append({"uT": np.ascontiguousarray(np.stack(uL)), "ucT": np.ascontiguousarray(np.stack(uC)), "sm": smh,
                        "w1": np.ascontiguousarray(hy_w1[l]), "w2": np.ascontiguousarray(hy_w2[l]),
                        "w3": np.ascontiguousarray(np.concatenate([w3r, w3r], axis=2)), "fb": fb,
                        "zfL": zfL, "t01L": t01L, "zfC": zfC, "t01C": t01C, "ident": ident, "anti": anti})
        ins = []
        for k in _CORES:
            dct = {}
            for pfx, lst in (("a_", ins_a), ("b_", ins_b), ("c_", ins_c), ("d_", ins_d)):
                for kk, vv in lst[k].items():
                    dct[pfx + kk] = vv
            ins.append(dct)
        del ins_a, ins_b, ins_c, ins_d
        r = _run("mix", build_mix, ins)
        for k in _CORES:
            b, h = divmod(k, 4)
            o = r[k]["a_oT"]
            OT_lat[b][h * 128:(h + 1) * 128] = o[:, :L]
            OT_ctx[b][h * 128:(h + 1) * 128] = o[:, L:]
            h0 = 2 * (k % 4)
            o = r[k]["b_oT"].reshape(128, T)
            OT_lat[b][512 + h0 * 64:512 + (h0 + 2) * 64] = o[:, :L]
            OT_ctx[b][512 + h0 * 64:512 + (h0 + 2) * 64] = o[:, L:]
            g = k % 4
            o = r[k]["c_yT"]
            OT_lat[b][1024 + g * 128:1024 + (g + 1) * 128] = o[:, :L]
            OT_ctx[b][1024 + g * 128:1024 + (g + 1) * 128] = o[:, L:]
            c0 = 64 * k
            o = r[k]["d_oT"]
            oc = r[k]["d_ocT"]
            for bb in range(B):
                OT_lat[bb][1536 + c0:1536 + c0 + 64] = o[bb * 64:(bb + 1) * 64]
                OT_ctx[bb][1536 + c0:1536 + c0 + 64] = oc[bb * 64:(bb + 1) * 64]
        del r, ins
        del PT_lat, PT_ctx
        wo_t = tile_w(w_out[l])
        wt_next = tile_w(w_in[l + 1]) if l + 1 < NL else None
        wu_t = tile_w(ffn_w_in[l])
        wd_t = tile_w(ffn_w_out[l])
        cwp = np.ascontiguousarray(np.stack([vec_pk(ffn_conv_w[l, 0]), vec_pk(ffn_conv_w[l, 1]), vec_pk(ffn_conv_w[l, 2]),
                                             vec_pk(ffn_conv_b[l])], axis=1))
        ins = []
        for k in _CORES:
            b, q = divmod(k, 4)
            ocol = np.concatenate([_seg(OT_lat[b], q * 2048, (q + 1) * 2048), _seg(OT_ctx[b], q * 64, (q + 1) * 64)], axis=1)
            xcol = np.concatenate([_seg(XT_lat[b], q * 2048, (q + 1) * 2048), _seg(XT_ctx[b], q * 64, (q + 1) * 64)], axis=1)
            TW = ocol.shape[1]
            vec = np.stack([vec_pk(m[b, 2]), vec_pk(m[2, 2]), vec_pk(norm2_g[l]), vec_pk(m[b, 4]), vec_pk(m[b, 3]),
                            vec_pk(m[2, 4]), vec_pk(m[2, 3]), vec_pk(m[b, 5]), vec_pk(m[2, 5])], axis=1)
            eg = np.ones((128, 4), np.float32)
            if q == 0:
                eg[:, 0] = 0
                eg[:, 2] = 0
            if q == 3:
                eg[:, 1] = 0
                eg[:, 3] = 0
            dct = {"oT": np.ascontiguousarray(ocol.reshape(16, 128, TW)), "xT": np.ascontiguousarray(xcol.reshape(16, 128, TW)),
                   "vec": np.ascontiguousarray(vec), "edge": eg, "w_out": wo_t, "w_up": wu_t, "cw": cwp, "w_dn": wd_t}
            if l + 1 < NL:
                for kk, vv in _k1_in(l + 1, k, mod, norm1_g, wt_next).items():
                    dct["k1_" + kk] = vv
            ins.append(dct)
        if l + 1 < NL:
            r = _run("k3k1", _build_k3k1, ins)
            pts = [r[k]["k1_pT"].reshape(4352, 2112) for k in _CORES]
        else:
            r = _run("k3", build_k3, ins)
        for k in _CORES:
            b, q = divmod(k, 4)
            o = r[k]["xo"].reshape(Dm, 2112)
            XT_lat[b][:, q * 2048:(q + 1) * 2048] = o[:, :2048]
            XT_ctx[b][:, q * 64:(q + 1) * 64] = o[:, 2048:]
        del r, ins
    return np.ascontiguousarray(np.stack([XT_lat[b].T for b in range(B)])).astype(np.float32)
```
